# Optimizing a Trainium2 kernel written in Bass

```python
import math
import jax, jax.numpy as jnp
from jax import lax
import numpy as np

D_MODEL = 1024
BATCH = 32
SEQ = 2048
DEPTH = 1

N_DIFF_HEADS = 4
DIFF_HEAD_DIM = 64
DIFF_WIDTH = N_DIFF_HEADS * 2 * DIFF_HEAD_DIM
N_MLA_HEADS = 4
MLA_NOPE_DIM = 128
MLA_ROPE_DIM = 64
MLA_QK_DIM = MLA_NOPE_DIM + MLA_ROPE_DIM
MLA_V_DIM = 128
MLA_Q_RANK = 384
MLA_KV_RANK = 256
MLA_WIDTH = N_MLA_HEADS * MLA_V_DIM
MIX_WIDTH = DIFF_WIDTH + MLA_WIDTH
IN_WIDTH = 3 * DIFF_WIDTH + MLA_Q_RANK + MLA_KV_RANK + MLA_ROPE_DIM
IN_SPLITS = [DIFF_WIDTH, 2 * DIFF_WIDTH, 3 * DIFF_WIDTH,
             3 * DIFF_WIDTH + MLA_Q_RANK, 3 * DIFF_WIDTH + MLA_Q_RANK + MLA_KV_RANK]
D_FF = -(-8 * D_MODEL // (3 * 256)) * 256
N_MOD = 6
Q_BLOCK = 128
ROPE_THETA = 10000.0
EPS = 1e-6

kernel_name = "hybrid_diffattn_mla_adaln_block"


def rms_norm(x, g):
    xf = x.astype(jnp.float32)
    y = xf * lax.rsqrt(jnp.mean(xf * xf, axis=-1, keepdims=True) + EPS)
    return (y * g.astype(jnp.float32)).astype(x.dtype)


def apply_rope(x, cos, sin):
    half = x.shape[-1] // 2
    x1, x2 = x[..., :half], x[..., half:]
    return jnp.concatenate([x1 * cos - x2 * sin, x1 * sin + x2 * cos], axis=-1)


def _scores(q, k, start, end, scale):
    s = jnp.einsum('bqhd,bkhd->bhqk', q[:, start:end], k[:, :end],
                   preferred_element_type=jnp.float32)
    return s * scale


def diff_attention(q1, q2, k1, k2, v, lam, slopes):
    S = q1.shape[1]
    scale = DIFF_HEAD_DIM ** -0.5
    outs = []
    for start in range(0, S, Q_BLOCK):
        end = start + Q_BLOCK
        rel = jnp.arange(start, end)[:, None] - jnp.arange(end)[None, :]
        causal = rel >= 0
        alibi = -slopes[:, None, None] * rel.astype(jnp.float32)[None]
        s1 = jnp.where(causal, _scores(q1, k1, start, end, scale) + alibi, -jnp.inf)
        s2 = jnp.where(causal, _scores(q2, k2, start, end, scale) + alibi, -jnp.inf)
        p = jax.nn.softmax(s1, axis=-1) - lam * jax.nn.softmax(s2, axis=-1)
        outs.append(jnp.einsum('bhqk,bkhd->bqhd', p.astype(v.dtype), v[:, :end]))
    return jnp.concatenate(outs, axis=1)


def causal_attention(q, k, v, scale):
    S = q.shape[1]
    outs = []
    for start in range(0, S, Q_BLOCK):
        end = start + Q_BLOCK
        causal = (jnp.arange(start, end)[:, None] - jnp.arange(end)[None, :]) >= 0
        s = jnp.where(causal, _scores(q, k, start, end, scale), -jnp.inf)
        p = jax.nn.softmax(s, axis=-1)
        outs.append(jnp.einsum('bhqk,bkhd->bqhd', p.astype(v.dtype), v[:, :end]))
    return jnp.concatenate(outs, axis=1)


def hybrid_mixer(h, w_in, lq1, lk1, lq2, lk2, g_diff_out, g_q_lat, w_q_up,
                 g_kv_lat, w_kv_up, w_out, cos, sin, lambda_init):
    B, S, _ = h.shape
    proj = h @ w_in
    dq, dk, dv, q_lat, kv_lat, k_rope = jnp.split(proj, IN_SPLITS, axis=-1)

    dq = dq.reshape(B, S, N_DIFF_HEADS, 2, DIFF_HEAD_DIM)
    dk = dk.reshape(B, S, N_DIFF_HEADS, 2, DIFF_HEAD_DIM)
    dv = dv.reshape(B, S, N_DIFF_HEADS, 2 * DIFF_HEAD_DIM)
    lam = (jnp.exp(jnp.sum(lq1.astype(jnp.float32) * lk1.astype(jnp.float32)))
           - jnp.exp(jnp.sum(lq2.astype(jnp.float32) * lk2.astype(jnp.float32)))
           + lambda_init)
    slopes = 2.0 ** (-8.0 * jnp.arange(1, N_DIFF_HEADS + 1, dtype=jnp.float32) / N_DIFF_HEADS)
    o_diff = diff_attention(dq[..., 0, :], dq[..., 1, :], dk[..., 0, :], dk[..., 1, :],
                            dv, lam, slopes)
    o_diff = rms_norm(o_diff, g_diff_out) * (1.0 - lambda_init)

    q = (rms_norm(q_lat, g_q_lat) @ w_q_up).reshape(B, S, N_MLA_HEADS, MLA_QK_DIM)
    q_nope, q_pe = q[..., :MLA_NOPE_DIM], q[..., MLA_NOPE_DIM:]
    q_pe = apply_rope(q_pe, cos[:, None, :], sin[:, None, :])
    kv = (rms_norm(kv_lat, g_kv_lat) @ w_kv_up).reshape(B, S, N_MLA_HEADS, MLA_NOPE_DIM + MLA_V_DIM)
    k_nope, v = kv[..., :MLA_NOPE_DIM], kv[..., MLA_NOPE_DIM:]
    k_pe = apply_rope(k_rope, cos, sin)
    k_pe = jnp.broadcast_to(k_pe[:, :, None, :], (B, S, N_MLA_HEADS, MLA_ROPE_DIM))
    q_full = jnp.concatenate([q_nope, q_pe], axis=-1)
    k_full = jnp.concatenate([k_nope, k_pe], axis=-1)
    o_mla = causal_attention(q_full, k_full, v, MLA_QK_DIM ** -0.5)

    merged = jnp.concatenate([o_diff.reshape(B, S, DIFF_WIDTH),
                              o_mla.reshape(B, S, MLA_WIDTH)], axis=-1)
    return merged @ w_out


def swiglu(h, w_gate, w_up, w_down):
    return (jax.nn.silu(h @ w_gate) * (h @ w_up)) @ w_down


def setup_inputs(seed: int = 0) -> dict:
    key = jax.random.key(seed)
    ks = jax.random.split(key, 24)
    f32 = jnp.float32

    def w(k, fan_in, fan_out, mult=1.0):
        return jax.random.normal(k, (DEPTH, fan_in, fan_out), f32) * (fan_in ** -0.5) * mult

    def gain(k, n):
        return 1.0 + 0.1 * jax.random.normal(k, (DEPTH, n), f32)

    gate_offset = jnp.repeat(jnp.array([0.0, 0.0, 1.0, 0.0, 0.0, 1.0], f32), D_MODEL)
    return {
        "x": jax.random.normal(ks[0], (BATCH, SEQ, D_MODEL), f32),
        "c": jax.random.normal(ks[1], (BATCH, D_MODEL), f32),
        "w_ada": w(ks[2], D_MODEL, N_MOD * D_MODEL, 0.1),
        "b_ada": 0.02 * jax.random.normal(ks[3], (DEPTH, N_MOD * D_MODEL), f32) + gate_offset[None],
        "g_mix": gain(ks[4], D_MODEL),
        "w_in": w(ks[5], D_MODEL, IN_WIDTH),
        "lambda_q1": 0.1 * jax.random.normal(ks[6], (DEPTH, DIFF_HEAD_DIM), f32),
        "lambda_k1": 0.1 * jax.random.normal(ks[7], (DEPTH, DIFF_HEAD_DIM), f32),
        "lambda_q2": 0.1 * jax.random.normal(ks[8], (DEPTH, DIFF_HEAD_DIM), f32),
        "lambda_k2": 0.1 * jax.random.normal(ks[9], (DEPTH, DIFF_HEAD_DIM), f32),
        "g_diff_out": gain(ks[10], 2 * DIFF_HEAD_DIM),
        "g_q_lat": gain(ks[11], MLA_Q_RANK),
        "w_q_up": w(ks[12], MLA_Q_RANK, N_MLA_HEADS * MLA_QK_DIM),
        "g_kv_lat": gain(ks[13], MLA_KV_RANK),
        "w_kv_up": w(ks[14], MLA_KV_RANK, N_MLA_HEADS * (MLA_NOPE_DIM + MLA_V_DIM)),
        "w_out": w(ks[15], MIX_WIDTH, D_MODEL),
        "g_ffn": gain(ks[16], D_MODEL),
        "w_ffn_gate": w(ks[17], D_MODEL, D_FF),
        "w_ffn_up": w(ks[18], D_MODEL, D_FF),
        "w_ffn_down": w(ks[19], D_FF, D_MODEL),
        "g_final": 1.0 + 0.1 * jax.random.normal(ks[20], (D_MODEL,), f32),
    }


def reference(x, c, w_ada, b_ada, g_mix, w_in, lambda_q1, lambda_k1, lambda_q2, lambda_k2,
              g_diff_out, g_q_lat, w_q_up, g_kv_lat, w_kv_up, w_out, g_ffn,
              w_ffn_gate, w_ffn_up, w_ffn_down, g_final):
    S = x.shape[1]
    inv_freq = ROPE_THETA ** (-jnp.arange(0, MLA_ROPE_DIM, 2, dtype=jnp.float32) / MLA_ROPE_DIM)
    ang = jnp.arange(S, dtype=jnp.float32)[:, None] * inv_freq[None, :]
    cos = jnp.cos(ang).astype(x.dtype)
    sin = jnp.sin(ang).astype(x.dtype)
    c_act = jax.nn.silu(c)
    for l in range(DEPTH):
        lambda_init = 0.8 - 0.6 * math.exp(-0.3 * l)
        mod = c_act @ w_ada[l] + b_ada[l]
        sh_a, sc_a, g_a, sh_f, sc_f, g_f = jnp.split(mod[:, None, :], N_MOD, axis=-1)
        h = rms_norm(x, g_mix[l]) * (1.0 + sc_a) + sh_a
        x = x + g_a * hybrid_mixer(h, w_in[l], lambda_q1[l], lambda_k1[l], lambda_q2[l],
                                   lambda_k2[l], g_diff_out[l], g_q_lat[l], w_q_up[l],
                                   g_kv_lat[l], w_kv_up[l], w_out[l], cos, sin, lambda_init)
        h = rms_norm(x, g_ffn[l]) * (1.0 + sc_f) + sh_f
        x = x + g_f * swiglu(h, w_ffn_gate[l], w_ffn_up[l], w_ffn_down[l])
    return rms_norm(x, g_final)
```

```python
import contextlib
import math
import numpy as np
import concourse.bass as bass
import concourse.mybir as mybir
from concourse.bass_utils import run_bass_kernel_spmd

F32 = mybir.dt.float32
BF16 = mybir.dt.bfloat16
AF = mybir.ActivationFunctionType
ALU = mybir.AluOpType

NCORES = 8
SEQ = 2048
D = 1024
BPC = 4
CH = 512
NCH = SEQ // CH
DFF = 2816
NJ = DFF // 128
INW = 2240
EPS = 1e-6
LAMBDA_INIT = 0.8 - 0.6 * math.exp(-0.3 * 0)
SLOPES = [2.0 ** (-8.0 * (i + 1) / 4) for i in range(4)]
SAME_ENGINE_SYNC = True
import os
STOP = os.environ.get("KSTOP", "")
DBG = bool(os.environ.get("KDBG", ""))
DBGW = 48000
DBG_MAP = {}
LAST_DBG = {}


class _Stop(Exception):
    pass


def ck(label):
    if STOP == label:
        raise _Stop()


class Prog:
    def __init__(self, nc, es):
        self.nc = nc
        self.es = es
        self.eng = {"pe": nc.tensor, "act": nc.scalar, "dve": nc.vector, "pool": nc.gpsimd, "sp": nc.sync}
        self.esem = {k: es.enter_context(nc.semaphore("sem_" + k)) for k in self.eng}
        self.dsem = {}
        self.meta = []
        self.mode = "analyze"
        self.idx = 0
        self.nwait = 0

    def op(self, eng, fn, r=(), w=(), dma=None):
        if self.mode == "analyze":
            self.meta.append((eng, tuple(r), tuple(w), dma))
            return
        i = self.idx
        self.idx += 1
        e = self.eng[eng]
        need = {}
        for j in self.deps[i]:
            key, val = self.tick[j]
            if need.get(key, 0) < val:
                need[key] = val
        wd = self.waited[eng]
        for key, val in need.items():
            if wd.get(key, 0) >= val:
                continue
            wd[key] = val
            sem = self.dsem[key[1]] if key[0] == "d" else self.esem[key[1]]
            e.wait_ge(sem, val)
            self.nwait += 1
        inst = fn(e)
        if dma is not None:
            inst.then_inc(self.dsem[dma], 16)
        elif i in self.signaled:
            inst.then_inc(self.esem[eng], 1)

    def analyze(self):
        ops = self.meta
        last_w = {}
        readers = {}
        last_dma = {}
        deps = []
        for i, (eng, r, w, dma) in enumerate(ops):
            d = set()
            for x in r:
                if x in last_w:
                    d.add(last_w[x])
                if x[0] == "P" and x[1:].isdigit():
                    rd = readers.get(x)
                    if rd:
                        d.update(j for k_, j in rd.items() if k_ != eng)
            for x in w:
                if x in last_w:
                    d.add(last_w[x])
                rd = readers.get(x)
                if rd:
                    d.update(rd.values())
            if dma is not None and dma in last_dma:
                d.add(last_dma[dma])
            d.discard(i)
            for x in w:
                last_w[x] = i
                readers[x] = {}
            for x in r:
                if x in w:
                    continue
                key = eng if dma is None else ("dma", i)
                readers.setdefault(x, {})[key] = i
            if dma is not None:
                last_dma[dma] = i
            kept = []
            for j in d:
                ej, _, _, dj = ops[j]
                if dj is None and dma is None and ej == eng and (eng == "pe" or not SAME_ENGINE_SYNC):
                    continue
                kept.append(j)
            deps.append(kept)
        signaled = set()
        for kept in deps:
            for j in kept:
                if ops[j][3] is None:
                    signaled.add(j)
        tick = {}
        cnt = {k: 0 for k in self.eng}
        dcnt = {}
        for i, (eng, r, w, dma) in enumerate(ops):
            if dma is not None:
                if dma not in self.dsem:
                    self.dsem[dma] = self.es.enter_context(self.nc.semaphore("d_" + dma))
                dcnt[dma] = dcnt.get(dma, 0) + 16
                tick[i] = (("d", dma), dcnt[dma])
            elif i in signaled:
                cnt[eng] += 1
                tick[i] = (("e", eng), cnt[eng])
        self.deps, self.signaled, self.tick, self.dcnt = deps, signaled, tick, dcnt
        self.waited = {k: {} for k in self.eng}

    def run(self, body):
        self.mode = "analyze"
        try:
            body()
        except _Stop:
            pass
        self.analyze()
        self.mode = "emit"
        self.idx = 0
        try:
            body()
        except _Stop:
            pass
        assert self.idx == len(self.meta), (self.idx, len(self.meta))
        sp = self.eng["sp"]
        for name, sem in self.dsem.items():
            if self.waited["sp"].get(("d", name), 0) < self.dcnt[name]:
                sp.wait_ge(sem, self.dcnt[name])
        return len(self.meta), self.nwait


def build():
    nc = bass.Bass("TRN2", target_bir_lowering=False)

    def dram(name, shape, dt=F32, kind="ExternalInput"):
        return nc.dram_tensor(name, list(shape), dt, kind=kind).ap()

    x_d = dram("x", [BPC * SEQ, D])
    out_d = dram("out", [BPC * SEQ, D], kind="ExternalOutput")
    cT_d = dram("cT", [128, 8 * BPC])
    wada_d = dram("w_ada", [D, 6 * D])
    badaT_d = dram("b_adaT", [128, 48])
    bgate_d = dram("b_gate", [1, 2 * D])
    gmixT_d = dram("g_mixT", [128, 8])
    gffnT_d = dram("g_ffnT", [128, 8])
    win_d = dram("w_in", [D, INW])
    lam_d = dram("lam", [1, 256])
    gdiff_d = dram("g_diff", [128, 1])
    gqlT_d = dram("g_qlT", [128, 3])
    wqup_d = dram("w_qup", [384, 768])
    gkvlT_d = dram("g_kvlT", [128, 2])
    wkvup_d = dram("w_kvup", [256, 1024])
    wout_d = dram("w_out", [D, D])
    wgu_d = dram("w_gu", [NJ, 128, 2048])
    wdn_d = dram("w_dn", [DFF, D])
    gfin_d = dram("g_final", [1, D])
    gates_d = dram("gates_scr", [BPC, 2 * D], kind="Internal")
    rope_d = dram("rope_scr", [2, 64, SEQ], kind="Internal")
    dbg_d = dram("dbg", [128, DBGW], kind="ExternalOutput") if DBG else None

    es = contextlib.ExitStack()
    with es:
        def sb(name, shape, dt):
            return es.enter_context(nc.sbuf_tensor(name, list(shape), dt))

        W_in = sb("W_in", [128, 8 * INW], BF16)
        W_qup = sb("W_qup", [128, 3 * 768], BF16)
        W_kvup = sb("W_kvup", [128, 2 * 1024], BF16)
        W_out = sb("W_out", [128, 8 * 1024], BF16)
        dkT = sb("dkT", [128, 4 * SEQ], BF16)
        dv = sb("dv", [128, 16 * 512], BF16)
        knT = sb("knT", [128, 4 * SEQ], BF16)
        kpeT = sb("kpeT", [128, SEQ], BF16)
        vv = sb("vv", [128, 16 * 512], BF16)
        xc = sb("xc", [128, 4 * 1024], F32)
        actT = sb("actT", [128, 8 * 512], BF16)
        S = sb("S", [128, NJ * 512], BF16)
        ring = sb("ring", [128, 4 * 2048], BF16)
        Ga = sb("Ga", [128, 1024], F32)
        Gf = sb("Gf", [128, 1024], F32)
        gfin = sb("gfin", [128, 1024], F32)
        sgt = sb("sgt", [128, 2 * 512], F32)
        ident = sb("ident", [128, 128], BF16)
        ones = sb("ones", [128, 128], BF16)
        tri = sb("tri", [128, 128], BF16)
        alibi = sb("alibi", [128, 64], F32)
        AB = sb("AB", [128, BPC * 2 * 2 * 8], F32)
        st = sb("st", [128, 16], F32)
        cols = sb("cols", [128, 32], F32)
        PS = [es.enter_context(nc.psum_tensor(f"ps{k}", [128, 512], F32)) for k in range(8)]

        P = Prog(nc, es)
        op = P.op

        def body():
            Sf = S[:].bitcast(F32)
            modT = sgt[:, 256:256 + 48 * BPC]
            xcb = xc[:].bitcast(BF16)

            def slot(j, n=1):
                return S[:, j * 512:(j + n) * 512]

            def slotf(j):
                return Sf[:, j * 256:j * 256 + 512]

            def SR(j, n=1):
                return [f"S{k}" for k in range(j, j + n)]

            dbg_off = [0]

            def dump(label, ap, res, nparts=128):
                if not DBG or label in DBG_MAP and P.mode == "analyze":
                    return
                wdt = ap.shape[-1]
                if P.mode == "analyze":
                    DBG_MAP[label] = (dbg_off[0], wdt, nparts)
                o = DBG_MAP[label][0]
                dbg_off[0] = o + wdt
                op("pool", lambda e: e.dma_start(out=dbg_d[0:nparts, o:o + wdt], in_=ap), r=res, w=["dbg_" + label],
                   dma="dbg")

            XC_ALL = [f"xc{t}h{h}" for t in range(4) for h in range(2)]
            S_ALL = SR(0, NJ)
            ACT_ALL = [f"act{k}t{t}" for k in range(8) for t in range(4)]

            def ACT(k):
                return [f"act{k}t{t}" for t in range(4)]

            def AB_col(b, af, ab, kc):
                o = ((b * 2 + af) * 2 + ab) * 8 + kc
                return AB[:, o:o + 1]

            C_NLAM, C_GDP, C_GD, C_GQ, C_GKV = 0, 1, 2, 3, 6

            tmpf = Sf[:, 0:128]
            op("pool", lambda e: e.memset(tmpf, 1.0), w=SR(0))
            op("pool", lambda e: e.affine_select(out=tmpf, in_=tmpf, pattern=[[-1, 128]], compare_op=ALU.is_equal,
                                                 fill=0.0, base=0, channel_multiplier=1), r=SR(0), w=SR(0))
            op("dve", lambda e: e.tensor_copy(out=ident[:], in_=tmpf), r=SR(0), w=["ident"])
            tmpf2 = Sf[:, 256:384]
            op("pool", lambda e: e.memset(tmpf2, 1.0), w=SR(1))
            op("pool", lambda e: e.affine_select(out=tmpf2, in_=tmpf2, pattern=[[1, 128]], compare_op=ALU.is_ge,
                                                 fill=0.0, base=0, channel_multiplier=-1), r=SR(1), w=SR(1))
            op("dve", lambda e: e.tensor_copy(out=tri[:], in_=tmpf2), r=SR(1), w=["tri"])
            op("dve", lambda e: e.memset(ones[:], 1.0), w=["ones"])
            op("dve", lambda e: e.memset(cols[:, 15:16], EPS), w=["cols"])
            tmpa = Sf[:, 512:528]
            op("pool", lambda e: e.iota(tmpa, pattern=[[-128, 16]], base=-127, channel_multiplier=1,
                                        allow_small_or_imprecise_dtypes=True), w=SR(2))
            tmpa0 = Sf[:, 768:784]
            op("pool", lambda e: e.iota(tmpa0, pattern=[[-128, 16]], base=0, channel_multiplier=1,
                                        allow_small_or_imprecise_dtypes=True), w=SR(3))
            op("dve", lambda e: e.tensor_scalar(out=alibi[:, 0:16], in0=tmpa0, scalar1=SLOPES[0], scalar2=None,
                                                op0=ALU.mult), r=SR(3), w=["alibi"])
            for h in range(1, 4):
                op("dve", lambda e, h=h: e.tensor_scalar(out=alibi[:, h * 16:(h + 1) * 16], in0=tmpa, scalar1=SLOPES[h],
                                                         scalar2=None, op0=ALU.mult), r=SR(2), w=["alibi"])

            ck("consts")
            win_v = win_d.rearrange("(kc p) n -> p kc n", p=128)
            for kc in range(8):
                op("pool", lambda e, kc=kc: e.dma_start(out=W_in[:, kc * INW:(kc + 1) * INW], in_=win_v[:, kc, :]),
                   w=["W_in"], dma=f"w{kc % 2}")
            wq_v = wqup_d.rearrange("(kc p) n -> p kc n", p=128)
            for kc in range(3):
                op("pool", lambda e, kc=kc: e.dma_start(out=W_qup[:, kc * 768:(kc + 1) * 768], in_=wq_v[:, kc, :]),
                   w=["W_qup"], dma="w2")
            wkv_v = wkvup_d.rearrange("(kc p) n -> p kc n", p=128)
            for kc in range(2):
                op("pool", lambda e, kc=kc: e.dma_start(out=W_kvup[:, kc * 1024:(kc + 1) * 1024], in_=wkv_v[:, kc, :]),
                   w=["W_kvup"], dma="w3")
            wo_v = wout_d.rearrange("(kc p) n -> p kc n", p=128)
            for kc in range(8):
                op("pool", lambda e, kc=kc: e.dma_start(out=W_out[:, kc * 1024:(kc + 1) * 1024], in_=wo_v[:, kc, :]),
                   w=["W_out"], dma=f"w{4 + kc % 2}")

            ck("weights")
            cTs = Sf[:, 1024:1024 + 8 * BPC]
            op("sp", lambda e: e.dma_start(out=cTs, in_=cT_d[:, :]), w=SR(4), dma="p0")
            op("sp", lambda e: e.dma_start(out=cols[:, 16:24], in_=gmixT_d[:, :]), w=["cols"], dma="p1")
            op("sp", lambda e: e.dma_start(out=cols[:, 24:32], in_=gffnT_d[:, :]), w=["cols"], dma="p1")
            op("sp", lambda e: e.dma_start(out=cols[:, C_GD:C_GD + 1], in_=gdiff_d[:, :]), w=["cols"], dma="p1")
            op("sp", lambda e: e.dma_start(out=cols[:, C_GQ:C_GQ + 3], in_=gqlT_d[:, :]), w=["cols"], dma="p1")
            op("sp", lambda e: e.dma_start(out=cols[:, C_GKV:C_GKV + 2], in_=gkvlT_d[:, :]), w=["cols"], dma="p1")
            badaT = Sf[:, 1280:1280 + 48]
            op("sp", lambda e: e.dma_start(out=badaT, in_=badaT_d[:, :]), w=SR(5), dma="p2")
            op("sp", lambda e: e.dma_start(out=gfin[:], in_=gfin_d[:, :].partition_broadcast(128)), w=["gfin"], dma="p3")
            lamb = Sf[:, 1536:1536 + 256]
            op("sp", lambda e: e.dma_start(out=lamb, in_=lam_d[:, :].partition_broadcast(128)), w=SR(6), dma="p4")
            bg = Sf[0:BPC, 2048:2048 + 2048]
            op("sp", lambda e: e.dma_start(out=bg, in_=bgate_d[:, :].partition_broadcast(BPC)), w=SR(8, 8), dma="p5")

            ck("params")
            lprod = Sf[:, 1792:1792 + 128]
            op("dve", lambda e: e.tensor_tensor(out=lprod[:, 0:64], in0=lamb[:, 0:64], in1=lamb[:, 64:128], op=ALU.mult),
               r=SR(6), w=SR(7))
            op("dve", lambda e: e.tensor_tensor(out=lprod[:, 64:128], in0=lamb[:, 128:192], in1=lamb[:, 192:256], op=ALU.mult),
               r=SR(6), w=SR(7))
            op("dve", lambda e: e.reduce_sum(out=cols[:, 8:9], in_=lprod[:, 0:64], axis=mybir.AxisListType.X), r=SR(7), w=["cols"])
            op("dve", lambda e: e.reduce_sum(out=cols[:, 9:10], in_=lprod[:, 64:128], axis=mybir.AxisListType.X), r=SR(7), w=["cols"])
            op("act", lambda e: e.activation(out=cols[:, 10:12], in_=cols[:, 8:10], func=AF.Exp), r=["cols"], w=["cols"])
            op("dve", lambda e: e.tensor_tensor(out=cols[:, C_NLAM:C_NLAM + 1], in0=cols[:, 11:12], in1=cols[:, 10:11],
                                                op=ALU.subtract), r=["cols"], w=["cols"])
            op("dve", lambda e: e.tensor_scalar(out=cols[:, C_NLAM:C_NLAM + 1], in0=cols[:, C_NLAM:C_NLAM + 1],
                                                scalar1=-LAMBDA_INIT, scalar2=None, op0=ALU.add), r=["cols"], w=["cols"])
            op("dve", lambda e: e.tensor_scalar(out=cols[:, C_GDP:C_GDP + 1], in0=cols[:, C_GD:C_GD + 1],
                                                scalar1=1.0 - LAMBDA_INIT, scalar2=None, op0=ALU.mult), r=["cols"], w=["cols"])

            ck("lam")
            scb = sgt[:].bitcast(BF16)[:, 0:8 * BPC]
            SCB = ["sgt0", "sgt1"]
            op("act", lambda e: e.activation(out=scb, in_=cTs, func=AF.Silu), r=SR(4), w=SCB)
            wada_v = wada_d.rearrange("(kc p) n -> p kc n", p=128)
            gsb = actT[:].bitcast(F32)[0:BPC, 0:2048]
            bank = 0
            for u in range(12):
                sl = u % 2
                unit = xcb[:, sl * 4096:(sl + 1) * 4096]
                ures = XC_ALL[sl * 4:(sl + 1) * 4]
                op("pool", lambda e, unit=unit, u=u: e.dma_start(
                    out=unit.rearrange("p (kc n) -> p kc n", kc=8), in_=wada_v[:, :, u * 512:(u + 1) * 512]),
                   w=ures, dma=f"ada{sl}")
                if u in (4, 5, 10, 11):
                    g = 0 if u < 6 else 1
                    half = u % 2 if u < 6 else (u - 10)
                    pb = PS[bank % 8]
                    pres = [f"P{bank % 8}"]
                    bank += 1
                    for kc in range(8):
                        op("pe", lambda e, pb=pb, unit=unit, kc=kc: e.matmul(
                            pb[0:BPC, :], scb[:, kc * BPC:(kc + 1) * BPC], unit[:, kc * 512:(kc + 1) * 512],
                            start=(kc == 0), stop=(kc == 7)), r=ures + SCB, w=pres)
                    o = g * 1024 + half * 512
                    op("dve", lambda e, pb=pb, o=o: e.tensor_tensor(out=gsb[:, o:o + 512], in0=pb[0:BPC, :],
                                                                    in1=bg[:, o:o + 512], op=ALU.add),
                       r=pres + SR(8, 8), w=ACT_ALL)
                else:
                    for mm in range(4):
                        m = 4 * u + mm
                        pb = PS[bank % 8]
                        pres = [f"P{bank % 8}"]
                        bank += 1
                        for kc in range(8):
                            op("pe", lambda e, pb=pb, unit=unit, kc=kc, mm=mm: e.matmul(
                                pb[:, 0:BPC], unit[:, kc * 512 + mm * 128:kc * 512 + (mm + 1) * 128],
                                scb[:, kc * BPC:(kc + 1) * BPC], start=(kc == 0), stop=(kc == 7)),
                               r=ures + SCB, w=pres)
                        op("dve", lambda e, pb=pb, m=m: e.tensor_scalar(
                            out=modT[:, m * BPC:(m + 1) * BPC], in0=pb[:, 0:BPC], scalar1=badaT[:, m:m + 1], scalar2=None,
                            op0=ALU.add), r=pres + SR(5), w=["sgt0"])
            op("sp", lambda e: e.dma_start(out=gates_d[:, :], in_=gsb), r=ACT_ALL, w=["gates_d"], dma="p6")
            modT3 = modT.rearrange("p (m b) -> p m b", b=BPC)
            for b in range(BPC):
                for af, (m_sh, m_sc, gcol) in enumerate([(0, 8, 16), (24, 32, 24)]):
                    oA = ((b * 2 + af) * 2 + 0) * 8
                    oB = ((b * 2 + af) * 2 + 1) * 8
                    op("dve", lambda e, b=b, m_sc=m_sc, gcol=gcol, oA=oA: e.scalar_tensor_tensor(
                        out=AB[:, oA:oA + 8], in0=modT3[:, m_sc:m_sc + 8, b], scalar=1.0, in1=cols[:, gcol:gcol + 8],
                        op0=ALU.add, op1=ALU.mult), r=["sgt0", "cols"], w=["AB"])
                    op("dve", lambda e, b=b, m_sh=m_sh, oB=oB: e.tensor_copy(out=AB[:, oB:oB + 8],
                                                                             in_=modT3[:, m_sh:m_sh + 8, b]),
                       r=["sgt0"], w=["AB"])

            ck("ada")
            xcf = xc[:]
            pos = xcf[0:64, 0:2048]
            ang = xcf[0:64, 2048:4096]
            op("pool", lambda e: e.iota(pos, pattern=[[1, 2048]], base=0, channel_multiplier=0,
                                        allow_small_or_imprecise_dtypes=True), w=XC_ALL)
            op("pool", lambda e: e.iota(cols[0:32, 12:13], pattern=[[0, 1]], base=0, channel_multiplier=1,
                                        allow_small_or_imprecise_dtypes=True), w=["cols"])
            op("pool", lambda e: e.iota(cols[32:64, 12:13], pattern=[[0, 1]], base=0, channel_multiplier=1,
                                        allow_small_or_imprecise_dtypes=True), w=["cols"])
            op("act", lambda e: e.activation(out=cols[0:64, 13:14], in_=cols[0:64, 12:13], func=AF.Exp,
                                             scale=-math.log(10000.0) / 32.0), r=["cols"], w=["cols"])
            op("dve", lambda e: e.tensor_scalar(out=ang, in0=pos, scalar1=cols[0:64, 13:14], scalar2=None, op0=ALU.mult),
               r=XC_ALL + ["cols"], w=XC_ALL)
            TWO_PI = 2.0 * math.pi
            MAGIC = 12582912.0
            PI_LO = 3.141592
            for which, off in ((0, 0.5 * math.pi), (1, 0.0)):
                dst = Sf[0:64, which * 2048:(which + 1) * 2048]
                op("dve", lambda e, off=off: e.tensor_scalar(out=pos, in0=ang, scalar1=off, scalar2=1.0 / TWO_PI, op0=ALU.add,
                                                             op1=ALU.mult), r=XC_ALL, w=XC_ALL)
                op("dve", lambda e: e.tensor_scalar(out=pos, in0=pos, scalar1=MAGIC, scalar2=None, op0=ALU.add),
                   r=XC_ALL, w=XC_ALL)
                op("dve", lambda e: e.tensor_scalar(out=pos, in0=pos, scalar1=-MAGIC, scalar2=None, op0=ALU.add),
                   r=XC_ALL, w=XC_ALL)
                op("dve", lambda e: e.scalar_tensor_tensor(out=pos, in0=pos, scalar=-TWO_PI, in1=ang, op0=ALU.mult,
                                                           op1=ALU.add), r=XC_ALL, w=XC_ALL)
                op("dve", lambda e, off=off: e.tensor_scalar(out=pos, in0=pos, scalar1=off, scalar2=None, op0=ALU.add),
                   r=XC_ALL, w=XC_ALL)
                op("dve", lambda e: e.tensor_scalar(out=pos, in0=pos, scalar1=-PI_LO, scalar2=PI_LO,
                                                    op0=ALU.max, op1=ALU.min), r=XC_ALL, w=XC_ALL)
                op("act", lambda e, dst=dst: e.activation(out=dst, in_=pos, func=AF.Sin), r=XC_ALL, w=S_ALL)
            sgn_hi = Sf[32:64, 2048:4096]
            op("dve", lambda e: e.tensor_scalar(out=sgn_hi, in0=sgn_hi, scalar1=-1.0, scalar2=None, op0=ALU.mult),
               r=S_ALL, w=S_ALL)
            op("sp", lambda e: e.dma_start(out=rope_d[0], in_=Sf[0:64, 0:2048]), r=S_ALL, w=["rope_d"], dma="p7")
            op("sp", lambda e: e.dma_start(out=rope_d[1], in_=Sf[0:64, 2048:4096]), r=S_ALL, w=["rope_d"], dma="p7")

            ck("rope")
            dump("AB", AB[:], ["AB"])
            dump("modT", modT, ["sgt0"])
            dump("cols", cols[:], ["cols"])
            dump("alibi", alibi[:], ["alibi"])
            dump("tri", tri[:], ["tri"])
            dump("ident", ident[:], ["ident"])
            evac_rr = [0]

            def evac_copy(dst, src, r, w):
                evac_rr[0] += 1
                if evac_rr[0] % 2 == 0:
                    op("act", lambda e: e.activation(out=dst, in_=src, func=AF.Copy), r=r, w=w)
                else:
                    op("dve", lambda e: e.tensor_copy(out=dst, in_=src), r=r, w=w)

            def hn_buf(t):
                return S[:, t * 1024:(t + 1) * 1024], SR(2 * t, 2)

            def norm_sq(t, col0):
                xr = [f"xc{t}h0", f"xc{t}h1"]
                xt = xc[:, t * 1024:(t + 1) * 1024]
                hn, hres = hn_buf(t)
                sres = [f"st{t}"]
                op("dve", lambda e: e.memset(st[:, col0 + t:col0 + t + 1], 0.0), w=sres)
                op("act", lambda e: e.activation(out=hn, in_=xt, func=AF.Square, accum_out=st[:, col0 + t:col0 + t + 1]),
                   r=xr + sres, w=hres + sres)

            def norm_rstd(t0, n, col0):
                sres = [f"st{t}" for t in range(t0, t0 + n)]
                src_ = st[:, col0 + t0:col0 + t0 + n]
                dst_ = st[:, col0 + 4 + t0:col0 + 4 + t0 + n]
                op("act", lambda e: e.activation(out=dst_, in_=src_, func=AF.Ln, bias=cols[:, 15:16], scale=1.0 / D),
                   r=sres + ["cols"], w=sres)
                op("act", lambda e: e.activation(out=dst_, in_=dst_, func=AF.Exp, scale=-0.5), r=sres, w=sres)

            def norm_scale(t, col0):
                xr = [f"xc{t}h0", f"xc{t}h1"]
                xt = xc[:, t * 1024:(t + 1) * 1024]
                hn, hres = hn_buf(t)
                op("dve", lambda e: e.tensor_scalar(out=hn, in0=xt, scalar1=st[:, col0 + 4 + t:col0 + 5 + t], scalar2=None,
                                                    op0=ALU.mult), r=xr + [f"st{t}"], w=hres)

            def norm_tr(b, af, t):
                hn, hres = hn_buf(t)
                pb = PS[t % 2]
                pbb = pb[:].bitcast(BF16)
                pres = [f"P{t % 2}"]
                for kc in range(8):
                    op("pe", lambda e, kc=kc: e.transpose(pbb[:, kc * 128:(kc + 1) * 128],
                                                          hn[:, kc * 128:(kc + 1) * 128], ident[:]),
                       r=hres + ["ident"], w=pres)
                for kc in range(8):
                    dst = actT[:, kc * 512 + t * 128:kc * 512 + (t + 1) * 128]
                    src_ = pbb[:, kc * 128:(kc + 1) * 128]
                    a_ = AB_col(b, af, 0, kc)
                    b_ = AB_col(b, af, 1, kc)
                    if t % 2 == 0:
                        op("dve", lambda e, dst=dst, src_=src_, a_=a_, b_=b_: e.tensor_scalar(
                            out=dst, in0=src_, scalar1=a_, scalar2=b_, op0=ALU.mult, op1=ALU.add),
                           r=pres + ["AB"], w=[f"act{kc}t{t}"])
                    else:
                        op("act", lambda e, dst=dst, src_=src_, a_=a_, b_=b_: e.activation(
                            out=dst, in_=src_, func=AF.Identity, bias=b_, scale=a_),
                           r=pres + ["AB"], w=[f"act{kc}t{t}"])

            def lat_norm(banks, n, gc0, rstd_slot, dst_act0, inv_n, ssbank):
                for i in range(n):
                    sq = actT[:, (5 + i % 2) * 512:(6 + i % 2) * 512]
                    op("act", lambda e, sq=sq, i=i: e.activation(out=sq, in_=PS[banks[i]][:], func=AF.Square),
                       r=[f"P{banks[i]}"], w=ACT(5 + i % 2))
                    op("pe", lambda e, sq=sq, i=i: e.matmul(PS[ssbank][:], ones[:], sq, start=(i == 0), stop=(i == n - 1)),
                       r=ACT(5 + i % 2) + ["ones"], w=[f"P{ssbank}"])
                rs = slotf(rstd_slot)
                op("act", lambda e: e.activation(out=rs, in_=PS[ssbank][:], func=AF.Ln, bias=cols[:, 15:16], scale=inv_n),
                   r=[f"P{ssbank}", "cols"], w=SR(rstd_slot, 2))
                op("act", lambda e: e.activation(out=rs, in_=rs, func=AF.Exp, scale=-0.5), r=SR(rstd_slot, 2), w=SR(rstd_slot, 2))
                for i in range(n):
                    dst = actT[:, (dst_act0 + i) * 512:(dst_act0 + i + 1) * 512]
                    op("dve", lambda e, dst=dst, i=i: e.scalar_tensor_tensor(
                        out=dst, in0=PS[banks[i]][:], scalar=cols[:, gc0 + i:gc0 + i + 1], in1=rs, op0=ALU.mult,
                        op1=ALU.mult), r=[f"P{banks[i]}", "cols"] + SR(rstd_slot, 2), w=ACT(dst_act0 + i))

            cosT = Sf[0:64, 12 * 256:12 * 256 + 512]
            sgnT = Sf[0:64, 14 * 256:14 * 256 + 512]
            t2 = Sf[0:64, 20 * 256:20 * 256 + 512]

            def rope(pbank, dst, dres):
                pb = PS[pbank]
                pres = [f"P{pbank}"]
                op("dve", lambda e: e.tensor_tensor(out=t2[0:32, :], in0=pb[32:64, :], in1=sgnT[32:64, :], op=ALU.mult),
                   r=pres + SR(14, 2), w=SR(20, 2))
                op("dve", lambda e: e.tensor_tensor(out=t2[32:64, :], in0=pb[0:32, :], in1=sgnT[0:32, :], op=ALU.mult),
                   r=pres + SR(14, 2), w=SR(20, 2))
                op("dve", lambda e: e.tensor_tensor(out=pb[0:64, :], in0=pb[0:64, :], in1=cosT, op=ALU.mult),
                   r=pres + SR(12, 2), w=pres)
                op("dve", lambda e: e.tensor_tensor(out=dst, in0=pb[0:64, :], in1=t2, op=ALU.add),
                   r=pres + SR(20, 2), w=dres)

            ring_ctr = [0]

            def ring_next():
                s = ring_ctr[0] % 4
                ring_ctr[0] += 1
                return s

            RING_PF = 4
            ring_issued = [0]

            def ring_issue(u):
                if u >= 2 * NJ:
                    return
                rs_ = ring_issued[0] % 4
                ring_issued[0] += 1
                if u < NJ:
                    ru = ring[:, rs_ * 2048:(rs_ + 1) * 2048]
                    op("pool", lambda e: e.dma_start(out=ru, in_=wgu_d[u]), w=[f"R{rs_}"], dma=f"rg{rs_}")
                else:
                    j = u - NJ
                    ru = ring[:, rs_ * 2048:rs_ * 2048 + 1024]
                    op("pool", lambda e: e.dma_start(out=ru, in_=wdn_d[j * 128:(j + 1) * 128, :]),
                       w=[f"R{rs_}"], dma=f"rg{rs_}")

            QSCALE_D = 64 ** -0.5
            QSCALE_M = 192 ** -0.5

            for b in range(BPC):
                op("sp", lambda e, b=b: e.dma_start(out=Ga[:], in_=gates_d[b:b + 1, 0:1024].partition_broadcast(128)),
                   r=["gates_d"], w=["Ga"], dma="ga")
                op("sp", lambda e, b=b: e.dma_start(out=Gf[:], in_=gates_d[b:b + 1, 1024:2048].partition_broadcast(128)),
                   r=["gates_d"], w=["Gf"], dma="gf")
                ck("g")
                for c in range(NCH):
                    tok0 = b * SEQ + c * CH
                    for t in range(4):
                        op("sp", lambda e, t=t, tok0=tok0: e.dma_start(out=xc[:, t * 1024:(t + 1) * 1024],
                                                                      in_=x_d[tok0 + t * 128:tok0 + (t + 1) * 128, :]),
                           w=[f"xc{t}h0", f"xc{t}h1"], dma=f"ld{t}")
                    ck("xl")
                    for t in range(4):
                        norm_sq(t, 0)
                    norm_rstd(0, 4, 0)
                    for t in range(4):
                        norm_scale(t, 0)
                        norm_tr(b, 0, t)
                    if b == 0 and c == 0:
                        dump("Ga", Ga[:], ["Ga"])
                        dump("hT", actT[:], ACT_ALL)
                        dump("st", st[:], [f"st{t}" for t in range(4)])
                    ck("s1")
                    rr = 0
                    for m in range(8):
                        k = 2 + rr % 6
                        rr += 1
                        for kc in range(8):
                            op("pe", lambda e, k=k, kc=kc, m=m: e.matmul(
                                PS[k][:], W_in[:, kc * INW + m * 128:kc * INW + (m + 1) * 128],
                                actT[:, kc * 512:(kc + 1) * 512], start=(kc == 0), stop=(kc == 7)),
                               r=["W_in"] + ACT(kc), w=[f"P{k}"])
                        if m < 4:
                            evac_copy(slot(m), PS[k][:], [f"P{k}"], SR(m))
                        else:
                            h = m - 4
                            evac_copy(dkT[:, h * SEQ + c * CH:h * SEQ + (c + 1) * CH], PS[k][:], [f"P{k}"], [f"dk{h}c{c}"])
                    for t in range(4):
                        k = 2 + rr % 6
                        rr += 1
                        for kc in range(8):
                            op("pe", lambda e, k=k, kc=kc, t=t: e.matmul(
                                PS[k][:], actT[:, kc * 512 + t * 128:kc * 512 + (t + 1) * 128],
                                W_in[:, kc * INW + 1024:kc * INW + 1536], start=(kc == 0), stop=(kc == 7)),
                               r=["W_in", f"act{kc}t{t}"], w=[f"P{k}"])
                        tile = c * 4 + t
                        evac_copy(dv[:, tile * 512:(tile + 1) * 512], PS[k][:], [f"P{k}"], [f"dv{tile}"])
                    if b == 0 and c == 0:
                        dump("dqT", slot(0, 4), SR(0, 4))
                        dump("dkT", dkT[:, 0:512], ["dk0c0"])
                        dump("dv0", dv[:, 0:512], ["dv0"])
                    ck("s2a")
                    lat_banks = [(0, 1536), (1, 1664), (2, 1792), (4, 1920), (5, 2048)]
                    for k, col0 in lat_banks:
                        for kc in range(8):
                            op("pe", lambda e, k=k, kc=kc, col0=col0: e.matmul(
                                PS[k][:], W_in[:, kc * INW + col0:kc * INW + col0 + 128], actT[:, kc * 512:(kc + 1) * 512],
                                start=(kc == 0), stop=(kc == 7)), r=["W_in"] + ACT(kc), w=[f"P{k}"])
                    for kc in range(8):
                        op("pe", lambda e, kc=kc: e.matmul(
                            PS[6][0:64, :], W_in[:, kc * INW + 2176:kc * INW + 2240], actT[:, kc * 512:(kc + 1) * 512],
                            start=(kc == 0), stop=(kc == 7)), r=["W_in"] + ACT(kc), w=["P6"])
                    lat_norm([0, 1, 2], 3, C_GQ, 16, 0, 1.0 / 384, 3)
                    lat_norm([4, 5], 2, C_GKV, 18, 3, 1.0 / 256, 7)
                    op("sp", lambda e, c=c: e.dma_start(out=cosT, in_=rope_d[0][:, c * CH:(c + 1) * CH]),
                       r=["rope_d"], w=SR(12, 2), dma="rp0")
                    op("sp", lambda e, c=c: e.dma_start(out=sgnT, in_=rope_d[1][:, c * CH:(c + 1) * CH]),
                       r=["rope_d"], w=SR(14, 2), dma="rp1")
                    ck("s2b")
                    for h in range(4):
                        for kc in range(3):
                            op("pe", lambda e, h=h, kc=kc: e.matmul(
                                PS[h][:], W_qup[:, kc * 768 + h * 128:kc * 768 + (h + 1) * 128],
                                actT[:, kc * 512:(kc + 1) * 512], start=(kc == 0), stop=(kc == 2)),
                               r=["W_qup"] + ACT(kc), w=[f"P{h}"])
                        evac_copy(slot(4 + h), PS[h][:], [f"P{h}"], SR(4 + h))
                    for h in range(4):
                        for kc in range(3):
                            op("pe", lambda e, h=h, kc=kc: e.matmul(
                                PS[h][0:64, :], W_qup[:, kc * 768 + 512 + h * 64:kc * 768 + 512 + (h + 1) * 64],
                                actT[:, kc * 512:(kc + 1) * 512], start=(kc == 0), stop=(kc == 2)),
                               r=["W_qup"] + ACT(kc), w=[f"P{h}"])
                    for h in range(4):
                        rope(h, S[0:64, (8 + h) * 512:(9 + h) * 512], SR(8 + h))
                    if b == 0 and c == 0:
                        dump("qnT", slot(4, 4), SR(4, 4))
                        dump("qpeT", S[0:64, 8 * 512:12 * 512], SR(8, 4), nparts=64)
                        dump("cosT", cosT, SR(12, 2), nparts=64)
                        dump("sgnT", sgnT, SR(14, 2), nparts=64)
                    ck("s3a")
                    for h in range(4):
                        for kc in range(2):
                            op("pe", lambda e, h=h, kc=kc: e.matmul(
                                PS[h][:], W_kvup[:, kc * 1024 + h * 128:kc * 1024 + (h + 1) * 128],
                                actT[:, (3 + kc) * 512:(4 + kc) * 512], start=(kc == 0), stop=(kc == 1)),
                               r=["W_kvup"] + ACT(3 + kc), w=[f"P{h}"])
                        evac_copy(knT[:, h * SEQ + c * CH:h * SEQ + (c + 1) * CH], PS[h][:], [f"P{h}"], [f"kn{h}c{c}"])
                    for t in range(4):
                        for kc in range(2):
                            op("pe", lambda e, t=t, kc=kc: e.matmul(
                                PS[t][:], actT[:, (3 + kc) * 512 + t * 128:(3 + kc) * 512 + (t + 1) * 128],
                                W_kvup[:, kc * 1024 + 512:kc * 1024 + 1024], start=(kc == 0), stop=(kc == 1)),
                               r=["W_kvup", f"act{3 + kc}t{t}"], w=[f"P{t}"])
                        tile = c * 4 + t
                        evac_copy(vv[:, tile * 512:(tile + 1) * 512], PS[t][:], [f"P{t}"], [f"v{tile}"])
                    rope(6, kpeT[0:64, c * CH:(c + 1) * CH], [f"kpe{c}"])
                    if b == 0 and c == 0:
                        dump("knT", knT[:, 0:512], ["kn0c0"])
                        dump("kpeT", kpeT[0:64, 0:512], ["kpe0"], nparts=64)
                        dump("vv0", vv[:, 0:512], ["v0"])
                    ck("s3b")
                    for u in range(RING_PF):
                        ring_issue(u)
                    alloc = [0]

                    def ring_alloc():
                        k_ = alloc[0] % 4
                        alloc[0] += 1
                        return k_

                    nkb = 4 * c + 4
                    pendA, pendB = [None], [None]

                    def make_epilogue(mi, h, mm, is_diff, ob, rb):
                        obr, rbr = [f"P{ob}"], [f"P{rb}"]
                        rinv = slotf(16)
                        E1 = slotf(18)
                        E2 = slotf(20)
                        sq = actT[:, 7 * 512:8 * 512]

                        def partA():
                            op("act", lambda e: e.activation(out=rinv, in_=PS[rb][:], func=AF.Ln), r=rbr, w=SR(16, 2))
                            op("act", lambda e: e.activation(out=rinv, in_=rinv, func=AF.Exp, scale=-1.0),
                               r=SR(16, 2), w=SR(16, 2))
                            if not is_diff:
                                f = 4 + h
                                op("dve", lambda e: e.tensor_tensor(out=actT[:, f * 512:(f + 1) * 512], in0=PS[ob][:],
                                                                    in1=rinv, op=ALU.mult), r=obr + SR(16, 2), w=ACT(f))
                            elif mm == 0:
                                op("dve", lambda e: e.tensor_tensor(out=E1, in0=PS[ob][:], in1=rinv, op=ALU.mult),
                                   r=obr + SR(16, 2), w=SR(18, 2))
                            else:
                                op("dve", lambda e: e.tensor_tensor(out=E2, in0=PS[ob][:], in1=rinv, op=ALU.mult),
                                   r=obr + SR(16, 2), w=SR(20, 2))
                                op("dve", lambda e: e.scalar_tensor_tensor(out=E1, in0=E2, scalar=cols[:, C_NLAM:C_NLAM + 1],
                                                                           in1=E1, op0=ALU.mult, op1=ALU.add),
                                   r=SR(18, 4) + ["cols"], w=SR(18, 2))
                                op("act", lambda e: e.activation(out=sq, in_=E1, func=AF.Square), r=SR(18, 2), w=ACT(7))

                        def partB():
                            if not (is_diff and mm == 1):
                                return
                            ssb = ring_alloc()
                            op("pe", lambda e: e.matmul(PS[ssb][:], ones[:], sq, start=True, stop=True),
                               r=ACT(7) + ["ones"], w=[f"P{ssb}"])
                            op("act", lambda e: e.activation(out=E2, in_=PS[ssb][:], func=AF.Ln, bias=cols[:, 15:16],
                                                             scale=1.0 / 128), r=[f"P{ssb}", "cols"], w=SR(20, 2))
                            op("act", lambda e: e.activation(out=E2, in_=E2, func=AF.Exp, scale=-0.5), r=SR(20, 2), w=SR(20, 2))
                            op("dve", lambda e: e.scalar_tensor_tensor(
                                out=actT[:, h * 512:(h + 1) * 512], in0=E1, scalar=cols[:, C_GDP:C_GDP + 1], in1=E2,
                                op0=ALU.mult, op1=ALU.mult), r=SR(18, 4) + ["cols"], w=ACT(h))

                        return partA, partB

                    for mi in range(12):
                        is_diff = mi < 8
                        if is_diff:
                            h, mm = mi // 2, mi % 2
                        else:
                            h, mm = mi - 8, 0
                        ob, rb = 4 + 2 * (mi % 2), 5 + 2 * (mi % 2)
                        obr, rbr = [f"P{ob}"], [f"P{rb}"]

                        def qk(kb, sbk):
                            qlo = max(0, kb - 4 * c) * 128
                            kc_ = kb // 4
                            if is_diff:
                                pr = slice(mm * 64, mm * 64 + 64)
                                op("pe", lambda e: e.matmul(
                                    PS[sbk][:, qlo:512], dkT[pr, h * SEQ + kb * 128:h * SEQ + kb * 128 + 128],
                                    S[pr, h * 512 + qlo:h * 512 + 512], start=True, stop=True),
                                   r=[f"dk{h}c{kc_}"] + SR(h), w=[f"P{sbk}"])
                            else:
                                op("pe", lambda e: e.matmul(
                                    PS[sbk][:, qlo:512], knT[:, h * SEQ + kb * 128:h * SEQ + kb * 128 + 128],
                                    S[:, (4 + h) * 512 + qlo:(4 + h) * 512 + 512], start=True, stop=False),
                                   r=[f"kn{h}c{kc_}"] + SR(4 + h), w=[f"P{sbk}"])
                                op("pe", lambda e: e.matmul(
                                    PS[sbk][:, qlo:512], kpeT[0:64, kb * 128:kb * 128 + 128],
                                    S[0:64, (8 + h) * 512 + qlo:(8 + h) * 512 + 512], start=False, stop=True),
                                   r=[f"kpe{kc_}"] + SR(8 + h), w=[f"P{sbk}"])

                        bank_of = {0: ring_alloc()}
                        qk(0, bank_of[0])
                        if nkb > 1:
                            bank_of[1] = ring_alloc()
                            qk(1, bank_of[1])
                        for kb in range(nkb):
                            sbk = bank_of[kb]
                            if kb + 2 < nkb:
                                bank_of[kb + 2] = ring_alloc()
                                qk(kb + 2, bank_of[kb + 2])
                            qb0 = max(0, kb - 4 * c)
                            qlo = qb0 * 128
                            pt = S[:, (12 + sbk) * 512:(13 + sbk) * 512]
                            ptr = SR(12 + sbk)
                            if is_diff:
                                groups = [(0, 1), (2, 3)] if h == 0 else [(0, 1, 2, 3)]
                                for g in groups:
                                    if g[-1] < qb0:
                                        continue
                                    lo, hi = max(g[0], qb0) * 128, (g[-1] + 1) * 128
                                    delta = (4 * c + g[0] + 1 - kb) if h == 0 else (4 * c + 3 - kb)
                                    op("act", lambda e, lo=lo, hi=hi, delta=delta: e.activation(
                                        out=pt[:, lo:hi], in_=PS[sbk][:, lo:hi], func=AF.Exp,
                                        bias=alibi[:, h * 16 + delta:h * 16 + delta + 1], scale=QSCALE_D),
                                       r=[f"P{sbk}", "alibi"], w=ptr)
                            else:
                                op("act", lambda e: e.activation(out=pt[:, qlo:512], in_=PS[sbk][:, qlo:512], func=AF.Exp,
                                                                 scale=QSCALE_M), r=[f"P{sbk}"], w=ptr)
                            if kb >= 4 * c:
                                op("dve", lambda e: e.tensor_tensor(out=pt[:, qlo:qlo + 128], in0=pt[:, qlo:qlo + 128],
                                                                    in1=tri[:], op=ALU.mult), r=ptr + ["tri"], w=ptr)
                            vsrc = dv if is_diff else vv
                            vres = f"dv{kb}" if is_diff else f"v{kb}"
                            op("pe", lambda e: e.matmul(
                                PS[ob][:, qlo:512], vsrc[:, kb * 512 + h * 128:kb * 512 + (h + 1) * 128], pt[:, qlo:512],
                                start=(kb == 0), stop=(kb == nkb - 1)), r=ptr + [vres], w=obr)
                            op("pe", lambda e: e.matmul(
                                PS[rb][:, qlo:512], ones[:], pt[:, qlo:512], start=(kb == 0), stop=(kb == nkb - 1)),
                               r=ptr + ["ones"], w=rbr)
                            if kb == 0 and pendA[0] is not None:
                                pendA[0]()
                                pendA[0] = None
                            if kb == min(3, nkb - 1) and pendB[0] is not None:
                                pendB[0]()
                                pendB[0] = None
                        pendA[0], pendB[0] = make_epilogue(mi, h, mm, is_diff, ob, rb)
                    pendA[0]()
                    pendB[0]()

                    if b == 0 and c == 0:
                        dump("mergedT", actT[:], ACT_ALL)
                    ck("s4")
                    for t in range(4):
                        for half in range(2):
                            k = t * 2 + half
                            for f in range(8):
                                op("pe", lambda e, f=f: e.matmul(
                                    PS[k][:], actT[:, f * 512 + t * 128:f * 512 + (t + 1) * 128],
                                    W_out[:, f * 1024 + half * 512:f * 1024 + (half + 1) * 512],
                                    start=(f == 0), stop=(f == 7)), r=["W_out", f"act{f}t{t}"], w=[f"P{k}"])
                            xs = xc[:, t * 1024 + half * 512:t * 1024 + (half + 1) * 512]
                            tsl = 8 + 2 * (k % 7)
                            tmp = slotf(tsl)
                            op("dve", lambda e: e.tensor_tensor(
                                out=tmp, in0=PS[k][:], in1=Ga[:, half * 512:(half + 1) * 512], op=ALU.mult),
                               r=[f"P{k}", "Ga"], w=SR(tsl, 2))
                            op("pool", lambda e: e.tensor_tensor(out=xs, in0=tmp, in1=xs, op=ALU.add),
                               r=SR(tsl, 2) + [f"xc{t}h{half}"], w=[f"xc{t}h{half}"])
                        norm_sq(t, 0)
                        norm_rstd(t, 1, 0)
                        norm_scale(t, 0)
                    for t in range(4):
                        norm_tr(b, 1, t)
                    if b == 0 and c == 0:
                        dump("x1", xc[:], XC_ALL)
                    ck("s5")
                    if b == 0 and c == 0:
                        dump("h2T", actT[:], ACT_ALL)
                    ck("s6")
                    for j in range(NJ):
                        rs_ = ring_next()
                        rres = [f"R{rs_}"]
                        ru = ring[:, rs_ * 2048:(rs_ + 1) * 2048]
                        pg, pu = 2 + 2 * (j % 3), 3 + 2 * (j % 3)
                        for kc in range(8):
                            op("pe", lambda e, pg=pg, ru=ru, kc=kc: e.matmul(
                                PS[pg][:], ru[:, kc * 128:(kc + 1) * 128], actT[:, kc * 512:(kc + 1) * 512],
                                start=(kc == 0), stop=(kc == 7)), r=rres + ACT(kc), w=[f"P{pg}"])
                        for kc in range(8):
                            op("pe", lambda e, pu=pu, ru=ru, kc=kc: e.matmul(
                                PS[pu][:], ru[:, 1024 + kc * 128:1024 + (kc + 1) * 128], actT[:, kc * 512:(kc + 1) * 512],
                                start=(kc == 0), stop=(kc == 7)), r=rres + ACT(kc), w=[f"P{pu}"])
                        sg = sgt[:, (j % 2) * 512:(j % 2 + 1) * 512]
                        op("act", lambda e, pg=pg, sg=sg: e.activation(out=sg, in_=PS[pg][:], func=AF.Silu),
                           r=[f"P{pg}"], w=[f"sgt{j % 2}"])
                        op("dve", lambda e, pu=pu, sg=sg, j=j: e.tensor_tensor(out=slot(j), in0=PS[pu][:], in1=sg, op=ALU.mult),
                           r=[f"P{pu}", f"sgt{j % 2}"], w=SR(j))
                        ring_issue(j + RING_PF)
                    if b == 0 and c == 0:
                        dump("aT", S[:], S_ALL)
                    ck("s7a")
                    for j in range(NJ):
                        rs_ = ring_next()
                        rres = [f"R{rs_}"]
                        ru = ring[:, rs_ * 2048:rs_ * 2048 + 1024]
                        for t in range(4):
                            for half in range(2):
                                k = t * 2 + half
                                op("pe", lambda e, k=k, t=t, half=half, j=j, ru=ru: e.matmul(
                                    PS[k][:], S[:, j * 512 + t * 128:j * 512 + (t + 1) * 128],
                                    ru[:, half * 512:(half + 1) * 512], start=(j == 0), stop=(j == NJ - 1)),
                                   r=rres + SR(j), w=[f"P{k}"])
                        ring_issue(NJ + j + RING_PF)
                    ck("s7b")
                    junk = S[:, 20 * 512:22 * 512]
                    stags = []
                    for t in range(4):
                        if t < 3:
                            stags.append((Sf[:, (8 + 4 * t) * 256:(8 + 4 * t) * 256 + 1024], SR(8 + 4 * t, 4)))
                        else:
                            stags.append((sgt[:, 0:1024], ["sgt0", "sgt1"]))
                    for t in range(4):
                        xr = [f"xc{t}h0", f"xc{t}h1"]
                        xt = xc[:, t * 1024:(t + 1) * 1024]
                        stag, stres = stags[t]
                        for half in range(2):
                            k = t * 2 + half
                            xs = xc[:, t * 1024 + half * 512:t * 1024 + (half + 1) * 512]
                            tmp = stag[:, half * 512:(half + 1) * 512]
                            op("dve", lambda e: e.tensor_tensor(
                                out=tmp, in0=PS[k][:], in1=Gf[:, half * 512:(half + 1) * 512], op=ALU.mult),
                               r=[f"P{k}", "Gf"], w=stres)
                            op("pool", lambda e: e.tensor_tensor(out=xs, in0=tmp, in1=xs, op=ALU.add),
                               r=stres + [f"xc{t}h{half}"], w=[f"xc{t}h{half}"])
                        sres = [f"st{8 + t}"]
                        op("dve", lambda e: e.memset(st[:, 8 + t:9 + t], 0.0), w=sres)
                        op("act", lambda e: e.activation(out=junk, in_=xt, func=AF.Square, accum_out=st[:, 8 + t:9 + t]),
                           r=xr + sres, w=SR(20, 2) + sres)
                        op("act", lambda e: e.activation(out=st[:, 12 + t:13 + t], in_=st[:, 8 + t:9 + t], func=AF.Ln,
                                                         bias=cols[:, 15:16], scale=1.0 / D), r=sres + ["cols"], w=sres)
                        op("act", lambda e: e.activation(out=st[:, 12 + t:13 + t], in_=st[:, 12 + t:13 + t], func=AF.Exp,
                                                         scale=-0.5), r=sres, w=sres)
                    for t in range(4):
                        xr = [f"xc{t}h0", f"xc{t}h1"]
                        xt = xc[:, t * 1024:(t + 1) * 1024]
                        stag, stres = stags[t]
                        sres = [f"st{8 + t}"]
                        op("dve", lambda e: e.scalar_tensor_tensor(out=stag, in0=xt, scalar=st[:, 12 + t:13 + t], in1=gfin[:],
                                                                   op0=ALU.mult, op1=ALU.mult),
                           r=xr + sres + ["gfin"], w=stres)
                        op("sp", lambda e: e.dma_start(out=out_d[tok0 + t * 128:tok0 + (t + 1) * 128, :], in_=stag),
                           r=stres, w=[f"out{t}"], dma=f"so{t}")


        nops, nwait = P.run(body)
        print(f"[kernel] ops={nops} waits={nwait}", flush=True)
    return nc


_NC_CACHE = {}


def _prep_shared(inp):
    f = np.float32
    w_qup = inp["w_q_up"][0]
    qcols = [h * 192 + i for h in range(4) for i in range(128)] + [h * 192 + 128 + i for h in range(4) for i in range(64)]
    w_kvup = inp["w_kv_up"][0]
    kvcols = [h * 256 + i for h in range(4) for i in range(128)] + [h * 256 + 128 + i for h in range(4) for i in range(128)]
    wg = inp["w_ffn_gate"][0].reshape(8, 128, NJ, 128)
    wu = inp["w_ffn_up"][0].reshape(8, 128, NJ, 128)
    wgu = np.stack([wg.transpose(2, 1, 0, 3), wu.transpose(2, 1, 0, 3)], axis=2)
    b_ada = inp["b_ada"][0]
    shared = {
        "w_ada": np.ascontiguousarray(inp["w_ada"][0], dtype=f),
        "b_adaT": np.ascontiguousarray(b_ada.reshape(48, 128).T, dtype=f),
        "b_gate": np.ascontiguousarray(np.concatenate([b_ada[2048:3072], b_ada[5120:6144]])[None, :], dtype=f),
        "g_mixT": np.ascontiguousarray(inp["g_mix"][0].reshape(8, 128).T, dtype=f),
        "g_ffnT": np.ascontiguousarray(inp["g_ffn"][0].reshape(8, 128).T, dtype=f),
        "w_in": np.ascontiguousarray(inp["w_in"][0], dtype=f),
        "lam": np.ascontiguousarray(np.concatenate([inp["lambda_q1"][0], inp["lambda_k1"][0],
                                                    inp["lambda_q2"][0], inp["lambda_k2"][0]])[None, :], dtype=f),
        "g_diff": np.ascontiguousarray(inp["g_diff_out"][0].reshape(128, 1), dtype=f),
        "g_qlT": np.ascontiguousarray(inp["g_q_lat"][0].reshape(3, 128).T, dtype=f),
        "w_qup": np.ascontiguousarray(w_qup[:, qcols], dtype=f),
        "g_kvlT": np.ascontiguousarray(inp["g_kv_lat"][0].reshape(2, 128).T, dtype=f),
        "w_kvup": np.ascontiguousarray(w_kvup[:, kvcols], dtype=f),
        "w_out": np.ascontiguousarray(inp["w_out"][0], dtype=f),
        "w_gu": np.ascontiguousarray(wgu.reshape(NJ, 128, 2048), dtype=f),
        "w_dn": np.ascontiguousarray(inp["w_ffn_down"][0], dtype=f),
        "g_final": np.ascontiguousarray(inp["g_final"].reshape(1, D), dtype=f),
    }
    return shared


def kernel(**inputs):
    inp = {k: np.asarray(v) for k, v in inputs.items()}
    if "nc" not in _NC_CACHE:
        _NC_CACHE["nc"] = build()
    nc = _NC_CACHE["nc"]
    shared = _prep_shared(inp)
    x = inp["x"].astype(np.float32, copy=False)
    c = inp["c"].astype(np.float32, copy=False)
    in_maps = []
    for core in range(NCORES):
        m = dict(shared)
        m["x"] = np.ascontiguousarray(x[core * BPC:(core + 1) * BPC].reshape(BPC * SEQ, D))
        cc = c[core * BPC:(core + 1) * BPC]
        m["cT"] = np.ascontiguousarray(cc.reshape(BPC, 8, 128).transpose(2, 1, 0).reshape(128, 8 * BPC))
        in_maps.append(m)
    res = run_bass_kernel_spmd(nc, in_maps, core_ids=list(range(NCORES)))
    if DBG:
        LAST_DBG["dbg"] = np.asarray(res.results[0]["dbg"])
        LAST_DBG["map"] = dict(DBG_MAP)
    outs = [np.asarray(r["out"]).reshape(BPC, SEQ, D) for r in res.results]
    return np.concatenate(outs, axis=0).astype(np.float32, copy=False)
```

```python
import contextlib
import math
import numpy as np
import concourse.bass as bass
import concourse.mybir as mybir
from concourse.bass_utils import run_bass_kernel_spmd

F32 = mybir.dt.float32
BF16 = mybir.dt.bfloat16
AF = mybir.ActivationFunctionType
ALU = mybir.AluOpType

NCORES = 8
SEQ = 2048
D = 1024
BPC = 4
CH = 512
NCH = SEQ // CH
DFF = 2816
NJ = DFF // 128
INW = 2240
EPS = 1e-6
LAMBDA_INIT = 0.8 - 0.6 * math.exp(-0.3 * 0)
SLOPES = [2.0 ** (-8.0 * (i + 1) / 4) for i in range(4)]
SAME_ENGINE_SYNC = True
import os
STOP = os.environ.get("KSTOP", "")
DBG = bool(os.environ.get("KDBG", ""))
DBGW = 48000
DBG_MAP = {}
LAST_DBG = {}


class _Stop(Exception):
    pass


def ck(label):
    if STOP == label:
        raise _Stop()


class Prog:
    def __init__(self, nc, es):
        self.nc = nc
        self.es = es
        self.eng = {"pe": nc.tensor, "act": nc.scalar, "dve": nc.vector, "pool": nc.gpsimd, "sp": nc.sync}
        self.esem = {k: es.enter_context(nc.semaphore("sem_" + k)) for k in self.eng}
        self.dsem = {}
        self.meta = []
        self.mode = "analyze"
        self.idx = 0
        self.nwait = 0

    def op(self, eng, fn, r=(), w=(), dma=None):
        if self.mode == "analyze":
            self.meta.append((eng, tuple(r), tuple(w), dma))
            return
        i = self.idx
        self.idx += 1
        e = self.eng[eng]
        need = {}
        for j in self.deps[i]:
            key, val = self.tick[j]
            if need.get(key, 0) < val:
                need[key] = val
        wd = self.waited[eng]
        for key, val in need.items():
            if wd.get(key, 0) >= val:
                continue
            wd[key] = val
            sem = self.dsem[key[1]] if key[0] == "d" else self.esem[key[1]]
            e.wait_ge(sem, val)
            self.nwait += 1
        inst = fn(e)
        if dma is not None:
            inst.then_inc(self.dsem[dma], 16)
        elif i in self.signaled:
            inst.then_inc(self.esem[eng], 1)

    def analyze(self):
        ops = self.meta
        last_w = {}
        readers = {}
        last_dma = {}
        deps = []
        for i, (eng, r, w, dma) in enumerate(ops):
            d = set()
            for x in r:
                if x in last_w:
                    d.add(last_w[x])
                if x[0] == "P" and x[1:].isdigit():
                    rd = readers.get(x)
                    if rd:
                        d.update(j for k_, j in rd.items() if k_ != eng)
            for x in w:
                if x in last_w:
                    d.add(last_w[x])
                rd = readers.get(x)
                if rd:
                    d.update(rd.values())
            if dma is not None and dma in last_dma:
                d.add(last_dma[dma])
            d.discard(i)
            for x in w:
                last_w[x] = i
                readers[x] = {}
            for x in r:
                if x in w:
                    continue
                key = eng if dma is None else ("dma", i)
                readers.setdefault(x, {})[key] = i
            if dma is not None:
                last_dma[dma] = i
            kept = []
            for j in d:
                ej, _, _, dj = ops[j]
                if dj is None and dma is None and ej == eng and (eng == "pe" or not SAME_ENGINE_SYNC):
                    continue
                kept.append(j)
            deps.append(kept)
        signaled = set()
        for kept in deps:
            for j in kept:
                if ops[j][3] is None:
                    signaled.add(j)
        tick = {}
        cnt = {k: 0 for k in self.eng}
        dcnt = {}
        for i, (eng, r, w, dma) in enumerate(ops):
            if dma is not None:
                if dma not in self.dsem:
                    self.dsem[dma] = self.es.enter_context(self.nc.semaphore("d_" + dma))
                dcnt[dma] = dcnt.get(dma, 0) + 16
                tick[i] = (("d", dma), dcnt[dma])
            elif i in signaled:
                cnt[eng] += 1
                tick[i] = (("e", eng), cnt[eng])
        self.deps, self.signaled, self.tick, self.dcnt = deps, signaled, tick, dcnt
        self.waited = {k: {} for k in self.eng}

    def run(self, body):
        self.mode = "analyze"
        try:
            body()
        except _Stop:
            pass
        self.analyze()
        self.mode = "emit"
        self.idx = 0
        try:
            body()
        except _Stop:
            pass
        assert self.idx == len(self.meta), (self.idx, len(self.meta))
        sp = self.eng["sp"]
        for name, sem in self.dsem.items():
            if self.waited["sp"].get(("d", name), 0) < self.dcnt[name]:
                sp.wait_ge(sem, self.dcnt[name])
        return len(self.meta), self.nwait


def build():
    nc = bass.Bass("TRN2", target_bir_lowering=False)

    def dram(name, shape, dt=F32, kind="ExternalInput"):
        return nc.dram_tensor(name, list(shape), dt, kind=kind).ap()

    x_d = dram("x", [BPC * SEQ, D])
    out_d = dram("out", [BPC * SEQ, D], kind="ExternalOutput")
    cT_d = dram("cT", [128, 8 * BPC])
    wada_d = dram("w_ada", [D, 6 * D])
    badaT_d = dram("b_adaT", [128, 48])
    bgate_d = dram("b_gate", [1, 2 * D])
    gmixT_d = dram("g_mixT", [128, 8])
    gffnT_d = dram("g_ffnT", [128, 8])
    win_d = dram("w_in", [D, INW])
    lam_d = dram("lam", [1, 256])
    gdiff_d = dram("g_diff", [128, 1])
    gqlT_d = dram("g_qlT", [128, 3])
    wqup_d = dram("w_qup", [384, 768])
    gkvlT_d = dram("g_kvlT", [128, 2])
    wkvup_d = dram("w_kvup", [256, 1024])
    wout_d = dram("w_out", [D, D])
    wgu_d = dram("w_gu", [NJ, 128, 2048])
    wdn_d = dram("w_dn", [DFF, D])
    gfin_d = dram("g_final", [1, D])
    gates_d = dram("gates_scr", [BPC, 2 * D], kind="Internal")
    rope_d = dram("rope_scr", [2, 64, SEQ], kind="Internal")
    dbg_d = dram("dbg", [128, DBGW], kind="ExternalOutput") if DBG else None

    es = contextlib.ExitStack()
    with es:
        def sb(name, shape, dt):
            return es.enter_context(nc.sbuf_tensor(name, list(shape), dt))

        W_in = sb("W_in", [128, 8 * INW], BF16)
        W_qup = sb("W_qup", [128, 3 * 768], BF16)
        W_kvup = sb("W_kvup", [128, 2 * 1024], BF16)
        W_out = sb("W_out", [128, 8 * 1024], BF16)
        dkT = sb("dkT", [128, 4 * SEQ], BF16)
        dv = sb("dv", [128, 16 * 512], BF16)
        knT = sb("knT", [128, 4 * SEQ], BF16)
        kpeT = sb("kpeT", [128, SEQ], BF16)
        vv = sb("vv", [128, 16 * 512], BF16)
        xc = sb("xc", [128, 4 * 1024], F32)
        actT = sb("actT", [128, 8 * 512], BF16)
        S = sb("S", [128, NJ * 512], BF16)
        ring = sb("ring", [128, 4 * 2048], BF16)
        Ga = sb("Ga", [128, 1024], F32)
        Gf = sb("Gf", [128, 1024], F32)
        gfin = sb("gfin", [128, 1024], F32)
        sgt = sb("sgt", [128, 2 * 512], F32)
        ident = sb("ident", [128, 128], BF16)
        ones = sb("ones", [128, 128], BF16)
        tri = sb("tri", [128, 128], BF16)
        alibi = sb("alibi", [128, 64], F32)
        AB = sb("AB", [128, BPC * 2 * 2 * 8], F32)
        st = sb("st", [128, 16], F32)
        cols = sb("cols", [128, 32], F32)
        PS = [es.enter_context(nc.psum_tensor(f"ps{k}", [128, 512], F32)) for k in range(8)]

        P = Prog(nc, es)
        op = P.op

        def body():
            Sf = S[:].bitcast(F32)
            modT = sgt[:, 256:256 + 48 * BPC]
            xcb = xc[:].bitcast(BF16)

            def slot(j, n=1):
                return S[:, j * 512:(j + n) * 512]

            def slotf(j):
                return Sf[:, j * 256:j * 256 + 512]

            def SR(j, n=1):
                return [f"S{k}" for k in range(j, j + n)]

            dbg_off = [0]

            def dump(label, ap, res, nparts=128):
                if not DBG or label in DBG_MAP and P.mode == "analyze":
                    return
                wdt = ap.shape[-1]
                if P.mode == "analyze":
                    DBG_MAP[label] = (dbg_off[0], wdt, nparts)
                o = DBG_MAP[label][0]
                dbg_off[0] = o + wdt
                op("pool", lambda e: e.dma_start(out=dbg_d[0:nparts, o:o + wdt], in_=ap), r=res, w=["dbg_" + label],
                   dma="dbg")

            XC_ALL = [f"xc{t}h{h}" for t in range(4) for h in range(2)]
            S_ALL = SR(0, NJ)
            ACT_ALL = [f"act{k}t{t}" for k in range(8) for t in range(4)]

            def ACT(k):
                return [f"act{k}t{t}" for t in range(4)]

            def AB_col(b, af, ab, kc):
                o = ((b * 2 + af) * 2 + ab) * 8 + kc
                return AB[:, o:o + 1]

            C_NLAM, C_GDP, C_GD, C_GQ, C_GKV = 0, 1, 2, 3, 6

            tmpf = Sf[:, 0:128]
            op("pool", lambda e: e.memset(tmpf, 1.0), w=SR(0))
            op("pool", lambda e: e.affine_select(out=tmpf, in_=tmpf, pattern=[[-1, 128]], compare_op=ALU.is_equal,
                                                 fill=0.0, base=0, channel_multiplier=1), r=SR(0), w=SR(0))
            op("dve", lambda e: e.tensor_copy(out=ident[:], in_=tmpf), r=SR(0), w=["ident"])
            tmpf2 = Sf[:, 256:384]
            op("pool", lambda e: e.memset(tmpf2, 1.0), w=SR(1))
            op("pool", lambda e: e.affine_select(out=tmpf2, in_=tmpf2, pattern=[[1, 128]], compare_op=ALU.is_ge,
                                                 fill=0.0, base=0, channel_multiplier=-1), r=SR(1), w=SR(1))
            op("dve", lambda e: e.tensor_copy(out=tri[:], in_=tmpf2), r=SR(1), w=["tri"])
            op("dve", lambda e: e.memset(ones[:], 1.0), w=["ones"])
            op("dve", lambda e: e.memset(cols[:, 15:16], EPS), w=["cols"])
            tmpa = Sf[:, 512:528]
            op("pool", lambda e: e.iota(tmpa, pattern=[[-128, 16]], base=-127, channel_multiplier=1,
                                        allow_small_or_imprecise_dtypes=True), w=SR(2))
            tmpa0 = Sf[:, 768:784]
            op("pool", lambda e: e.iota(tmpa0, pattern=[[-128, 16]], base=0, channel_multiplier=1,
                                        allow_small_or_imprecise_dtypes=True), w=SR(3))
            op("dve", lambda e: e.tensor_scalar(out=alibi[:, 0:16], in0=tmpa0, scalar1=SLOPES[0], scalar2=None,
                                                op0=ALU.mult), r=SR(3), w=["alibi"])
            for h in range(1, 4):
                op("dve", lambda e, h=h: e.tensor_scalar(out=alibi[:, h * 16:(h + 1) * 16], in0=tmpa, scalar1=SLOPES[h],
                                                         scalar2=None, op0=ALU.mult), r=SR(2), w=["alibi"])

            ck("consts")
            win_v = win_d.rearrange("(kc p) n -> p kc n", p=128)
            for kc in range(8):
                op("pool", lambda e, kc=kc: e.dma_start(out=W_in[:, kc * INW:(kc + 1) * INW], in_=win_v[:, kc, :]),
                   w=["W_in"], dma=f"w{kc % 2}")
            wq_v = wqup_d.rearrange("(kc p) n -> p kc n", p=128)
            for kc in range(3):
                op("pool", lambda e, kc=kc: e.dma_start(out=W_qup[:, kc * 768:(kc + 1) * 768], in_=wq_v[:, kc, :]),
                   w=["W_qup"], dma="w2")
            wkv_v = wkvup_d.rearrange("(kc p) n -> p kc n", p=128)
            for kc in range(2):
                op("pool", lambda e, kc=kc: e.dma_start(out=W_kvup[:, kc * 1024:(kc + 1) * 1024], in_=wkv_v[:, kc, :]),
                   w=["W_kvup"], dma="w3")
            wo_v = wout_d.rearrange("(kc p) n -> p kc n", p=128)

            ck("weights")
            cTs = Sf[:, 1024:1024 + 8 * BPC]
            op("sp", lambda e: e.dma_start(out=cTs, in_=cT_d[:, :]), w=SR(4), dma="p0")
            op("sp", lambda e: e.dma_start(out=cols[:, 16:24], in_=gmixT_d[:, :]), w=["cols"], dma="p1")
            op("sp", lambda e: e.dma_start(out=cols[:, 24:32], in_=gffnT_d[:, :]), w=["cols"], dma="p1")
            op("sp", lambda e: e.dma_start(out=cols[:, C_GD:C_GD + 1], in_=gdiff_d[:, :]), w=["cols"], dma="p1")
            op("sp", lambda e: e.dma_start(out=cols[:, C_GQ:C_GQ + 3], in_=gqlT_d[:, :]), w=["cols"], dma="p1")
            op("sp", lambda e: e.dma_start(out=cols[:, C_GKV:C_GKV + 2], in_=gkvlT_d[:, :]), w=["cols"], dma="p1")
            badaT = Sf[:, 1280:1280 + 48]
            op("sp", lambda e: e.dma_start(out=badaT, in_=badaT_d[:, :]), w=SR(5), dma="p2")
            op("sp", lambda e: e.dma_start(out=gfin[:], in_=gfin_d[:, :].partition_broadcast(128)), w=["gfin"], dma="p3")
            lamb = Sf[:, 1536:1536 + 256]
            op("sp", lambda e: e.dma_start(out=lamb, in_=lam_d[:, :].partition_broadcast(128)), w=SR(6), dma="p4")
            bg = Sf[0:BPC, 2048:2048 + 2048]
            op("sp", lambda e: e.dma_start(out=bg, in_=bgate_d[:, :].partition_broadcast(BPC)), w=SR(8, 8), dma="p5")

            ck("params")
            lprod = Sf[:, 1792:1792 + 128]
            op("dve", lambda e: e.tensor_tensor(out=lprod[:, 0:64], in0=lamb[:, 0:64], in1=lamb[:, 64:128], op=ALU.mult),
               r=SR(6), w=SR(7))
            op("dve", lambda e: e.tensor_tensor(out=lprod[:, 64:128], in0=lamb[:, 128:192], in1=lamb[:, 192:256], op=ALU.mult),
               r=SR(6), w=SR(7))
            op("dve", lambda e: e.reduce_sum(out=cols[:, 8:9], in_=lprod[:, 0:64], axis=mybir.AxisListType.X), r=SR(7), w=["cols"])
            op("dve", lambda e: e.reduce_sum(out=cols[:, 9:10], in_=lprod[:, 64:128], axis=mybir.AxisListType.X), r=SR(7), w=["cols"])
            op("act", lambda e: e.activation(out=cols[:, 10:12], in_=cols[:, 8:10], func=AF.Exp), r=["cols"], w=["cols"])
            op("dve", lambda e: e.tensor_tensor(out=cols[:, C_NLAM:C_NLAM + 1], in0=cols[:, 11:12], in1=cols[:, 10:11],
                                                op=ALU.subtract), r=["cols"], w=["cols"])
            op("dve", lambda e: e.tensor_scalar(out=cols[:, C_NLAM:C_NLAM + 1], in0=cols[:, C_NLAM:C_NLAM + 1],
                                                scalar1=-LAMBDA_INIT, scalar2=None, op0=ALU.add), r=["cols"], w=["cols"])
            op("dve", lambda e: e.tensor_scalar(out=cols[:, C_GDP:C_GDP + 1], in0=cols[:, C_GD:C_GD + 1],
                                                scalar1=1.0 - LAMBDA_INIT, scalar2=None, op0=ALU.mult), r=["cols"], w=["cols"])

            ck("lam")
            scb = sgt[:].bitcast(BF16)[:, 0:8 * BPC]
            SCB = ["sgt0", "sgt1"]
            op("act", lambda e: e.activation(out=scb, in_=cTs, func=AF.Silu), r=SR(4), w=SCB)
            wada_v = wada_d.rearrange("(kc p) n -> p kc n", p=128)
            gsb = actT[:].bitcast(F32)[0:BPC, 0:2048]
            bank = 0
            for u in range(12):
                sl = u % 2
                unit = xcb[:, sl * 4096:(sl + 1) * 4096]
                ures = XC_ALL[sl * 4:(sl + 1) * 4]
                op("pool", lambda e, unit=unit, u=u: e.dma_start(
                    out=unit.rearrange("p (kc n) -> p kc n", kc=8), in_=wada_v[:, :, u * 512:(u + 1) * 512]),
                   w=ures, dma=f"ada{sl}")
                if u in (4, 5, 10, 11):
                    g = 0 if u < 6 else 1
                    half = u % 2 if u < 6 else (u - 10)
                    pb = PS[bank % 8]
                    pres = [f"P{bank % 8}"]
                    bank += 1
                    for kc in range(8):
                        op("pe", lambda e, pb=pb, unit=unit, kc=kc: e.matmul(
                            pb[0:BPC, :], scb[:, kc * BPC:(kc + 1) * BPC], unit[:, kc * 512:(kc + 1) * 512],
                            start=(kc == 0), stop=(kc == 7)), r=ures + SCB, w=pres)
                    o = g * 1024 + half * 512
                    op("dve", lambda e, pb=pb, o=o: e.tensor_tensor(out=gsb[:, o:o + 512], in0=pb[0:BPC, :],
                                                                    in1=bg[:, o:o + 512], op=ALU.add),
                       r=pres + SR(8, 8), w=ACT_ALL)
                else:
                    for mm in range(4):
                        m = 4 * u + mm
                        pb = PS[bank % 8]
                        pres = [f"P{bank % 8}"]
                        bank += 1
                        for kc in range(8):
                            op("pe", lambda e, pb=pb, unit=unit, kc=kc, mm=mm: e.matmul(
                                pb[:, 0:BPC], unit[:, kc * 512 + mm * 128:kc * 512 + (mm + 1) * 128],
                                scb[:, kc * BPC:(kc + 1) * BPC], start=(kc == 0), stop=(kc == 7)),
                               r=ures + SCB, w=pres)
                        op("dve", lambda e, pb=pb, m=m: e.tensor_scalar(
                            out=modT[:, m * BPC:(m + 1) * BPC], in0=pb[:, 0:BPC], scalar1=badaT[:, m:m + 1], scalar2=None,
                            op0=ALU.add), r=pres + SR(5), w=["sgt0"])
            op("sp", lambda e: e.dma_start(out=gates_d[:, :], in_=gsb), r=ACT_ALL, w=["gates_d"], dma="p6")
            modT3 = modT.rearrange("p (m b) -> p m b", b=BPC)
            for b in range(BPC):
                for af, (m_sh, m_sc, gcol) in enumerate([(0, 8, 16), (24, 32, 24)]):
                    oA = ((b * 2 + af) * 2 + 0) * 8
                    oB = ((b * 2 + af) * 2 + 1) * 8
                    op("dve", lambda e, b=b, m_sc=m_sc, gcol=gcol, oA=oA: e.scalar_tensor_tensor(
                        out=AB[:, oA:oA + 8], in0=modT3[:, m_sc:m_sc + 8, b], scalar=1.0, in1=cols[:, gcol:gcol + 8],
                        op0=ALU.add, op1=ALU.mult), r=["sgt0", "cols"], w=["AB"])
                    op("dve", lambda e, b=b, m_sh=m_sh, oB=oB: e.tensor_copy(out=AB[:, oB:oB + 8],
                                                                             in_=modT3[:, m_sh:m_sh + 8, b]),
                       r=["sgt0"], w=["AB"])

            ck("ada")
            xcf = xc[:]
            pos = xcf[0:64, 0:2048]
            ang = xcf[0:64, 2048:4096]
            op("pool", lambda e: e.iota(pos, pattern=[[1, 2048]], base=0, channel_multiplier=0,
                                        allow_small_or_imprecise_dtypes=True), w=XC_ALL)
            op("pool", lambda e: e.iota(cols[0:32, 12:13], pattern=[[0, 1]], base=0, channel_multiplier=1,
                                        allow_small_or_imprecise_dtypes=True), w=["cols"])
            op("pool", lambda e: e.iota(cols[32:64, 12:13], pattern=[[0, 1]], base=0, channel_multiplier=1,
                                        allow_small_or_imprecise_dtypes=True), w=["cols"])
            op("act", lambda e: e.activation(out=cols[0:64, 13:14], in_=cols[0:64, 12:13], func=AF.Exp,
                                             scale=-math.log(10000.0) / 32.0), r=["cols"], w=["cols"])
            op("dve", lambda e: e.tensor_scalar(out=ang, in0=pos, scalar1=cols[0:64, 13:14], scalar2=None, op0=ALU.mult),
               r=XC_ALL + ["cols"], w=XC_ALL)
            TWO_PI = 2.0 * math.pi
            MAGIC = 12582912.0
            PI_LO = 3.141592
            for which, off in ((0, 0.5 * math.pi), (1, 0.0)):
                dst = Sf[0:64, which * 2048:(which + 1) * 2048]
                op("dve", lambda e, off=off: e.tensor_scalar(out=pos, in0=ang, scalar1=off, scalar2=1.0 / TWO_PI, op0=ALU.add,
                                                             op1=ALU.mult), r=XC_ALL, w=XC_ALL)
                op("dve", lambda e: e.tensor_scalar(out=pos, in0=pos, scalar1=MAGIC, scalar2=None, op0=ALU.add),
                   r=XC_ALL, w=XC_ALL)
                op("dve", lambda e: e.tensor_scalar(out=pos, in0=pos, scalar1=-MAGIC, scalar2=None, op0=ALU.add),
                   r=XC_ALL, w=XC_ALL)
                op("dve", lambda e: e.scalar_tensor_tensor(out=pos, in0=pos, scalar=-TWO_PI, in1=ang, op0=ALU.mult,
                                                           op1=ALU.add), r=XC_ALL, w=XC_ALL)
                op("dve", lambda e, off=off: e.tensor_scalar(out=pos, in0=pos, scalar1=off, scalar2=None, op0=ALU.add),
                   r=XC_ALL, w=XC_ALL)
                op("dve", lambda e: e.tensor_scalar(out=pos, in0=pos, scalar1=-PI_LO, scalar2=PI_LO,
                                                    op0=ALU.max, op1=ALU.min), r=XC_ALL, w=XC_ALL)
                op("act", lambda e, dst=dst: e.activation(out=dst, in_=pos, func=AF.Sin), r=XC_ALL, w=S_ALL)
            sgn_hi = Sf[32:64, 2048:4096]
            op("dve", lambda e: e.tensor_scalar(out=sgn_hi, in0=sgn_hi, scalar1=-1.0, scalar2=None, op0=ALU.mult),
               r=S_ALL, w=S_ALL)
            op("sp", lambda e: e.dma_start(out=rope_d[0], in_=Sf[0:64, 0:2048]), r=S_ALL, w=["rope_d"], dma="p7")
            op("sp", lambda e: e.dma_start(out=rope_d[1], in_=Sf[0:64, 2048:4096]), r=S_ALL, w=["rope_d"], dma="p7")

            ck("rope")
            dump("AB", AB[:], ["AB"])
            dump("modT", modT, ["sgt0"])
            dump("cols", cols[:], ["cols"])
            dump("alibi", alibi[:], ["alibi"])
            dump("tri", tri[:], ["tri"])
            dump("ident", ident[:], ["ident"])
            evac_rr = [0]

            def evac_copy(dst, src, r, w):
                evac_rr[0] += 1
                if evac_rr[0] % 2 == 0:
                    op("act", lambda e: e.activation(out=dst, in_=src, func=AF.Copy), r=r, w=w)
                else:
                    op("dve", lambda e: e.tensor_copy(out=dst, in_=src), r=r, w=w)

            def hn_buf(t):
                return S[:, t * 1024:(t + 1) * 1024], SR(2 * t, 2)

            def norm_sq(t, col0):
                xr = [f"xc{t}h0", f"xc{t}h1"]
                xt = xc[:, t * 1024:(t + 1) * 1024]
                hn, hres = hn_buf(t)
                sres = [f"st{t}"]
                op("dve", lambda e: e.memset(st[:, col0 + t:col0 + t + 1], 0.0), w=sres)
                op("act", lambda e: e.activation(out=hn, in_=xt, func=AF.Square, accum_out=st[:, col0 + t:col0 + t + 1]),
                   r=xr + sres, w=hres + sres)

            def norm_rstd(t0, n, col0):
                sres = [f"st{t}" for t in range(t0, t0 + n)]
                src_ = st[:, col0 + t0:col0 + t0 + n]
                dst_ = st[:, col0 + 4 + t0:col0 + 4 + t0 + n]
                op("act", lambda e: e.activation(out=dst_, in_=src_, func=AF.Ln, bias=cols[:, 15:16], scale=1.0 / D),
                   r=sres + ["cols"], w=sres)
                op("act", lambda e: e.activation(out=dst_, in_=dst_, func=AF.Exp, scale=-0.5), r=sres, w=sres)

            def norm_scale(t, col0):
                xr = [f"xc{t}h0", f"xc{t}h1"]
                xt = xc[:, t * 1024:(t + 1) * 1024]
                hn, hres = hn_buf(t)
                op("dve", lambda e: e.tensor_scalar(out=hn, in0=xt, scalar1=st[:, col0 + 4 + t:col0 + 5 + t], scalar2=None,
                                                    op0=ALU.mult), r=xr + [f"st{t}"], w=hres)

            def norm_tr(b, af, t):
                hn, hres = hn_buf(t)
                pb = PS[t % 2]
                pbb = pb[:].bitcast(BF16)
                pres = [f"P{t % 2}"]
                for kc in range(8):
                    op("pe", lambda e, kc=kc: e.transpose(pbb[:, kc * 128:(kc + 1) * 128],
                                                          hn[:, kc * 128:(kc + 1) * 128], ident[:]),
                       r=hres + ["ident"], w=pres)
                for kc in range(8):
                    dst = actT[:, kc * 512 + t * 128:kc * 512 + (t + 1) * 128]
                    src_ = pbb[:, kc * 128:(kc + 1) * 128]
                    a_ = AB_col(b, af, 0, kc)
                    b_ = AB_col(b, af, 1, kc)
                    if t % 2 == 0:
                        op("dve", lambda e, dst=dst, src_=src_, a_=a_, b_=b_: e.tensor_scalar(
                            out=dst, in0=src_, scalar1=a_, scalar2=b_, op0=ALU.mult, op1=ALU.add),
                           r=pres + ["AB"], w=[f"act{kc}t{t}"])
                    else:
                        op("act", lambda e, dst=dst, src_=src_, a_=a_, b_=b_: e.activation(
                            out=dst, in_=src_, func=AF.Identity, bias=b_, scale=a_),
                           r=pres + ["AB"], w=[f"act{kc}t{t}"])

            def lat_norm(banks, n, gc0, rstd_slot, dst_act0, inv_n, ssbank):
                for i in range(n):
                    sq = actT[:, (5 + i % 2) * 512:(6 + i % 2) * 512]
                    op("act", lambda e, sq=sq, i=i: e.activation(out=sq, in_=PS[banks[i]][:], func=AF.Square),
                       r=[f"P{banks[i]}"], w=ACT(5 + i % 2))
                    op("pe", lambda e, sq=sq, i=i: e.matmul(PS[ssbank][:], ones[:], sq, start=(i == 0), stop=(i == n - 1)),
                       r=ACT(5 + i % 2) + ["ones"], w=[f"P{ssbank}"])
                rs = slotf(rstd_slot)
                op("act", lambda e: e.activation(out=rs, in_=PS[ssbank][:], func=AF.Ln, bias=cols[:, 15:16], scale=inv_n),
                   r=[f"P{ssbank}", "cols"], w=SR(rstd_slot, 2))
                op("act", lambda e: e.activation(out=rs, in_=rs, func=AF.Exp, scale=-0.5), r=SR(rstd_slot, 2), w=SR(rstd_slot, 2))
                for i in range(n):
                    dst = actT[:, (dst_act0 + i) * 512:(dst_act0 + i + 1) * 512]
                    op("dve", lambda e, dst=dst, i=i: e.scalar_tensor_tensor(
                        out=dst, in0=PS[banks[i]][:], scalar=cols[:, gc0 + i:gc0 + i + 1], in1=rs, op0=ALU.mult,
                        op1=ALU.mult), r=[f"P{banks[i]}", "cols"] + SR(rstd_slot, 2), w=ACT(dst_act0 + i))

            cosT = Sf[0:64, 12 * 256:12 * 256 + 512]
            sgnT = Sf[0:64, 14 * 256:14 * 256 + 512]
            t2 = Sf[0:64, 20 * 256:20 * 256 + 512]

            def rope(pbank, dst, dres):
                pb = PS[pbank]
                pres = [f"P{pbank}"]
                op("dve", lambda e: e.tensor_tensor(out=t2[0:32, :], in0=pb[32:64, :], in1=sgnT[32:64, :], op=ALU.mult),
                   r=pres + SR(14, 2), w=SR(20, 2))
                op("dve", lambda e: e.tensor_tensor(out=t2[32:64, :], in0=pb[0:32, :], in1=sgnT[0:32, :], op=ALU.mult),
                   r=pres + SR(14, 2), w=SR(20, 2))
                op("dve", lambda e: e.tensor_tensor(out=pb[0:64, :], in0=pb[0:64, :], in1=cosT, op=ALU.mult),
                   r=pres + SR(12, 2), w=pres)
                op("dve", lambda e: e.tensor_tensor(out=dst, in0=pb[0:64, :], in1=t2, op=ALU.add),
                   r=pres + SR(20, 2), w=dres)

            ring_ctr = [0]

            def ring_next():
                s = ring_ctr[0] % 4
                ring_ctr[0] += 1
                return s

            RING_PF = 4
            ring_issued = [0]

            def ring_issue(u):
                if u >= 2 * NJ:
                    return
                rs_ = ring_issued[0] % 4
                ring_issued[0] += 1
                if u < NJ:
                    ru = ring[:, rs_ * 2048:(rs_ + 1) * 2048]
                    op("pool", lambda e: e.dma_start(out=ru, in_=wgu_d[u]), w=[f"R{rs_}"], dma=f"rg{rs_}")
                else:
                    j = u - NJ
                    ru = ring[:, rs_ * 2048:rs_ * 2048 + 1024]
                    op("pool", lambda e: e.dma_start(out=ru, in_=wdn_d[j * 128:(j + 1) * 128, :]),
                       w=[f"R{rs_}"], dma=f"rg{rs_}")

            pending_stores = []

            def flush_stores():
                for stag_, stres_, t_, tok0_ in pending_stores:
                    op("sp", lambda e, stag_=stag_, t_=t_, tok0_=tok0_: e.dma_start(
                        out=out_d[tok0_ + t_ * 128:tok0_ + (t_ + 1) * 128, :], in_=stag_),
                       r=stres_, w=[f"out{t_}"], dma=f"so{t_}")
                pending_stores.clear()

            QSCALE_D = 64 ** -0.5
            QSCALE_M = 192 ** -0.5

            for b in range(BPC):
                op("sp", lambda e, b=b: e.dma_start(out=Ga[:], in_=gates_d[b:b + 1, 0:1024].partition_broadcast(128)),
                   r=["gates_d"], w=["Ga"], dma="ga")
                op("sp", lambda e, b=b: e.dma_start(out=Gf[:], in_=gates_d[b:b + 1, 1024:2048].partition_broadcast(128)),
                   r=["gates_d"], w=["Gf"], dma="gf")
                for kc in range(8):
                    op("pool", lambda e, kc=kc: e.dma_start(out=W_out[:, kc * 1024:(kc + 1) * 1024], in_=wo_v[:, kc, :]),
                       w=[f"W_out{kc}"], dma=f"w{4 + kc % 2}")
                    op("pool", lambda e, kc=kc: e.tensor_tensor(out=W_out[:, kc * 1024:(kc + 1) * 1024],
                                                                in0=W_out[:, kc * 1024:(kc + 1) * 1024], in1=Ga[:], op=ALU.mult),
                       r=[f"W_out{kc}", "Ga"], w=[f"W_out{kc}"])
                ck("g")
                for c in range(NCH):
                    tok0 = b * SEQ + c * CH
                    for t in range(4):
                        op("sp", lambda e, t=t, tok0=tok0: e.dma_start(out=xc[:, t * 1024:(t + 1) * 1024],
                                                                      in_=x_d[tok0 + t * 128:tok0 + (t + 1) * 128, :]),
                           w=[f"xc{t}h0", f"xc{t}h1"], dma=f"ld{t}")
                    ck("xl")
                    for t in range(4):
                        norm_sq(t, 0)
                    flush_stores()
                    norm_rstd(0, 4, 0)
                    for t in range(4):
                        norm_scale(t, 0)
                        norm_tr(b, 0, t)
                    if b == 0 and c == 0:
                        dump("Ga", Ga[:], ["Ga"])
                        dump("hT", actT[:], ACT_ALL)
                        dump("st", st[:], [f"st{t}" for t in range(4)])
                    ck("s1")
                    rr = 0
                    for m in range(8):
                        k = 2 + rr % 6
                        rr += 1
                        for kc in range(8):
                            op("pe", lambda e, k=k, kc=kc, m=m: e.matmul(
                                PS[k][:], W_in[:, kc * INW + m * 128:kc * INW + (m + 1) * 128],
                                actT[:, kc * 512:(kc + 1) * 512], start=(kc == 0), stop=(kc == 7)),
                               r=["W_in"] + ACT(kc), w=[f"P{k}"])
                        if m < 4:
                            evac_copy(slot(m), PS[k][:], [f"P{k}"], SR(m))
                        else:
                            h = m - 4
                            evac_copy(dkT[:, h * SEQ + c * CH:h * SEQ + (c + 1) * CH], PS[k][:], [f"P{k}"], [f"dk{h}c{c}"])
                    for t in range(4):
                        k = 2 + rr % 6
                        rr += 1
                        for kc in range(8):
                            op("pe", lambda e, k=k, kc=kc, t=t: e.matmul(
                                PS[k][:], actT[:, kc * 512 + t * 128:kc * 512 + (t + 1) * 128],
                                W_in[:, kc * INW + 1024:kc * INW + 1536], start=(kc == 0), stop=(kc == 7)),
                               r=["W_in", f"act{kc}t{t}"], w=[f"P{k}"])
                        tile = c * 4 + t
                        evac_copy(dv[:, tile * 512:(tile + 1) * 512], PS[k][:], [f"P{k}"], [f"dv{tile}"])
                    if b == 0 and c == 0:
                        dump("dqT", slot(0, 4), SR(0, 4))
                        dump("dkT", dkT[:, 0:512], ["dk0c0"])
                        dump("dv0", dv[:, 0:512], ["dv0"])
                    ck("s2a")
                    lat_banks = [(0, 1536), (1, 1664), (2, 1792), (4, 1920), (5, 2048)]
                    for k, col0 in lat_banks:
                        for kc in range(8):
                            op("pe", lambda e, k=k, kc=kc, col0=col0: e.matmul(
                                PS[k][:], W_in[:, kc * INW + col0:kc * INW + col0 + 128], actT[:, kc * 512:(kc + 1) * 512],
                                start=(kc == 0), stop=(kc == 7)), r=["W_in"] + ACT(kc), w=[f"P{k}"])
                    for kc in range(8):
                        op("pe", lambda e, kc=kc: e.matmul(
                            PS[6][0:64, :], W_in[:, kc * INW + 2176:kc * INW + 2240], actT[:, kc * 512:(kc + 1) * 512],
                            start=(kc == 0), stop=(kc == 7)), r=["W_in"] + ACT(kc), w=["P6"])
                    lat_norm([0, 1, 2], 3, C_GQ, 16, 0, 1.0 / 384, 3)
                    lat_norm([4, 5], 2, C_GKV, 18, 3, 1.0 / 256, 7)
                    op("sp", lambda e, c=c: e.dma_start(out=cosT, in_=rope_d[0][:, c * CH:(c + 1) * CH]),
                       r=["rope_d"], w=SR(12, 2), dma="rp0")
                    op("sp", lambda e, c=c: e.dma_start(out=sgnT, in_=rope_d[1][:, c * CH:(c + 1) * CH]),
                       r=["rope_d"], w=SR(14, 2), dma="rp1")
                    ck("s2b")
                    for h in range(4):
                        for kc in range(3):
                            op("pe", lambda e, h=h, kc=kc: e.matmul(
                                PS[h][:], W_qup[:, kc * 768 + h * 128:kc * 768 + (h + 1) * 128],
                                actT[:, kc * 512:(kc + 1) * 512], start=(kc == 0), stop=(kc == 2)),
                               r=["W_qup"] + ACT(kc), w=[f"P{h}"])
                        evac_copy(slot(4 + h), PS[h][:], [f"P{h}"], SR(4 + h))
                    for h in range(4):
                        for kc in range(3):
                            op("pe", lambda e, h=h, kc=kc: e.matmul(
                                PS[h][0:64, :], W_qup[:, kc * 768 + 512 + h * 64:kc * 768 + 512 + (h + 1) * 64],
                                actT[:, kc * 512:(kc + 1) * 512], start=(kc == 0), stop=(kc == 2)),
                               r=["W_qup"] + ACT(kc), w=[f"P{h}"])
                    for h in range(4):
                        rope(h, S[0:64, (8 + h) * 512:(9 + h) * 512], SR(8 + h))
                    if b == 0 and c == 0:
                        dump("qnT", slot(4, 4), SR(4, 4))
                        dump("qpeT", S[0:64, 8 * 512:12 * 512], SR(8, 4), nparts=64)
                        dump("cosT", cosT, SR(12, 2), nparts=64)
                        dump("sgnT", sgnT, SR(14, 2), nparts=64)
                    ck("s3a")
                    for h in range(4):
                        for kc in range(2):
                            op("pe", lambda e, h=h, kc=kc: e.matmul(
                                PS[h][:], W_kvup[:, kc * 1024 + h * 128:kc * 1024 + (h + 1) * 128],
                                actT[:, (3 + kc) * 512:(4 + kc) * 512], start=(kc == 0), stop=(kc == 1)),
                               r=["W_kvup"] + ACT(3 + kc), w=[f"P{h}"])
                        evac_copy(knT[:, h * SEQ + c * CH:h * SEQ + (c + 1) * CH], PS[h][:], [f"P{h}"], [f"kn{h}c{c}"])
                    for t in range(4):
                        for kc in range(2):
                            op("pe", lambda e, t=t, kc=kc: e.matmul(
                                PS[t][:], actT[:, (3 + kc) * 512 + t * 128:(3 + kc) * 512 + (t + 1) * 128],
                                W_kvup[:, kc * 1024 + 512:kc * 1024 + 1024], start=(kc == 0), stop=(kc == 1)),
                               r=["W_kvup", f"act{3 + kc}t{t}"], w=[f"P{t}"])
                        tile = c * 4 + t
                        evac_copy(vv[:, tile * 512:(tile + 1) * 512], PS[t][:], [f"P{t}"], [f"v{tile}"])
                    rope(6, kpeT[0:64, c * CH:(c + 1) * CH], [f"kpe{c}"])
                    if b == 0 and c == 0:
                        dump("knT", knT[:, 0:512], ["kn0c0"])
                        dump("kpeT", kpeT[0:64, 0:512], ["kpe0"], nparts=64)
                        dump("vv0", vv[:, 0:512], ["v0"])
                    ck("s3b")
                    for u in range(RING_PF):
                        ring_issue(u)
                    alloc = [0]

                    def ring_alloc():
                        k_ = alloc[0] % 4
                        alloc[0] += 1
                        return k_

                    nkb = 4 * c + 4
                    pendA, pendB = [None], [None]

                    def make_epilogue(mi, h, mm, is_diff, ob, rb):
                        obr, rbr = [f"P{ob}"], [f"P{rb}"]
                        rinv = slotf(16)
                        E1 = slotf(18)
                        E2 = slotf(20)
                        sq = actT[:, 7 * 512:8 * 512]

                        def partA():
                            op("act", lambda e: e.activation(out=rinv, in_=PS[rb][:], func=AF.Ln), r=rbr, w=SR(16, 2))
                            op("act", lambda e: e.activation(out=rinv, in_=rinv, func=AF.Exp, scale=-1.0),
                               r=SR(16, 2), w=SR(16, 2))
                            if not is_diff:
                                f = 4 + h
                                op("dve", lambda e: e.tensor_tensor(out=actT[:, f * 512:(f + 1) * 512], in0=PS[ob][:],
                                                                    in1=rinv, op=ALU.mult), r=obr + SR(16, 2), w=ACT(f))
                            elif mm == 0:
                                op("dve", lambda e: e.tensor_tensor(out=E1, in0=PS[ob][:], in1=rinv, op=ALU.mult),
                                   r=obr + SR(16, 2), w=SR(18, 2))
                            else:
                                op("dve", lambda e: e.tensor_tensor(out=E2, in0=PS[ob][:], in1=rinv, op=ALU.mult),
                                   r=obr + SR(16, 2), w=SR(20, 2))
                                op("dve", lambda e: e.scalar_tensor_tensor(out=E1, in0=E2, scalar=cols[:, C_NLAM:C_NLAM + 1],
                                                                           in1=E1, op0=ALU.mult, op1=ALU.add),
                                   r=SR(18, 4) + ["cols"], w=SR(18, 2))
                                op("act", lambda e: e.activation(out=sq, in_=E1, func=AF.Square), r=SR(18, 2), w=ACT(7))

                        def partB():
                            if not (is_diff and mm == 1):
                                return
                            ssb = ring_alloc()
                            op("pe", lambda e: e.matmul(PS[ssb][:], ones[:], sq, start=True, stop=True),
                               r=ACT(7) + ["ones"], w=[f"P{ssb}"])
                            op("act", lambda e: e.activation(out=E2, in_=PS[ssb][:], func=AF.Ln, bias=cols[:, 15:16],
                                                             scale=1.0 / 128), r=[f"P{ssb}", "cols"], w=SR(20, 2))
                            op("act", lambda e: e.activation(out=E2, in_=E2, func=AF.Exp, scale=-0.5), r=SR(20, 2), w=SR(20, 2))
                            op("dve", lambda e: e.scalar_tensor_tensor(
                                out=actT[:, h * 512:(h + 1) * 512], in0=E1, scalar=cols[:, C_GDP:C_GDP + 1], in1=E2,
                                op0=ALU.mult, op1=ALU.mult), r=SR(18, 4) + ["cols"], w=ACT(h))

                        return partA, partB

                    for mi in range(12):
                        is_diff = mi < 8
                        if is_diff:
                            h, mm = mi // 2, mi % 2
                        else:
                            h, mm = mi - 8, 0
                        ob, rb = 4 + 2 * (mi % 2), 5 + 2 * (mi % 2)
                        obr, rbr = [f"P{ob}"], [f"P{rb}"]

                        def qk(kb, sbk):
                            qlo = max(0, kb - 4 * c) * 128
                            kc_ = kb // 4
                            if is_diff:
                                pr = slice(mm * 64, mm * 64 + 64)
                                op("pe", lambda e: e.matmul(
                                    PS[sbk][:, qlo:512], dkT[pr, h * SEQ + kb * 128:h * SEQ + kb * 128 + 128],
                                    S[pr, h * 512 + qlo:h * 512 + 512], start=True, stop=True),
                                   r=[f"dk{h}c{kc_}"] + SR(h), w=[f"P{sbk}"])
                            else:
                                op("pe", lambda e: e.matmul(
                                    PS[sbk][:, qlo:512], knT[:, h * SEQ + kb * 128:h * SEQ + kb * 128 + 128],
                                    S[:, (4 + h) * 512 + qlo:(4 + h) * 512 + 512], start=True, stop=False),
                                   r=[f"kn{h}c{kc_}"] + SR(4 + h), w=[f"P{sbk}"])
                                op("pe", lambda e: e.matmul(
                                    PS[sbk][:, qlo:512], kpeT[0:64, kb * 128:kb * 128 + 128],
                                    S[0:64, (8 + h) * 512 + qlo:(8 + h) * 512 + 512], start=False, stop=True),
                                   r=[f"kpe{kc_}"] + SR(8 + h), w=[f"P{sbk}"])

                        bank_of = {0: ring_alloc()}
                        qk(0, bank_of[0])
                        if nkb > 1:
                            bank_of[1] = ring_alloc()
                            qk(1, bank_of[1])
                        for kb in range(nkb):
                            sbk = bank_of[kb]
                            if kb + 2 < nkb:
                                bank_of[kb + 2] = ring_alloc()
                                qk(kb + 2, bank_of[kb + 2])
                            qb0 = max(0, kb - 4 * c)
                            qlo = qb0 * 128
                            pt = S[:, (12 + sbk) * 512:(13 + sbk) * 512]
                            ptr = SR(12 + sbk)
                            if is_diff:
                                groups = [(0, 1), (2, 3)] if h == 0 else [(0, 1, 2, 3)]
                                for g in groups:
                                    if g[-1] < qb0:
                                        continue
                                    lo, hi = max(g[0], qb0) * 128, (g[-1] + 1) * 128
                                    delta = (4 * c + g[0] + 1 - kb) if h == 0 else (4 * c + 3 - kb)
                                    op("act", lambda e, lo=lo, hi=hi, delta=delta: e.activation(
                                        out=pt[:, lo:hi], in_=PS[sbk][:, lo:hi], func=AF.Exp,
                                        bias=alibi[:, h * 16 + delta:h * 16 + delta + 1], scale=QSCALE_D),
                                       r=[f"P{sbk}", "alibi"], w=ptr)
                            else:
                                op("act", lambda e: e.activation(out=pt[:, qlo:512], in_=PS[sbk][:, qlo:512], func=AF.Exp,
                                                                 scale=QSCALE_M), r=[f"P{sbk}"], w=ptr)
                            if kb >= 4 * c:
                                op("dve", lambda e: e.tensor_tensor(out=pt[:, qlo:qlo + 128], in0=pt[:, qlo:qlo + 128],
                                                                    in1=tri[:], op=ALU.mult), r=ptr + ["tri"], w=ptr)
                            vsrc = dv if is_diff else vv
                            vres = f"dv{kb}" if is_diff else f"v{kb}"
                            op("pe", lambda e: e.matmul(
                                PS[ob][:, qlo:512], vsrc[:, kb * 512 + h * 128:kb * 512 + (h + 1) * 128], pt[:, qlo:512],
                                start=(kb == 0), stop=(kb == nkb - 1)), r=ptr + [vres], w=obr)
                            op("pe", lambda e: e.matmul(
                                PS[rb][:, qlo:512], ones[:], pt[:, qlo:512], start=(kb == 0), stop=(kb == nkb - 1)),
                               r=ptr + ["ones"], w=rbr)
                            if kb == 0 and pendA[0] is not None:
                                pendA[0]()
                                pendA[0] = None
                            if kb == min(3, nkb - 1) and pendB[0] is not None:
                                pendB[0]()
                                pendB[0] = None
                        pendA[0], pendB[0] = make_epilogue(mi, h, mm, is_diff, ob, rb)
                    pendA[0]()
                    pendB[0]()

                    if b == 0 and c == 0:
                        dump("mergedT", actT[:], ACT_ALL)
                    ck("s4")
                    for t in range(4):
                        for half in range(2):
                            k = t * 2 + half
                            for f in range(8):
                                op("pe", lambda e, f=f: e.matmul(
                                    PS[k][:], actT[:, f * 512 + t * 128:f * 512 + (t + 1) * 128],
                                    W_out[:, f * 1024 + half * 512:f * 1024 + (half + 1) * 512],
                                    start=(f == 0), stop=(f == 7)), r=[f"W_out{f}", f"act{f}t{t}"], w=[f"P{k}"])
                            xs = xc[:, t * 1024 + half * 512:t * 1024 + (half + 1) * 512]
                            op("dve", lambda e: e.tensor_tensor(out=xs, in0=PS[k][:], in1=xs, op=ALU.add),
                               r=[f"P{k}", f"xc{t}h{half}"], w=[f"xc{t}h{half}"])
                        norm_sq(t, 0)
                        norm_rstd(t, 1, 0)
                        norm_scale(t, 0)
                    for t in range(4):
                        norm_tr(b, 1, t)
                    if b == 0 and c == 0:
                        dump("x1", xc[:], XC_ALL)
                    ck("s5")
                    if b == 0 and c == 0:
                        dump("h2T", actT[:], ACT_ALL)
                    ck("s6")
                    for j in range(NJ):
                        rs_ = ring_next()
                        rres = [f"R{rs_}"]
                        ru = ring[:, rs_ * 2048:(rs_ + 1) * 2048]
                        pg, pu = 2 + 2 * (j % 3), 3 + 2 * (j % 3)
                        for kc in range(8):
                            op("pe", lambda e, pg=pg, ru=ru, kc=kc: e.matmul(
                                PS[pg][:], ru[:, kc * 128:(kc + 1) * 128], actT[:, kc * 512:(kc + 1) * 512],
                                start=(kc == 0), stop=(kc == 7)), r=rres + ACT(kc), w=[f"P{pg}"])
                        for kc in range(8):
                            op("pe", lambda e, pu=pu, ru=ru, kc=kc: e.matmul(
                                PS[pu][:], ru[:, 1024 + kc * 128:1024 + (kc + 1) * 128], actT[:, kc * 512:(kc + 1) * 512],
                                start=(kc == 0), stop=(kc == 7)), r=rres + ACT(kc), w=[f"P{pu}"])
                        sg = sgt[:, (j % 2) * 512:(j % 2 + 1) * 512]
                        op("act", lambda e, pg=pg, sg=sg: e.activation(out=sg, in_=PS[pg][:], func=AF.Silu),
                           r=[f"P{pg}"], w=[f"sgt{j % 2}"])
                        op("dve", lambda e, pu=pu, sg=sg, j=j: e.tensor_tensor(out=slot(j), in0=PS[pu][:], in1=sg, op=ALU.mult),
                           r=[f"P{pu}", f"sgt{j % 2}"], w=SR(j))
                        ring_issue(j + RING_PF)
                    if b == 0 and c == 0:
                        dump("aT", S[:], S_ALL)
                    ck("s7a")
                    for j in range(NJ):
                        rs_ = ring_next()
                        rres = [f"R{rs_}"]
                        ru = ring[:, rs_ * 2048:rs_ * 2048 + 1024]
                        op("dve", lambda e, ru=ru: e.tensor_tensor(out=ru, in0=ru, in1=Gf[:], op=ALU.mult),
                           r=rres + ["Gf"], w=rres)
                        for t in range(4):
                            for half in range(2):
                                k = t * 2 + half
                                op("pe", lambda e, k=k, t=t, half=half, j=j, ru=ru: e.matmul(
                                    PS[k][:], S[:, j * 512 + t * 128:j * 512 + (t + 1) * 128],
                                    ru[:, half * 512:(half + 1) * 512], start=(j == 0), stop=(j == NJ - 1)),
                                   r=rres + SR(j), w=[f"P{k}"])
                        ring_issue(NJ + j + RING_PF)
                    ck("s7b")
                    junk = S[:, 20 * 512:22 * 512]
                    stags = []
                    for t in range(4):
                        if t < 3:
                            stags.append((Sf[:, (8 + 4 * t) * 256:(8 + 4 * t) * 256 + 1024], SR(8 + 4 * t, 4)))
                        else:
                            stags.append((sgt[:, 0:1024], ["sgt0", "sgt1"]))
                    for t in range(4):
                        xr = [f"xc{t}h0", f"xc{t}h1"]
                        xt = xc[:, t * 1024:(t + 1) * 1024]
                        stag, stres = stags[t]
                        for half in range(2):
                            k = t * 2 + half
                            xs = xc[:, t * 1024 + half * 512:t * 1024 + (half + 1) * 512]
                            op("dve", lambda e: e.tensor_tensor(out=xs, in0=PS[k][:], in1=xs, op=ALU.add),
                               r=[f"P{k}", f"xc{t}h{half}"], w=[f"xc{t}h{half}"])
                        sres = [f"st{8 + t}"]
                        op("dve", lambda e: e.memset(st[:, 8 + t:9 + t], 0.0), w=sres)
                        op("act", lambda e: e.activation(out=junk, in_=xt, func=AF.Square, accum_out=st[:, 8 + t:9 + t]),
                           r=xr + sres, w=SR(20, 2) + sres)
                        op("act", lambda e: e.activation(out=st[:, 12 + t:13 + t], in_=st[:, 8 + t:9 + t], func=AF.Ln,
                                                         bias=cols[:, 15:16], scale=1.0 / D), r=sres + ["cols"], w=sres)
                        op("act", lambda e: e.activation(out=st[:, 12 + t:13 + t], in_=st[:, 12 + t:13 + t], func=AF.Exp,
                                                         scale=-0.5), r=sres, w=sres)
                    for t in range(4):
                        xr = [f"xc{t}h0", f"xc{t}h1"]
                        xt = xc[:, t * 1024:(t + 1) * 1024]
                        stag, stres = stags[t]
                        sres = [f"st{8 + t}"]
                        op("dve", lambda e: e.scalar_tensor_tensor(out=stag, in0=xt, scalar=st[:, 12 + t:13 + t], in1=gfin[:],
                                                                   op0=ALU.mult, op1=ALU.mult),
                           r=xr + sres + ["gfin"], w=stres)
                        pending_stores.append((stag, stres, t, tok0))

            flush_stores()

        nops, nwait = P.run(body)
        print(f"[kernel] ops={nops} waits={nwait}", flush=True)
    return nc


_NC_CACHE = {}


def _prep_shared(inp):
    f = np.float32
    w_qup = inp["w_q_up"][0]
    qcols = [h * 192 + i for h in range(4) for i in range(128)] + [h * 192 + 128 + i for h in range(4) for i in range(64)]
    w_kvup = inp["w_kv_up"][0]
    kvcols = [h * 256 + i for h in range(4) for i in range(128)] + [h * 256 + 128 + i for h in range(4) for i in range(128)]
    wg = inp["w_ffn_gate"][0].reshape(8, 128, NJ, 128)
    wu = inp["w_ffn_up"][0].reshape(8, 128, NJ, 128)
    wgu = np.stack([wg.transpose(2, 1, 0, 3), wu.transpose(2, 1, 0, 3)], axis=2)
    b_ada = inp["b_ada"][0]
    shared = {
        "w_ada": np.ascontiguousarray(inp["w_ada"][0], dtype=f),
        "b_adaT": np.ascontiguousarray(b_ada.reshape(48, 128).T, dtype=f),
        "b_gate": np.ascontiguousarray(np.concatenate([b_ada[2048:3072], b_ada[5120:6144]])[None, :], dtype=f),
        "g_mixT": np.ascontiguousarray(inp["g_mix"][0].reshape(8, 128).T, dtype=f),
        "g_ffnT": np.ascontiguousarray(inp["g_ffn"][0].reshape(8, 128).T, dtype=f),
        "w_in": np.ascontiguousarray(inp["w_in"][0], dtype=f),
        "lam": np.ascontiguousarray(np.concatenate([inp["lambda_q1"][0], inp["lambda_k1"][0],
                                                    inp["lambda_q2"][0], inp["lambda_k2"][0]])[None, :], dtype=f),
        "g_diff": np.ascontiguousarray(inp["g_diff_out"][0].reshape(128, 1), dtype=f),
        "g_qlT": np.ascontiguousarray(inp["g_q_lat"][0].reshape(3, 128).T, dtype=f),
        "w_qup": np.ascontiguousarray(w_qup[:, qcols], dtype=f),
        "g_kvlT": np.ascontiguousarray(inp["g_kv_lat"][0].reshape(2, 128).T, dtype=f),
        "w_kvup": np.ascontiguousarray(w_kvup[:, kvcols], dtype=f),
        "w_out": np.ascontiguousarray(inp["w_out"][0], dtype=f),
        "w_gu": np.ascontiguousarray(wgu.reshape(NJ, 128, 2048), dtype=f),
        "w_dn": np.ascontiguousarray(inp["w_ffn_down"][0], dtype=f),
        "g_final": np.ascontiguousarray(inp["g_final"].reshape(1, D), dtype=f),
    }
    return shared


def kernel(**inputs):
    inp = {k: np.asarray(v) for k, v in inputs.items()}
    if "nc" not in _NC_CACHE:
        _NC_CACHE["nc"] = build()
    nc = _NC_CACHE["nc"]
    shared = _prep_shared(inp)
    x = inp["x"].astype(np.float32, copy=False)
    c = inp["c"].astype(np.float32, copy=False)
    in_maps = []
    for core in range(NCORES):
        m = dict(shared)
        m["x"] = np.ascontiguousarray(x[core * BPC:(core + 1) * BPC].reshape(BPC * SEQ, D))
        cc = c[core * BPC:(core + 1) * BPC]
        m["cT"] = np.ascontiguousarray(cc.reshape(BPC, 8, 128).transpose(2, 1, 0).reshape(128, 8 * BPC))
        in_maps.append(m)
    res = run_bass_kernel_spmd(nc, in_maps, core_ids=list(range(NCORES)))
    if DBG:
        LAST_DBG["dbg"] = np.asarray(res.results[0]["dbg"])
        LAST_DBG["map"] = dict(DBG_MAP)
    outs = [np.asarray(r["out"]).reshape(BPC, SEQ, D) for r in res.results]
    return np.concatenate(outs, axis=0).astype(np.float32, copy=False)
```

```python
import contextlib
import math
import numpy as np
import concourse.bass as bass
import concourse.mybir as mybir
from concourse.bass_utils import run_bass_kernel_spmd

F32 = mybir.dt.float32
BF16 = mybir.dt.bfloat16
AF = mybir.ActivationFunctionType
ALU = mybir.AluOpType

NCORES = 8
SEQ = 2048
D = 1024
BPC = 4
CH = 512
NCH = SEQ // CH
DFF = 2816
NJ = DFF // 128
INW = 2240
EPS = 1e-6
LAMBDA_INIT = 0.8 - 0.6 * math.exp(-0.3 * 0)
SLOPES = [2.0 ** (-8.0 * (i + 1) / 4) for i in range(4)]
SAME_ENGINE_SYNC = True
import os
STOP = os.environ.get("KSTOP", "")
DBG = bool(os.environ.get("KDBG", ""))
DBGW = 48000
DBG_MAP = {}
LAST_DBG = {}


class _Stop(Exception):
    pass


def ck(label):
    if STOP == label:
        raise _Stop()


class Prog:
    def __init__(self, nc, es):
        self.nc = nc
        self.es = es
        self.eng = {"pe": nc.tensor, "act": nc.scalar, "dve": nc.vector, "pool": nc.gpsimd, "sp": nc.sync}
        self.esem = {k: es.enter_context(nc.semaphore("sem_" + k)) for k in self.eng}
        self.dsem = {}
        self.meta = []
        self.mode = "analyze"
        self.idx = 0
        self.nwait = 0

    def op(self, eng, fn, r=(), w=(), dma=None):
        if self.mode == "analyze":
            self.meta.append((eng, tuple(r), tuple(w), dma))
            return
        i = self.idx
        self.idx += 1
        e = self.eng[eng]
        need = {}
        for j in self.deps[i]:
            key, val = self.tick[j]
            if need.get(key, 0) < val:
                need[key] = val
        wd = self.waited[eng]
        for key, val in need.items():
            if wd.get(key, 0) >= val:
                continue
            wd[key] = val
            sem = self.dsem[key[1]] if key[0] == "d" else self.esem[key[1]]
            e.wait_ge(sem, val)
            self.nwait += 1
        inst = fn(e)
        if dma is not None:
            inst.then_inc(self.dsem[dma], 16)
        elif i in self.signaled:
            inst.then_inc(self.esem[eng], 1)

    def analyze(self):
        ops = self.meta
        last_w = {}
        readers = {}
        last_dma = {}
        deps = []
        for i, (eng, r, w, dma) in enumerate(ops):
            d = set()
            for x in r:
                if x in last_w:
                    d.add(last_w[x])
                if x[0] == "P" and x[1:].isdigit():
                    rd = readers.get(x)
                    if rd:
                        d.update(j for k_, j in rd.items() if k_ != eng)
            for x in w:
                if x in last_w:
                    d.add(last_w[x])
                rd = readers.get(x)
                if rd:
                    d.update(rd.values())
            if dma is not None and dma in last_dma:
                d.add(last_dma[dma])
            d.discard(i)
            for x in w:
                last_w[x] = i
                readers[x] = {}
            for x in r:
                if x in w:
                    continue
                key = eng if dma is None else ("dma", i)
                readers.setdefault(x, {})[key] = i
            if dma is not None:
                last_dma[dma] = i
            kept = []
            for j in d:
                ej, _, _, dj = ops[j]
                if dj is None and dma is None and ej == eng and (eng == "pe" or not SAME_ENGINE_SYNC):
                    continue
                kept.append(j)
            deps.append(kept)
        signaled = set()
        for kept in deps:
            for j in kept:
                if ops[j][3] is None:
                    signaled.add(j)
        tick = {}
        cnt = {k: 0 for k in self.eng}
        dcnt = {}
        for i, (eng, r, w, dma) in enumerate(ops):
            if dma is not None:
                if dma not in self.dsem:
                    self.dsem[dma] = self.es.enter_context(self.nc.semaphore("d_" + dma))
                dcnt[dma] = dcnt.get(dma, 0) + 16
                tick[i] = (("d", dma), dcnt[dma])
            elif i in signaled:
                cnt[eng] += 1
                tick[i] = (("e", eng), cnt[eng])
        self.deps, self.signaled, self.tick, self.dcnt = deps, signaled, tick, dcnt
        self.waited = {k: {} for k in self.eng}

    def run(self, body):
        self.mode = "analyze"
        try:
            body()
        except _Stop:
            pass
        self.analyze()
        self.mode = "emit"
        self.idx = 0
        try:
            body()
        except _Stop:
            pass
        assert self.idx == len(self.meta), (self.idx, len(self.meta))
        sp = self.eng["sp"]
        for name, sem in self.dsem.items():
            if self.waited["sp"].get(("d", name), 0) < self.dcnt[name]:
                sp.wait_ge(sem, self.dcnt[name])
        return len(self.meta), self.nwait


def build():
    nc = bass.Bass("TRN2", target_bir_lowering=False)

    def dram(name, shape, dt=F32, kind="ExternalInput"):
        return nc.dram_tensor(name, list(shape), dt, kind=kind).ap()

    x_d = dram("x", [BPC * SEQ, D])
    out_d = dram("out", [BPC * SEQ, D], kind="ExternalOutput")
    cT_d = dram("cT", [128, 8 * BPC])
    wada_d = dram("w_ada", [D, 6 * D])
    badaT_d = dram("b_adaT", [128, 48])
    bgate_d = dram("b_gate", [1, 2 * D])
    gmixT_d = dram("g_mixT", [128, 8])
    gffnT_d = dram("g_ffnT", [128, 8])
    win_d = dram("w_in", [D, INW])
    lam_d = dram("lam", [1, 256])
    gdiff_d = dram("g_diff", [128, 1])
    gqlT_d = dram("g_qlT", [128, 3])
    wqup_d = dram("w_qup", [384, 768])
    gkvlT_d = dram("g_kvlT", [128, 2])
    wkvup_d = dram("w_kvup", [256, 1024])
    wout_d = dram("w_out", [D, D])
    wgu_d = dram("w_gu", [NJ, 128, 2048])
    wdn_d = dram("w_dn", [DFF, D])
    gfin_d = dram("g_final", [1, D])
    gates_d = dram("gates_scr", [BPC, 2 * D], kind="Internal")
    rope_d = dram("rope_scr", [2, 64, SEQ], kind="Internal")
    wgu_s = dram("wgu_bf16_scr", [NJ, 128, 2048], BF16, kind="Internal")
    wdn_s = dram("wdn_bf16_scr", [DFF, D], BF16, kind="Internal")
    dbg_d = dram("dbg", [128, DBGW], kind="ExternalOutput") if DBG else None

    es = contextlib.ExitStack()
    with es:
        def sb(name, shape, dt):
            return es.enter_context(nc.sbuf_tensor(name, list(shape), dt))

        W_in = sb("W_in", [128, 8 * INW], BF16)
        W_qup = sb("W_qup", [128, 3 * 768], BF16)
        W_kvup = sb("W_kvup", [128, 2 * 1024], BF16)
        W_out = sb("W_out", [128, 8 * 1024], BF16)
        dkT = sb("dkT", [128, 4 * SEQ], BF16)
        dv = sb("dv", [128, 16 * 512], BF16)
        knT = sb("knT", [128, 4 * SEQ], BF16)
        kpeT = sb("kpeT", [128, SEQ], BF16)
        vv = sb("vv", [128, 16 * 512], BF16)
        xc = sb("xc", [128, 4 * 1024], F32)
        actT = sb("actT", [128, 8 * 512], BF16)
        S = sb("S", [128, NJ * 512], BF16)
        ring = sb("ring", [128, 4 * 2048], BF16)
        Ga = sb("Ga", [128, 1024], F32)
        Gf = sb("Gf", [128, 1024], F32)
        gfin = sb("gfin", [128, 1024], F32)
        sgt = sb("sgt", [128, 2 * 512], F32)
        ident = sb("ident", [128, 128], BF16)
        ones = sb("ones", [128, 128], BF16)
        tri = sb("tri", [128, 128], BF16)
        alibi = sb("alibi", [128, 64], F32)
        AB = sb("AB", [128, BPC * 2 * 2 * 8], F32)
        st = sb("st", [128, 16], F32)
        cols = sb("cols", [128, 32], F32)
        PS = [es.enter_context(nc.psum_tensor(f"ps{k}", [128, 512], F32)) for k in range(8)]

        P = Prog(nc, es)
        op = P.op

        def body():
            Sf = S[:].bitcast(F32)
            modT = sgt[:, 256:256 + 48 * BPC]
            xcb = xc[:].bitcast(BF16)

            def slot(j, n=1):
                return S[:, j * 512:(j + n) * 512]

            def slotf(j):
                return Sf[:, j * 256:j * 256 + 512]

            def SR(j, n=1):
                return [f"S{k}" for k in range(j, j + n)]

            dbg_off = [0]

            def dump(label, ap, res, nparts=128):
                if not DBG or label in DBG_MAP and P.mode == "analyze":
                    return
                wdt = ap.shape[-1]
                if P.mode == "analyze":
                    DBG_MAP[label] = (dbg_off[0], wdt, nparts)
                o = DBG_MAP[label][0]
                dbg_off[0] = o + wdt
                op("pool", lambda e: e.dma_start(out=dbg_d[0:nparts, o:o + wdt], in_=ap), r=res, w=["dbg_" + label],
                   dma="dbg")

            XC_ALL = [f"xc{t}h{h}" for t in range(4) for h in range(2)]
            S_ALL = SR(0, NJ)
            ACT_ALL = [f"act{k}t{t}" for k in range(8) for t in range(4)]

            def ACT(k):
                return [f"act{k}t{t}" for t in range(4)]

            def AB_col(b, af, ab, kc):
                o = ((b * 2 + af) * 2 + ab) * 8 + kc
                return AB[:, o:o + 1]

            C_NLAM, C_GDP, C_GD, C_GQ, C_GKV = 0, 1, 2, 3, 6

            tmpf = Sf[:, 0:128]
            op("pool", lambda e: e.memset(tmpf, 1.0), w=SR(0))
            op("pool", lambda e: e.affine_select(out=tmpf, in_=tmpf, pattern=[[-1, 128]], compare_op=ALU.is_equal,
                                                 fill=0.0, base=0, channel_multiplier=1), r=SR(0), w=SR(0))
            op("dve", lambda e: e.tensor_copy(out=ident[:], in_=tmpf), r=SR(0), w=["ident"])
            tmpf2 = Sf[:, 256:384]
            op("pool", lambda e: e.memset(tmpf2, 1.0), w=SR(1))
            op("pool", lambda e: e.affine_select(out=tmpf2, in_=tmpf2, pattern=[[1, 128]], compare_op=ALU.is_ge,
                                                 fill=0.0, base=0, channel_multiplier=-1), r=SR(1), w=SR(1))
            op("dve", lambda e: e.tensor_copy(out=tri[:], in_=tmpf2), r=SR(1), w=["tri"])
            op("dve", lambda e: e.memset(ones[:], 1.0), w=["ones"])
            op("dve", lambda e: e.memset(cols[:, 15:16], EPS), w=["cols"])
            tmpa = Sf[:, 512:528]
            op("pool", lambda e: e.iota(tmpa, pattern=[[-128, 16]], base=-127, channel_multiplier=1,
                                        allow_small_or_imprecise_dtypes=True), w=SR(2))
            tmpa0 = Sf[:, 768:784]
            op("pool", lambda e: e.iota(tmpa0, pattern=[[-128, 16]], base=0, channel_multiplier=1,
                                        allow_small_or_imprecise_dtypes=True), w=SR(3))
            op("dve", lambda e: e.tensor_scalar(out=alibi[:, 0:16], in0=tmpa0, scalar1=SLOPES[0], scalar2=None,
                                                op0=ALU.mult), r=SR(3), w=["alibi"])
            for h in range(1, 4):
                op("dve", lambda e, h=h: e.tensor_scalar(out=alibi[:, h * 16:(h + 1) * 16], in0=tmpa, scalar1=SLOPES[h],
                                                         scalar2=None, op0=ALU.mult), r=SR(2), w=["alibi"])

            ck("consts")
            win_v = win_d.rearrange("(kc p) n -> p kc n", p=128)
            for kc in range(8):
                op("pool", lambda e, kc=kc: e.dma_start(out=W_in[:, kc * INW:(kc + 1) * INW], in_=win_v[:, kc, :]),
                   w=["W_in"], dma=f"w{kc % 2}")
            wq_v = wqup_d.rearrange("(kc p) n -> p kc n", p=128)
            for kc in range(3):
                op("pool", lambda e, kc=kc: e.dma_start(out=W_qup[:, kc * 768:(kc + 1) * 768], in_=wq_v[:, kc, :]),
                   w=["W_qup"], dma="w2")
            wkv_v = wkvup_d.rearrange("(kc p) n -> p kc n", p=128)
            for kc in range(2):
                op("pool", lambda e, kc=kc: e.dma_start(out=W_kvup[:, kc * 1024:(kc + 1) * 1024], in_=wkv_v[:, kc, :]),
                   w=["W_kvup"], dma="w3")
            wo_v = wout_d.rearrange("(kc p) n -> p kc n", p=128)

            ck("weights")
            cTs = Sf[:, 1024:1024 + 8 * BPC]
            op("sp", lambda e: e.dma_start(out=cTs, in_=cT_d[:, :]), w=SR(4), dma="p0")
            op("sp", lambda e: e.dma_start(out=cols[:, 16:24], in_=gmixT_d[:, :]), w=["cols"], dma="p1")
            op("sp", lambda e: e.dma_start(out=cols[:, 24:32], in_=gffnT_d[:, :]), w=["cols"], dma="p1")
            op("sp", lambda e: e.dma_start(out=cols[:, C_GD:C_GD + 1], in_=gdiff_d[:, :]), w=["cols"], dma="p1")
            op("sp", lambda e: e.dma_start(out=cols[:, C_GQ:C_GQ + 3], in_=gqlT_d[:, :]), w=["cols"], dma="p1")
            op("sp", lambda e: e.dma_start(out=cols[:, C_GKV:C_GKV + 2], in_=gkvlT_d[:, :]), w=["cols"], dma="p1")
            badaT = Sf[:, 1280:1280 + 48]
            op("sp", lambda e: e.dma_start(out=badaT, in_=badaT_d[:, :]), w=SR(5), dma="p2")
            op("sp", lambda e: e.dma_start(out=gfin[:], in_=gfin_d[:, :].partition_broadcast(128)), w=["gfin"], dma="p3")
            lamb = Sf[:, 1536:1536 + 256]
            op("sp", lambda e: e.dma_start(out=lamb, in_=lam_d[:, :].partition_broadcast(128)), w=SR(6), dma="p4")
            bg = Sf[0:BPC, 2048:2048 + 2048]
            op("sp", lambda e: e.dma_start(out=bg, in_=bgate_d[:, :].partition_broadcast(BPC)), w=SR(8, 8), dma="p5")

            ck("params")
            lprod = Sf[:, 1792:1792 + 128]
            op("dve", lambda e: e.tensor_tensor(out=lprod[:, 0:64], in0=lamb[:, 0:64], in1=lamb[:, 64:128], op=ALU.mult),
               r=SR(6), w=SR(7))
            op("dve", lambda e: e.tensor_tensor(out=lprod[:, 64:128], in0=lamb[:, 128:192], in1=lamb[:, 192:256], op=ALU.mult),
               r=SR(6), w=SR(7))
            op("dve", lambda e: e.reduce_sum(out=cols[:, 8:9], in_=lprod[:, 0:64], axis=mybir.AxisListType.X), r=SR(7), w=["cols"])
            op("dve", lambda e: e.reduce_sum(out=cols[:, 9:10], in_=lprod[:, 64:128], axis=mybir.AxisListType.X), r=SR(7), w=["cols"])
            op("act", lambda e: e.activation(out=cols[:, 10:12], in_=cols[:, 8:10], func=AF.Exp), r=["cols"], w=["cols"])
            op("dve", lambda e: e.tensor_tensor(out=cols[:, C_NLAM:C_NLAM + 1], in0=cols[:, 11:12], in1=cols[:, 10:11],
                                                op=ALU.subtract), r=["cols"], w=["cols"])
            op("dve", lambda e: e.tensor_scalar(out=cols[:, C_NLAM:C_NLAM + 1], in0=cols[:, C_NLAM:C_NLAM + 1],
                                                scalar1=-LAMBDA_INIT, scalar2=None, op0=ALU.add), r=["cols"], w=["cols"])
            op("dve", lambda e: e.tensor_scalar(out=cols[:, C_GDP:C_GDP + 1], in0=cols[:, C_GD:C_GD + 1],
                                                scalar1=1.0 - LAMBDA_INIT, scalar2=None, op0=ALU.mult), r=["cols"], w=["cols"])

            ck("lam")
            scb = sgt[:].bitcast(BF16)[:, 0:8 * BPC]
            SCB = ["sgt0", "sgt1"]
            op("act", lambda e: e.activation(out=scb, in_=cTs, func=AF.Silu), r=SR(4), w=SCB)
            wada_v = wada_d.rearrange("(kc p) n -> p kc n", p=128)
            gsb = actT[:].bitcast(F32)[0:BPC, 0:2048]
            bank = 0
            for u in range(12):
                sl = u % 2
                unit = xcb[:, sl * 4096:(sl + 1) * 4096]
                ures = XC_ALL[sl * 4:(sl + 1) * 4]
                op("pool", lambda e, unit=unit, u=u: e.dma_start(
                    out=unit.rearrange("p (kc n) -> p kc n", kc=8), in_=wada_v[:, :, u * 512:(u + 1) * 512]),
                   w=ures, dma=f"ada{sl}")
                if u in (4, 5, 10, 11):
                    g = 0 if u < 6 else 1
                    half = u % 2 if u < 6 else (u - 10)
                    pb = PS[bank % 8]
                    pres = [f"P{bank % 8}"]
                    bank += 1
                    for kc in range(8):
                        op("pe", lambda e, pb=pb, unit=unit, kc=kc: e.matmul(
                            pb[0:BPC, :], scb[:, kc * BPC:(kc + 1) * BPC], unit[:, kc * 512:(kc + 1) * 512],
                            start=(kc == 0), stop=(kc == 7)), r=ures + SCB, w=pres)
                    o = g * 1024 + half * 512
                    op("dve", lambda e, pb=pb, o=o: e.tensor_tensor(out=gsb[:, o:o + 512], in0=pb[0:BPC, :],
                                                                    in1=bg[:, o:o + 512], op=ALU.add),
                       r=pres + SR(8, 8), w=ACT_ALL)
                else:
                    for mm in range(4):
                        m = 4 * u + mm
                        pb = PS[bank % 8]
                        pres = [f"P{bank % 8}"]
                        bank += 1
                        for kc in range(8):
                            op("pe", lambda e, pb=pb, unit=unit, kc=kc, mm=mm: e.matmul(
                                pb[:, 0:BPC], unit[:, kc * 512 + mm * 128:kc * 512 + (mm + 1) * 128],
                                scb[:, kc * BPC:(kc + 1) * BPC], start=(kc == 0), stop=(kc == 7)),
                               r=ures + SCB, w=pres)
                        op("dve", lambda e, pb=pb, m=m: e.tensor_scalar(
                            out=modT[:, m * BPC:(m + 1) * BPC], in0=pb[:, 0:BPC], scalar1=badaT[:, m:m + 1], scalar2=None,
                            op0=ALU.add), r=pres + SR(5), w=["sgt0"])
            op("sp", lambda e: e.dma_start(out=gates_d[:, :], in_=gsb), r=ACT_ALL, w=["gates_d"], dma="p6")
            modT3 = modT.rearrange("p (m b) -> p m b", b=BPC)
            for b in range(BPC):
                for af, (m_sh, m_sc, gcol) in enumerate([(0, 8, 16), (24, 32, 24)]):
                    oA = ((b * 2 + af) * 2 + 0) * 8
                    oB = ((b * 2 + af) * 2 + 1) * 8
                    op("dve", lambda e, b=b, m_sc=m_sc, gcol=gcol, oA=oA: e.scalar_tensor_tensor(
                        out=AB[:, oA:oA + 8], in0=modT3[:, m_sc:m_sc + 8, b], scalar=1.0, in1=cols[:, gcol:gcol + 8],
                        op0=ALU.add, op1=ALU.mult), r=["sgt0", "cols"], w=["AB"])
                    op("dve", lambda e, b=b, m_sh=m_sh, oB=oB: e.tensor_copy(out=AB[:, oB:oB + 8],
                                                                             in_=modT3[:, m_sh:m_sh + 8, b]),
                       r=["sgt0"], w=["AB"])

            ck("ada")
            for j in range(NJ):
                op("pool", lambda e, j=j: e.dma_start(out=wgu_s[j], in_=wgu_d[j]), w=[f"wgu_s{j}"], dma=f"pc{j % 2}")
            for j in range(NJ):
                op("pool", lambda e, j=j: e.dma_start(out=wdn_s[j * 128:(j + 1) * 128, :],
                                                      in_=wdn_d[j * 128:(j + 1) * 128, :]),
                   w=[f"wdn_s{j}"], dma=f"pc{j % 2}")
            xcf = xc[:]
            pos = xcf[0:64, 0:2048]
            ang = xcf[0:64, 2048:4096]
            op("pool", lambda e: e.iota(pos, pattern=[[1, 2048]], base=0, channel_multiplier=0,
                                        allow_small_or_imprecise_dtypes=True), w=XC_ALL)
            op("pool", lambda e: e.iota(cols[0:32, 12:13], pattern=[[0, 1]], base=0, channel_multiplier=1,
                                        allow_small_or_imprecise_dtypes=True), w=["cols"])
            op("pool", lambda e: e.iota(cols[32:64, 12:13], pattern=[[0, 1]], base=0, channel_multiplier=1,
                                        allow_small_or_imprecise_dtypes=True), w=["cols"])
            op("act", lambda e: e.activation(out=cols[0:64, 13:14], in_=cols[0:64, 12:13], func=AF.Exp,
                                             scale=-math.log(10000.0) / 32.0), r=["cols"], w=["cols"])
            op("dve", lambda e: e.tensor_scalar(out=ang, in0=pos, scalar1=cols[0:64, 13:14], scalar2=None, op0=ALU.mult),
               r=XC_ALL + ["cols"], w=XC_ALL)
            TWO_PI = 2.0 * math.pi
            MAGIC = 12582912.0
            PI_LO = 3.141592
            for which, off in ((0, 0.5 * math.pi), (1, 0.0)):
                dst = Sf[0:64, which * 2048:(which + 1) * 2048]
                op("dve", lambda e, off=off: e.tensor_scalar(out=pos, in0=ang, scalar1=off, scalar2=1.0 / TWO_PI, op0=ALU.add,
                                                             op1=ALU.mult), r=XC_ALL, w=XC_ALL)
                op("dve", lambda e: e.tensor_scalar(out=pos, in0=pos, scalar1=MAGIC, scalar2=None, op0=ALU.add),
                   r=XC_ALL, w=XC_ALL)
                op("dve", lambda e: e.tensor_scalar(out=pos, in0=pos, scalar1=-MAGIC, scalar2=None, op0=ALU.add),
                   r=XC_ALL, w=XC_ALL)
                op("dve", lambda e: e.scalar_tensor_tensor(out=pos, in0=pos, scalar=-TWO_PI, in1=ang, op0=ALU.mult,
                                                           op1=ALU.add), r=XC_ALL, w=XC_ALL)
                op("dve", lambda e, off=off: e.tensor_scalar(out=pos, in0=pos, scalar1=off, scalar2=None, op0=ALU.add),
                   r=XC_ALL, w=XC_ALL)
                op("dve", lambda e: e.tensor_scalar(out=pos, in0=pos, scalar1=-PI_LO, scalar2=PI_LO,
                                                    op0=ALU.max, op1=ALU.min), r=XC_ALL, w=XC_ALL)
                op("act", lambda e, dst=dst: e.activation(out=dst, in_=pos, func=AF.Sin), r=XC_ALL, w=S_ALL)
            sgn_hi = Sf[32:64, 2048:4096]
            op("dve", lambda e: e.tensor_scalar(out=sgn_hi, in0=sgn_hi, scalar1=-1.0, scalar2=None, op0=ALU.mult),
               r=S_ALL, w=S_ALL)
            op("sp", lambda e: e.dma_start(out=rope_d[0], in_=Sf[0:64, 0:2048]), r=S_ALL, w=["rope_d"], dma="p7")
            op("sp", lambda e: e.dma_start(out=rope_d[1], in_=Sf[0:64, 2048:4096]), r=S_ALL, w=["rope_d"], dma="p7")

            ck("rope")
            dump("AB", AB[:], ["AB"])
            dump("modT", modT, ["sgt0"])
            dump("cols", cols[:], ["cols"])
            dump("alibi", alibi[:], ["alibi"])
            dump("tri", tri[:], ["tri"])
            dump("ident", ident[:], ["ident"])
            evac_rr = [0]

            def evac_copy(dst, src, r, w, force_act=False):
                evac_rr[0] += 1
                if force_act or evac_rr[0] % 2 == 0:
                    op("act", lambda e: e.activation(out=dst, in_=src, func=AF.Copy), r=r, w=w)
                else:
                    op("dve", lambda e: e.tensor_copy(out=dst, in_=src), r=r, w=w)

            def hn_buf(t):
                return S[:, t * 1024:(t + 1) * 1024], SR(2 * t, 2)

            def norm_sq(t, col0):
                xr = [f"xc{t}h0", f"xc{t}h1"]
                xt = xc[:, t * 1024:(t + 1) * 1024]
                hn, hres = hn_buf(t)
                sres = [f"st{t}"]
                op("dve", lambda e: e.memset(st[:, col0 + t:col0 + t + 1], 0.0), w=sres)
                op("act", lambda e: e.activation(out=hn, in_=xt, func=AF.Square, accum_out=st[:, col0 + t:col0 + t + 1]),
                   r=xr + sres, w=hres + sres)

            def norm_rstd(t0, n, col0):
                sres = [f"st{t}" for t in range(t0, t0 + n)]
                src_ = st[:, col0 + t0:col0 + t0 + n]
                dst_ = st[:, col0 + 4 + t0:col0 + 4 + t0 + n]
                op("act", lambda e: e.activation(out=dst_, in_=src_, func=AF.Ln, bias=cols[:, 15:16], scale=1.0 / D),
                   r=sres + ["cols"], w=sres)
                op("act", lambda e: e.activation(out=dst_, in_=dst_, func=AF.Exp, scale=-0.5), r=sres, w=sres)

            def norm_scale(t, col0):
                xr = [f"xc{t}h0", f"xc{t}h1"]
                xt = xc[:, t * 1024:(t + 1) * 1024]
                hn, hres = hn_buf(t)
                op("dve", lambda e: e.tensor_scalar(out=hn, in0=xt, scalar1=st[:, col0 + 4 + t:col0 + 5 + t], scalar2=None,
                                                    op0=ALU.mult), r=xr + [f"st{t}"], w=hres)

            def norm_tr(b, af, t):
                hn, hres = hn_buf(t)
                pb = PS[t % 2]
                pbb = pb[:].bitcast(BF16)
                pres = [f"P{t % 2}"]
                for kc in range(8):
                    op("pe", lambda e, kc=kc: e.transpose(pbb[:, kc * 128:(kc + 1) * 128],
                                                          hn[:, kc * 128:(kc + 1) * 128], ident[:]),
                       r=hres + ["ident"], w=pres)
                for kc in range(8):
                    dst = actT[:, kc * 512 + t * 128:kc * 512 + (t + 1) * 128]
                    src_ = pbb[:, kc * 128:(kc + 1) * 128]
                    a_ = AB_col(b, af, 0, kc)
                    b_ = AB_col(b, af, 1, kc)
                    if t % 2 == 0:
                        op("dve", lambda e, dst=dst, src_=src_, a_=a_, b_=b_: e.tensor_scalar(
                            out=dst, in0=src_, scalar1=a_, scalar2=b_, op0=ALU.mult, op1=ALU.add),
                           r=pres + ["AB"], w=[f"act{kc}t{t}"])
                    else:
                        op("act", lambda e, dst=dst, src_=src_, a_=a_, b_=b_: e.activation(
                            out=dst, in_=src_, func=AF.Identity, bias=b_, scale=a_),
                           r=pres + ["AB"], w=[f"act{kc}t{t}"])

            def lat_norm(banks, n, gc0, rstd_slot, dst_act0, inv_n, ssbank):
                for i in range(n):
                    sq = actT[:, (5 + i % 2) * 512:(6 + i % 2) * 512]
                    op("act", lambda e, sq=sq, i=i: e.activation(out=sq, in_=PS[banks[i]][:], func=AF.Square),
                       r=[f"P{banks[i]}"], w=ACT(5 + i % 2))
                    op("pe", lambda e, sq=sq, i=i: e.matmul(PS[ssbank][:], ones[:], sq, start=(i == 0), stop=(i == n - 1)),
                       r=ACT(5 + i % 2) + ["ones"], w=[f"P{ssbank}"])
                rs = slotf(rstd_slot)
                op("act", lambda e: e.activation(out=rs, in_=PS[ssbank][:], func=AF.Ln, bias=cols[:, 15:16], scale=inv_n),
                   r=[f"P{ssbank}", "cols"], w=SR(rstd_slot, 2))
                op("act", lambda e: e.activation(out=rs, in_=rs, func=AF.Exp, scale=-0.5), r=SR(rstd_slot, 2), w=SR(rstd_slot, 2))
                for i in range(n):
                    dst = actT[:, (dst_act0 + i) * 512:(dst_act0 + i + 1) * 512]
                    op("dve", lambda e, dst=dst, i=i: e.scalar_tensor_tensor(
                        out=dst, in0=PS[banks[i]][:], scalar=cols[:, gc0 + i:gc0 + i + 1], in1=rs, op0=ALU.mult,
                        op1=ALU.mult), r=[f"P{banks[i]}", "cols"] + SR(rstd_slot, 2), w=ACT(dst_act0 + i))

            cosT = Sf[0:64, 12 * 256:12 * 256 + 512]
            sgnT = Sf[0:64, 14 * 256:14 * 256 + 512]
            t2 = Sf[0:64, 20 * 256:20 * 256 + 512]

            def rope(pbank, dst, dres):
                pb = PS[pbank]
                pres = [f"P{pbank}"]
                op("dve", lambda e: e.tensor_tensor(out=t2[0:32, :], in0=pb[32:64, :], in1=sgnT[32:64, :], op=ALU.mult),
                   r=pres + SR(14, 2), w=SR(20, 2))
                op("dve", lambda e: e.tensor_tensor(out=t2[32:64, :], in0=pb[0:32, :], in1=sgnT[0:32, :], op=ALU.mult),
                   r=pres + SR(14, 2), w=SR(20, 2))
                op("dve", lambda e: e.tensor_tensor(out=pb[0:64, :], in0=pb[0:64, :], in1=cosT, op=ALU.mult),
                   r=pres + SR(12, 2), w=pres)
                op("dve", lambda e: e.tensor_tensor(out=dst, in0=pb[0:64, :], in1=t2, op=ALU.add),
                   r=pres + SR(20, 2), w=dres)

            ring_ctr = [0]

            def ring_next():
                s = ring_ctr[0] % 4
                ring_ctr[0] += 1
                return s

            RING_PF = 4
            QK_PF = 3
            ring_issued = [0]

            def ring_issue(u):
                if u >= 2 * NJ:
                    return
                rs_ = ring_issued[0] % 4
                ring_issued[0] += 1
                if u < NJ:
                    ru = ring[:, rs_ * 2048:(rs_ + 1) * 2048]
                    op("pool", lambda e: e.dma_start(out=ru, in_=wgu_s[u]), r=[f"wgu_s{u}"], w=[f"R{rs_}"], dma=f"rg{rs_}")
                else:
                    j = u - NJ
                    ru = ring[:, rs_ * 2048:rs_ * 2048 + 1024]
                    op("pool", lambda e: e.dma_start(out=ru, in_=wdn_s[j * 128:(j + 1) * 128, :]),
                       r=[f"wdn_s{j}"], w=[f"R{rs_}"], dma=f"rg{rs_}")

            pending_stores = []

            def flush_stores():
                for stag_, stres_, t_, tok0_ in pending_stores:
                    op("sp", lambda e, stag_=stag_, t_=t_, tok0_=tok0_: e.dma_start(
                        out=out_d[tok0_ + t_ * 128:tok0_ + (t_ + 1) * 128, :], in_=stag_),
                       r=stres_, w=[f"out{t_}"], dma=f"so{t_}")
                pending_stores.clear()

            QSCALE_D = 64 ** -0.5
            QSCALE_M = 192 ** -0.5

            for b in range(BPC):
                op("sp", lambda e, b=b: e.dma_start(out=Ga[:], in_=gates_d[b:b + 1, 0:1024].partition_broadcast(128)),
                   r=["gates_d"], w=["Ga"], dma="ga")
                op("sp", lambda e, b=b: e.dma_start(out=Gf[:], in_=gates_d[b:b + 1, 1024:2048].partition_broadcast(128)),
                   r=["gates_d"], w=["Gf"], dma="gf")
                for kc in range(8):
                    op("pool", lambda e, kc=kc: e.dma_start(out=W_out[:, kc * 1024:(kc + 1) * 1024], in_=wo_v[:, kc, :]),
                       w=[f"W_out{kc}"], dma=f"w{4 + kc % 2}")
                    op("pool", lambda e, kc=kc: e.tensor_tensor(out=W_out[:, kc * 1024:(kc + 1) * 1024],
                                                                in0=W_out[:, kc * 1024:(kc + 1) * 1024], in1=Ga[:], op=ALU.mult),
                       r=[f"W_out{kc}", "Ga"], w=[f"W_out{kc}"])
                ck("g")
                for c in range(NCH):
                    tok0 = b * SEQ + c * CH
                    for t in range(4):
                        op("sp", lambda e, t=t, tok0=tok0: e.dma_start(out=xc[:, t * 1024:(t + 1) * 1024],
                                                                      in_=x_d[tok0 + t * 128:tok0 + (t + 1) * 128, :]),
                           w=[f"xc{t}h0", f"xc{t}h1"], dma=f"ld{t}")
                    ck("xl")
                    for t in range(4):
                        norm_sq(t, 0)
                    flush_stores()
                    norm_rstd(0, 4, 0)
                    for t in range(4):
                        norm_scale(t, 0)
                    for t in range(4):
                        norm_tr(b, 0, t)
                    if b == 0 and c == 0:
                        dump("Ga", Ga[:], ["Ga"])
                        dump("hT", actT[:], ACT_ALL)
                        dump("st", st[:], [f"st{t}" for t in range(4)])
                    ck("s1")
                    rr = 0
                    for m in range(8):
                        k = 2 + rr % 6
                        rr += 1
                        for kc in range(8):
                            op("pe", lambda e, k=k, kc=kc, m=m: e.matmul(
                                PS[k][:], W_in[:, kc * INW + m * 128:kc * INW + (m + 1) * 128],
                                actT[:, kc * 512:(kc + 1) * 512], start=(kc == 0), stop=(kc == 7)),
                               r=["W_in"] + ACT(kc), w=[f"P{k}"])
                        if m < 4:
                            evac_copy(slot(m), PS[k][:], [f"P{k}"], SR(m))
                        else:
                            h = m - 4
                            evac_copy(dkT[:, h * SEQ + c * CH:h * SEQ + (c + 1) * CH], PS[k][:], [f"P{k}"], [f"dk{h}c{c}"])
                    for t in range(4):
                        k = 2 + rr % 6
                        rr += 1
                        for kc in range(8):
                            op("pe", lambda e, k=k, kc=kc, t=t: e.matmul(
                                PS[k][:], actT[:, kc * 512 + t * 128:kc * 512 + (t + 1) * 128],
                                W_in[:, kc * INW + 1024:kc * INW + 1536], start=(kc == 0), stop=(kc == 7)),
                               r=["W_in", f"act{kc}t{t}"], w=[f"P{k}"])
                        tile = c * 4 + t
                        evac_copy(dv[:, tile * 512:(tile + 1) * 512], PS[k][:], [f"P{k}"], [f"dv{tile}"])
                    if b == 0 and c == 0:
                        dump("dqT", slot(0, 4), SR(0, 4))
                        dump("dkT", dkT[:, 0:512], ["dk0c0"])
                        dump("dv0", dv[:, 0:512], ["dv0"])
                    ck("s2a")
                    lat_banks = [(0, 1536), (1, 1664), (2, 1792), (4, 1920), (5, 2048)]
                    for k, col0 in lat_banks:
                        for kc in range(8):
                            op("pe", lambda e, k=k, kc=kc, col0=col0: e.matmul(
                                PS[k][:], W_in[:, kc * INW + col0:kc * INW + col0 + 128], actT[:, kc * 512:(kc + 1) * 512],
                                start=(kc == 0), stop=(kc == 7)), r=["W_in"] + ACT(kc), w=[f"P{k}"])
                    for kc in range(8):
                        op("pe", lambda e, kc=kc: e.matmul(
                            PS[6][0:64, :], W_in[:, kc * INW + 2176:kc * INW + 2240], actT[:, kc * 512:(kc + 1) * 512],
                            start=(kc == 0), stop=(kc == 7)), r=["W_in"] + ACT(kc), w=["P6"])
                    lat_norm([0, 1, 2], 3, C_GQ, 16, 0, 1.0 / 384, 3)
                    lat_norm([4, 5], 2, C_GKV, 18, 3, 1.0 / 256, 7)
                    op("sp", lambda e, c=c: e.dma_start(out=cosT, in_=rope_d[0][:, c * CH:(c + 1) * CH]),
                       r=["rope_d"], w=SR(12, 2), dma="rp0")
                    op("sp", lambda e, c=c: e.dma_start(out=sgnT, in_=rope_d[1][:, c * CH:(c + 1) * CH]),
                       r=["rope_d"], w=SR(14, 2), dma="rp1")
                    ck("s2b")
                    for h in range(4):
                        for kc in range(3):
                            op("pe", lambda e, h=h, kc=kc: e.matmul(
                                PS[h][:], W_qup[:, kc * 768 + h * 128:kc * 768 + (h + 1) * 128],
                                actT[:, kc * 512:(kc + 1) * 512], start=(kc == 0), stop=(kc == 2)),
                               r=["W_qup"] + ACT(kc), w=[f"P{h}"])
                        evac_copy(slot(4 + h), PS[h][:], [f"P{h}"], SR(4 + h), force_act=True)
                    for h in range(4):
                        for kc in range(3):
                            op("pe", lambda e, h=h, kc=kc: e.matmul(
                                PS[h][0:64, :], W_qup[:, kc * 768 + 512 + h * 64:kc * 768 + 512 + (h + 1) * 64],
                                actT[:, kc * 512:(kc + 1) * 512], start=(kc == 0), stop=(kc == 2)),
                               r=["W_qup"] + ACT(kc), w=[f"P{h}"])
                    for h in range(4):
                        rope(h, S[0:64, (8 + h) * 512:(9 + h) * 512], SR(8 + h))
                    if b == 0 and c == 0:
                        dump("qnT", slot(4, 4), SR(4, 4))
                        dump("qpeT", S[0:64, 8 * 512:12 * 512], SR(8, 4), nparts=64)
                        dump("cosT", cosT, SR(12, 2), nparts=64)
                        dump("sgnT", sgnT, SR(14, 2), nparts=64)
                    ck("s3a")
                    kvb = [4, 5, 7, 4, 5, 7, 4, 5]
                    for h in range(4):
                        kb_ = kvb[h]
                        for kc in range(2):
                            op("pe", lambda e, h=h, kc=kc, kb_=kb_: e.matmul(
                                PS[kb_][:], W_kvup[:, kc * 1024 + h * 128:kc * 1024 + (h + 1) * 128],
                                actT[:, (3 + kc) * 512:(4 + kc) * 512], start=(kc == 0), stop=(kc == 1)),
                               r=["W_kvup"] + ACT(3 + kc), w=[f"P{kb_}"])
                        evac_copy(knT[:, h * SEQ + c * CH:h * SEQ + (c + 1) * CH], PS[kb_][:], [f"P{kb_}"], [f"kn{h}c{c}"], force_act=True)
                    for t in range(4):
                        kb_ = kvb[4 + t]
                        for kc in range(2):
                            op("pe", lambda e, t=t, kc=kc, kb_=kb_: e.matmul(
                                PS[kb_][:], actT[:, (3 + kc) * 512 + t * 128:(3 + kc) * 512 + (t + 1) * 128],
                                W_kvup[:, kc * 1024 + 512:kc * 1024 + 1024], start=(kc == 0), stop=(kc == 1)),
                               r=["W_kvup", f"act{3 + kc}t{t}"], w=[f"P{kb_}"])
                        tile = c * 4 + t
                        evac_copy(vv[:, tile * 512:(tile + 1) * 512], PS[kb_][:], [f"P{kb_}"], [f"v{tile}"], force_act=True)
                    rope(6, kpeT[0:64, c * CH:(c + 1) * CH], [f"kpe{c}"])
                    if b == 0 and c == 0:
                        dump("knT", knT[:, 0:512], ["kn0c0"])
                        dump("kpeT", kpeT[0:64, 0:512], ["kpe0"], nparts=64)
                        dump("vv0", vv[:, 0:512], ["v0"])
                    ck("s3b")
                    for u in range(RING_PF):
                        ring_issue(u)
                    free_banks = [0, 1, 2, 3]

                    def ring_alloc():
                        return free_banks.pop(0)

                    def ring_free(k_):
                        free_banks.append(k_)

                    nkb = 4 * c + 4
                    pendA, pendB = [None], [None]

                    def make_epilogue(mi, h, mm, is_diff, ob, rb):
                        obr, rbr = [f"P{ob}"], [f"P{rb}"]
                        rinv = slotf(16)
                        E1 = slotf(18)
                        E2 = slotf(20)
                        sq = actT[:, 7 * 512:8 * 512]

                        def partA():
                            op("act", lambda e: e.activation(out=rinv, in_=PS[rb][:], func=AF.Ln), r=rbr, w=SR(16, 2))
                            op("act", lambda e: e.activation(out=rinv, in_=rinv, func=AF.Exp, scale=-1.0),
                               r=SR(16, 2), w=SR(16, 2))
                            if not is_diff:
                                f = 4 + h
                                op("dve", lambda e: e.tensor_tensor(out=actT[:, f * 512:(f + 1) * 512], in0=PS[ob][:],
                                                                    in1=rinv, op=ALU.mult), r=obr + SR(16, 2), w=ACT(f))
                            elif mm == 0:
                                op("dve", lambda e: e.tensor_tensor(out=E1, in0=PS[ob][:], in1=rinv, op=ALU.mult),
                                   r=obr + SR(16, 2), w=SR(18, 2))
                            else:
                                op("dve", lambda e: e.tensor_tensor(out=E2, in0=PS[ob][:], in1=rinv, op=ALU.mult),
                                   r=obr + SR(16, 2), w=SR(20, 2))
                                op("dve", lambda e: e.scalar_tensor_tensor(out=E1, in0=E2, scalar=cols[:, C_NLAM:C_NLAM + 1],
                                                                           in1=E1, op0=ALU.mult, op1=ALU.add),
                                   r=SR(18, 4) + ["cols"], w=SR(18, 2))
                                op("act", lambda e: e.activation(out=sq, in_=E1, func=AF.Square), r=SR(18, 2), w=ACT(7))

                        def partB():
                            if not (is_diff and mm == 1):
                                return
                            ssb = ring_alloc()
                            op("pe", lambda e: e.matmul(PS[ssb][:], ones[:], sq, start=True, stop=True),
                               r=ACT(7) + ["ones"], w=[f"P{ssb}"])
                            op("act", lambda e: e.activation(out=E2, in_=PS[ssb][:], func=AF.Ln, bias=cols[:, 15:16],
                                                             scale=1.0 / 128), r=[f"P{ssb}", "cols"], w=SR(20, 2))
                            ring_free(ssb)
                            op("act", lambda e: e.activation(out=E2, in_=E2, func=AF.Exp, scale=-0.5), r=SR(20, 2), w=SR(20, 2))
                            op("dve", lambda e: e.scalar_tensor_tensor(
                                out=actT[:, h * 512:(h + 1) * 512], in0=E1, scalar=cols[:, C_GDP:C_GDP + 1], in1=E2,
                                op0=ALU.mult, op1=ALU.mult), r=SR(18, 4) + ["cols"], w=ACT(h))

                        return partA, partB

                    for mi in range(12):
                        is_diff = mi < 8
                        if is_diff:
                            h, mm = mi // 2, mi % 2
                        else:
                            h, mm = mi - 8, 0
                        ob, rb = 4 + 2 * (mi % 2), 5 + 2 * (mi % 2)
                        obr, rbr = [f"P{ob}"], [f"P{rb}"]

                        def qk(kb, sbk):
                            qlo = max(0, kb - 4 * c) * 128
                            kc_ = kb // 4
                            if is_diff:
                                pr = slice(mm * 64, mm * 64 + 64)
                                op("pe", lambda e: e.matmul(
                                    PS[sbk][:, qlo:512], dkT[pr, h * SEQ + kb * 128:h * SEQ + kb * 128 + 128],
                                    S[pr, h * 512 + qlo:h * 512 + 512], start=True, stop=True),
                                   r=[f"dk{h}c{kc_}"] + SR(h), w=[f"P{sbk}"])
                            else:
                                op("pe", lambda e: e.matmul(
                                    PS[sbk][:, qlo:512], knT[:, h * SEQ + kb * 128:h * SEQ + kb * 128 + 128],
                                    S[:, (4 + h) * 512 + qlo:(4 + h) * 512 + 512], start=True, stop=False),
                                   r=[f"kn{h}c{kc_}"] + SR(4 + h), w=[f"P{sbk}"])
                                op("pe", lambda e: e.matmul(
                                    PS[sbk][:, qlo:512], kpeT[0:64, kb * 128:kb * 128 + 128],
                                    S[0:64, (8 + h) * 512 + qlo:(8 + h) * 512 + 512], start=False, stop=True),
                                   r=[f"kpe{kc_}"] + SR(8 + h), w=[f"P{sbk}"])

                        bank_of = {}
                        for kb0 in range(min(QK_PF, nkb)):
                            bank_of[kb0] = ring_alloc()
                            qk(kb0, bank_of[kb0])
                        for kb in range(nkb):
                            sbk = bank_of[kb]
                            if kb + QK_PF < nkb:
                                bank_of[kb + QK_PF] = ring_alloc()
                                qk(kb + QK_PF, bank_of[kb + QK_PF])
                            qb0 = max(0, kb - 4 * c)
                            qlo = qb0 * 128
                            pt = S[:, (12 + sbk) * 512:(13 + sbk) * 512]
                            ptr = SR(12 + sbk)
                            if is_diff:
                                groups = [(0, 1), (2, 3)] if h == 0 else [(0, 1, 2, 3)]
                                for g in groups:
                                    if g[-1] < qb0:
                                        continue
                                    lo, hi = max(g[0], qb0) * 128, (g[-1] + 1) * 128
                                    delta = (4 * c + g[0] + 1 - kb) if h == 0 else (4 * c + 3 - kb)
                                    op("act", lambda e, lo=lo, hi=hi, delta=delta: e.activation(
                                        out=pt[:, lo:hi], in_=PS[sbk][:, lo:hi], func=AF.Exp,
                                        bias=alibi[:, h * 16 + delta:h * 16 + delta + 1], scale=QSCALE_D),
                                       r=[f"P{sbk}", "alibi"], w=ptr)
                            else:
                                op("act", lambda e: e.activation(out=pt[:, qlo:512], in_=PS[sbk][:, qlo:512], func=AF.Exp,
                                                                 scale=QSCALE_M), r=[f"P{sbk}"], w=ptr)
                            ring_free(sbk)
                            if kb >= 4 * c:
                                op("dve", lambda e: e.tensor_tensor(out=pt[:, qlo:qlo + 128], in0=pt[:, qlo:qlo + 128],
                                                                    in1=tri[:], op=ALU.mult), r=ptr + ["tri"], w=ptr)
                            vsrc = dv if is_diff else vv
                            vres = f"dv{kb}" if is_diff else f"v{kb}"
                            op("pe", lambda e: e.matmul(
                                PS[ob][:, qlo:512], vsrc[:, kb * 512 + h * 128:kb * 512 + (h + 1) * 128], pt[:, qlo:512],
                                start=(kb == 0), stop=(kb == nkb - 1)), r=ptr + [vres], w=obr)
                            op("pe", lambda e: e.matmul(
                                PS[rb][:, qlo:512], ones[:], pt[:, qlo:512], start=(kb == 0), stop=(kb == nkb - 1)),
                               r=ptr + ["ones"], w=rbr)
                            if kb == 0 and pendA[0] is not None:
                                pendA[0]()
                                pendA[0] = None
                            if kb == min(3, nkb - 1) and pendB[0] is not None:
                                pendB[0]()
                                pendB[0] = None
                        pendA[0], pendB[0] = make_epilogue(mi, h, mm, is_diff, ob, rb)
                    pendA[0]()
                    pendB[0]()

                    if b == 0 and c == 0:
                        dump("mergedT", actT[:], ACT_ALL)
                    ck("s4")
                    for t in range(4):
                        for half in range(2):
                            k = t * 2 + half
                            for f in range(8):
                                op("pe", lambda e, f=f: e.matmul(
                                    PS[k][:], actT[:, f * 512 + t * 128:f * 512 + (t + 1) * 128],
                                    W_out[:, f * 1024 + half * 512:f * 1024 + (half + 1) * 512],
                                    start=(f == 0), stop=(f == 7)), r=[f"W_out{f}", f"act{f}t{t}"], w=[f"P{k}"])
                            xs = xc[:, t * 1024 + half * 512:t * 1024 + (half + 1) * 512]
                            op("dve", lambda e: e.tensor_tensor(out=xs, in0=PS[k][:], in1=xs, op=ALU.add),
                               r=[f"P{k}", f"xc{t}h{half}"], w=[f"xc{t}h{half}"])
                        norm_sq(t, 0)
                        norm_rstd(t, 1, 0)
                        norm_scale(t, 0)
                    for t in range(4):
                        norm_tr(b, 1, t)
                    if b == 0 and c == 0:
                        dump("x1", xc[:], XC_ALL)
                    ck("s5")
                    if b == 0 and c == 0:
                        dump("h2T", actT[:], ACT_ALL)
                    ck("s6")
                    for j in range(NJ):
                        rs_ = ring_next()
                        rres = [f"R{rs_}"]
                        ru = ring[:, rs_ * 2048:(rs_ + 1) * 2048]
                        pg, pu = 2 + 2 * (j % 3), 3 + 2 * (j % 3)
                        for kc in range(8):
                            op("pe", lambda e, pg=pg, ru=ru, kc=kc: e.matmul(
                                PS[pg][:], ru[:, kc * 128:(kc + 1) * 128], actT[:, kc * 512:(kc + 1) * 512],
                                start=(kc == 0), stop=(kc == 7)), r=rres + ACT(kc), w=[f"P{pg}"])
                        for kc in range(8):
                            op("pe", lambda e, pu=pu, ru=ru, kc=kc: e.matmul(
                                PS[pu][:], ru[:, 1024 + kc * 128:1024 + (kc + 1) * 128], actT[:, kc * 512:(kc + 1) * 512],
                                start=(kc == 0), stop=(kc == 7)), r=rres + ACT(kc), w=[f"P{pu}"])
                        sg = sgt[:, (j % 2) * 512:(j % 2 + 1) * 512]
                        op("act", lambda e, pg=pg, sg=sg: e.activation(out=sg, in_=PS[pg][:], func=AF.Silu),
                           r=[f"P{pg}"], w=[f"sgt{j % 2}"])
                        op("dve", lambda e, pu=pu, sg=sg, j=j: e.tensor_tensor(out=slot(j), in0=PS[pu][:], in1=sg, op=ALU.mult),
                           r=[f"P{pu}", f"sgt{j % 2}"], w=SR(j))
                        ring_issue(j + RING_PF)
                    if b == 0 and c == 0:
                        dump("aT", S[:], S_ALL)
                    ck("s7a")
                    for j in range(NJ):
                        rs_ = ring_next()
                        rres = [f"R{rs_}"]
                        ru = ring[:, rs_ * 2048:rs_ * 2048 + 1024]
                        op("dve", lambda e, ru=ru: e.tensor_tensor(out=ru, in0=ru, in1=Gf[:], op=ALU.mult),
                           r=rres + ["Gf"], w=rres)
                        for t in range(4):
                            for half in range(2):
                                k = t * 2 + half
                                op("pe", lambda e, k=k, t=t, half=half, j=j, ru=ru: e.matmul(
                                    PS[k][:], S[:, j * 512 + t * 128:j * 512 + (t + 1) * 128],
                                    ru[:, half * 512:(half + 1) * 512], start=(j == 0), stop=(j == NJ - 1)),
                                   r=rres + SR(j), w=[f"P{k}"])
                        ring_issue(NJ + j + RING_PF)
                    ck("s7b")
                    junk = S[:, 20 * 512:22 * 512]
                    stags = []
                    for t in range(4):
                        if t < 3:
                            stags.append((Sf[:, (8 + 4 * t) * 256:(8 + 4 * t) * 256 + 1024], SR(8 + 4 * t, 4)))
                        else:
                            stags.append((sgt[:, 0:1024], ["sgt0", "sgt1"]))
                    for t in range(4):
                        xr = [f"xc{t}h0", f"xc{t}h1"]
                        xt = xc[:, t * 1024:(t + 1) * 1024]
                        stag, stres = stags[t]
                        for half in range(2):
                            k = t * 2 + half
                            xs = xc[:, t * 1024 + half * 512:t * 1024 + (half + 1) * 512]
                            op("dve", lambda e: e.tensor_tensor(out=xs, in0=PS[k][:], in1=xs, op=ALU.add),
                               r=[f"P{k}", f"xc{t}h{half}"], w=[f"xc{t}h{half}"])
                        sres = [f"st{8 + t}"]
                        op("dve", lambda e: e.memset(st[:, 8 + t:9 + t], 0.0), w=sres)
                        op("act", lambda e: e.activation(out=junk, in_=xt, func=AF.Square, accum_out=st[:, 8 + t:9 + t]),
                           r=xr + sres, w=SR(20, 2) + sres)
                        op("act", lambda e: e.activation(out=st[:, 12 + t:13 + t], in_=st[:, 8 + t:9 + t], func=AF.Ln,
                                                         bias=cols[:, 15:16], scale=1.0 / D), r=sres + ["cols"], w=sres)
                        op("act", lambda e: e.activation(out=st[:, 12 + t:13 + t], in_=st[:, 12 + t:13 + t], func=AF.Exp,
                                                         scale=-0.5), r=sres, w=sres)
                    for t in range(4):
                        xr = [f"xc{t}h0", f"xc{t}h1"]
                        xt = xc[:, t * 1024:(t + 1) * 1024]
                        stag, stres = stags[t]
                        sres = [f"st{8 + t}"]
                        op("dve", lambda e: e.scalar_tensor_tensor(out=stag, in0=xt, scalar=st[:, 12 + t:13 + t], in1=gfin[:],
                                                                   op0=ALU.mult, op1=ALU.mult),
                           r=xr + sres + ["gfin"], w=stres)
                        pending_stores.append((stag, stres, t, tok0))

            flush_stores()

        nops, nwait = P.run(body)
        print(f"[kernel] ops={nops} waits={nwait}", flush=True)
    return nc


_NC_CACHE = {}


def _prep_shared(inp):
    f = np.float32
    w_qup = inp["w_q_up"][0]
    qcols = [h * 192 + i for h in range(4) for i in range(128)] + [h * 192 + 128 + i for h in range(4) for i in range(64)]
    w_kvup = inp["w_kv_up"][0]
    kvcols = [h * 256 + i for h in range(4) for i in range(128)] + [h * 256 + 128 + i for h in range(4) for i in range(128)]
    wg = inp["w_ffn_gate"][0].reshape(8, 128, NJ, 128)
    wu = inp["w_ffn_up"][0].reshape(8, 128, NJ, 128)
    wgu = np.stack([wg.transpose(2, 1, 0, 3), wu.transpose(2, 1, 0, 3)], axis=2)
    b_ada = inp["b_ada"][0]
    shared = {
        "w_ada": np.ascontiguousarray(inp["w_ada"][0], dtype=f),
        "b_adaT": np.ascontiguousarray(b_ada.reshape(48, 128).T, dtype=f),
        "b_gate": np.ascontiguousarray(np.concatenate([b_ada[2048:3072], b_ada[5120:6144]])[None, :], dtype=f),
        "g_mixT": np.ascontiguousarray(inp["g_mix"][0].reshape(8, 128).T, dtype=f),
        "g_ffnT": np.ascontiguousarray(inp["g_ffn"][0].reshape(8, 128).T, dtype=f),
        "w_in": np.ascontiguousarray(inp["w_in"][0], dtype=f),
        "lam": np.ascontiguousarray(np.concatenate([inp["lambda_q1"][0], inp["lambda_k1"][0],
                                                    inp["lambda_q2"][0], inp["lambda_k2"][0]])[None, :], dtype=f),
        "g_diff": np.ascontiguousarray(inp["g_diff_out"][0].reshape(128, 1), dtype=f),
        "g_qlT": np.ascontiguousarray(inp["g_q_lat"][0].reshape(3, 128).T, dtype=f),
        "w_qup": np.ascontiguousarray(w_qup[:, qcols], dtype=f),
        "g_kvlT": np.ascontiguousarray(inp["g_kv_lat"][0].reshape(2, 128).T, dtype=f),
        "w_kvup": np.ascontiguousarray(w_kvup[:, kvcols], dtype=f),
        "w_out": np.ascontiguousarray(inp["w_out"][0], dtype=f),
        "w_gu": np.ascontiguousarray(wgu.reshape(NJ, 128, 2048), dtype=f),
        "w_dn": np.ascontiguousarray(inp["w_ffn_down"][0], dtype=f),
        "g_final": np.ascontiguousarray(inp["g_final"].reshape(1, D), dtype=f),
    }
    return shared


def kernel(**inputs):
    inp = {k: np.asarray(v) for k, v in inputs.items()}
    if "nc" not in _NC_CACHE:
        _NC_CACHE["nc"] = build()
    nc = _NC_CACHE["nc"]
    shared = _prep_shared(inp)
    x = inp["x"].astype(np.float32, copy=False)
    c = inp["c"].astype(np.float32, copy=False)
    in_maps = []
    for core in range(NCORES):
        m = dict(shared)
        m["x"] = np.ascontiguousarray(x[core * BPC:(core + 1) * BPC].reshape(BPC * SEQ, D))
        cc = c[core * BPC:(core + 1) * BPC]
        m["cT"] = np.ascontiguousarray(cc.reshape(BPC, 8, 128).transpose(2, 1, 0).reshape(128, 8 * BPC))
        in_maps.append(m)
    res = run_bass_kernel_spmd(nc, in_maps, core_ids=list(range(NCORES)))
    if DBG:
        LAST_DBG["dbg"] = np.asarray(res.results[0]["dbg"])
        LAST_DBG["map"] = dict(DBG_MAP)
    outs = [np.asarray(r["out"]).reshape(BPC, SEQ, D) for r in res.results]
    return np.concatenate(outs, axis=0).astype(np.float32, copy=False)
```

```python
import contextlib
import math
import numpy as np
import concourse.bass as bass
import concourse.mybir as mybir
from concourse.bass_utils import run_bass_kernel_spmd

F32 = mybir.dt.float32
BF16 = mybir.dt.bfloat16
AF = mybir.ActivationFunctionType
ALU = mybir.AluOpType

NCORES = 8
SEQ = 2048
D = 1024
BPC = 4
CH = 512
NCH = SEQ // CH
DFF = 2816
NJ = DFF // 128
INW = 2240
EPS = 1e-6
LAMBDA_INIT = 0.8 - 0.6 * math.exp(-0.3 * 0)
SLOPES = [2.0 ** (-8.0 * (i + 1) / 4) for i in range(4)]
SAME_ENGINE_SYNC = True
import os
STOP = os.environ.get("KSTOP", "")
DBG = bool(os.environ.get("KDBG", ""))
DBGW = 48000
DBG_MAP = {}
LAST_DBG = {}


class _Stop(Exception):
    pass


def ck(label):
    if STOP == label:
        raise _Stop()


class Prog:
    def __init__(self, nc, es):
        self.nc = nc
        self.es = es
        self.eng = {"pe": nc.tensor, "act": nc.scalar, "dve": nc.vector, "pool": nc.gpsimd, "sp": nc.sync}
        self.esem = {k: es.enter_context(nc.semaphore("sem_" + k)) for k in self.eng}
        self.dsem = {}
        self.meta = []
        self.mode = "analyze"
        self.idx = 0
        self.nwait = 0

    def op(self, eng, fn, r=(), w=(), dma=None):
        if self.mode == "analyze":
            self.meta.append((eng, tuple(r), tuple(w), dma))
            return
        i = self.idx
        self.idx += 1
        e = self.eng[eng]
        need = {}
        for j in self.deps[i]:
            key, val = self.tick[j]
            if need.get(key, 0) < val:
                need[key] = val
        wd = self.waited[eng]
        for key, val in need.items():
            if wd.get(key, 0) >= val:
                continue
            wd[key] = val
            sem = self.dsem[key[1]] if key[0] == "d" else self.esem[key[1]]
            e.wait_ge(sem, val)
            self.nwait += 1
        inst = fn(e)
        if dma is not None:
            inst.then_inc(self.dsem[dma], 16)
        elif i in self.signaled:
            inst.then_inc(self.esem[eng], 1)

    def analyze(self):
        ops = self.meta
        last_w = {}
        readers = {}
        last_dma = {}
        deps = []
        for i, (eng, r, w, dma) in enumerate(ops):
            d = set()
            for x in r:
                if x in last_w:
                    d.add(last_w[x])
                if x[0] == "P" and x[1:].isdigit():
                    rd = readers.get(x)
                    if rd:
                        d.update(j for k_, j in rd.items() if k_ != eng)
            for x in w:
                if x in last_w:
                    d.add(last_w[x])
                rd = readers.get(x)
                if rd:
                    d.update(rd.values())
            if dma is not None and dma in last_dma:
                d.add(last_dma[dma])
            d.discard(i)
            for x in w:
                last_w[x] = i
                readers[x] = {}
            for x in r:
                if x in w:
                    continue
                key = eng if dma is None else ("dma", i)
                readers.setdefault(x, {})[key] = i
            if dma is not None:
                last_dma[dma] = i
            kept = []
            for j in d:
                ej, _, _, dj = ops[j]
                if dj is None and dma is None and ej == eng and (eng == "pe" or not SAME_ENGINE_SYNC):
                    continue
                kept.append(j)
            deps.append(kept)
        signaled = set()
        for kept in deps:
            for j in kept:
                if ops[j][3] is None:
                    signaled.add(j)
        tick = {}
        cnt = {k: 0 for k in self.eng}
        dcnt = {}
        for i, (eng, r, w, dma) in enumerate(ops):
            if dma is not None:
                if dma not in self.dsem:
                    self.dsem[dma] = self.es.enter_context(self.nc.semaphore("d_" + dma))
                dcnt[dma] = dcnt.get(dma, 0) + 16
                tick[i] = (("d", dma), dcnt[dma])
            elif i in signaled:
                cnt[eng] += 1
                tick[i] = (("e", eng), cnt[eng])
        self.deps, self.signaled, self.tick, self.dcnt = deps, signaled, tick, dcnt
        self.waited = {k: {} for k in self.eng}

    def run(self, body):
        self.mode = "analyze"
        try:
            body()
        except _Stop:
            pass
        self.analyze()
        self.mode = "emit"
        self.idx = 0
        try:
            body()
        except _Stop:
            pass
        assert self.idx == len(self.meta), (self.idx, len(self.meta))
        sp = self.eng["sp"]
        for name, sem in self.dsem.items():
            if self.waited["sp"].get(("d", name), 0) < self.dcnt[name]:
                sp.wait_ge(sem, self.dcnt[name])
        return len(self.meta), self.nwait


def build():
    nc = bass.Bass("TRN2", target_bir_lowering=False)

    def dram(name, shape, dt=F32, kind="ExternalInput"):
        return nc.dram_tensor(name, list(shape), dt, kind=kind).ap()

    x_d = dram("x", [BPC * SEQ, D])
    out_d = dram("out", [BPC * SEQ, D], kind="ExternalOutput")
    cT_d = dram("cT", [128, 8 * BPC])
    wada_d = dram("w_ada", [D, 6 * D])
    badaT_d = dram("b_adaT", [128, 48])
    bgate_d = dram("b_gate", [1, 2 * D])
    gmixT_d = dram("g_mixT", [128, 8])
    gffnT_d = dram("g_ffnT", [128, 8])
    win_d = dram("w_in", [D, INW])
    lam_d = dram("lam", [1, 256])
    gdiff_d = dram("g_diff", [128, 1])
    gqlT_d = dram("g_qlT", [128, 3])
    wqup_d = dram("w_qup", [384, 768])
    gkvlT_d = dram("g_kvlT", [128, 2])
    wkvup_d = dram("w_kvup", [256, 1024])
    wout_d = dram("w_out", [D, D])
    wgu_d = dram("w_gu", [NJ, 128, 2048])
    wdn_d = dram("w_dn", [DFF, D])
    gfin_d = dram("g_final", [1, D])
    gates_d = dram("gates_scr", [BPC, 2 * D], kind="Internal")
    rope_d = dram("rope_scr", [2, 64, SEQ], kind="Internal")
    wgu_s = dram("wgu_bf16_scr", [NJ, 128, 2048], BF16, kind="Internal")
    wdn_s = dram("wdn_bf16_scr", [DFF, D], BF16, kind="Internal")
    dbg_d = dram("dbg", [128, DBGW], kind="ExternalOutput") if DBG else None

    es = contextlib.ExitStack()
    with es:
        def sb(name, shape, dt):
            return es.enter_context(nc.sbuf_tensor(name, list(shape), dt))

        W_in = sb("W_in", [128, 8 * INW], BF16)
        W_qup = sb("W_qup", [128, 3 * 768], BF16)
        W_kvup = sb("W_kvup", [128, 2 * 1024], BF16)
        W_out = sb("W_out", [128, 8 * 1024], BF16)
        dkT = sb("dkT", [128, 4 * SEQ], BF16)
        dv = sb("dv", [128, 16 * 512], BF16)
        knT = sb("knT", [128, 4 * SEQ], BF16)
        kpeT = sb("kpeT", [128, SEQ], BF16)
        vv = sb("vv", [128, 16 * 512], BF16)
        xc = sb("xc", [128, 4 * 1024], F32)
        actT = sb("actT", [128, 8 * 512], BF16)
        S = sb("S", [128, NJ * 512], BF16)
        ring = sb("ring", [128, 4 * 2048], BF16)
        Ga = sb("Ga", [128, 1024], F32)
        Gf = sb("Gf", [128, 1024], F32)
        gfin = sb("gfin", [128, 1024], F32)
        sgt = sb("sgt", [128, 2 * 512], F32)
        ident = sb("ident", [128, 128], BF16)
        ones = sb("ones", [128, 128], BF16)
        tri = sb("tri", [128, 128], BF16)
        alibi = sb("alibi", [128, 64], F32)
        AB = sb("AB", [128, BPC * 2 * 2 * 8], F32)
        st = sb("st", [128, 16], F32)
        cols = sb("cols", [128, 32], F32)
        PS = [es.enter_context(nc.psum_tensor(f"ps{k}", [128, 512], F32)) for k in range(8)]

        P = Prog(nc, es)
        op = P.op

        def body():
            Sf = S[:].bitcast(F32)
            modT = sgt[:, 256:256 + 48 * BPC]
            xcb = xc[:].bitcast(BF16)

            def slot(j, n=1):
                return S[:, j * 512:(j + n) * 512]

            def slotf(j):
                return Sf[:, j * 256:j * 256 + 512]

            def SR(j, n=1):
                return [f"S{k}" for k in range(j, j + n)]

            dbg_off = [0]

            def dump(label, ap, res, nparts=128):
                if not DBG or label in DBG_MAP and P.mode == "analyze":
                    return
                wdt = ap.shape[-1]
                if P.mode == "analyze":
                    DBG_MAP[label] = (dbg_off[0], wdt, nparts)
                o = DBG_MAP[label][0]
                dbg_off[0] = o + wdt
                op("pool", lambda e: e.dma_start(out=dbg_d[0:nparts, o:o + wdt], in_=ap), r=res, w=["dbg_" + label],
                   dma="dbg")

            XC_ALL = [f"xc{t}h{h}" for t in range(4) for h in range(2)]
            S_ALL = SR(0, NJ)
            ACT_ALL = [f"act{k}t{t}" for k in range(8) for t in range(4)]

            def ACT(k):
                return [f"act{k}t{t}" for t in range(4)]

            def AB_col(b, af, ab, kc):
                o = ((b * 2 + af) * 2 + ab) * 8 + kc
                return AB[:, o:o + 1]

            C_NLAM, C_GDP, C_GD, C_GQ, C_GKV = 0, 1, 2, 3, 6

            tmpf = Sf[:, 0:128]
            op("pool", lambda e: e.memset(tmpf, 1.0), w=SR(0))
            op("pool", lambda e: e.affine_select(out=tmpf, in_=tmpf, pattern=[[-1, 128]], compare_op=ALU.is_equal,
                                                 fill=0.0, base=0, channel_multiplier=1), r=SR(0), w=SR(0))
            op("dve", lambda e: e.tensor_copy(out=ident[:], in_=tmpf), r=SR(0), w=["ident"])
            tmpf2 = Sf[:, 256:384]
            op("pool", lambda e: e.memset(tmpf2, 1.0), w=SR(1))
            op("pool", lambda e: e.affine_select(out=tmpf2, in_=tmpf2, pattern=[[1, 128]], compare_op=ALU.is_ge,
                                                 fill=0.0, base=0, channel_multiplier=-1), r=SR(1), w=SR(1))
            op("dve", lambda e: e.tensor_copy(out=tri[:], in_=tmpf2), r=SR(1), w=["tri"])
            op("dve", lambda e: e.memset(ones[:], 1.0), w=["ones"])
            op("dve", lambda e: e.memset(cols[:, 15:16], EPS), w=["cols"])
            tmpa = Sf[:, 512:528]
            op("pool", lambda e: e.iota(tmpa, pattern=[[-128, 16]], base=-127, channel_multiplier=1,
                                        allow_small_or_imprecise_dtypes=True), w=SR(2))
            tmpa0 = Sf[:, 768:784]
            op("pool", lambda e: e.iota(tmpa0, pattern=[[-128, 16]], base=0, channel_multiplier=1,
                                        allow_small_or_imprecise_dtypes=True), w=SR(3))
            op("dve", lambda e: e.tensor_scalar(out=alibi[:, 0:16], in0=tmpa0, scalar1=SLOPES[0], scalar2=None,
                                                op0=ALU.mult), r=SR(3), w=["alibi"])
            for h in range(1, 4):
                op("dve", lambda e, h=h: e.tensor_scalar(out=alibi[:, h * 16:(h + 1) * 16], in0=tmpa, scalar1=SLOPES[h],
                                                         scalar2=None, op0=ALU.mult), r=SR(2), w=["alibi"])

            ck("consts")
            win_v = win_d.rearrange("(kc p) n -> p kc n", p=128)
            for kc in range(8):
                op("pool", lambda e, kc=kc: e.dma_start(out=W_in[:, kc * INW:(kc + 1) * INW], in_=win_v[:, kc, :]),
                   w=["W_in"], dma=f"w{kc % 2}")
            wq_v = wqup_d.rearrange("(kc p) n -> p kc n", p=128)
            for kc in range(3):
                op("pool", lambda e, kc=kc: e.dma_start(out=W_qup[:, kc * 768:(kc + 1) * 768], in_=wq_v[:, kc, :]),
                   w=["W_qup"], dma="w2")
            wkv_v = wkvup_d.rearrange("(kc p) n -> p kc n", p=128)
            for kc in range(2):
                op("pool", lambda e, kc=kc: e.dma_start(out=W_kvup[:, kc * 1024:(kc + 1) * 1024], in_=wkv_v[:, kc, :]),
                   w=["W_kvup"], dma="w3")
            wo_v = wout_d.rearrange("(kc p) n -> p kc n", p=128)

            ck("weights")
            cTs = Sf[:, 1024:1024 + 8 * BPC]
            op("sp", lambda e: e.dma_start(out=cTs, in_=cT_d[:, :]), w=SR(4), dma="p0")
            op("sp", lambda e: e.dma_start(out=cols[:, 16:24], in_=gmixT_d[:, :]), w=["cols"], dma="p1")
            op("sp", lambda e: e.dma_start(out=cols[:, 24:32], in_=gffnT_d[:, :]), w=["cols"], dma="p1")
            op("sp", lambda e: e.dma_start(out=cols[:, C_GD:C_GD + 1], in_=gdiff_d[:, :]), w=["cols"], dma="p1")
            op("sp", lambda e: e.dma_start(out=cols[:, C_GQ:C_GQ + 3], in_=gqlT_d[:, :]), w=["cols"], dma="p1")
            op("sp", lambda e: e.dma_start(out=cols[:, C_GKV:C_GKV + 2], in_=gkvlT_d[:, :]), w=["cols"], dma="p1")
            badaT = Sf[:, 1280:1280 + 48]
            op("sp", lambda e: e.dma_start(out=badaT, in_=badaT_d[:, :]), w=SR(5), dma="p2")
            op("sp", lambda e: e.dma_start(out=gfin[:], in_=gfin_d[:, :].partition_broadcast(128)), w=["gfin"], dma="p3")
            lamb = Sf[:, 1536:1536 + 256]
            op("sp", lambda e: e.dma_start(out=lamb, in_=lam_d[:, :].partition_broadcast(128)), w=SR(6), dma="p4")
            bg = Sf[0:BPC, 2048:2048 + 2048]
            op("sp", lambda e: e.dma_start(out=bg, in_=bgate_d[:, :].partition_broadcast(BPC)), w=SR(8, 8), dma="p5")

            ck("params")
            lprod = Sf[:, 1792:1792 + 128]
            op("dve", lambda e: e.tensor_tensor(out=lprod[:, 0:64], in0=lamb[:, 0:64], in1=lamb[:, 64:128], op=ALU.mult),
               r=SR(6), w=SR(7))
            op("dve", lambda e: e.tensor_tensor(out=lprod[:, 64:128], in0=lamb[:, 128:192], in1=lamb[:, 192:256], op=ALU.mult),
               r=SR(6), w=SR(7))
            op("dve", lambda e: e.reduce_sum(out=cols[:, 8:9], in_=lprod[:, 0:64], axis=mybir.AxisListType.X), r=SR(7), w=["cols"])
            op("dve", lambda e: e.reduce_sum(out=cols[:, 9:10], in_=lprod[:, 64:128], axis=mybir.AxisListType.X), r=SR(7), w=["cols"])
            op("act", lambda e: e.activation(out=cols[:, 10:12], in_=cols[:, 8:10], func=AF.Exp), r=["cols"], w=["cols"])
            op("dve", lambda e: e.tensor_tensor(out=cols[:, C_NLAM:C_NLAM + 1], in0=cols[:, 11:12], in1=cols[:, 10:11],
                                                op=ALU.subtract), r=["cols"], w=["cols"])
            op("dve", lambda e: e.tensor_scalar(out=cols[:, C_NLAM:C_NLAM + 1], in0=cols[:, C_NLAM:C_NLAM + 1],
                                                scalar1=-LAMBDA_INIT, scalar2=None, op0=ALU.add), r=["cols"], w=["cols"])
            op("dve", lambda e: e.tensor_scalar(out=cols[:, C_GDP:C_GDP + 1], in0=cols[:, C_GD:C_GD + 1],
                                                scalar1=1.0 - LAMBDA_INIT, scalar2=None, op0=ALU.mult), r=["cols"], w=["cols"])

            ck("lam")
            scb = sgt[:].bitcast(BF16)[:, 0:8 * BPC]
            SCB = ["sgt0", "sgt1"]
            op("act", lambda e: e.activation(out=scb, in_=cTs, func=AF.Silu), r=SR(4), w=SCB)
            wada_v = wada_d.rearrange("(kc p) n -> p kc n", p=128)
            gsb = actT[:].bitcast(F32)[0:BPC, 0:2048]
            bank = 0
            for u in range(12):
                sl = u % 2
                unit = xcb[:, sl * 4096:(sl + 1) * 4096]
                ures = XC_ALL[sl * 4:(sl + 1) * 4]
                op("pool", lambda e, unit=unit, u=u: e.dma_start(
                    out=unit.rearrange("p (kc n) -> p kc n", kc=8), in_=wada_v[:, :, u * 512:(u + 1) * 512]),
                   w=ures, dma=f"ada{sl}")
                if u in (4, 5, 10, 11):
                    g = 0 if u < 6 else 1
                    half = u % 2 if u < 6 else (u - 10)
                    pb = PS[bank % 8]
                    pres = [f"P{bank % 8}"]
                    bank += 1
                    for kc in range(8):
                        op("pe", lambda e, pb=pb, unit=unit, kc=kc: e.matmul(
                            pb[0:BPC, :], scb[:, kc * BPC:(kc + 1) * BPC], unit[:, kc * 512:(kc + 1) * 512],
                            start=(kc == 0), stop=(kc == 7)), r=ures + SCB, w=pres)
                    o = g * 1024 + half * 512
                    op("dve", lambda e, pb=pb, o=o: e.tensor_tensor(out=gsb[:, o:o + 512], in0=pb[0:BPC, :],
                                                                    in1=bg[:, o:o + 512], op=ALU.add),
                       r=pres + SR(8, 8), w=ACT_ALL)
                else:
                    for mm in range(4):
                        m = 4 * u + mm
                        pb = PS[bank % 8]
                        pres = [f"P{bank % 8}"]
                        bank += 1
                        for kc in range(8):
                            op("pe", lambda e, pb=pb, unit=unit, kc=kc, mm=mm: e.matmul(
                                pb[:, 0:BPC], unit[:, kc * 512 + mm * 128:kc * 512 + (mm + 1) * 128],
                                scb[:, kc * BPC:(kc + 1) * BPC], start=(kc == 0), stop=(kc == 7)),
                               r=ures + SCB, w=pres)
                        op("dve", lambda e, pb=pb, m=m: e.tensor_scalar(
                            out=modT[:, m * BPC:(m + 1) * BPC], in0=pb[:, 0:BPC], scalar1=badaT[:, m:m + 1], scalar2=None,
                            op0=ALU.add), r=pres + SR(5), w=["sgt0"])
            op("sp", lambda e: e.dma_start(out=gates_d[:, :], in_=gsb), r=ACT_ALL, w=["gates_d"], dma="p6")
            modT3 = modT.rearrange("p (m b) -> p m b", b=BPC)
            for b in range(BPC):
                for af, (m_sh, m_sc, gcol) in enumerate([(0, 8, 16), (24, 32, 24)]):
                    oA = ((b * 2 + af) * 2 + 0) * 8
                    oB = ((b * 2 + af) * 2 + 1) * 8
                    op("dve", lambda e, b=b, m_sc=m_sc, gcol=gcol, oA=oA: e.scalar_tensor_tensor(
                        out=AB[:, oA:oA + 8], in0=modT3[:, m_sc:m_sc + 8, b], scalar=1.0, in1=cols[:, gcol:gcol + 8],
                        op0=ALU.add, op1=ALU.mult), r=["sgt0", "cols"], w=["AB"])
                    op("dve", lambda e, b=b, m_sh=m_sh, oB=oB: e.tensor_copy(out=AB[:, oB:oB + 8],
                                                                             in_=modT3[:, m_sh:m_sh + 8, b]),
                       r=["sgt0"], w=["AB"])

            ck("ada")
            xcf = xc[:]
            pos = xcf[0:64, 0:2048]
            ang = xcf[0:64, 2048:4096]
            op("pool", lambda e: e.iota(pos, pattern=[[1, 2048]], base=0, channel_multiplier=0,
                                        allow_small_or_imprecise_dtypes=True), w=XC_ALL)
            op("pool", lambda e: e.iota(cols[0:32, 12:13], pattern=[[0, 1]], base=0, channel_multiplier=1,
                                        allow_small_or_imprecise_dtypes=True), w=["cols"])
            op("pool", lambda e: e.iota(cols[32:64, 12:13], pattern=[[0, 1]], base=0, channel_multiplier=1,
                                        allow_small_or_imprecise_dtypes=True), w=["cols"])
            op("act", lambda e: e.activation(out=cols[0:64, 13:14], in_=cols[0:64, 12:13], func=AF.Exp,
                                             scale=-math.log(10000.0) / 32.0), r=["cols"], w=["cols"])
            op("dve", lambda e: e.tensor_scalar(out=ang, in0=pos, scalar1=cols[0:64, 13:14], scalar2=None, op0=ALU.mult),
               r=XC_ALL + ["cols"], w=XC_ALL)
            TWO_PI = 2.0 * math.pi
            MAGIC = 12582912.0
            PI_LO = 3.141592
            for which, off in ((0, 0.5 * math.pi), (1, 0.0)):
                dst = Sf[0:64, which * 2048:(which + 1) * 2048]
                op("dve", lambda e, off=off: e.tensor_scalar(out=pos, in0=ang, scalar1=off, scalar2=1.0 / TWO_PI, op0=ALU.add,
                                                             op1=ALU.mult), r=XC_ALL, w=XC_ALL)
                op("dve", lambda e: e.tensor_scalar(out=pos, in0=pos, scalar1=MAGIC, scalar2=None, op0=ALU.add),
                   r=XC_ALL, w=XC_ALL)
                op("dve", lambda e: e.tensor_scalar(out=pos, in0=pos, scalar1=-MAGIC, scalar2=None, op0=ALU.add),
                   r=XC_ALL, w=XC_ALL)
                op("dve", lambda e: e.scalar_tensor_tensor(out=pos, in0=pos, scalar=-TWO_PI, in1=ang, op0=ALU.mult,
                                                           op1=ALU.add), r=XC_ALL, w=XC_ALL)
                op("dve", lambda e, off=off: e.tensor_scalar(out=pos, in0=pos, scalar1=off, scalar2=None, op0=ALU.add),
                   r=XC_ALL, w=XC_ALL)
                op("dve", lambda e: e.tensor_scalar(out=pos, in0=pos, scalar1=-PI_LO, scalar2=PI_LO,
                                                    op0=ALU.max, op1=ALU.min), r=XC_ALL, w=XC_ALL)
                op("act", lambda e, dst=dst: e.activation(out=dst, in_=pos, func=AF.Sin), r=XC_ALL, w=S_ALL)
            sgn_hi = Sf[32:64, 2048:4096]
            op("dve", lambda e: e.tensor_scalar(out=sgn_hi, in0=sgn_hi, scalar1=-1.0, scalar2=None, op0=ALU.mult),
               r=S_ALL, w=S_ALL)
            op("sp", lambda e: e.dma_start(out=rope_d[0], in_=Sf[0:64, 0:2048]), r=S_ALL, w=["rope_d"], dma="p7")
            op("sp", lambda e: e.dma_start(out=rope_d[1], in_=Sf[0:64, 2048:4096]), r=S_ALL, w=["rope_d"], dma="p7")

            ck("rope")
            dump("AB", AB[:], ["AB"])
            dump("modT", modT, ["sgt0"])
            dump("cols", cols[:], ["cols"])
            dump("alibi", alibi[:], ["alibi"])
            dump("tri", tri[:], ["tri"])
            dump("ident", ident[:], ["ident"])
            evac_rr = [0]

            def evac_copy(dst, src, r, w, force_act=False):
                evac_rr[0] += 1
                if force_act or evac_rr[0] % 2 == 0:
                    op("act", lambda e: e.activation(out=dst, in_=src, func=AF.Copy), r=r, w=w)
                else:
                    op("dve", lambda e: e.tensor_copy(out=dst, in_=src), r=r, w=w)

            def hn_buf(t):
                return S[:, t * 1024:(t + 1) * 1024], SR(2 * t, 2)

            def norm_sq(t, col0):
                xr = [f"xc{t}h0", f"xc{t}h1"]
                xt = xc[:, t * 1024:(t + 1) * 1024]
                hn, hres = hn_buf(t)
                sres = [f"st{t}"]
                op("dve", lambda e: e.memset(st[:, col0 + t:col0 + t + 1], 0.0), w=sres)
                op("act", lambda e: e.activation(out=hn, in_=xt, func=AF.Square, accum_out=st[:, col0 + t:col0 + t + 1]),
                   r=xr + sres, w=hres + sres)

            def norm_rstd(t0, n, col0):
                sres = [f"st{t}" for t in range(t0, t0 + n)]
                src_ = st[:, col0 + t0:col0 + t0 + n]
                dst_ = st[:, col0 + 4 + t0:col0 + 4 + t0 + n]
                op("act", lambda e: e.activation(out=dst_, in_=src_, func=AF.Ln, bias=cols[:, 15:16], scale=1.0 / D),
                   r=sres + ["cols"], w=sres)
                op("act", lambda e: e.activation(out=dst_, in_=dst_, func=AF.Exp, scale=-0.5), r=sres, w=sres)

            def norm_scale(t, col0):
                xr = [f"xc{t}h0", f"xc{t}h1"]
                xt = xc[:, t * 1024:(t + 1) * 1024]
                hn, hres = hn_buf(t)
                op("dve", lambda e: e.tensor_scalar(out=hn, in0=xt, scalar1=st[:, col0 + 4 + t:col0 + 5 + t], scalar2=None,
                                                    op0=ALU.mult), r=xr + [f"st{t}"], w=hres)

            def norm_tr(b, af, t):
                hn, hres = hn_buf(t)
                pb = PS[t % 2]
                pbb = pb[:].bitcast(BF16)
                pres = [f"P{t % 2}"]
                for kc in range(8):
                    op("pe", lambda e, kc=kc: e.transpose(pbb[:, kc * 128:(kc + 1) * 128],
                                                          hn[:, kc * 128:(kc + 1) * 128], ident[:]),
                       r=hres + ["ident"], w=pres)
                for kc in range(8):
                    dst = actT[:, kc * 512 + t * 128:kc * 512 + (t + 1) * 128]
                    src_ = pbb[:, kc * 128:(kc + 1) * 128]
                    a_ = AB_col(b, af, 0, kc)
                    b_ = AB_col(b, af, 1, kc)
                    if t % 2 == 0:
                        op("dve", lambda e, dst=dst, src_=src_, a_=a_, b_=b_: e.tensor_scalar(
                            out=dst, in0=src_, scalar1=a_, scalar2=b_, op0=ALU.mult, op1=ALU.add),
                           r=pres + ["AB"], w=[f"act{kc}t{t}"])
                    else:
                        op("act", lambda e, dst=dst, src_=src_, a_=a_, b_=b_: e.activation(
                            out=dst, in_=src_, func=AF.Identity, bias=b_, scale=a_),
                           r=pres + ["AB"], w=[f"act{kc}t{t}"])

            def lat_norm(banks, n, gc0, rstd_slot, dst_act0, inv_n, ssbank):
                for i in range(n):
                    sq = actT[:, (5 + i % 2) * 512:(6 + i % 2) * 512]
                    op("act", lambda e, sq=sq, i=i: e.activation(out=sq, in_=PS[banks[i]][:], func=AF.Square),
                       r=[f"P{banks[i]}"], w=ACT(5 + i % 2))
                    op("pe", lambda e, sq=sq, i=i: e.matmul(PS[ssbank][:], ones[:], sq, start=(i == 0), stop=(i == n - 1)),
                       r=ACT(5 + i % 2) + ["ones"], w=[f"P{ssbank}"])
                rs = slotf(rstd_slot)
                op("act", lambda e: e.activation(out=rs, in_=PS[ssbank][:], func=AF.Ln, bias=cols[:, 15:16], scale=inv_n),
                   r=[f"P{ssbank}", "cols"], w=SR(rstd_slot, 2))
                op("act", lambda e: e.activation(out=rs, in_=rs, func=AF.Exp, scale=-0.5), r=SR(rstd_slot, 2), w=SR(rstd_slot, 2))
                for i in range(n):
                    dst = actT[:, (dst_act0 + i) * 512:(dst_act0 + i + 1) * 512]
                    op("dve", lambda e, dst=dst, i=i: e.scalar_tensor_tensor(
                        out=dst, in0=PS[banks[i]][:], scalar=cols[:, gc0 + i:gc0 + i + 1], in1=rs, op0=ALU.mult,
                        op1=ALU.mult), r=[f"P{banks[i]}", "cols"] + SR(rstd_slot, 2), w=ACT(dst_act0 + i))

            cosT = Sf[0:64, 12 * 256:12 * 256 + 512]
            sgnT = Sf[0:64, 14 * 256:14 * 256 + 512]
            t2 = Sf[0:64, 20 * 256:20 * 256 + 512]

            def rope(pbank, dst, dres):
                pb = PS[pbank]
                pres = [f"P{pbank}"]
                op("dve", lambda e: e.tensor_tensor(out=t2[0:32, :], in0=pb[32:64, :], in1=sgnT[32:64, :], op=ALU.mult),
                   r=pres + SR(14, 2), w=SR(20, 2))
                op("dve", lambda e: e.tensor_tensor(out=t2[32:64, :], in0=pb[0:32, :], in1=sgnT[0:32, :], op=ALU.mult),
                   r=pres + SR(14, 2), w=SR(20, 2))
                op("dve", lambda e: e.tensor_tensor(out=pb[0:64, :], in0=pb[0:64, :], in1=cosT, op=ALU.mult),
                   r=pres + SR(12, 2), w=pres)
                op("dve", lambda e: e.tensor_tensor(out=dst, in0=pb[0:64, :], in1=t2, op=ALU.add),
                   r=pres + SR(20, 2), w=dres)

            ring_ctr = [0]

            def ring_next():
                s = ring_ctr[0] % 4
                ring_ctr[0] += 1
                return s

            RING_PF = 4
            QK_PF = 3
            ring_issued = [0]

            def ring_issue(u):
                if u >= 2 * NJ:
                    return
                rs_ = ring_issued[0] % 4
                ring_issued[0] += 1
                if u < NJ:
                    ru = ring[:, rs_ * 2048:(rs_ + 1) * 2048]
                    op("pool", lambda e: e.dma_start(out=ru, in_=wgu_s[u]), r=[f"wgu_s{u}"], w=[f"R{rs_}"], dma=f"rg{rs_}")
                else:
                    j = u - NJ
                    ru = ring[:, rs_ * 2048:rs_ * 2048 + 1024]
                    op("pool", lambda e: e.dma_start(out=ru, in_=wdn_s[j * 128:(j + 1) * 128, :]),
                       r=[f"wdn_s{j}"], w=[f"R{rs_}"], dma=f"rg{rs_}")

            pending_stores = []

            def flush_stores():
                for stag_, stres_, t_, tok0_ in pending_stores:
                    op("sp", lambda e, stag_=stag_, t_=t_, tok0_=tok0_: e.dma_start(
                        out=out_d[tok0_ + t_ * 128:tok0_ + (t_ + 1) * 128, :], in_=stag_),
                       r=stres_, w=[f"out{t_}"], dma=f"so{t_}")
                pending_stores.clear()

            QSCALE_D = 64 ** -0.5
            QSCALE_M = 192 ** -0.5

            for b in range(BPC):
                op("sp", lambda e, b=b: e.dma_start(out=Ga[:], in_=gates_d[b:b + 1, 0:1024].partition_broadcast(128)),
                   r=["gates_d"], w=["Ga"], dma="ga")
                op("sp", lambda e, b=b: e.dma_start(out=Gf[:], in_=gates_d[b:b + 1, 1024:2048].partition_broadcast(128)),
                   r=["gates_d"], w=["Gf"], dma="gf")
                for kc in range(8):
                    op("pool", lambda e, kc=kc: e.dma_start(out=W_out[:, kc * 1024:(kc + 1) * 1024], in_=wo_v[:, kc, :]),
                       w=[f"W_out{kc}"], dma=f"w{4 + kc % 2}")
                    op("pool", lambda e, kc=kc: e.tensor_tensor(out=W_out[:, kc * 1024:(kc + 1) * 1024],
                                                                in0=W_out[:, kc * 1024:(kc + 1) * 1024], in1=Ga[:], op=ALU.mult),
                       r=[f"W_out{kc}", "Ga"], w=[f"W_out{kc}"])
                if b == 0:
                    for j in range(NJ):
                        op("pool", lambda e, j=j: e.dma_start(out=wgu_s[j], in_=wgu_d[j]), w=[f"wgu_s{j}"], dma=f"pc{j % 4}")
                    for j in range(NJ):
                        op("pool", lambda e, j=j: e.dma_start(out=wdn_s[j * 128:(j + 1) * 128, :],
                                                              in_=wdn_d[j * 128:(j + 1) * 128, :]),
                           w=[f"wdn_s{j}"], dma=f"pc{j % 4}")
                ck("g")
                for c in range(NCH):
                    tok0 = b * SEQ + c * CH
                    for t in range(4):
                        op("sp", lambda e, t=t, tok0=tok0: e.dma_start(out=xc[:, t * 1024:(t + 1) * 1024],
                                                                      in_=x_d[tok0 + t * 128:tok0 + (t + 1) * 128, :]),
                           w=[f"xc{t}h0", f"xc{t}h1"], dma=f"ld{t}")
                    ck("xl")
                    for t in range(4):
                        norm_sq(t, 0)
                    flush_stores()
                    norm_rstd(0, 4, 0)
                    for t in range(4):
                        norm_scale(t, 0)
                    for t in range(4):
                        norm_tr(b, 0, t)
                    if b == 0 and c == 0:
                        dump("Ga", Ga[:], ["Ga"])
                        dump("hT", actT[:], ACT_ALL)
                        dump("st", st[:], [f"st{t}" for t in range(4)])
                    ck("s1")
                    rr = 0
                    for m in range(8):
                        k = 2 + rr % 6
                        rr += 1
                        for kc in range(8):
                            op("pe", lambda e, k=k, kc=kc, m=m: e.matmul(
                                PS[k][:], W_in[:, kc * INW + m * 128:kc * INW + (m + 1) * 128],
                                actT[:, kc * 512:(kc + 1) * 512], start=(kc == 0), stop=(kc == 7)),
                               r=["W_in"] + ACT(kc), w=[f"P{k}"])
                        if m < 4:
                            evac_copy(slot(m), PS[k][:], [f"P{k}"], SR(m))
                        else:
                            h = m - 4
                            evac_copy(dkT[:, h * SEQ + c * CH:h * SEQ + (c + 1) * CH], PS[k][:], [f"P{k}"], [f"dk{h}c{c}"])
                    for t in range(4):
                        k = 2 + rr % 6
                        rr += 1
                        for kc in range(8):
                            op("pe", lambda e, k=k, kc=kc, t=t: e.matmul(
                                PS[k][:], actT[:, kc * 512 + t * 128:kc * 512 + (t + 1) * 128],
                                W_in[:, kc * INW + 1024:kc * INW + 1536], start=(kc == 0), stop=(kc == 7)),
                               r=["W_in", f"act{kc}t{t}"], w=[f"P{k}"])
                        tile = c * 4 + t
                        evac_copy(dv[:, tile * 512:(tile + 1) * 512], PS[k][:], [f"P{k}"], [f"dv{tile}"])
                    if b == 0 and c == 0:
                        dump("dqT", slot(0, 4), SR(0, 4))
                        dump("dkT", dkT[:, 0:512], ["dk0c0"])
                        dump("dv0", dv[:, 0:512], ["dv0"])
                    ck("s2a")
                    lat_banks = [(0, 1536), (1, 1664), (2, 1792), (4, 1920), (5, 2048)]
                    for k, col0 in lat_banks:
                        for kc in range(8):
                            op("pe", lambda e, k=k, kc=kc, col0=col0: e.matmul(
                                PS[k][:], W_in[:, kc * INW + col0:kc * INW + col0 + 128], actT[:, kc * 512:(kc + 1) * 512],
                                start=(kc == 0), stop=(kc == 7)), r=["W_in"] + ACT(kc), w=[f"P{k}"])
                    for kc in range(8):
                        op("pe", lambda e, kc=kc: e.matmul(
                            PS[6][0:64, :], W_in[:, kc * INW + 2176:kc * INW + 2240], actT[:, kc * 512:(kc + 1) * 512],
                            start=(kc == 0), stop=(kc == 7)), r=["W_in"] + ACT(kc), w=["P6"])
                    lat_norm([0, 1, 2], 3, C_GQ, 16, 0, 1.0 / 384, 3)
                    lat_norm([4, 5], 2, C_GKV, 18, 3, 1.0 / 256, 7)
                    op("sp", lambda e, c=c: e.dma_start(out=cosT, in_=rope_d[0][:, c * CH:(c + 1) * CH]),
                       r=["rope_d"], w=SR(12, 2), dma="rp0")
                    op("sp", lambda e, c=c: e.dma_start(out=sgnT, in_=rope_d[1][:, c * CH:(c + 1) * CH]),
                       r=["rope_d"], w=SR(14, 2), dma="rp1")
                    ck("s2b")
                    for h in range(4):
                        for kc in range(3):
                            op("pe", lambda e, h=h, kc=kc: e.matmul(
                                PS[h][:], W_qup[:, kc * 768 + h * 128:kc * 768 + (h + 1) * 128],
                                actT[:, kc * 512:(kc + 1) * 512], start=(kc == 0), stop=(kc == 2)),
                               r=["W_qup"] + ACT(kc), w=[f"P{h}"])
                        evac_copy(slot(4 + h), PS[h][:], [f"P{h}"], SR(4 + h), force_act=True)
                    for h in range(4):
                        for kc in range(3):
                            op("pe", lambda e, h=h, kc=kc: e.matmul(
                                PS[h][0:64, :], W_qup[:, kc * 768 + 512 + h * 64:kc * 768 + 512 + (h + 1) * 64],
                                actT[:, kc * 512:(kc + 1) * 512], start=(kc == 0), stop=(kc == 2)),
                               r=["W_qup"] + ACT(kc), w=[f"P{h}"])
                    for h in range(4):
                        rope(h, S[0:64, (8 + h) * 512:(9 + h) * 512], SR(8 + h))
                    if b == 0 and c == 0:
                        dump("qnT", slot(4, 4), SR(4, 4))
                        dump("qpeT", S[0:64, 8 * 512:12 * 512], SR(8, 4), nparts=64)
                        dump("cosT", cosT, SR(12, 2), nparts=64)
                        dump("sgnT", sgnT, SR(14, 2), nparts=64)
                    ck("s3a")
                    kvb = [4, 5, 7, 4, 5, 7, 4, 5]
                    for h in range(4):
                        kb_ = kvb[h]
                        for kc in range(2):
                            op("pe", lambda e, h=h, kc=kc, kb_=kb_: e.matmul(
                                PS[kb_][:], W_kvup[:, kc * 1024 + h * 128:kc * 1024 + (h + 1) * 128],
                                actT[:, (3 + kc) * 512:(4 + kc) * 512], start=(kc == 0), stop=(kc == 1)),
                               r=["W_kvup"] + ACT(3 + kc), w=[f"P{kb_}"])
                        evac_copy(knT[:, h * SEQ + c * CH:h * SEQ + (c + 1) * CH], PS[kb_][:], [f"P{kb_}"], [f"kn{h}c{c}"], force_act=True)
                    for t in range(4):
                        kb_ = kvb[4 + t]
                        for kc in range(2):
                            op("pe", lambda e, t=t, kc=kc, kb_=kb_: e.matmul(
                                PS[kb_][:], actT[:, (3 + kc) * 512 + t * 128:(3 + kc) * 512 + (t + 1) * 128],
                                W_kvup[:, kc * 1024 + 512:kc * 1024 + 1024], start=(kc == 0), stop=(kc == 1)),
                               r=["W_kvup", f"act{3 + kc}t{t}"], w=[f"P{kb_}"])
                        tile = c * 4 + t
                        evac_copy(vv[:, tile * 512:(tile + 1) * 512], PS[kb_][:], [f"P{kb_}"], [f"v{tile}"], force_act=True)
                    rope(6, kpeT[0:64, c * CH:(c + 1) * CH], [f"kpe{c}"])
                    if b == 0 and c == 0:
                        dump("knT", knT[:, 0:512], ["kn0c0"])
                        dump("kpeT", kpeT[0:64, 0:512], ["kpe0"], nparts=64)
                        dump("vv0", vv[:, 0:512], ["v0"])
                    ck("s3b")
                    for u in range(RING_PF):
                        ring_issue(u)
                    free_banks = [0, 1, 2, 3]

                    def ring_alloc():
                        return free_banks.pop(0)

                    def ring_free(k_):
                        free_banks.append(k_)

                    nkb = 4 * c + 4
                    pendA, pendB = [None], [None]

                    def make_epilogue(mi, h, mm, is_diff, ob, rb):
                        obr, rbr = [f"P{ob}"], [f"P{rb}"]
                        rinv = slotf(16)
                        E1 = slotf(18)
                        E2 = slotf(20)
                        sq = actT[:, 7 * 512:8 * 512]

                        def partA():
                            op("dve", lambda e: e.reciprocal(out=rinv, in_=PS[rb][:]), r=rbr, w=SR(16, 2))
                            if not is_diff:
                                f = 4 + h
                                op("dve", lambda e: e.tensor_tensor(out=actT[:, f * 512:(f + 1) * 512], in0=PS[ob][:],
                                                                    in1=rinv, op=ALU.mult), r=obr + SR(16, 2), w=ACT(f))
                            elif mm == 0:
                                op("dve", lambda e: e.tensor_tensor(out=E1, in0=PS[ob][:], in1=rinv, op=ALU.mult),
                                   r=obr + SR(16, 2), w=SR(18, 2))
                            else:
                                op("dve", lambda e: e.tensor_tensor(out=E2, in0=PS[ob][:], in1=rinv, op=ALU.mult),
                                   r=obr + SR(16, 2), w=SR(20, 2))
                                op("dve", lambda e: e.scalar_tensor_tensor(out=E1, in0=E2, scalar=cols[:, C_NLAM:C_NLAM + 1],
                                                                           in1=E1, op0=ALU.mult, op1=ALU.add),
                                   r=SR(18, 4) + ["cols"], w=SR(18, 2))
                                op("act", lambda e: e.activation(out=sq, in_=E1, func=AF.Square), r=SR(18, 2), w=ACT(7))

                        def partB():
                            if not (is_diff and mm == 1):
                                return
                            ssb = ring_alloc()
                            op("pe", lambda e: e.matmul(PS[ssb][:], ones[:], sq, start=True, stop=True),
                               r=ACT(7) + ["ones"], w=[f"P{ssb}"])
                            op("act", lambda e: e.activation(out=E2, in_=PS[ssb][:], func=AF.Ln, bias=cols[:, 15:16],
                                                             scale=1.0 / 128), r=[f"P{ssb}", "cols"], w=SR(20, 2))
                            ring_free(ssb)
                            op("act", lambda e: e.activation(out=E2, in_=E2, func=AF.Exp, scale=-0.5), r=SR(20, 2), w=SR(20, 2))
                            op("dve", lambda e: e.scalar_tensor_tensor(
                                out=actT[:, h * 512:(h + 1) * 512], in0=E1, scalar=cols[:, C_GDP:C_GDP + 1], in1=E2,
                                op0=ALU.mult, op1=ALU.mult), r=SR(18, 4) + ["cols"], w=ACT(h))

                        return partA, partB

                    for mi in range(12):
                        is_diff = mi < 8
                        if is_diff:
                            h, mm = mi // 2, mi % 2
                        else:
                            h, mm = mi - 8, 0
                        ob, rb = 4 + 2 * (mi % 2), 5 + 2 * (mi % 2)
                        obr, rbr = [f"P{ob}"], [f"P{rb}"]

                        def qk(kb, sbk):
                            qlo = max(0, kb - 4 * c) * 128
                            kc_ = kb // 4
                            if is_diff:
                                pr = slice(mm * 64, mm * 64 + 64)
                                op("pe", lambda e: e.matmul(
                                    PS[sbk][:, qlo:512], dkT[pr, h * SEQ + kb * 128:h * SEQ + kb * 128 + 128],
                                    S[pr, h * 512 + qlo:h * 512 + 512], start=True, stop=True),
                                   r=[f"dk{h}c{kc_}"] + SR(h), w=[f"P{sbk}"])
                            else:
                                op("pe", lambda e: e.matmul(
                                    PS[sbk][:, qlo:512], knT[:, h * SEQ + kb * 128:h * SEQ + kb * 128 + 128],
                                    S[:, (4 + h) * 512 + qlo:(4 + h) * 512 + 512], start=True, stop=False),
                                   r=[f"kn{h}c{kc_}"] + SR(4 + h), w=[f"P{sbk}"])
                                op("pe", lambda e: e.matmul(
                                    PS[sbk][:, qlo:512], kpeT[0:64, kb * 128:kb * 128 + 128],
                                    S[0:64, (8 + h) * 512 + qlo:(8 + h) * 512 + 512], start=False, stop=True),
                                   r=[f"kpe{kc_}"] + SR(8 + h), w=[f"P{sbk}"])

                        bank_of = {}
                        for kb0 in range(min(QK_PF, nkb)):
                            bank_of[kb0] = ring_alloc()
                            qk(kb0, bank_of[kb0])
                        for kb in range(nkb):
                            sbk = bank_of[kb]
                            if kb + QK_PF < nkb:
                                bank_of[kb + QK_PF] = ring_alloc()
                                qk(kb + QK_PF, bank_of[kb + QK_PF])
                            qb0 = max(0, kb - 4 * c)
                            qlo = qb0 * 128
                            pt = S[:, (12 + sbk) * 512:(13 + sbk) * 512]
                            ptr = SR(12 + sbk)
                            if is_diff:
                                groups = [(0, 1), (2, 3)] if h == 0 else [(0, 1, 2, 3)]
                                for g in groups:
                                    if g[-1] < qb0:
                                        continue
                                    lo, hi = max(g[0], qb0) * 128, (g[-1] + 1) * 128
                                    delta = (4 * c + g[0] + 1 - kb) if h == 0 else (4 * c + 3 - kb)
                                    op("act", lambda e, lo=lo, hi=hi, delta=delta: e.activation(
                                        out=pt[:, lo:hi], in_=PS[sbk][:, lo:hi], func=AF.Exp,
                                        bias=alibi[:, h * 16 + delta:h * 16 + delta + 1], scale=QSCALE_D),
                                       r=[f"P{sbk}", "alibi"], w=ptr)
                            else:
                                op("act", lambda e: e.activation(out=pt[:, qlo:512], in_=PS[sbk][:, qlo:512], func=AF.Exp,
                                                                 scale=QSCALE_M), r=[f"P{sbk}"], w=ptr)
                            ring_free(sbk)
                            if kb >= 4 * c:
                                op("dve", lambda e: e.tensor_tensor(out=pt[:, qlo:qlo + 128], in0=pt[:, qlo:qlo + 128],
                                                                    in1=tri[:], op=ALU.mult), r=ptr + ["tri"], w=ptr)
                            vsrc = dv if is_diff else vv
                            vres = f"dv{kb}" if is_diff else f"v{kb}"
                            op("pe", lambda e: e.matmul(
                                PS[ob][:, qlo:512], vsrc[:, kb * 512 + h * 128:kb * 512 + (h + 1) * 128], pt[:, qlo:512],
                                start=(kb == 0), stop=(kb == nkb - 1)), r=ptr + [vres], w=obr)
                            op("pe", lambda e: e.matmul(
                                PS[rb][:, qlo:512], ones[:], pt[:, qlo:512], start=(kb == 0), stop=(kb == nkb - 1)),
                               r=ptr + ["ones"], w=rbr)
                            if kb == 0 and pendA[0] is not None:
                                pendA[0]()
                                pendA[0] = None
                            if kb == min(3, nkb - 1) and pendB[0] is not None:
                                pendB[0]()
                                pendB[0] = None
                        pendA[0], pendB[0] = make_epilogue(mi, h, mm, is_diff, ob, rb)
                    pendA[0]()
                    pendB[0]()

                    if b == 0 and c == 0:
                        dump("mergedT", actT[:], ACT_ALL)
                    ck("s4")
                    for t in range(4):
                        for half in range(2):
                            k = t * 2 + half
                            for f in range(8):
                                op("pe", lambda e, f=f: e.matmul(
                                    PS[k][:], actT[:, f * 512 + t * 128:f * 512 + (t + 1) * 128],
                                    W_out[:, f * 1024 + half * 512:f * 1024 + (half + 1) * 512],
                                    start=(f == 0), stop=(f == 7)), r=[f"W_out{f}", f"act{f}t{t}"], w=[f"P{k}"])
                            xs = xc[:, t * 1024 + half * 512:t * 1024 + (half + 1) * 512]
                            op("dve", lambda e: e.tensor_tensor(out=xs, in0=PS[k][:], in1=xs, op=ALU.add),
                               r=[f"P{k}", f"xc{t}h{half}"], w=[f"xc{t}h{half}"])
                        norm_sq(t, 0)
                        norm_rstd(t, 1, 0)
                        norm_scale(t, 0)
                    for t in range(4):
                        norm_tr(b, 1, t)
                    if b == 0 and c == 0:
                        dump("x1", xc[:], XC_ALL)
                    ck("s5")
                    if b == 0 and c == 0:
                        dump("h2T", actT[:], ACT_ALL)
                    ck("s6")
                    for j in range(NJ):
                        rs_ = ring_next()
                        rres = [f"R{rs_}"]
                        ru = ring[:, rs_ * 2048:(rs_ + 1) * 2048]
                        pg, pu = 2 + 2 * (j % 3), 3 + 2 * (j % 3)
                        for kc in range(8):
                            op("pe", lambda e, pg=pg, ru=ru, kc=kc: e.matmul(
                                PS[pg][:], ru[:, kc * 128:(kc + 1) * 128], actT[:, kc * 512:(kc + 1) * 512],
                                start=(kc == 0), stop=(kc == 7)), r=rres + ACT(kc), w=[f"P{pg}"])
                        for kc in range(8):
                            op("pe", lambda e, pu=pu, ru=ru, kc=kc: e.matmul(
                                PS[pu][:], ru[:, 1024 + kc * 128:1024 + (kc + 1) * 128], actT[:, kc * 512:(kc + 1) * 512],
                                start=(kc == 0), stop=(kc == 7)), r=rres + ACT(kc), w=[f"P{pu}"])
                        sg = sgt[:, (j % 2) * 512:(j % 2 + 1) * 512]
                        op("act", lambda e, pg=pg, sg=sg: e.activation(out=sg, in_=PS[pg][:], func=AF.Silu),
                           r=[f"P{pg}"], w=[f"sgt{j % 2}"])
                        op("dve", lambda e, pu=pu, sg=sg, j=j: e.tensor_tensor(out=slot(j), in0=PS[pu][:], in1=sg, op=ALU.mult),
                           r=[f"P{pu}", f"sgt{j % 2}"], w=SR(j))
                        ring_issue(j + RING_PF)
                    if b == 0 and c == 0:
                        dump("aT", S[:], S_ALL)
                    ck("s7a")
                    for j in range(NJ):
                        rs_ = ring_next()
                        rres = [f"R{rs_}"]
                        ru = ring[:, rs_ * 2048:rs_ * 2048 + 1024]
                        op("dve", lambda e, ru=ru: e.tensor_tensor(out=ru, in0=ru, in1=Gf[:], op=ALU.mult),
                           r=rres + ["Gf"], w=rres)
                        for t in range(4):
                            for half in range(2):
                                k = t * 2 + half
                                op("pe", lambda e, k=k, t=t, half=half, j=j, ru=ru: e.matmul(
                                    PS[k][:], S[:, j * 512 + t * 128:j * 512 + (t + 1) * 128],
                                    ru[:, half * 512:(half + 1) * 512], start=(j == 0), stop=(j == NJ - 1)),
                                   r=rres + SR(j), w=[f"P{k}"])
                        ring_issue(NJ + j + RING_PF)
                    ck("s7b")
                    junk = S[:, 20 * 512:22 * 512]
                    stags = []
                    for t in range(4):
                        if t < 3:
                            stags.append((Sf[:, (8 + 4 * t) * 256:(8 + 4 * t) * 256 + 1024], SR(8 + 4 * t, 4)))
                        else:
                            stags.append((sgt[:, 0:1024], ["sgt0", "sgt1"]))
                    for t in range(4):
                        xr = [f"xc{t}h0", f"xc{t}h1"]
                        xt = xc[:, t * 1024:(t + 1) * 1024]
                        stag, stres = stags[t]
                        for half in range(2):
                            k = t * 2 + half
                            xs = xc[:, t * 1024 + half * 512:t * 1024 + (half + 1) * 512]
                            op("dve", lambda e: e.tensor_tensor(out=xs, in0=PS[k][:], in1=xs, op=ALU.add),
                               r=[f"P{k}", f"xc{t}h{half}"], w=[f"xc{t}h{half}"])
                        sres = [f"st{8 + t}"]
                        op("dve", lambda e: e.memset(st[:, 8 + t:9 + t], 0.0), w=sres)
                        op("act", lambda e: e.activation(out=junk, in_=xt, func=AF.Square, accum_out=st[:, 8 + t:9 + t]),
                           r=xr + sres, w=SR(20, 2) + sres)
                        op("act", lambda e: e.activation(out=st[:, 12 + t:13 + t], in_=st[:, 8 + t:9 + t], func=AF.Ln,
                                                         bias=cols[:, 15:16], scale=1.0 / D), r=sres + ["cols"], w=sres)
                        op("act", lambda e: e.activation(out=st[:, 12 + t:13 + t], in_=st[:, 12 + t:13 + t], func=AF.Exp,
                                                         scale=-0.5), r=sres, w=sres)
                    for t in range(4):
                        xr = [f"xc{t}h0", f"xc{t}h1"]
                        xt = xc[:, t * 1024:(t + 1) * 1024]
                        stag, stres = stags[t]
                        sres = [f"st{8 + t}"]
                        op("dve", lambda e: e.scalar_tensor_tensor(out=stag, in0=xt, scalar=st[:, 12 + t:13 + t], in1=gfin[:],
                                                                   op0=ALU.mult, op1=ALU.mult),
                           r=xr + sres + ["gfin"], w=stres)
                        pending_stores.append((stag, stres, t, tok0))

            flush_stores()

        nops, nwait = P.run(body)
        print(f"[kernel] ops={nops} waits={nwait}", flush=True)
    return nc


_NC_CACHE = {}


def _prep_shared(inp):
    f = np.float32
    w_qup = inp["w_q_up"][0]
    qcols = [h * 192 + i for h in range(4) for i in range(128)] + [h * 192 + 128 + i for h in range(4) for i in range(64)]
    w_kvup = inp["w_kv_up"][0]
    kvcols = [h * 256 + i for h in range(4) for i in range(128)] + [h * 256 + 128 + i for h in range(4) for i in range(128)]
    wg = inp["w_ffn_gate"][0].reshape(8, 128, NJ, 128)
    wu = inp["w_ffn_up"][0].reshape(8, 128, NJ, 128)
    wgu = np.stack([wg.transpose(2, 1, 0, 3), wu.transpose(2, 1, 0, 3)], axis=2)
    b_ada = inp["b_ada"][0]
    shared = {
        "w_ada": np.ascontiguousarray(inp["w_ada"][0], dtype=f),
        "b_adaT": np.ascontiguousarray(b_ada.reshape(48, 128).T, dtype=f),
        "b_gate": np.ascontiguousarray(np.concatenate([b_ada[2048:3072], b_ada[5120:6144]])[None, :], dtype=f),
        "g_mixT": np.ascontiguousarray(inp["g_mix"][0].reshape(8, 128).T, dtype=f),
        "g_ffnT": np.ascontiguousarray(inp["g_ffn"][0].reshape(8, 128).T, dtype=f),
        "w_in": np.ascontiguousarray(inp["w_in"][0], dtype=f),
        "lam": np.ascontiguousarray(np.concatenate([inp["lambda_q1"][0], inp["lambda_k1"][0],
                                                    inp["lambda_q2"][0], inp["lambda_k2"][0]])[None, :], dtype=f),
        "g_diff": np.ascontiguousarray(inp["g_diff_out"][0].reshape(128, 1), dtype=f),
        "g_qlT": np.ascontiguousarray(inp["g_q_lat"][0].reshape(3, 128).T, dtype=f),
        "w_qup": np.ascontiguousarray(w_qup[:, qcols], dtype=f),
        "g_kvlT": np.ascontiguousarray(inp["g_kv_lat"][0].reshape(2, 128).T, dtype=f),
        "w_kvup": np.ascontiguousarray(w_kvup[:, kvcols], dtype=f),
        "w_out": np.ascontiguousarray(inp["w_out"][0], dtype=f),
        "w_gu": np.ascontiguousarray(wgu.reshape(NJ, 128, 2048), dtype=f),
        "w_dn": np.ascontiguousarray(inp["w_ffn_down"][0], dtype=f),
        "g_final": np.ascontiguousarray(inp["g_final"].reshape(1, D), dtype=f),
    }
    return shared


def kernel(**inputs):
    inp = {k: np.asarray(v) for k, v in inputs.items()}
    if "nc" not in _NC_CACHE:
        _NC_CACHE["nc"] = build()
    nc = _NC_CACHE["nc"]
    shared = _prep_shared(inp)
    x = inp["x"].astype(np.float32, copy=False)
    c = inp["c"].astype(np.float32, copy=False)
    in_maps = []
    for core in range(NCORES):
        m = dict(shared)
        m["x"] = np.ascontiguousarray(x[core * BPC:(core + 1) * BPC].reshape(BPC * SEQ, D))
        cc = c[core * BPC:(core + 1) * BPC]
        m["cT"] = np.ascontiguousarray(cc.reshape(BPC, 8, 128).transpose(2, 1, 0).reshape(128, 8 * BPC))
        in_maps.append(m)
    res = run_bass_kernel_spmd(nc, in_maps, core_ids=list(range(NCORES)))
    if DBG:
        LAST_DBG["dbg"] = np.asarray(res.results[0]["dbg"])
        LAST_DBG["map"] = dict(DBG_MAP)
    outs = [np.asarray(r["out"]).reshape(BPC, SEQ, D) for r in res.results]
    return np.concatenate(outs, axis=0).astype(np.float32, copy=False)
```

```python
import contextlib
import math
import numpy as np
import concourse.bass as bass
import concourse.mybir as mybir
from concourse.bass_utils import run_bass_kernel_spmd

F32 = mybir.dt.float32
BF16 = mybir.dt.bfloat16
AF = mybir.ActivationFunctionType
ALU = mybir.AluOpType

NCORES = 8
SEQ = 2048
D = 1024
BPC = 4
CH = 512
NCH = SEQ // CH
DFF = 2816
NJ = DFF // 128
INW = 2240
EPS = 1e-6
LAMBDA_INIT = 0.8 - 0.6 * math.exp(-0.3 * 0)
SLOPES = [2.0 ** (-8.0 * (i + 1) / 4) for i in range(4)]
SAME_ENGINE_SYNC = True
import os
STOP = os.environ.get("KSTOP", "")
DBG = bool(os.environ.get("KDBG", ""))
DBGW = 48000
DBG_MAP = {}
LAST_DBG = {}


class _Stop(Exception):
    pass


def ck(label):
    if STOP == label:
        raise _Stop()


class Prog:
    def __init__(self, nc, es):
        self.nc = nc
        self.es = es
        self.eng = {"pe": nc.tensor, "act": nc.scalar, "dve": nc.vector, "pool": nc.gpsimd, "sp": nc.sync}
        self.esem = {k: es.enter_context(nc.semaphore("sem_" + k)) for k in self.eng}
        self.dsem = {}
        self.meta = []
        self.mode = "analyze"
        self.idx = 0
        self.nwait = 0

    def op(self, eng, fn, r=(), w=(), dma=None):
        if self.mode == "analyze":
            self.meta.append((eng, tuple(r), tuple(w), dma))
            return
        i = self.idx
        self.idx += 1
        e = self.eng[eng]
        need = {}
        for j in self.deps[i]:
            key, val = self.tick[j]
            if need.get(key, 0) < val:
                need[key] = val
        wd = self.waited[eng]
        for key, val in need.items():
            if wd.get(key, 0) >= val:
                continue
            wd[key] = val
            sem = self.dsem[key[1]] if key[0] == "d" else self.esem[key[1]]
            e.wait_ge(sem, val)
            self.nwait += 1
        inst = fn(e)
        if dma is not None:
            inst.then_inc(self.dsem[dma], 16)
        elif i in self.signaled:
            inst.then_inc(self.esem[eng], 1)

    def analyze(self):
        ops = self.meta
        last_w = {}
        readers = {}
        last_dma = {}
        deps = []
        for i, (eng, r, w, dma) in enumerate(ops):
            d = set()
            for x in r:
                if x in last_w:
                    d.add(last_w[x])
                if x[0] == "P" and x[1:].isdigit():
                    rd = readers.get(x)
                    if rd:
                        d.update(j for k_, j in rd.items() if k_ != eng)
            for x in w:
                if x in last_w:
                    d.add(last_w[x])
                rd = readers.get(x)
                if rd:
                    d.update(rd.values())
            if dma is not None and dma in last_dma:
                d.add(last_dma[dma])
            d.discard(i)
            for x in w:
                last_w[x] = i
                readers[x] = {}
            for x in r:
                if x in w:
                    continue
                key = eng if dma is None else ("dma", i)
                readers.setdefault(x, {})[key] = i
            if dma is not None:
                last_dma[dma] = i
            kept = []
            for j in d:
                ej, _, _, dj = ops[j]
                if dj is None and dma is None and ej == eng and (eng == "pe" or not SAME_ENGINE_SYNC):
                    continue
                kept.append(j)
            deps.append(kept)
        signaled = set()
        for kept in deps:
            for j in kept:
                if ops[j][3] is None:
                    signaled.add(j)
        tick = {}
        cnt = {k: 0 for k in self.eng}
        dcnt = {}
        for i, (eng, r, w, dma) in enumerate(ops):
            if dma is not None:
                if dma not in self.dsem:
                    self.dsem[dma] = self.es.enter_context(self.nc.semaphore("d_" + dma))
                dcnt[dma] = dcnt.get(dma, 0) + 16
                tick[i] = (("d", dma), dcnt[dma])
            elif i in signaled:
                cnt[eng] += 1
                tick[i] = (("e", eng), cnt[eng])
        self.deps, self.signaled, self.tick, self.dcnt = deps, signaled, tick, dcnt
        self.waited = {k: {} for k in self.eng}

    def run(self, body):
        self.mode = "analyze"
        try:
            body()
        except _Stop:
            pass
        self.analyze()
        self.mode = "emit"
        self.idx = 0
        try:
            body()
        except _Stop:
            pass
        assert self.idx == len(self.meta), (self.idx, len(self.meta))
        sp = self.eng["sp"]
        for name, sem in self.dsem.items():
            if self.waited["sp"].get(("d", name), 0) < self.dcnt[name]:
                sp.wait_ge(sem, self.dcnt[name])
        return len(self.meta), self.nwait


def build():
    nc = bass.Bass("TRN2", target_bir_lowering=False)

    def dram(name, shape, dt=F32, kind="ExternalInput"):
        return nc.dram_tensor(name, list(shape), dt, kind=kind).ap()

    x_d = dram("x", [BPC * SEQ, D])
    out_d = dram("out", [BPC * SEQ, D], kind="ExternalOutput")
    cT_d = dram("cT", [128, 8 * BPC])
    wada_d = dram("w_ada", [D, 6 * D])
    badaT_d = dram("b_adaT", [128, 48])
    bgate_d = dram("b_gate", [1, 2 * D])
    gmixT_d = dram("g_mixT", [128, 8])
    gffnT_d = dram("g_ffnT", [128, 8])
    win_d = dram("w_in", [D, INW])
    lam_d = dram("lam", [1, 256])
    gdiff_d = dram("g_diff", [128, 1])
    gqlT_d = dram("g_qlT", [128, 3])
    wqup_d = dram("w_qup", [384, 768])
    gkvlT_d = dram("g_kvlT", [128, 2])
    wkvup_d = dram("w_kvup", [256, 1024])
    wout_d = dram("w_out", [D, D])
    wgu_d = dram("w_gu", [NJ, 128, 2048])
    wdn_d = dram("w_dn", [DFF, D])
    gfin_d = dram("g_final", [1, D])
    gates_d = dram("gates_scr", [BPC, 2 * D], kind="Internal")
    rope_d = dram("rope_scr", [2, 64, SEQ], kind="Internal")
    wgu_s = dram("wgu_bf16_scr", [NJ, 128, 2048], BF16, kind="Internal")
    wdn_s = dram("wdn_bf16_scr", [DFF, D], BF16, kind="Internal")
    dbg_d = dram("dbg", [128, DBGW], kind="ExternalOutput") if DBG else None

    es = contextlib.ExitStack()
    with es:
        def sb(name, shape, dt):
            return es.enter_context(nc.sbuf_tensor(name, list(shape), dt))

        W_in = sb("W_in", [128, 8 * INW], BF16)
        W_qup = sb("W_qup", [128, 3 * 768], BF16)
        W_kvup = sb("W_kvup", [128, 2 * 1024], BF16)
        W_out = sb("W_out", [128, 8 * 1024], BF16)
        dkT = sb("dkT", [128, 4 * SEQ], BF16)
        dv = sb("dv", [128, 16 * 512], BF16)
        knT = sb("knT", [128, 4 * SEQ], BF16)
        kpeT = sb("kpeT", [128, SEQ], BF16)
        vv = sb("vv", [128, 16 * 512], BF16)
        xc = sb("xc", [128, 4 * 1024], F32)
        actT = sb("actT", [128, 8 * 512], BF16)
        S = sb("S", [128, NJ * 512], BF16)
        ring = sb("ring", [128, 4 * 2048], BF16)
        Ga = sb("Ga", [128, 1024], F32)
        Gf = sb("Gf", [128, 1024], F32)
        gfin = sb("gfin", [128, 1024], F32)
        sgt = sb("sgt", [128, 2 * 512], F32)
        ident = sb("ident", [128, 128], BF16)
        ones = sb("ones", [128, 128], BF16)
        tri = sb("tri", [128, 128], BF16)
        alibi = sb("alibi", [128, 64], F32)
        AB = sb("AB", [128, BPC * 2 * 2 * 8], F32)
        st = sb("st", [128, 16], F32)
        cols = sb("cols", [128, 32], F32)
        PS = [es.enter_context(nc.psum_tensor(f"ps{k}", [128, 512], F32)) for k in range(8)]

        P = Prog(nc, es)
        op = P.op

        def body():
            Sf = S[:].bitcast(F32)
            modT = sgt[:, 256:256 + 48 * BPC]
            xcb = xc[:].bitcast(BF16)

            def slot(j, n=1):
                return S[:, j * 512:(j + n) * 512]

            def slotf(j):
                return Sf[:, j * 256:j * 256 + 512]

            def SR(j, n=1):
                return [f"S{k}" for k in range(j, j + n)]

            dbg_off = [0]

            def dump(label, ap, res, nparts=128):
                if not DBG or label in DBG_MAP and P.mode == "analyze":
                    return
                wdt = ap.shape[-1]
                if P.mode == "analyze":
                    DBG_MAP[label] = (dbg_off[0], wdt, nparts)
                o = DBG_MAP[label][0]
                dbg_off[0] = o + wdt
                op("pool", lambda e: e.dma_start(out=dbg_d[0:nparts, o:o + wdt], in_=ap), r=res, w=["dbg_" + label],
                   dma="dbg")

            XC_ALL = [f"xc{t}h{h}" for t in range(4) for h in range(2)]
            S_ALL = SR(0, NJ)
            ACT_ALL = [f"act{k}t{t}" for k in range(8) for t in range(4)]

            def ACT(k):
                return [f"act{k}t{t}" for t in range(4)]

            def AB_col(b, af, ab, kc):
                o = ((b * 2 + af) * 2 + ab) * 8 + kc
                return AB[:, o:o + 1]

            C_NLAM, C_GDP, C_GD, C_GQ, C_GKV = 0, 1, 2, 3, 6

            tmpf = Sf[:, 0:128]
            op("pool", lambda e: e.memset(tmpf, 1.0), w=SR(0))
            op("pool", lambda e: e.affine_select(out=tmpf, in_=tmpf, pattern=[[-1, 128]], compare_op=ALU.is_equal,
                                                 fill=0.0, base=0, channel_multiplier=1), r=SR(0), w=SR(0))
            op("dve", lambda e: e.tensor_copy(out=ident[:], in_=tmpf), r=SR(0), w=["ident"])
            tmpf2 = Sf[:, 256:384]
            op("pool", lambda e: e.memset(tmpf2, 1.0), w=SR(1))
            op("pool", lambda e: e.affine_select(out=tmpf2, in_=tmpf2, pattern=[[1, 128]], compare_op=ALU.is_ge,
                                                 fill=0.0, base=0, channel_multiplier=-1), r=SR(1), w=SR(1))
            op("dve", lambda e: e.tensor_copy(out=tri[:], in_=tmpf2), r=SR(1), w=["tri"])
            op("dve", lambda e: e.memset(ones[:], 1.0), w=["ones"])
            op("dve", lambda e: e.memset(cols[:, 15:16], EPS), w=["cols"])
            tmpa = Sf[:, 512:528]
            op("pool", lambda e: e.iota(tmpa, pattern=[[-128, 16]], base=-127, channel_multiplier=1,
                                        allow_small_or_imprecise_dtypes=True), w=SR(2))
            tmpa0 = Sf[:, 768:784]
            op("pool", lambda e: e.iota(tmpa0, pattern=[[-128, 16]], base=0, channel_multiplier=1,
                                        allow_small_or_imprecise_dtypes=True), w=SR(3))
            op("dve", lambda e: e.tensor_scalar(out=alibi[:, 0:16], in0=tmpa0, scalar1=SLOPES[0], scalar2=None,
                                                op0=ALU.mult), r=SR(3), w=["alibi"])
            for h in range(1, 4):
                op("dve", lambda e, h=h: e.tensor_scalar(out=alibi[:, h * 16:(h + 1) * 16], in0=tmpa, scalar1=SLOPES[h],
                                                         scalar2=None, op0=ALU.mult), r=SR(2), w=["alibi"])

            ck("consts")
            win_v = win_d.rearrange("(kc p) n -> p kc n", p=128)
            for kc in range(8):
                op("pool", lambda e, kc=kc: e.dma_start(out=W_in[:, kc * INW:(kc + 1) * INW], in_=win_v[:, kc, :]),
                   w=["W_in"], dma=f"w{kc % 2}")
            wq_v = wqup_d.rearrange("(kc p) n -> p kc n", p=128)
            for kc in range(3):
                op("pool", lambda e, kc=kc: e.dma_start(out=W_qup[:, kc * 768:(kc + 1) * 768], in_=wq_v[:, kc, :]),
                   w=["W_qup"], dma="w2")
            wkv_v = wkvup_d.rearrange("(kc p) n -> p kc n", p=128)
            for kc in range(2):
                op("pool", lambda e, kc=kc: e.dma_start(out=W_kvup[:, kc * 1024:(kc + 1) * 1024], in_=wkv_v[:, kc, :]),
                   w=["W_kvup"], dma="w3")
            wo_v = wout_d.rearrange("(kc p) n -> p kc n", p=128)

            ck("weights")
            cTs = Sf[:, 1024:1024 + 8 * BPC]
            op("sp", lambda e: e.dma_start(out=cTs, in_=cT_d[:, :]), w=SR(4), dma="p0")
            op("sp", lambda e: e.dma_start(out=cols[:, 16:24], in_=gmixT_d[:, :]), w=["cols"], dma="p1")
            op("sp", lambda e: e.dma_start(out=cols[:, 24:32], in_=gffnT_d[:, :]), w=["cols"], dma="p1")
            op("sp", lambda e: e.dma_start(out=cols[:, C_GD:C_GD + 1], in_=gdiff_d[:, :]), w=["cols"], dma="p1")
            op("sp", lambda e: e.dma_start(out=cols[:, C_GQ:C_GQ + 3], in_=gqlT_d[:, :]), w=["cols"], dma="p1")
            op("sp", lambda e: e.dma_start(out=cols[:, C_GKV:C_GKV + 2], in_=gkvlT_d[:, :]), w=["cols"], dma="p1")
            badaT = Sf[:, 1280:1280 + 48]
            op("sp", lambda e: e.dma_start(out=badaT, in_=badaT_d[:, :]), w=SR(5), dma="p2")
            op("sp", lambda e: e.dma_start(out=gfin[:], in_=gfin_d[:, :].partition_broadcast(128)), w=["gfin"], dma="p3")
            lamb = Sf[:, 1536:1536 + 256]
            op("sp", lambda e: e.dma_start(out=lamb, in_=lam_d[:, :].partition_broadcast(128)), w=SR(6), dma="p4")
            bg = Sf[0:BPC, 2048:2048 + 2048]
            op("sp", lambda e: e.dma_start(out=bg, in_=bgate_d[:, :].partition_broadcast(BPC)), w=SR(8, 8), dma="p5")

            ck("params")
            lprod = Sf[:, 1792:1792 + 128]
            op("dve", lambda e: e.tensor_tensor(out=lprod[:, 0:64], in0=lamb[:, 0:64], in1=lamb[:, 64:128], op=ALU.mult),
               r=SR(6), w=SR(7))
            op("dve", lambda e: e.tensor_tensor(out=lprod[:, 64:128], in0=lamb[:, 128:192], in1=lamb[:, 192:256], op=ALU.mult),
               r=SR(6), w=SR(7))
            op("dve", lambda e: e.reduce_sum(out=cols[:, 8:9], in_=lprod[:, 0:64], axis=mybir.AxisListType.X), r=SR(7), w=["cols"])
            op("dve", lambda e: e.reduce_sum(out=cols[:, 9:10], in_=lprod[:, 64:128], axis=mybir.AxisListType.X), r=SR(7), w=["cols"])
            op("act", lambda e: e.activation(out=cols[:, 10:12], in_=cols[:, 8:10], func=AF.Exp), r=["cols"], w=["cols"])
            op("dve", lambda e: e.tensor_tensor(out=cols[:, C_NLAM:C_NLAM + 1], in0=cols[:, 11:12], in1=cols[:, 10:11],
                                                op=ALU.subtract), r=["cols"], w=["cols"])
            op("dve", lambda e: e.tensor_scalar(out=cols[:, C_NLAM:C_NLAM + 1], in0=cols[:, C_NLAM:C_NLAM + 1],
                                                scalar1=-LAMBDA_INIT, scalar2=None, op0=ALU.add), r=["cols"], w=["cols"])
            op("dve", lambda e: e.tensor_scalar(out=cols[:, C_GDP:C_GDP + 1], in0=cols[:, C_GD:C_GD + 1],
                                                scalar1=1.0 - LAMBDA_INIT, scalar2=None, op0=ALU.mult), r=["cols"], w=["cols"])

            ck("lam")
            scb = sgt[:].bitcast(BF16)[:, 0:8 * BPC]
            SCB = ["sgt0", "sgt1"]
            op("act", lambda e: e.activation(out=scb, in_=cTs, func=AF.Silu), r=SR(4), w=SCB)
            wada_v = wada_d.rearrange("(kc p) n -> p kc n", p=128)
            gsb = actT[:].bitcast(F32)[0:BPC, 0:2048]
            bank = 0
            for u in range(12):
                sl = u % 2
                unit = xcb[:, sl * 4096:(sl + 1) * 4096]
                ures = XC_ALL[sl * 4:(sl + 1) * 4]
                op("pool", lambda e, unit=unit, u=u: e.dma_start(
                    out=unit.rearrange("p (kc n) -> p kc n", kc=8), in_=wada_v[:, :, u * 512:(u + 1) * 512]),
                   w=ures, dma=f"ada{sl}")
                if u in (4, 5, 10, 11):
                    g = 0 if u < 6 else 1
                    half = u % 2 if u < 6 else (u - 10)
                    pb = PS[bank % 8]
                    pres = [f"P{bank % 8}"]
                    bank += 1
                    for kc in range(8):
                        op("pe", lambda e, pb=pb, unit=unit, kc=kc: e.matmul(
                            pb[0:BPC, :], scb[:, kc * BPC:(kc + 1) * BPC], unit[:, kc * 512:(kc + 1) * 512],
                            start=(kc == 0), stop=(kc == 7)), r=ures + SCB, w=pres)
                    o = g * 1024 + half * 512
                    op("dve", lambda e, pb=pb, o=o: e.tensor_tensor(out=gsb[:, o:o + 512], in0=pb[0:BPC, :],
                                                                    in1=bg[:, o:o + 512], op=ALU.add),
                       r=pres + SR(8, 8), w=ACT_ALL)
                else:
                    for mm in range(4):
                        m = 4 * u + mm
                        pb = PS[bank % 8]
                        pres = [f"P{bank % 8}"]
                        bank += 1
                        for kc in range(8):
                            op("pe", lambda e, pb=pb, unit=unit, kc=kc, mm=mm: e.matmul(
                                pb[:, 0:BPC], unit[:, kc * 512 + mm * 128:kc * 512 + (mm + 1) * 128],
                                scb[:, kc * BPC:(kc + 1) * BPC], start=(kc == 0), stop=(kc == 7)),
                               r=ures + SCB, w=pres)
                        op("dve", lambda e, pb=pb, m=m: e.tensor_scalar(
                            out=modT[:, m * BPC:(m + 1) * BPC], in0=pb[:, 0:BPC], scalar1=badaT[:, m:m + 1], scalar2=None,
                            op0=ALU.add), r=pres + SR(5), w=["sgt0"])
            op("sp", lambda e: e.dma_start(out=gates_d[:, :], in_=gsb), r=ACT_ALL, w=["gates_d"], dma="p6")
            modT3 = modT.rearrange("p (m b) -> p m b", b=BPC)
            for b in range(BPC):
                for af, (m_sh, m_sc, gcol) in enumerate([(0, 8, 16), (24, 32, 24)]):
                    oA = ((b * 2 + af) * 2 + 0) * 8
                    oB = ((b * 2 + af) * 2 + 1) * 8
                    op("dve", lambda e, b=b, m_sc=m_sc, gcol=gcol, oA=oA: e.scalar_tensor_tensor(
                        out=AB[:, oA:oA + 8], in0=modT3[:, m_sc:m_sc + 8, b], scalar=1.0, in1=cols[:, gcol:gcol + 8],
                        op0=ALU.add, op1=ALU.mult), r=["sgt0", "cols"], w=["AB"])
                    op("dve", lambda e, b=b, m_sh=m_sh, oB=oB: e.tensor_copy(out=AB[:, oB:oB + 8],
                                                                             in_=modT3[:, m_sh:m_sh + 8, b]),
                       r=["sgt0"], w=["AB"])

            ck("ada")
            xcf = xc[:]
            pos = xcf[0:64, 0:2048]
            ang = xcf[0:64, 2048:4096]
            op("pool", lambda e: e.iota(pos, pattern=[[1, 2048]], base=0, channel_multiplier=0,
                                        allow_small_or_imprecise_dtypes=True), w=XC_ALL)
            op("pool", lambda e: e.iota(cols[0:32, 12:13], pattern=[[0, 1]], base=0, channel_multiplier=1,
                                        allow_small_or_imprecise_dtypes=True), w=["cols"])
            op("pool", lambda e: e.iota(cols[32:64, 12:13], pattern=[[0, 1]], base=0, channel_multiplier=1,
                                        allow_small_or_imprecise_dtypes=True), w=["cols"])
            op("act", lambda e: e.activation(out=cols[0:64, 13:14], in_=cols[0:64, 12:13], func=AF.Exp,
                                             scale=-math.log(10000.0) / 32.0), r=["cols"], w=["cols"])
            op("dve", lambda e: e.tensor_scalar(out=ang, in0=pos, scalar1=cols[0:64, 13:14], scalar2=None, op0=ALU.mult),
               r=XC_ALL + ["cols"], w=XC_ALL)
            TWO_PI = 2.0 * math.pi
            MAGIC = 12582912.0
            PI_LO = 3.141592
            for which, off in ((0, 0.5 * math.pi), (1, 0.0)):
                dst = Sf[0:64, which * 2048:(which + 1) * 2048]
                op("dve", lambda e, off=off: e.tensor_scalar(out=pos, in0=ang, scalar1=off, scalar2=1.0 / TWO_PI, op0=ALU.add,
                                                             op1=ALU.mult), r=XC_ALL, w=XC_ALL)
                op("dve", lambda e: e.tensor_scalar(out=pos, in0=pos, scalar1=MAGIC, scalar2=None, op0=ALU.add),
                   r=XC_ALL, w=XC_ALL)
                op("dve", lambda e: e.tensor_scalar(out=pos, in0=pos, scalar1=-MAGIC, scalar2=None, op0=ALU.add),
                   r=XC_ALL, w=XC_ALL)
                op("dve", lambda e: e.scalar_tensor_tensor(out=pos, in0=pos, scalar=-TWO_PI, in1=ang, op0=ALU.mult,
                                                           op1=ALU.add), r=XC_ALL, w=XC_ALL)
                op("dve", lambda e, off=off: e.tensor_scalar(out=pos, in0=pos, scalar1=off, scalar2=None, op0=ALU.add),
                   r=XC_ALL, w=XC_ALL)
                op("dve", lambda e: e.tensor_scalar(out=pos, in0=pos, scalar1=-PI_LO, scalar2=PI_LO,
                                                    op0=ALU.max, op1=ALU.min), r=XC_ALL, w=XC_ALL)
                op("act", lambda e, dst=dst: e.activation(out=dst, in_=pos, func=AF.Sin), r=XC_ALL, w=S_ALL)
            sgn_hi = Sf[32:64, 2048:4096]
            op("dve", lambda e: e.tensor_scalar(out=sgn_hi, in0=sgn_hi, scalar1=-1.0, scalar2=None, op0=ALU.mult),
               r=S_ALL, w=S_ALL)
            op("sp", lambda e: e.dma_start(out=rope_d[0], in_=Sf[0:64, 0:2048]), r=S_ALL, w=["rope_d"], dma="p7")
            op("sp", lambda e: e.dma_start(out=rope_d[1], in_=Sf[0:64, 2048:4096]), r=S_ALL, w=["rope_d"], dma="p7")

            ck("rope")
            dump("AB", AB[:], ["AB"])
            dump("modT", modT, ["sgt0"])
            dump("cols", cols[:], ["cols"])
            dump("alibi", alibi[:], ["alibi"])
            dump("tri", tri[:], ["tri"])
            dump("ident", ident[:], ["ident"])
            evac_rr = [0]

            def evac_copy(dst, src, r, w, force_act=False):
                evac_rr[0] += 1
                if force_act or evac_rr[0] % 2 == 0:
                    op("act", lambda e: e.activation(out=dst, in_=src, func=AF.Copy), r=r, w=w)
                else:
                    op("dve", lambda e: e.tensor_copy(out=dst, in_=src), r=r, w=w)

            def hn_buf(t):
                return S[:, t * 1024:(t + 1) * 1024], SR(2 * t, 2)

            def norm_sq(t, col0):
                xr = [f"xc{t}h0", f"xc{t}h1"]
                xt = xc[:, t * 1024:(t + 1) * 1024]
                hn, hres = hn_buf(t)
                sres = [f"st{t}"]
                op("dve", lambda e: e.memset(st[:, col0 + t:col0 + t + 1], 0.0), w=sres)
                op("act", lambda e: e.activation(out=hn, in_=xt, func=AF.Square, accum_out=st[:, col0 + t:col0 + t + 1]),
                   r=xr + sres, w=hres + sres)

            def norm_rstd(t0, n, col0):
                sres = [f"st{t}" for t in range(t0, t0 + n)]
                src_ = st[:, col0 + t0:col0 + t0 + n]
                dst_ = st[:, col0 + 4 + t0:col0 + 4 + t0 + n]
                op("act", lambda e: e.activation(out=dst_, in_=src_, func=AF.Ln, bias=cols[:, 15:16], scale=1.0 / D),
                   r=sres + ["cols"], w=sres)
                op("act", lambda e: e.activation(out=dst_, in_=dst_, func=AF.Exp, scale=-0.5), r=sres, w=sres)

            def norm_scale(t, col0):
                xr = [f"xc{t}h0", f"xc{t}h1"]
                xt = xc[:, t * 1024:(t + 1) * 1024]
                hn, hres = hn_buf(t)
                op("dve", lambda e: e.tensor_scalar(out=hn, in0=xt, scalar1=st[:, col0 + 4 + t:col0 + 5 + t], scalar2=None,
                                                    op0=ALU.mult), r=xr + [f"st{t}"], w=hres)

            def norm_tr(b, af, t):
                hn, hres = hn_buf(t)
                pb = PS[t % 2]
                pbb = pb[:].bitcast(BF16)
                pres = [f"P{t % 2}"]
                for kc in range(8):
                    op("pe", lambda e, kc=kc: e.transpose(pbb[:, kc * 128:(kc + 1) * 128],
                                                          hn[:, kc * 128:(kc + 1) * 128], ident[:]),
                       r=hres + ["ident"], w=pres)
                for kc in range(8):
                    dst = actT[:, kc * 512 + t * 128:kc * 512 + (t + 1) * 128]
                    src_ = pbb[:, kc * 128:(kc + 1) * 128]
                    a_ = AB_col(b, af, 0, kc)
                    b_ = AB_col(b, af, 1, kc)
                    if t % 2 == 0:
                        op("dve", lambda e, dst=dst, src_=src_, a_=a_, b_=b_: e.tensor_scalar(
                            out=dst, in0=src_, scalar1=a_, scalar2=b_, op0=ALU.mult, op1=ALU.add),
                           r=pres + ["AB"], w=[f"act{kc}t{t}"])
                    else:
                        op("act", lambda e, dst=dst, src_=src_, a_=a_, b_=b_: e.activation(
                            out=dst, in_=src_, func=AF.Identity, bias=b_, scale=a_),
                           r=pres + ["AB"], w=[f"act{kc}t{t}"])

            def lat_norm(banks, n, gc0, rstd_slot, dst_act0, inv_n, ssbank):
                for i in range(n):
                    sq = actT[:, (5 + i % 2) * 512:(6 + i % 2) * 512]
                    op("act", lambda e, sq=sq, i=i: e.activation(out=sq, in_=PS[banks[i]][:], func=AF.Square),
                       r=[f"P{banks[i]}"], w=ACT(5 + i % 2))
                    op("pe", lambda e, sq=sq, i=i: e.matmul(PS[ssbank][:], ones[:], sq, start=(i == 0), stop=(i == n - 1)),
                       r=ACT(5 + i % 2) + ["ones"], w=[f"P{ssbank}"])
                rs = slotf(rstd_slot)
                op("act", lambda e: e.activation(out=rs, in_=PS[ssbank][:], func=AF.Ln, bias=cols[:, 15:16], scale=inv_n),
                   r=[f"P{ssbank}", "cols"], w=SR(rstd_slot, 2))
                op("act", lambda e: e.activation(out=rs, in_=rs, func=AF.Exp, scale=-0.5), r=SR(rstd_slot, 2), w=SR(rstd_slot, 2))
                for i in range(n):
                    dst = actT[:, (dst_act0 + i) * 512:(dst_act0 + i + 1) * 512]
                    op("dve", lambda e, dst=dst, i=i: e.scalar_tensor_tensor(
                        out=dst, in0=PS[banks[i]][:], scalar=cols[:, gc0 + i:gc0 + i + 1], in1=rs, op0=ALU.mult,
                        op1=ALU.mult), r=[f"P{banks[i]}", "cols"] + SR(rstd_slot, 2), w=ACT(dst_act0 + i))

            cosT = Sf[0:64, 12 * 256:12 * 256 + 512]
            sgnT = Sf[0:64, 14 * 256:14 * 256 + 512]
            t2 = Sf[0:64, 20 * 256:20 * 256 + 512]

            def rope(pbank, dst, dres):
                pb = PS[pbank]
                pres = [f"P{pbank}"]
                op("dve", lambda e: e.tensor_tensor(out=t2[0:32, :], in0=pb[32:64, :], in1=sgnT[32:64, :], op=ALU.mult),
                   r=pres + SR(14, 2), w=SR(20, 2))
                op("dve", lambda e: e.tensor_tensor(out=t2[32:64, :], in0=pb[0:32, :], in1=sgnT[0:32, :], op=ALU.mult),
                   r=pres + SR(14, 2), w=SR(20, 2))
                op("dve", lambda e: e.tensor_tensor(out=pb[0:64, :], in0=pb[0:64, :], in1=cosT, op=ALU.mult),
                   r=pres + SR(12, 2), w=pres)
                op("dve", lambda e: e.tensor_tensor(out=dst, in0=pb[0:64, :], in1=t2, op=ALU.add),
                   r=pres + SR(20, 2), w=dres)

            ring_ctr = [0]

            def ring_next():
                s = ring_ctr[0] % 4
                ring_ctr[0] += 1
                return s

            RING_PF = 4
            QK_PF = 3
            ring_issued = [0]

            def ring_issue_gu(u):
                if u >= NJ:
                    return
                rs_ = u % 4
                ru = ring[:, rs_ * 2048:(rs_ + 1) * 2048]
                op("pool", lambda e: e.dma_start(out=ru, in_=wgu_s[u]), r=[f"wgu_s{u}"],
                   w=[f"R{rs_}a", f"R{rs_}b"], dma=f"rg{rs_}a")

            def ring_issue_dn(j):
                if j >= NJ:
                    return
                hs = j % 8
                ru = ring[:, hs * 1024:(hs + 1) * 1024]
                nm = f"R{hs // 2}{'ab'[hs % 2]}"
                op("pool", lambda e: e.dma_start(out=ru, in_=wdn_s[j * 128:(j + 1) * 128, :]),
                   r=[f"wdn_s{j}"], w=[nm], dma="rg" + nm[1:])

            pending_stores = []

            def flush_stores():
                for stag_, stres_, t_, tok0_ in pending_stores:
                    op("sp", lambda e, stag_=stag_, t_=t_, tok0_=tok0_: e.dma_start(
                        out=out_d[tok0_ + t_ * 128:tok0_ + (t_ + 1) * 128, :], in_=stag_),
                       r=stres_, w=[f"out{t_}"], dma=f"so{t_}")
                pending_stores.clear()

            QSCALE_D = 64 ** -0.5
            QSCALE_M = 192 ** -0.5

            for b in range(BPC):
                op("sp", lambda e, b=b: e.dma_start(out=Ga[:], in_=gates_d[b:b + 1, 0:1024].partition_broadcast(128)),
                   r=["gates_d"], w=["Ga"], dma="ga")
                op("sp", lambda e, b=b: e.dma_start(out=Gf[:], in_=gates_d[b:b + 1, 1024:2048].partition_broadcast(128)),
                   r=["gates_d"], w=["Gf"], dma="gf")
                for kc in range(8):
                    op("pool", lambda e, kc=kc: e.dma_start(out=W_out[:, kc * 1024:(kc + 1) * 1024], in_=wo_v[:, kc, :]),
                       w=[f"W_out{kc}"], dma=f"w{4 + kc % 2}")
                    op("pool", lambda e, kc=kc: e.tensor_tensor(out=W_out[:, kc * 1024:(kc + 1) * 1024],
                                                                in0=W_out[:, kc * 1024:(kc + 1) * 1024], in1=Ga[:], op=ALU.mult),
                       r=[f"W_out{kc}", "Ga"], w=[f"W_out{kc}"])
                if b == 0:
                    for j in range(NJ):
                        op("pool", lambda e, j=j: e.dma_start(out=wgu_s[j], in_=wgu_d[j]), w=[f"wgu_s{j}"], dma=f"pc{j % 4}")
                    for j in range(NJ):
                        op("pool", lambda e, j=j: e.dma_start(out=wdn_s[j * 128:(j + 1) * 128, :],
                                                              in_=wdn_d[j * 128:(j + 1) * 128, :]),
                           w=[f"wdn_s{j}"], dma=f"pc{j % 4}")
                ck("g")
                for c in range(NCH):
                    tok0 = b * SEQ + c * CH
                    for t in range(4):
                        op("sp", lambda e, t=t, tok0=tok0: e.dma_start(out=xc[:, t * 1024:(t + 1) * 1024],
                                                                      in_=x_d[tok0 + t * 128:tok0 + (t + 1) * 128, :]),
                           w=[f"xc{t}h0", f"xc{t}h1"], dma=f"ld{t}")
                    ck("xl")
                    for t in range(4):
                        norm_sq(t, 0)
                    flush_stores()
                    norm_rstd(0, 4, 0)
                    for t in range(4):
                        norm_scale(t, 0)
                    for t in range(4):
                        norm_tr(b, 0, t)
                    if b == 0 and c == 0:
                        dump("Ga", Ga[:], ["Ga"])
                        dump("hT", actT[:], ACT_ALL)
                        dump("st", st[:], [f"st{t}" for t in range(4)])
                    ck("s1")
                    rr = 0
                    for m in range(8):
                        k = 2 + rr % 6
                        rr += 1
                        for kc in range(8):
                            op("pe", lambda e, k=k, kc=kc, m=m: e.matmul(
                                PS[k][:], W_in[:, kc * INW + m * 128:kc * INW + (m + 1) * 128],
                                actT[:, kc * 512:(kc + 1) * 512], start=(kc == 0), stop=(kc == 7)),
                               r=["W_in"] + ACT(kc), w=[f"P{k}"])
                        if m < 4:
                            evac_copy(slot(m), PS[k][:], [f"P{k}"], SR(m))
                        else:
                            h = m - 4
                            evac_copy(dkT[:, h * SEQ + c * CH:h * SEQ + (c + 1) * CH], PS[k][:], [f"P{k}"], [f"dk{h}c{c}"])
                    for t in range(4):
                        k = 2 + rr % 6
                        rr += 1
                        for kc in range(8):
                            op("pe", lambda e, k=k, kc=kc, t=t: e.matmul(
                                PS[k][:], actT[:, kc * 512 + t * 128:kc * 512 + (t + 1) * 128],
                                W_in[:, kc * INW + 1024:kc * INW + 1536], start=(kc == 0), stop=(kc == 7)),
                               r=["W_in", f"act{kc}t{t}"], w=[f"P{k}"])
                        tile = c * 4 + t
                        evac_copy(dv[:, tile * 512:(tile + 1) * 512], PS[k][:], [f"P{k}"], [f"dv{tile}"])
                    if b == 0 and c == 0:
                        dump("dqT", slot(0, 4), SR(0, 4))
                        dump("dkT", dkT[:, 0:512], ["dk0c0"])
                        dump("dv0", dv[:, 0:512], ["dv0"])
                    ck("s2a")
                    lat_banks = [(0, 1536), (1, 1664), (2, 1792), (4, 1920), (5, 2048)]
                    for k, col0 in lat_banks:
                        for kc in range(8):
                            op("pe", lambda e, k=k, kc=kc, col0=col0: e.matmul(
                                PS[k][:], W_in[:, kc * INW + col0:kc * INW + col0 + 128], actT[:, kc * 512:(kc + 1) * 512],
                                start=(kc == 0), stop=(kc == 7)), r=["W_in"] + ACT(kc), w=[f"P{k}"])
                    for kc in range(8):
                        op("pe", lambda e, kc=kc: e.matmul(
                            PS[6][0:64, :], W_in[:, kc * INW + 2176:kc * INW + 2240], actT[:, kc * 512:(kc + 1) * 512],
                            start=(kc == 0), stop=(kc == 7)), r=["W_in"] + ACT(kc), w=["P6"])
                    lat_norm([0, 1, 2], 3, C_GQ, 16, 0, 1.0 / 384, 3)
                    lat_norm([4, 5], 2, C_GKV, 18, 3, 1.0 / 256, 7)
                    op("sp", lambda e, c=c: e.dma_start(out=cosT, in_=rope_d[0][:, c * CH:(c + 1) * CH]),
                       r=["rope_d"], w=SR(12, 2), dma="rp0")
                    op("sp", lambda e, c=c: e.dma_start(out=sgnT, in_=rope_d[1][:, c * CH:(c + 1) * CH]),
                       r=["rope_d"], w=SR(14, 2), dma="rp1")
                    ck("s2b")
                    for h in range(4):
                        for kc in range(3):
                            op("pe", lambda e, h=h, kc=kc: e.matmul(
                                PS[h][:], W_qup[:, kc * 768 + h * 128:kc * 768 + (h + 1) * 128],
                                actT[:, kc * 512:(kc + 1) * 512], start=(kc == 0), stop=(kc == 2)),
                               r=["W_qup"] + ACT(kc), w=[f"P{h}"])
                        evac_copy(slot(4 + h), PS[h][:], [f"P{h}"], SR(4 + h), force_act=True)
                    for h in range(4):
                        for kc in range(3):
                            op("pe", lambda e, h=h, kc=kc: e.matmul(
                                PS[h][0:64, :], W_qup[:, kc * 768 + 512 + h * 64:kc * 768 + 512 + (h + 1) * 64],
                                actT[:, kc * 512:(kc + 1) * 512], start=(kc == 0), stop=(kc == 2)),
                               r=["W_qup"] + ACT(kc), w=[f"P{h}"])
                    for h in range(4):
                        rope(h, S[0:64, (8 + h) * 512:(9 + h) * 512], SR(8 + h))
                    if b == 0 and c == 0:
                        dump("qnT", slot(4, 4), SR(4, 4))
                        dump("qpeT", S[0:64, 8 * 512:12 * 512], SR(8, 4), nparts=64)
                        dump("cosT", cosT, SR(12, 2), nparts=64)
                        dump("sgnT", sgnT, SR(14, 2), nparts=64)
                    ck("s3a")
                    kvb = [4, 5, 7, 4, 5, 7, 4, 5]
                    for h in range(4):
                        kb_ = kvb[h]
                        for kc in range(2):
                            op("pe", lambda e, h=h, kc=kc, kb_=kb_: e.matmul(
                                PS[kb_][:], W_kvup[:, kc * 1024 + h * 128:kc * 1024 + (h + 1) * 128],
                                actT[:, (3 + kc) * 512:(4 + kc) * 512], start=(kc == 0), stop=(kc == 1)),
                               r=["W_kvup"] + ACT(3 + kc), w=[f"P{kb_}"])
                        evac_copy(knT[:, h * SEQ + c * CH:h * SEQ + (c + 1) * CH], PS[kb_][:], [f"P{kb_}"], [f"kn{h}c{c}"], force_act=True)
                    for t in range(4):
                        kb_ = kvb[4 + t]
                        for kc in range(2):
                            op("pe", lambda e, t=t, kc=kc, kb_=kb_: e.matmul(
                                PS[kb_][:], actT[:, (3 + kc) * 512 + t * 128:(3 + kc) * 512 + (t + 1) * 128],
                                W_kvup[:, kc * 1024 + 512:kc * 1024 + 1024], start=(kc == 0), stop=(kc == 1)),
                               r=["W_kvup", f"act{3 + kc}t{t}"], w=[f"P{kb_}"])
                        tile = c * 4 + t
                        evac_copy(vv[:, tile * 512:(tile + 1) * 512], PS[kb_][:], [f"P{kb_}"], [f"v{tile}"], force_act=True)
                    rope(6, kpeT[0:64, c * CH:(c + 1) * CH], [f"kpe{c}"])
                    if b == 0 and c == 0:
                        dump("knT", knT[:, 0:512], ["kn0c0"])
                        dump("kpeT", kpeT[0:64, 0:512], ["kpe0"], nparts=64)
                        dump("vv0", vv[:, 0:512], ["v0"])
                    ck("s3b")
                    for u in range(RING_PF):
                        ring_issue_gu(u)
                    free_banks = [0, 1, 2, 3]

                    def ring_alloc():
                        return free_banks.pop(0)

                    def ring_free(k_):
                        free_banks.append(k_)

                    nkb = 4 * c + 4
                    pendA, pendB = [None], [None]

                    def make_epilogue(mi, h, mm, is_diff, ob, rb):
                        obr, rbr = [f"P{ob}"], [f"P{rb}"]
                        rinv = slotf(16)
                        E1 = slotf(18)
                        E2 = slotf(20)
                        sq = actT[:, 7 * 512:8 * 512]

                        def partA():
                            op("dve", lambda e: e.reciprocal(out=rinv, in_=PS[rb][:]), r=rbr, w=SR(16, 2))
                            if not is_diff:
                                f = 4 + h
                                op("dve", lambda e: e.tensor_tensor(out=actT[:, f * 512:(f + 1) * 512], in0=PS[ob][:],
                                                                    in1=rinv, op=ALU.mult), r=obr + SR(16, 2), w=ACT(f))
                            elif mm == 0:
                                op("dve", lambda e: e.tensor_tensor(out=E1, in0=PS[ob][:], in1=rinv, op=ALU.mult),
                                   r=obr + SR(16, 2), w=SR(18, 2))
                            else:
                                op("dve", lambda e: e.tensor_tensor(out=E2, in0=PS[ob][:], in1=rinv, op=ALU.mult),
                                   r=obr + SR(16, 2), w=SR(20, 2))
                                op("dve", lambda e: e.scalar_tensor_tensor(out=E1, in0=E2, scalar=cols[:, C_NLAM:C_NLAM + 1],
                                                                           in1=E1, op0=ALU.mult, op1=ALU.add),
                                   r=SR(18, 4) + ["cols"], w=SR(18, 2))
                                op("act", lambda e: e.activation(out=sq, in_=E1, func=AF.Square), r=SR(18, 2), w=ACT(7))

                        def partB():
                            if not (is_diff and mm == 1):
                                return
                            ssb = ring_alloc()
                            op("pe", lambda e: e.matmul(PS[ssb][:], ones[:], sq, start=True, stop=True),
                               r=ACT(7) + ["ones"], w=[f"P{ssb}"])
                            op("act", lambda e: e.activation(out=E2, in_=PS[ssb][:], func=AF.Ln, bias=cols[:, 15:16],
                                                             scale=1.0 / 128), r=[f"P{ssb}", "cols"], w=SR(20, 2))
                            ring_free(ssb)
                            op("act", lambda e: e.activation(out=E2, in_=E2, func=AF.Exp, scale=-0.5), r=SR(20, 2), w=SR(20, 2))
                            op("dve", lambda e: e.scalar_tensor_tensor(
                                out=actT[:, h * 512:(h + 1) * 512], in0=E1, scalar=cols[:, C_GDP:C_GDP + 1], in1=E2,
                                op0=ALU.mult, op1=ALU.mult), r=SR(18, 4) + ["cols"], w=ACT(h))

                        return partA, partB

                    for mi in range(12):
                        is_diff = mi < 8
                        if is_diff:
                            h, mm = mi // 2, mi % 2
                        else:
                            h, mm = mi - 8, 0
                        ob, rb = 4 + 2 * (mi % 2), 5 + 2 * (mi % 2)
                        obr, rbr = [f"P{ob}"], [f"P{rb}"]

                        def qk(kb, sbk):
                            qlo = max(0, kb - 4 * c) * 128
                            kc_ = kb // 4
                            if is_diff:
                                pr = slice(mm * 64, mm * 64 + 64)
                                op("pe", lambda e: e.matmul(
                                    PS[sbk][:, qlo:512], dkT[pr, h * SEQ + kb * 128:h * SEQ + kb * 128 + 128],
                                    S[pr, h * 512 + qlo:h * 512 + 512], start=True, stop=True),
                                   r=[f"dk{h}c{kc_}"] + SR(h), w=[f"P{sbk}"])
                            else:
                                op("pe", lambda e: e.matmul(
                                    PS[sbk][:, qlo:512], knT[:, h * SEQ + kb * 128:h * SEQ + kb * 128 + 128],
                                    S[:, (4 + h) * 512 + qlo:(4 + h) * 512 + 512], start=True, stop=False),
                                   r=[f"kn{h}c{kc_}"] + SR(4 + h), w=[f"P{sbk}"])
                                op("pe", lambda e: e.matmul(
                                    PS[sbk][:, qlo:512], kpeT[0:64, kb * 128:kb * 128 + 128],
                                    S[0:64, (8 + h) * 512 + qlo:(8 + h) * 512 + 512], start=False, stop=True),
                                   r=[f"kpe{kc_}"] + SR(8 + h), w=[f"P{sbk}"])

                        bank_of = {}
                        for kb0 in range(min(QK_PF, nkb)):
                            bank_of[kb0] = ring_alloc()
                            qk(kb0, bank_of[kb0])
                        for kb in range(nkb):
                            sbk = bank_of[kb]
                            if kb + QK_PF < nkb:
                                bank_of[kb + QK_PF] = ring_alloc()
                                qk(kb + QK_PF, bank_of[kb + QK_PF])
                            qb0 = max(0, kb - 4 * c)
                            qlo = qb0 * 128
                            pt = S[:, (12 + sbk) * 512:(13 + sbk) * 512]
                            ptr = SR(12 + sbk)
                            if is_diff:
                                groups = [(0, 1), (2, 3)] if h == 0 else [(0, 1, 2, 3)]
                                for g in groups:
                                    if g[-1] < qb0:
                                        continue
                                    lo, hi = max(g[0], qb0) * 128, (g[-1] + 1) * 128
                                    delta = (4 * c + g[0] + 1 - kb) if h == 0 else (4 * c + 3 - kb)
                                    op("act", lambda e, lo=lo, hi=hi, delta=delta: e.activation(
                                        out=pt[:, lo:hi], in_=PS[sbk][:, lo:hi], func=AF.Exp,
                                        bias=alibi[:, h * 16 + delta:h * 16 + delta + 1], scale=QSCALE_D),
                                       r=[f"P{sbk}", "alibi"], w=ptr)
                            else:
                                op("act", lambda e: e.activation(out=pt[:, qlo:512], in_=PS[sbk][:, qlo:512], func=AF.Exp,
                                                                 scale=QSCALE_M), r=[f"P{sbk}"], w=ptr)
                            ring_free(sbk)
                            if kb >= 4 * c:
                                op("pool", lambda e: e.tensor_tensor(out=pt[:, qlo:qlo + 128], in0=pt[:, qlo:qlo + 128],
                                                                     in1=tri[:], op=ALU.mult), r=ptr + ["tri"], w=ptr)
                            vsrc = dv if is_diff else vv
                            vres = f"dv{kb}" if is_diff else f"v{kb}"
                            op("pe", lambda e: e.matmul(
                                PS[ob][:, qlo:512], vsrc[:, kb * 512 + h * 128:kb * 512 + (h + 1) * 128], pt[:, qlo:512],
                                start=(kb == 0), stop=(kb == nkb - 1)), r=ptr + [vres], w=obr)
                            op("pe", lambda e: e.matmul(
                                PS[rb][:, qlo:512], ones[:], pt[:, qlo:512], start=(kb == 0), stop=(kb == nkb - 1)),
                               r=ptr + ["ones"], w=rbr)
                            if kb == 0 and pendA[0] is not None:
                                pendA[0]()
                                pendA[0] = None
                            if kb == min(3, nkb - 1) and pendB[0] is not None:
                                pendB[0]()
                                pendB[0] = None
                        pendA[0], pendB[0] = make_epilogue(mi, h, mm, is_diff, ob, rb)
                    pendA[0]()
                    pendB[0]()

                    if b == 0 and c == 0:
                        dump("mergedT", actT[:], ACT_ALL)
                    ck("s4")
                    for t in range(4):
                        for half in range(2):
                            k = t * 2 + half
                            for f in range(8):
                                op("pe", lambda e, f=f: e.matmul(
                                    PS[k][:], actT[:, f * 512 + t * 128:f * 512 + (t + 1) * 128],
                                    W_out[:, f * 1024 + half * 512:f * 1024 + (half + 1) * 512],
                                    start=(f == 0), stop=(f == 7)), r=[f"W_out{f}", f"act{f}t{t}"], w=[f"P{k}"])
                            xs = xc[:, t * 1024 + half * 512:t * 1024 + (half + 1) * 512]
                            op("dve", lambda e: e.tensor_tensor(out=xs, in0=PS[k][:], in1=xs, op=ALU.add),
                               r=[f"P{k}", f"xc{t}h{half}"], w=[f"xc{t}h{half}"])
                        norm_sq(t, 0)
                        norm_rstd(t, 1, 0)
                        norm_scale(t, 0)
                    for t in range(4):
                        norm_tr(b, 1, t)
                    if b == 0 and c == 0:
                        dump("x1", xc[:], XC_ALL)
                    ck("s5")
                    if b == 0 and c == 0:
                        dump("h2T", actT[:], ACT_ALL)
                    ck("s6")
                    for j in range(NJ):
                        rs_ = j % 4
                        rres = [f"R{rs_}a", f"R{rs_}b"]
                        ru = ring[:, rs_ * 2048:(rs_ + 1) * 2048]
                        pg, pu = 2 + 2 * (j % 3), 3 + 2 * (j % 3)
                        for kc in range(8):
                            op("pe", lambda e, pg=pg, ru=ru, kc=kc: e.matmul(
                                PS[pg][:], ru[:, kc * 128:(kc + 1) * 128], actT[:, kc * 512:(kc + 1) * 512],
                                start=(kc == 0), stop=(kc == 7)), r=rres + ACT(kc), w=[f"P{pg}"])
                        for kc in range(8):
                            op("pe", lambda e, pu=pu, ru=ru, kc=kc: e.matmul(
                                PS[pu][:], ru[:, 1024 + kc * 128:1024 + (kc + 1) * 128], actT[:, kc * 512:(kc + 1) * 512],
                                start=(kc == 0), stop=(kc == 7)), r=rres + ACT(kc), w=[f"P{pu}"])
                        sg = sgt[:, (j % 2) * 512:(j % 2 + 1) * 512]
                        op("act", lambda e, pg=pg, sg=sg: e.activation(out=sg, in_=PS[pg][:], func=AF.Silu),
                           r=[f"P{pg}"], w=[f"sgt{j % 2}"])
                        op("dve", lambda e, pu=pu, sg=sg, j=j: e.tensor_tensor(out=slot(j), in0=PS[pu][:], in1=sg, op=ALU.mult),
                           r=[f"P{pu}", f"sgt{j % 2}"], w=SR(j))
                        if j + RING_PF < NJ:
                            ring_issue_gu(j + RING_PF)
                        else:
                            ring_issue_dn(2 * rs_)
                            ring_issue_dn(2 * rs_ + 1)
                    if b == 0 and c == 0:
                        dump("aT", S[:], S_ALL)
                    ck("s7a")
                    for j in range(NJ):
                        hs = j % 8
                        rres = [f"R{hs // 2}{'ab'[hs % 2]}"]
                        ru = ring[:, hs * 1024:(hs + 1) * 1024]
                        op("dve", lambda e, ru=ru: e.tensor_tensor(out=ru, in0=ru, in1=Gf[:], op=ALU.mult),
                           r=rres + ["Gf"], w=rres)
                        for t in range(4):
                            for half in range(2):
                                k = t * 2 + half
                                op("pe", lambda e, k=k, t=t, half=half, j=j, ru=ru: e.matmul(
                                    PS[k][:], S[:, j * 512 + t * 128:j * 512 + (t + 1) * 128],
                                    ru[:, half * 512:(half + 1) * 512], start=(j == 0), stop=(j == NJ - 1)),
                                   r=rres + SR(j), w=[f"P{k}"])
                        ring_issue_dn(j + 8)
                    ck("s7b")
                    junk = S[:, 20 * 512:22 * 512]
                    stags = []
                    for t in range(4):
                        if t < 3:
                            stags.append((Sf[:, (8 + 4 * t) * 256:(8 + 4 * t) * 256 + 1024], SR(8 + 4 * t, 4)))
                        else:
                            stags.append((sgt[:, 0:1024], ["sgt0", "sgt1"]))
                    for t in range(4):
                        xr = [f"xc{t}h0", f"xc{t}h1"]
                        xt = xc[:, t * 1024:(t + 1) * 1024]
                        stag, stres = stags[t]
                        for half in range(2):
                            k = t * 2 + half
                            xs = xc[:, t * 1024 + half * 512:t * 1024 + (half + 1) * 512]
                            op("dve", lambda e: e.tensor_tensor(out=xs, in0=PS[k][:], in1=xs, op=ALU.add),
                               r=[f"P{k}", f"xc{t}h{half}"], w=[f"xc{t}h{half}"])
                        sres = [f"st{8 + t}"]
                        op("dve", lambda e: e.memset(st[:, 8 + t:9 + t], 0.0), w=sres)
                        op("act", lambda e: e.activation(out=junk, in_=xt, func=AF.Square, accum_out=st[:, 8 + t:9 + t]),
                           r=xr + sres, w=SR(20, 2) + sres)
                        op("act", lambda e: e.activation(out=st[:, 12 + t:13 + t], in_=st[:, 8 + t:9 + t], func=AF.Ln,
                                                         bias=cols[:, 15:16], scale=1.0 / D), r=sres + ["cols"], w=sres)
                        op("act", lambda e: e.activation(out=st[:, 12 + t:13 + t], in_=st[:, 12 + t:13 + t], func=AF.Exp,
                                                         scale=-0.5), r=sres, w=sres)
                    for t in range(4):
                        xr = [f"xc{t}h0", f"xc{t}h1"]
                        xt = xc[:, t * 1024:(t + 1) * 1024]
                        stag, stres = stags[t]
                        sres = [f"st{8 + t}"]
                        op("dve", lambda e: e.scalar_tensor_tensor(out=stag, in0=xt, scalar=st[:, 12 + t:13 + t], in1=gfin[:],
                                                                   op0=ALU.mult, op1=ALU.mult),
                           r=xr + sres + ["gfin"], w=stres)
                        pending_stores.append((stag, stres, t, tok0))

            flush_stores()

        nops, nwait = P.run(body)
        print(f"[kernel] ops={nops} waits={nwait}", flush=True)
    return nc


_NC_CACHE = {}


def _prep_shared(inp):
    f = np.float32
    w_qup = inp["w_q_up"][0]
    qcols = [h * 192 + i for h in range(4) for i in range(128)] + [h * 192 + 128 + i for h in range(4) for i in range(64)]
    w_kvup = inp["w_kv_up"][0]
    kvcols = [h * 256 + i for h in range(4) for i in range(128)] + [h * 256 + 128 + i for h in range(4) for i in range(128)]
    wg = inp["w_ffn_gate"][0].reshape(8, 128, NJ, 128)
    wu = inp["w_ffn_up"][0].reshape(8, 128, NJ, 128)
    wgu = np.stack([wg.transpose(2, 1, 0, 3), wu.transpose(2, 1, 0, 3)], axis=2)
    b_ada = inp["b_ada"][0]
    shared = {
        "w_ada": np.ascontiguousarray(inp["w_ada"][0], dtype=f),
        "b_adaT": np.ascontiguousarray(b_ada.reshape(48, 128).T, dtype=f),
        "b_gate": np.ascontiguousarray(np.concatenate([b_ada[2048:3072], b_ada[5120:6144]])[None, :], dtype=f),
        "g_mixT": np.ascontiguousarray(inp["g_mix"][0].reshape(8, 128).T, dtype=f),
        "g_ffnT": np.ascontiguousarray(inp["g_ffn"][0].reshape(8, 128).T, dtype=f),
        "w_in": np.ascontiguousarray(inp["w_in"][0], dtype=f),
        "lam": np.ascontiguousarray(np.concatenate([inp["lambda_q1"][0], inp["lambda_k1"][0],
                                                    inp["lambda_q2"][0], inp["lambda_k2"][0]])[None, :], dtype=f),
        "g_diff": np.ascontiguousarray(inp["g_diff_out"][0].reshape(128, 1), dtype=f),
        "g_qlT": np.ascontiguousarray(inp["g_q_lat"][0].reshape(3, 128).T, dtype=f),
        "w_qup": np.ascontiguousarray(w_qup[:, qcols], dtype=f),
        "g_kvlT": np.ascontiguousarray(inp["g_kv_lat"][0].reshape(2, 128).T, dtype=f),
        "w_kvup": np.ascontiguousarray(w_kvup[:, kvcols], dtype=f),
        "w_out": np.ascontiguousarray(inp["w_out"][0], dtype=f),
        "w_gu": np.ascontiguousarray(wgu.reshape(NJ, 128, 2048), dtype=f),
        "w_dn": np.ascontiguousarray(inp["w_ffn_down"][0], dtype=f),
        "g_final": np.ascontiguousarray(inp["g_final"].reshape(1, D), dtype=f),
    }
    return shared


def kernel(**inputs):
    inp = {k: np.asarray(v) for k, v in inputs.items()}
    if "nc" not in _NC_CACHE:
        _NC_CACHE["nc"] = build()
    nc = _NC_CACHE["nc"]
    shared = _prep_shared(inp)
    x = inp["x"].astype(np.float32, copy=False)
    c = inp["c"].astype(np.float32, copy=False)
    in_maps = []
    for core in range(NCORES):
        m = dict(shared)
        m["x"] = np.ascontiguousarray(x[core * BPC:(core + 1) * BPC].reshape(BPC * SEQ, D))
        cc = c[core * BPC:(core + 1) * BPC]
        m["cT"] = np.ascontiguousarray(cc.reshape(BPC, 8, 128).transpose(2, 1, 0).reshape(128, 8 * BPC))
        in_maps.append(m)
    res = run_bass_kernel_spmd(nc, in_maps, core_ids=list(range(NCORES)))
    if DBG:
        LAST_DBG["dbg"] = np.asarray(res.results[0]["dbg"])
        LAST_DBG["map"] = dict(DBG_MAP)
    outs = [np.asarray(r["out"]).reshape(BPC, SEQ, D) for r in res.results]
    return np.concatenate(outs, axis=0).astype(np.float32, copy=False)
```

```python
import contextlib
import math
import numpy as np
import concourse.bass as bass
import concourse.mybir as mybir
from concourse.bass_utils import run_bass_kernel_spmd

F32 = mybir.dt.float32
BF16 = mybir.dt.bfloat16
AF = mybir.ActivationFunctionType
ALU = mybir.AluOpType

NCORES = 8
SEQ = 2048
D = 1024
BPC = 4
CH = 512
NCH = SEQ // CH
DFF = 2816
NJ = DFF // 128
INW = 2240
EPS = 1e-6
LAMBDA_INIT = 0.8 - 0.6 * math.exp(-0.3 * 0)
SLOPES = [2.0 ** (-8.0 * (i + 1) / 4) for i in range(4)]
SAME_ENGINE_SYNC = True
import os
STOP = os.environ.get("KSTOP", "")
DBG = bool(os.environ.get("KDBG", ""))
DBGW = 48000
DBG_MAP = {}
LAST_DBG = {}


class _Stop(Exception):
    pass


def ck(label):
    if STOP == label:
        raise _Stop()


class Prog:
    def __init__(self, nc, es):
        self.nc = nc
        self.es = es
        self.eng = {"pe": nc.tensor, "act": nc.scalar, "dve": nc.vector, "pool": nc.gpsimd, "sp": nc.sync}
        self.esem = {k: es.enter_context(nc.semaphore("sem_" + k)) for k in self.eng}
        self.dsem = {}
        self.meta = []
        self.mode = "analyze"
        self.idx = 0
        self.nwait = 0

    def op(self, eng, fn, r=(), w=(), dma=None):
        if self.mode == "analyze":
            self.meta.append((eng, tuple(r), tuple(w), dma))
            return
        i = self.idx
        self.idx += 1
        e = self.eng[eng]
        need = {}
        for j in self.deps[i]:
            key, val = self.tick[j]
            if need.get(key, 0) < val:
                need[key] = val
        wd = self.waited[eng]
        for key, val in need.items():
            if wd.get(key, 0) >= val:
                continue
            wd[key] = val
            sem = self.dsem[key[1]] if key[0] == "d" else self.esem[key[1]]
            e.wait_ge(sem, val)
            self.nwait += 1
        inst = fn(e)
        if dma is not None:
            inst.then_inc(self.dsem[dma], 16)
        elif i in self.signaled:
            inst.then_inc(self.esem[eng], 1)

    def analyze(self):
        ops = self.meta
        last_w = {}
        readers = {}
        last_dma = {}
        deps = []
        for i, (eng, r, w, dma) in enumerate(ops):
            d = set()
            for x in r:
                if x in last_w:
                    d.add(last_w[x])
                if x[0] == "P" and x[1:].isdigit():
                    rd = readers.get(x)
                    if rd:
                        d.update(j for k_, j in rd.items() if k_ != eng)
            for x in w:
                if x in last_w:
                    d.add(last_w[x])
                rd = readers.get(x)
                if rd:
                    d.update(rd.values())
            if dma is not None and dma in last_dma:
                d.add(last_dma[dma])
            d.discard(i)
            for x in w:
                last_w[x] = i
                readers[x] = {}
            for x in r:
                if x in w:
                    continue
                key = eng if dma is None else ("dma", i)
                readers.setdefault(x, {})[key] = i
            if dma is not None:
                last_dma[dma] = i
            kept = []
            for j in d:
                ej, _, _, dj = ops[j]
                if dj is None and dma is None and ej == eng and (eng == "pe" or not SAME_ENGINE_SYNC):
                    continue
                kept.append(j)
            deps.append(kept)
        signaled = set()
        for kept in deps:
            for j in kept:
                if ops[j][3] is None:
                    signaled.add(j)
        tick = {}
        cnt = {k: 0 for k in self.eng}
        dcnt = {}
        for i, (eng, r, w, dma) in enumerate(ops):
            if dma is not None:
                if dma not in self.dsem:
                    self.dsem[dma] = self.es.enter_context(self.nc.semaphore("d_" + dma))
                dcnt[dma] = dcnt.get(dma, 0) + 16
                tick[i] = (("d", dma), dcnt[dma])
            elif i in signaled:
                cnt[eng] += 1
                tick[i] = (("e", eng), cnt[eng])
        self.deps, self.signaled, self.tick, self.dcnt = deps, signaled, tick, dcnt
        self.waited = {k: {} for k in self.eng}

    def run(self, body):
        self.mode = "analyze"
        try:
            body()
        except _Stop:
            pass
        self.analyze()
        self.mode = "emit"
        self.idx = 0
        try:
            body()
        except _Stop:
            pass
        assert self.idx == len(self.meta), (self.idx, len(self.meta))
        sp = self.eng["sp"]
        for name, sem in self.dsem.items():
            if self.waited["sp"].get(("d", name), 0) < self.dcnt[name]:
                sp.wait_ge(sem, self.dcnt[name])
        return len(self.meta), self.nwait


def build():
    nc = bass.Bass("TRN2", target_bir_lowering=False)

    def dram(name, shape, dt=F32, kind="ExternalInput"):
        return nc.dram_tensor(name, list(shape), dt, kind=kind).ap()

    x_d = dram("x", [BPC * SEQ, D])
    out_d = dram("out", [BPC * SEQ, D], kind="ExternalOutput")
    cT_d = dram("cT", [128, 8 * BPC])
    wada_d = dram("w_ada", [D, 6 * D])
    badaT_d = dram("b_adaT", [128, 48])
    bgate_d = dram("b_gate", [1, 2 * D])
    gmixT_d = dram("g_mixT", [128, 8])
    gffnT_d = dram("g_ffnT", [128, 8])
    win_d = dram("w_in", [D, INW])
    lam_d = dram("lam", [1, 256])
    gdiff_d = dram("g_diff", [128, 1])
    gqlT_d = dram("g_qlT", [128, 3])
    wqup_d = dram("w_qup", [384, 768])
    gkvlT_d = dram("g_kvlT", [128, 2])
    wkvup_d = dram("w_kvup", [256, 1024])
    wout_d = dram("w_out", [D, D])
    wgu_d = dram("w_gu", [NJ, 128, 2048])
    wdn_d = dram("w_dn", [DFF, D])
    gfin_d = dram("g_final", [1, D])
    gates_d = dram("gates_scr", [BPC, 2 * D], kind="Internal")
    rope_d = dram("rope_scr", [2, 64, SEQ], kind="Internal")
    wgu_s = dram("wgu_bf16_scr", [NJ, 128, 2048], BF16, kind="Internal")
    wdn_s = dram("wdn_bf16_scr", [DFF, D], BF16, kind="Internal")
    dbg_d = dram("dbg", [128, DBGW], kind="ExternalOutput") if DBG else None

    es = contextlib.ExitStack()
    with es:
        def sb(name, shape, dt):
            return es.enter_context(nc.sbuf_tensor(name, list(shape), dt))

        W_in = sb("W_in", [128, 8 * INW], BF16)
        W_qup = sb("W_qup", [128, 3 * 768], BF16)
        W_kvup = sb("W_kvup", [128, 2 * 1024], BF16)
        W_out = sb("W_out", [128, 8 * 1024], BF16)
        dkT = sb("dkT", [128, 4 * SEQ], BF16)
        dv = sb("dv", [128, 16 * 512], BF16)
        knT = sb("knT", [128, 4 * SEQ], BF16)
        kpeT = sb("kpeT", [128, SEQ], BF16)
        vv = sb("vv", [128, 16 * 512], BF16)
        xc = sb("xc", [128, 4 * 1024], F32)
        actT = sb("actT", [128, 8 * 512], BF16)
        S = sb("S", [128, NJ * 512], BF16)
        ring = sb("ring", [128, 4 * 2048], BF16)
        Ga = sb("Ga", [128, 1024], F32)
        Gf = sb("Gf", [128, 1024], F32)
        gfin = sb("gfin", [128, 1024], F32)
        sgt = sb("sgt", [128, 2 * 512], F32)
        ident = sb("ident", [128, 128], BF16)
        ones = sb("ones", [128, 128], BF16)
        tri = sb("tri", [128, 128], BF16)
        alibi = sb("alibi", [128, 64], F32)
        AB = sb("AB", [128, BPC * 2 * 2 * 8], F32)
        st = sb("st", [128, 16], F32)
        cols = sb("cols", [128, 32], F32)
        PS = [es.enter_context(nc.psum_tensor(f"ps{k}", [128, 512], F32)) for k in range(8)]

        P = Prog(nc, es)
        op = P.op

        def body():
            Sf = S[:].bitcast(F32)
            modT = sgt[:, 256:256 + 48 * BPC]
            xcb = xc[:].bitcast(BF16)

            def slot(j, n=1):
                return S[:, j * 512:(j + n) * 512]

            def slotf(j):
                return Sf[:, j * 256:j * 256 + 512]

            def SR(j, n=1):
                return [f"S{k}" for k in range(j, j + n)]

            dbg_off = [0]

            def dump(label, ap, res, nparts=128):
                if not DBG or label in DBG_MAP and P.mode == "analyze":
                    return
                wdt = ap.shape[-1]
                if P.mode == "analyze":
                    DBG_MAP[label] = (dbg_off[0], wdt, nparts)
                o = DBG_MAP[label][0]
                dbg_off[0] = o + wdt
                op("pool", lambda e: e.dma_start(out=dbg_d[0:nparts, o:o + wdt], in_=ap), r=res, w=["dbg_" + label],
                   dma="dbg")

            XC_ALL = [f"xc{t}h{h}" for t in range(4) for h in range(2)]
            S_ALL = SR(0, NJ)
            ACT_ALL = [f"act{k}t{t}" for k in range(8) for t in range(4)]

            def ACT(k):
                return [f"act{k}t{t}" for t in range(4)]

            def AB_col(b, af, ab, kc):
                o = ((b * 2 + af) * 2 + ab) * 8 + kc
                return AB[:, o:o + 1]

            C_NLAM, C_GDP, C_GD, C_GQ, C_GKV = 0, 1, 2, 3, 6

            tmpf = Sf[:, 0:128]
            op("pool", lambda e: e.memset(tmpf, 1.0), w=SR(0))
            op("pool", lambda e: e.affine_select(out=tmpf, in_=tmpf, pattern=[[-1, 128]], compare_op=ALU.is_equal,
                                                 fill=0.0, base=0, channel_multiplier=1), r=SR(0), w=SR(0))
            op("dve", lambda e: e.tensor_copy(out=ident[:], in_=tmpf), r=SR(0), w=["ident"])
            tmpf2 = Sf[:, 256:384]
            op("pool", lambda e: e.memset(tmpf2, 1.0), w=SR(1))
            op("pool", lambda e: e.affine_select(out=tmpf2, in_=tmpf2, pattern=[[1, 128]], compare_op=ALU.is_ge,
                                                 fill=0.0, base=0, channel_multiplier=-1), r=SR(1), w=SR(1))
            op("dve", lambda e: e.tensor_copy(out=tri[:], in_=tmpf2), r=SR(1), w=["tri"])
            op("dve", lambda e: e.memset(ones[:], 1.0), w=["ones"])
            op("dve", lambda e: e.memset(cols[:, 15:16], EPS), w=["cols"])
            tmpa = Sf[:, 512:528]
            op("pool", lambda e: e.iota(tmpa, pattern=[[-128, 16]], base=-127, channel_multiplier=1,
                                        allow_small_or_imprecise_dtypes=True), w=SR(2))
            tmpa0 = Sf[:, 768:784]
            op("pool", lambda e: e.iota(tmpa0, pattern=[[-128, 16]], base=0, channel_multiplier=1,
                                        allow_small_or_imprecise_dtypes=True), w=SR(3))
            op("dve", lambda e: e.tensor_scalar(out=alibi[:, 0:16], in0=tmpa0, scalar1=SLOPES[0], scalar2=None,
                                                op0=ALU.mult), r=SR(3), w=["alibi"])
            for h in range(1, 4):
                op("dve", lambda e, h=h: e.tensor_scalar(out=alibi[:, h * 16:(h + 1) * 16], in0=tmpa, scalar1=SLOPES[h],
                                                         scalar2=None, op0=ALU.mult), r=SR(2), w=["alibi"])

            ck("consts")
            win_v = win_d.rearrange("(kc p) n -> p kc n", p=128)
            for kc in range(8):
                op("pool", lambda e, kc=kc: e.dma_start(out=W_in[:, kc * INW:(kc + 1) * INW], in_=win_v[:, kc, :]),
                   w=["W_in"], dma=f"w{kc % 2}")
            wq_v = wqup_d.rearrange("(kc p) n -> p kc n", p=128)
            for kc in range(3):
                op("pool", lambda e, kc=kc: e.dma_start(out=W_qup[:, kc * 768:(kc + 1) * 768], in_=wq_v[:, kc, :]),
                   w=["W_qup"], dma="w2")
            wkv_v = wkvup_d.rearrange("(kc p) n -> p kc n", p=128)
            for kc in range(2):
                op("pool", lambda e, kc=kc: e.dma_start(out=W_kvup[:, kc * 1024:(kc + 1) * 1024], in_=wkv_v[:, kc, :]),
                   w=["W_kvup"], dma="w3")
            wo_v = wout_d.rearrange("(kc p) n -> p kc n", p=128)

            ck("weights")
            cTs = Sf[:, 1024:1024 + 8 * BPC]
            op("sp", lambda e: e.dma_start(out=cTs, in_=cT_d[:, :]), w=SR(4), dma="p0")
            op("sp", lambda e: e.dma_start(out=cols[:, 16:24], in_=gmixT_d[:, :]), w=["cols"], dma="p1")
            op("sp", lambda e: e.dma_start(out=cols[:, 24:32], in_=gffnT_d[:, :]), w=["cols"], dma="p1")
            op("sp", lambda e: e.dma_start(out=cols[:, C_GD:C_GD + 1], in_=gdiff_d[:, :]), w=["cols"], dma="p1")
            op("sp", lambda e: e.dma_start(out=cols[:, C_GQ:C_GQ + 3], in_=gqlT_d[:, :]), w=["cols"], dma="p1")
            op("sp", lambda e: e.dma_start(out=cols[:, C_GKV:C_GKV + 2], in_=gkvlT_d[:, :]), w=["cols"], dma="p1")
            badaT = Sf[:, 1280:1280 + 48]
            op("sp", lambda e: e.dma_start(out=badaT, in_=badaT_d[:, :]), w=SR(5), dma="p2")
            op("sp", lambda e: e.dma_start(out=gfin[:], in_=gfin_d[:, :].partition_broadcast(128)), w=["gfin"], dma="p3")
            lamb = Sf[:, 1536:1536 + 256]
            op("sp", lambda e: e.dma_start(out=lamb, in_=lam_d[:, :].partition_broadcast(128)), w=SR(6), dma="p4")
            bg = Sf[0:BPC, 2048:2048 + 2048]
            op("sp", lambda e: e.dma_start(out=bg, in_=bgate_d[:, :].partition_broadcast(BPC)), w=SR(8, 8), dma="p5")

            ck("params")
            lprod = Sf[:, 1792:1792 + 128]
            op("dve", lambda e: e.tensor_tensor(out=lprod[:, 0:64], in0=lamb[:, 0:64], in1=lamb[:, 64:128], op=ALU.mult),
               r=SR(6), w=SR(7))
            op("dve", lambda e: e.tensor_tensor(out=lprod[:, 64:128], in0=lamb[:, 128:192], in1=lamb[:, 192:256], op=ALU.mult),
               r=SR(6), w=SR(7))
            op("dve", lambda e: e.reduce_sum(out=cols[:, 8:9], in_=lprod[:, 0:64], axis=mybir.AxisListType.X), r=SR(7), w=["cols"])
            op("dve", lambda e: e.reduce_sum(out=cols[:, 9:10], in_=lprod[:, 64:128], axis=mybir.AxisListType.X), r=SR(7), w=["cols"])
            op("act", lambda e: e.activation(out=cols[:, 10:12], in_=cols[:, 8:10], func=AF.Exp), r=["cols"], w=["cols"])
            op("dve", lambda e: e.tensor_tensor(out=cols[:, C_NLAM:C_NLAM + 1], in0=cols[:, 11:12], in1=cols[:, 10:11],
                                                op=ALU.subtract), r=["cols"], w=["cols"])
            op("dve", lambda e: e.tensor_scalar(out=cols[:, C_NLAM:C_NLAM + 1], in0=cols[:, C_NLAM:C_NLAM + 1],
                                                scalar1=-LAMBDA_INIT, scalar2=None, op0=ALU.add), r=["cols"], w=["cols"])
            op("dve", lambda e: e.tensor_scalar(out=cols[:, C_GDP:C_GDP + 1], in0=cols[:, C_GD:C_GD + 1],
                                                scalar1=1.0 - LAMBDA_INIT, scalar2=None, op0=ALU.mult), r=["cols"], w=["cols"])

            ck("lam")
            scb = sgt[:].bitcast(BF16)[:, 0:8 * BPC]
            SCB = ["sgt0", "sgt1"]
            op("act", lambda e: e.activation(out=scb, in_=cTs, func=AF.Silu), r=SR(4), w=SCB)
            wada_v = wada_d.rearrange("(kc p) n -> p kc n", p=128)
            gsb = actT[:].bitcast(F32)[0:BPC, 0:2048]
            bank = 0
            for u in range(12):
                sl = u % 2
                unit = xcb[:, sl * 4096:(sl + 1) * 4096]
                ures = XC_ALL[sl * 4:(sl + 1) * 4]
                op("pool", lambda e, unit=unit, u=u: e.dma_start(
                    out=unit.rearrange("p (kc n) -> p kc n", kc=8), in_=wada_v[:, :, u * 512:(u + 1) * 512]),
                   w=ures, dma=f"ada{sl}")
                if u in (4, 5, 10, 11):
                    g = 0 if u < 6 else 1
                    half = u % 2 if u < 6 else (u - 10)
                    pb = PS[bank % 8]
                    pres = [f"P{bank % 8}"]
                    bank += 1
                    for kc in range(8):
                        op("pe", lambda e, pb=pb, unit=unit, kc=kc: e.matmul(
                            pb[0:BPC, :], scb[:, kc * BPC:(kc + 1) * BPC], unit[:, kc * 512:(kc + 1) * 512],
                            start=(kc == 0), stop=(kc == 7)), r=ures + SCB, w=pres)
                    o = g * 1024 + half * 512
                    op("dve", lambda e, pb=pb, o=o: e.tensor_tensor(out=gsb[:, o:o + 512], in0=pb[0:BPC, :],
                                                                    in1=bg[:, o:o + 512], op=ALU.add),
                       r=pres + SR(8, 8), w=ACT_ALL)
                else:
                    for mm in range(4):
                        m = 4 * u + mm
                        pb = PS[bank % 8]
                        pres = [f"P{bank % 8}"]
                        bank += 1
                        for kc in range(8):
                            op("pe", lambda e, pb=pb, unit=unit, kc=kc, mm=mm: e.matmul(
                                pb[:, 0:BPC], unit[:, kc * 512 + mm * 128:kc * 512 + (mm + 1) * 128],
                                scb[:, kc * BPC:(kc + 1) * BPC], start=(kc == 0), stop=(kc == 7)),
                               r=ures + SCB, w=pres)
                        op("dve", lambda e, pb=pb, m=m: e.tensor_scalar(
                            out=modT[:, m * BPC:(m + 1) * BPC], in0=pb[:, 0:BPC], scalar1=badaT[:, m:m + 1], scalar2=None,
                            op0=ALU.add), r=pres + SR(5), w=["sgt0"])
            op("sp", lambda e: e.dma_start(out=gates_d[:, :], in_=gsb), r=ACT_ALL, w=["gates_d"], dma="p6")
            modT3 = modT.rearrange("p (m b) -> p m b", b=BPC)
            for b in range(BPC):
                for af, (m_sh, m_sc, gcol) in enumerate([(0, 8, 16), (24, 32, 24)]):
                    oA = ((b * 2 + af) * 2 + 0) * 8
                    oB = ((b * 2 + af) * 2 + 1) * 8
                    op("dve", lambda e, b=b, m_sc=m_sc, gcol=gcol, oA=oA: e.scalar_tensor_tensor(
                        out=AB[:, oA:oA + 8], in0=modT3[:, m_sc:m_sc + 8, b], scalar=1.0, in1=cols[:, gcol:gcol + 8],
                        op0=ALU.add, op1=ALU.mult), r=["sgt0", "cols"], w=["AB"])
                    op("dve", lambda e, b=b, m_sh=m_sh, oB=oB: e.tensor_copy(out=AB[:, oB:oB + 8],
                                                                             in_=modT3[:, m_sh:m_sh + 8, b]),
                       r=["sgt0"], w=["AB"])

            ck("ada")
            xcf = xc[:]
            pos = xcf[0:64, 0:2048]
            ang = xcf[0:64, 2048:4096]
            op("pool", lambda e: e.iota(pos, pattern=[[1, 2048]], base=0, channel_multiplier=0,
                                        allow_small_or_imprecise_dtypes=True), w=XC_ALL)
            op("pool", lambda e: e.iota(cols[0:32, 12:13], pattern=[[0, 1]], base=0, channel_multiplier=1,
                                        allow_small_or_imprecise_dtypes=True), w=["cols"])
            op("pool", lambda e: e.iota(cols[32:64, 12:13], pattern=[[0, 1]], base=0, channel_multiplier=1,
                                        allow_small_or_imprecise_dtypes=True), w=["cols"])
            op("act", lambda e: e.activation(out=cols[0:64, 13:14], in_=cols[0:64, 12:13], func=AF.Exp,
                                             scale=-math.log(10000.0) / 32.0), r=["cols"], w=["cols"])
            op("dve", lambda e: e.tensor_scalar(out=ang, in0=pos, scalar1=cols[0:64, 13:14], scalar2=None, op0=ALU.mult),
               r=XC_ALL + ["cols"], w=XC_ALL)
            TWO_PI = 2.0 * math.pi
            MAGIC = 12582912.0
            PI_LO = 3.141592
            for which, off in ((0, 0.5 * math.pi), (1, 0.0)):
                dst = Sf[0:64, which * 2048:(which + 1) * 2048]
                op("dve", lambda e, off=off: e.tensor_scalar(out=pos, in0=ang, scalar1=off, scalar2=1.0 / TWO_PI, op0=ALU.add,
                                                             op1=ALU.mult), r=XC_ALL, w=XC_ALL)
                op("dve", lambda e: e.tensor_scalar(out=pos, in0=pos, scalar1=MAGIC, scalar2=None, op0=ALU.add),
                   r=XC_ALL, w=XC_ALL)
                op("dve", lambda e: e.tensor_scalar(out=pos, in0=pos, scalar1=-MAGIC, scalar2=None, op0=ALU.add),
                   r=XC_ALL, w=XC_ALL)
                op("dve", lambda e: e.scalar_tensor_tensor(out=pos, in0=pos, scalar=-TWO_PI, in1=ang, op0=ALU.mult,
                                                           op1=ALU.add), r=XC_ALL, w=XC_ALL)
                op("dve", lambda e, off=off: e.tensor_scalar(out=pos, in0=pos, scalar1=off, scalar2=None, op0=ALU.add),
                   r=XC_ALL, w=XC_ALL)
                op("dve", lambda e: e.tensor_scalar(out=pos, in0=pos, scalar1=-PI_LO, scalar2=PI_LO,
                                                    op0=ALU.max, op1=ALU.min), r=XC_ALL, w=XC_ALL)
                op("act", lambda e, dst=dst: e.activation(out=dst, in_=pos, func=AF.Sin), r=XC_ALL, w=S_ALL)
            sgn_hi = Sf[32:64, 2048:4096]
            op("dve", lambda e: e.tensor_scalar(out=sgn_hi, in0=sgn_hi, scalar1=-1.0, scalar2=None, op0=ALU.mult),
               r=S_ALL, w=S_ALL)
            op("sp", lambda e: e.dma_start(out=rope_d[0], in_=Sf[0:64, 0:2048]), r=S_ALL, w=["rope_d"], dma="p7")
            op("sp", lambda e: e.dma_start(out=rope_d[1], in_=Sf[0:64, 2048:4096]), r=S_ALL, w=["rope_d"], dma="p7")

            ck("rope")
            dump("AB", AB[:], ["AB"])
            dump("modT", modT, ["sgt0"])
            dump("cols", cols[:], ["cols"])
            dump("alibi", alibi[:], ["alibi"])
            dump("tri", tri[:], ["tri"])
            dump("ident", ident[:], ["ident"])
            evac_rr = [0]

            def evac_copy(dst, src, r, w, force_act=False):
                evac_rr[0] += 1
                if force_act or evac_rr[0] % 2 == 0:
                    op("act", lambda e: e.activation(out=dst, in_=src, func=AF.Copy), r=r, w=w)
                else:
                    op("dve", lambda e: e.tensor_copy(out=dst, in_=src), r=r, w=w)

            def hn_buf(t):
                return S[:, t * 1024:(t + 1) * 1024], SR(2 * t, 2)

            def norm_sq(t, col0):
                xr = [f"xc{t}h0", f"xc{t}h1"]
                xt = xc[:, t * 1024:(t + 1) * 1024]
                hn, hres = hn_buf(t)
                sres = [f"st{t}"]
                op("dve", lambda e: e.memset(st[:, col0 + t:col0 + t + 1], 0.0), w=sres)
                op("act", lambda e: e.activation(out=hn, in_=xt, func=AF.Square, accum_out=st[:, col0 + t:col0 + t + 1]),
                   r=xr + sres, w=hres + sres)

            def norm_rstd(t0, n, col0):
                sres = [f"st{t}" for t in range(t0, t0 + n)]
                src_ = st[:, col0 + t0:col0 + t0 + n]
                dst_ = st[:, col0 + 4 + t0:col0 + 4 + t0 + n]
                op("act", lambda e: e.activation(out=dst_, in_=src_, func=AF.Ln, bias=cols[:, 15:16], scale=1.0 / D),
                   r=sres + ["cols"], w=sres)
                op("act", lambda e: e.activation(out=dst_, in_=dst_, func=AF.Exp, scale=-0.5), r=sres, w=sres)

            def norm_scale(t, col0):
                xr = [f"xc{t}h0", f"xc{t}h1"]
                xt = xc[:, t * 1024:(t + 1) * 1024]
                hn, hres = hn_buf(t)
                op("dve", lambda e: e.tensor_scalar(out=hn, in0=xt, scalar1=st[:, col0 + 4 + t:col0 + 5 + t], scalar2=None,
                                                    op0=ALU.mult), r=xr + [f"st{t}"], w=hres)

            def norm_tr(b, af, t):
                hn, hres = hn_buf(t)
                pb = PS[t % 2]
                pbb = pb[:].bitcast(BF16)
                pres = [f"P{t % 2}"]
                for kc in range(8):
                    op("pe", lambda e, kc=kc: e.transpose(pbb[:, kc * 128:(kc + 1) * 128],
                                                          hn[:, kc * 128:(kc + 1) * 128], ident[:]),
                       r=hres + ["ident"], w=pres)
                for kc in range(8):
                    dst = actT[:, kc * 512 + t * 128:kc * 512 + (t + 1) * 128]
                    src_ = pbb[:, kc * 128:(kc + 1) * 128]
                    a_ = AB_col(b, af, 0, kc)
                    b_ = AB_col(b, af, 1, kc)
                    if t % 2 == 0:
                        op("dve", lambda e, dst=dst, src_=src_, a_=a_, b_=b_: e.tensor_scalar(
                            out=dst, in0=src_, scalar1=a_, scalar2=b_, op0=ALU.mult, op1=ALU.add),
                           r=pres + ["AB"], w=[f"act{kc}t{t}"])
                    else:
                        op("act", lambda e, dst=dst, src_=src_, a_=a_, b_=b_: e.activation(
                            out=dst, in_=src_, func=AF.Identity, bias=b_, scale=a_),
                           r=pres + ["AB"], w=[f"act{kc}t{t}"])

            def lat_norm(banks, n, gc0, rstd_slot, dst_act0, inv_n, ssbank):
                for i in range(n):
                    sq = actT[:, (5 + i % 2) * 512:(6 + i % 2) * 512]
                    op("act", lambda e, sq=sq, i=i: e.activation(out=sq, in_=PS[banks[i]][:], func=AF.Square),
                       r=[f"P{banks[i]}"], w=ACT(5 + i % 2))
                    op("pe", lambda e, sq=sq, i=i: e.matmul(PS[ssbank][:], ones[:], sq, start=(i == 0), stop=(i == n - 1)),
                       r=ACT(5 + i % 2) + ["ones"], w=[f"P{ssbank}"])
                rs = slotf(rstd_slot)
                op("act", lambda e: e.activation(out=rs, in_=PS[ssbank][:], func=AF.Ln, bias=cols[:, 15:16], scale=inv_n),
                   r=[f"P{ssbank}", "cols"], w=SR(rstd_slot, 2))
                op("act", lambda e: e.activation(out=rs, in_=rs, func=AF.Exp, scale=-0.5), r=SR(rstd_slot, 2), w=SR(rstd_slot, 2))
                for i in range(n):
                    dst = actT[:, (dst_act0 + i) * 512:(dst_act0 + i + 1) * 512]
                    op("dve", lambda e, dst=dst, i=i: e.scalar_tensor_tensor(
                        out=dst, in0=PS[banks[i]][:], scalar=cols[:, gc0 + i:gc0 + i + 1], in1=rs, op0=ALU.mult,
                        op1=ALU.mult), r=[f"P{banks[i]}", "cols"] + SR(rstd_slot, 2), w=ACT(dst_act0 + i))

            cosT = Sf[0:64, 12 * 256:12 * 256 + 512]
            sgnT = Sf[0:64, 14 * 256:14 * 256 + 512]
            t2 = Sf[0:64, 20 * 256:20 * 256 + 512]

            def rope(pbank, dst, dres):
                pb = PS[pbank]
                pres = [f"P{pbank}"]
                op("dve", lambda e: e.tensor_tensor(out=t2[0:32, :], in0=pb[32:64, :], in1=sgnT[32:64, :], op=ALU.mult),
                   r=pres + SR(14, 2), w=SR(20, 2))
                op("dve", lambda e: e.tensor_tensor(out=t2[32:64, :], in0=pb[0:32, :], in1=sgnT[0:32, :], op=ALU.mult),
                   r=pres + SR(14, 2), w=SR(20, 2))
                op("dve", lambda e: e.tensor_tensor(out=pb[0:64, :], in0=pb[0:64, :], in1=cosT, op=ALU.mult),
                   r=pres + SR(12, 2), w=pres)
                op("dve", lambda e: e.tensor_tensor(out=dst, in0=pb[0:64, :], in1=t2, op=ALU.add),
                   r=pres + SR(20, 2), w=dres)

            ring_ctr = [0]

            def ring_next():
                s = ring_ctr[0] % 4
                ring_ctr[0] += 1
                return s

            RING_PF = 4
            QK_PF = 3
            ring_issued = [0]

            def ring_issue_gu(u):
                if u >= NJ:
                    return
                rs_ = u % 4
                ru = ring[:, rs_ * 2048:(rs_ + 1) * 2048]
                op("pool", lambda e: e.dma_start(out=ru, in_=wgu_s[u]), r=[f"wgu_s{u}"],
                   w=[f"R{rs_}a", f"R{rs_}b"], dma=f"rg{rs_}a")

            def ring_issue_dn(j):
                if j >= NJ:
                    return
                hs = j % 8
                ru = ring[:, hs * 1024:(hs + 1) * 1024]
                nm = f"R{hs // 2}{'ab'[hs % 2]}"
                op("pool", lambda e: e.dma_start(out=ru, in_=wdn_s[j * 128:(j + 1) * 128, :]),
                   r=[f"wdn_s{j}"], w=[nm], dma="rg" + nm[1:])

            pending_stores = []

            def flush_stores():
                for stag_, stres_, t_, tok0_ in pending_stores:
                    op("sp", lambda e, stag_=stag_, t_=t_, tok0_=tok0_: e.dma_start(
                        out=out_d[tok0_ + t_ * 128:tok0_ + (t_ + 1) * 128, :], in_=stag_),
                       r=stres_, w=[f"out{t_}"], dma=f"so{t_}")
                pending_stores.clear()

            QSCALE_D = 64 ** -0.5
            QSCALE_M = 192 ** -0.5

            for b in range(BPC):
                op("sp", lambda e, b=b: e.dma_start(out=Ga[:], in_=gates_d[b:b + 1, 0:1024].partition_broadcast(128)),
                   r=["gates_d"], w=["Ga"], dma="ga")
                op("sp", lambda e, b=b: e.dma_start(out=Gf[:], in_=gates_d[b:b + 1, 1024:2048].partition_broadcast(128)),
                   r=["gates_d"], w=["Gf"], dma="gf")
                for kc in range(8):
                    op("pool", lambda e, kc=kc: e.dma_start(out=W_out[:, kc * 1024:(kc + 1) * 1024], in_=wo_v[:, kc, :]),
                       w=[f"W_out{kc}"], dma=f"w{4 + kc % 2}")
                    op("pool", lambda e, kc=kc: e.tensor_tensor(out=W_out[:, kc * 1024:(kc + 1) * 1024],
                                                                in0=W_out[:, kc * 1024:(kc + 1) * 1024], in1=Ga[:], op=ALU.mult),
                       r=[f"W_out{kc}", "Ga"], w=[f"W_out{kc}"])
                if b == 0:
                    for j in range(NJ):
                        op("pool", lambda e, j=j: e.dma_start(out=wgu_s[j], in_=wgu_d[j]), w=[f"wgu_s{j}"], dma=f"pc{j % 4}")
                    for j in range(NJ):
                        op("pool", lambda e, j=j: e.dma_start(out=wdn_s[j * 128:(j + 1) * 128, :],
                                                              in_=wdn_d[j * 128:(j + 1) * 128, :]),
                           w=[f"wdn_s{j}"], dma=f"pc{j % 4}")
                ck("g")
                for c in range(NCH):
                    tok0 = b * SEQ + c * CH
                    for t in range(4):
                        op("sp", lambda e, t=t, tok0=tok0: e.dma_start(out=xc[:, t * 1024:(t + 1) * 1024],
                                                                      in_=x_d[tok0 + t * 128:tok0 + (t + 1) * 128, :]),
                           w=[f"xc{t}h0", f"xc{t}h1"], dma=f"ld{t}")
                    ck("xl")
                    for t in range(4):
                        norm_sq(t, 0)
                    flush_stores()
                    norm_rstd(0, 4, 0)
                    for t in range(4):
                        norm_scale(t, 0)
                    for t in range(4):
                        norm_tr(b, 0, t)
                    if b == 0 and c == 0:
                        dump("Ga", Ga[:], ["Ga"])
                        dump("hT", actT[:], ACT_ALL)
                        dump("st", st[:], [f"st{t}" for t in range(4)])
                    ck("s1")
                    rr = 0
                    for m in range(8):
                        k = 2 + rr % 6
                        rr += 1
                        for kc in range(8):
                            op("pe", lambda e, k=k, kc=kc, m=m: e.matmul(
                                PS[k][:], W_in[:, kc * INW + m * 128:kc * INW + (m + 1) * 128],
                                actT[:, kc * 512:(kc + 1) * 512], start=(kc == 0), stop=(kc == 7)),
                               r=["W_in"] + ACT(kc), w=[f"P{k}"])
                        if m < 4:
                            evac_copy(slot(m), PS[k][:], [f"P{k}"], SR(m))
                        else:
                            h = m - 4
                            evac_copy(dkT[:, h * SEQ + c * CH:h * SEQ + (c + 1) * CH], PS[k][:], [f"P{k}"], [f"dk{h}c{c}"])
                    for t in range(4):
                        k = 2 + rr % 6
                        rr += 1
                        for kc in range(8):
                            op("pe", lambda e, k=k, kc=kc, t=t: e.matmul(
                                PS[k][:], actT[:, kc * 512 + t * 128:kc * 512 + (t + 1) * 128],
                                W_in[:, kc * INW + 1024:kc * INW + 1536], start=(kc == 0), stop=(kc == 7)),
                               r=["W_in", f"act{kc}t{t}"], w=[f"P{k}"])
                        tile = c * 4 + t
                        evac_copy(dv[:, tile * 512:(tile + 1) * 512], PS[k][:], [f"P{k}"], [f"dv{tile}"])
                    if b == 0 and c == 0:
                        dump("dqT", slot(0, 4), SR(0, 4))
                        dump("dkT", dkT[:, 0:512], ["dk0c0"])
                        dump("dv0", dv[:, 0:512], ["dv0"])
                    ck("s2a")
                    lat_banks = [(0, 1536), (1, 1664), (2, 1792), (4, 1920), (5, 2048)]
                    for k, col0 in lat_banks:
                        for kc in range(8):
                            op("pe", lambda e, k=k, kc=kc, col0=col0: e.matmul(
                                PS[k][:], W_in[:, kc * INW + col0:kc * INW + col0 + 128], actT[:, kc * 512:(kc + 1) * 512],
                                start=(kc == 0), stop=(kc == 7)), r=["W_in"] + ACT(kc), w=[f"P{k}"])
                    for kc in range(8):
                        op("pe", lambda e, kc=kc: e.matmul(
                            PS[6][0:64, :], W_in[:, kc * INW + 2176:kc * INW + 2240], actT[:, kc * 512:(kc + 1) * 512],
                            start=(kc == 0), stop=(kc == 7)), r=["W_in"] + ACT(kc), w=["P6"])
                    lat_norm([0, 1, 2], 3, C_GQ, 16, 0, 1.0 / 384, 3)
                    lat_norm([4, 5], 2, C_GKV, 18, 3, 1.0 / 256, 7)
                    op("sp", lambda e, c=c: e.dma_start(out=cosT, in_=rope_d[0][:, c * CH:(c + 1) * CH]),
                       r=["rope_d"], w=SR(12, 2), dma="rp0")
                    op("sp", lambda e, c=c: e.dma_start(out=sgnT, in_=rope_d[1][:, c * CH:(c + 1) * CH]),
                       r=["rope_d"], w=SR(14, 2), dma="rp1")
                    ck("s2b")
                    for h in range(4):
                        for kc in range(3):
                            op("pe", lambda e, h=h, kc=kc: e.matmul(
                                PS[h][:], W_qup[:, kc * 768 + h * 128:kc * 768 + (h + 1) * 128],
                                actT[:, kc * 512:(kc + 1) * 512], start=(kc == 0), stop=(kc == 2)),
                               r=["W_qup"] + ACT(kc), w=[f"P{h}"])
                        evac_copy(slot(4 + h), PS[h][:], [f"P{h}"], SR(4 + h), force_act=True)
                    for h in range(4):
                        for kc in range(3):
                            op("pe", lambda e, h=h, kc=kc: e.matmul(
                                PS[h][0:64, :], W_qup[:, kc * 768 + 512 + h * 64:kc * 768 + 512 + (h + 1) * 64],
                                actT[:, kc * 512:(kc + 1) * 512], start=(kc == 0), stop=(kc == 2)),
                               r=["W_qup"] + ACT(kc), w=[f"P{h}"])
                    for h in range(4):
                        rope(h, S[0:64, (8 + h) * 512:(9 + h) * 512], SR(8 + h))
                    if b == 0 and c == 0:
                        dump("qnT", slot(4, 4), SR(4, 4))
                        dump("qpeT", S[0:64, 8 * 512:12 * 512], SR(8, 4), nparts=64)
                        dump("cosT", cosT, SR(12, 2), nparts=64)
                        dump("sgnT", sgnT, SR(14, 2), nparts=64)
                    ck("s3a")
                    kvb = [4, 5, 7, 4, 5, 7, 4, 5]
                    for h in range(4):
                        kb_ = kvb[h]
                        for kc in range(2):
                            op("pe", lambda e, h=h, kc=kc, kb_=kb_: e.matmul(
                                PS[kb_][:], W_kvup[:, kc * 1024 + h * 128:kc * 1024 + (h + 1) * 128],
                                actT[:, (3 + kc) * 512:(4 + kc) * 512], start=(kc == 0), stop=(kc == 1)),
                               r=["W_kvup"] + ACT(3 + kc), w=[f"P{kb_}"])
                        evac_copy(knT[:, h * SEQ + c * CH:h * SEQ + (c + 1) * CH], PS[kb_][:], [f"P{kb_}"], [f"kn{h}c{c}"], force_act=True)
                    for t in range(4):
                        kb_ = kvb[4 + t]
                        for kc in range(2):
                            op("pe", lambda e, t=t, kc=kc, kb_=kb_: e.matmul(
                                PS[kb_][:], actT[:, (3 + kc) * 512 + t * 128:(3 + kc) * 512 + (t + 1) * 128],
                                W_kvup[:, kc * 1024 + 512:kc * 1024 + 1024], start=(kc == 0), stop=(kc == 1)),
                               r=["W_kvup", f"act{3 + kc}t{t}"], w=[f"P{kb_}"])
                        tile = c * 4 + t
                        evac_copy(vv[:, tile * 512:(tile + 1) * 512], PS[kb_][:], [f"P{kb_}"], [f"v{tile}"], force_act=True)
                    rope(6, kpeT[0:64, c * CH:(c + 1) * CH], [f"kpe{c}"])
                    if b == 0 and c == 0:
                        dump("knT", knT[:, 0:512], ["kn0c0"])
                        dump("kpeT", kpeT[0:64, 0:512], ["kpe0"], nparts=64)
                        dump("vv0", vv[:, 0:512], ["v0"])
                    ck("s3b")
                    for u in range(RING_PF):
                        ring_issue_gu(u)
                    free_banks = [4, 5, 7, 6]

                    def ring_alloc():
                        return free_banks.pop(0)

                    def ring_free(k_):
                        free_banks.append(k_)

                    nkb = 4 * c + 4
                    pendA, pendS, pendB = [None], [None], [None]

                    def make_epilogue(mi, h, mm, is_diff, ob, rb):
                        obr, rbr = [f"P{ob}"], [f"P{rb}"]
                        rinv = slotf(16)
                        E1 = slotf(18)
                        E2 = slotf(20)
                        sq = actT[:, 7 * 512:8 * 512]

                        def partA():
                            op("dve", lambda e: e.reciprocal(out=rinv, in_=PS[rb][:]), r=rbr, w=SR(16, 2))
                            if not is_diff:
                                f = 4 + h
                                op("dve", lambda e: e.tensor_tensor(out=actT[:, f * 512:(f + 1) * 512], in0=PS[ob][:],
                                                                    in1=rinv, op=ALU.mult), r=obr + SR(16, 2), w=ACT(f))
                            elif mm == 0:
                                op("dve", lambda e: e.tensor_tensor(out=E1, in0=PS[ob][:], in1=rinv, op=ALU.mult),
                                   r=obr + SR(16, 2), w=SR(18, 2))
                            else:
                                op("dve", lambda e: e.tensor_tensor(out=E2, in0=PS[ob][:], in1=rinv, op=ALU.mult),
                                   r=obr + SR(16, 2), w=SR(20, 2))
                                op("dve", lambda e: e.scalar_tensor_tensor(out=E1, in0=E2, scalar=cols[:, C_NLAM:C_NLAM + 1],
                                                                           in1=E1, op0=ALU.mult, op1=ALU.add),
                                   r=SR(18, 4) + ["cols"], w=SR(18, 2))

                        def partS():
                            if is_diff and mm == 1:
                                op("act", lambda e: e.activation(out=sq, in_=E1, func=AF.Square), r=SR(18, 2), w=ACT(7))

                        def partB():
                            if not (is_diff and mm == 1):
                                return
                            ssb = ring_alloc()
                            op("pe", lambda e: e.matmul(PS[ssb][:], ones[:], sq, start=True, stop=True),
                               r=ACT(7) + ["ones"], w=[f"P{ssb}"])
                            op("act", lambda e: e.activation(out=E2, in_=PS[ssb][:], func=AF.Ln, bias=cols[:, 15:16],
                                                             scale=1.0 / 128), r=[f"P{ssb}", "cols"], w=SR(20, 2))
                            ring_free(ssb)
                            op("act", lambda e: e.activation(out=E2, in_=E2, func=AF.Exp, scale=-0.5), r=SR(20, 2), w=SR(20, 2))
                            op("dve", lambda e: e.scalar_tensor_tensor(
                                out=actT[:, h * 512:(h + 1) * 512], in0=E1, scalar=cols[:, C_GDP:C_GDP + 1], in1=E2,
                                op0=ALU.mult, op1=ALU.mult), r=SR(18, 4) + ["cols"], w=ACT(h))

                        return partA, partS, partB

                    for mi in range(12):
                        is_diff = mi < 8
                        if is_diff:
                            h, mm = mi // 2, mi % 2
                        else:
                            h, mm = mi - 8, 0
                        ob, rb = 2 * (mi % 2), 1 + 2 * (mi % 2)
                        obr, rbr = [f"P{ob}"], [f"P{rb}"]

                        def qk(kb, sbk):
                            qlo = max(0, kb - 4 * c) * 128
                            kc_ = kb // 4
                            if is_diff:
                                pr = slice(mm * 64, mm * 64 + 64)
                                op("pe", lambda e: e.matmul(
                                    PS[sbk][:, qlo:512], dkT[pr, h * SEQ + kb * 128:h * SEQ + kb * 128 + 128],
                                    S[pr, h * 512 + qlo:h * 512 + 512], start=True, stop=True),
                                   r=[f"dk{h}c{kc_}"] + SR(h), w=[f"P{sbk}"])
                            else:
                                op("pe", lambda e: e.matmul(
                                    PS[sbk][:, qlo:512], knT[:, h * SEQ + kb * 128:h * SEQ + kb * 128 + 128],
                                    S[:, (4 + h) * 512 + qlo:(4 + h) * 512 + 512], start=True, stop=False),
                                   r=[f"kn{h}c{kc_}"] + SR(4 + h), w=[f"P{sbk}"])
                                op("pe", lambda e: e.matmul(
                                    PS[sbk][:, qlo:512], kpeT[0:64, kb * 128:kb * 128 + 128],
                                    S[0:64, (8 + h) * 512 + qlo:(8 + h) * 512 + 512], start=False, stop=True),
                                   r=[f"kpe{kc_}"] + SR(8 + h), w=[f"P{sbk}"])

                        bank_of = {}
                        for kb0 in range(min(QK_PF, nkb)):
                            bank_of[kb0] = ring_alloc()
                            qk(kb0, bank_of[kb0])
                        for kb in range(nkb):
                            sbk = bank_of[kb]
                            if kb + QK_PF < nkb:
                                bank_of[kb + QK_PF] = ring_alloc()
                                qk(kb + QK_PF, bank_of[kb + QK_PF])
                            qb0 = max(0, kb - 4 * c)
                            qlo = qb0 * 128
                            pt = S[:, (8 + sbk) * 512:(9 + sbk) * 512]
                            ptr = SR(8 + sbk)
                            if is_diff:
                                groups = [(0, 1), (2, 3)] if h == 0 else [(0, 1, 2, 3)]
                                for g in groups:
                                    if g[-1] < qb0:
                                        continue
                                    lo, hi = max(g[0], qb0) * 128, (g[-1] + 1) * 128
                                    delta = (4 * c + g[0] + 1 - kb) if h == 0 else (4 * c + 3 - kb)
                                    op("act", lambda e, lo=lo, hi=hi, delta=delta: e.activation(
                                        out=pt[:, lo:hi], in_=PS[sbk][:, lo:hi], func=AF.Exp,
                                        bias=alibi[:, h * 16 + delta:h * 16 + delta + 1], scale=QSCALE_D),
                                       r=[f"P{sbk}", "alibi"], w=ptr)
                            else:
                                op("act", lambda e: e.activation(out=pt[:, qlo:512], in_=PS[sbk][:, qlo:512], func=AF.Exp,
                                                                 scale=QSCALE_M), r=[f"P{sbk}"], w=ptr)
                            ring_free(sbk)
                            if kb >= 4 * c:
                                op("pool", lambda e: e.tensor_tensor(out=pt[:, qlo:qlo + 128], in0=pt[:, qlo:qlo + 128],
                                                                     in1=tri[:], op=ALU.mult), r=ptr + ["tri"], w=ptr)
                            vsrc = dv if is_diff else vv
                            vres = f"dv{kb}" if is_diff else f"v{kb}"
                            op("pe", lambda e: e.matmul(
                                PS[ob][:, qlo:512], vsrc[:, kb * 512 + h * 128:kb * 512 + (h + 1) * 128], pt[:, qlo:512],
                                start=(kb == 0), stop=(kb == nkb - 1)), r=ptr + [vres], w=obr)
                            op("pe", lambda e: e.matmul(
                                PS[rb][:, qlo:512], ones[:], pt[:, qlo:512], start=(kb == 0), stop=(kb == nkb - 1)),
                               r=ptr + ["ones"], w=rbr)
                            if kb == 0 and pendA[0] is not None:
                                pendA[0]()
                                pendA[0] = None
                            if kb == min(5, nkb - 1) and pendS[0] is not None:
                                pendS[0]()
                                pendS[0] = None
                            if kb == min(8, nkb - 1) and pendB[0] is not None:
                                pendB[0]()
                                pendB[0] = None
                        pendA[0], pendS[0], pendB[0] = make_epilogue(mi, h, mm, is_diff, ob, rb)
                    pendA[0]()
                    pendS[0]()
                    pendB[0]()

                    if b == 0 and c == 0:
                        dump("mergedT", actT[:], ACT_ALL)
                    ck("s4")
                    for t in range(4):
                        for half in range(2):
                            k = t * 2 + half
                            for f in range(8):
                                op("pe", lambda e, f=f: e.matmul(
                                    PS[k][:], actT[:, f * 512 + t * 128:f * 512 + (t + 1) * 128],
                                    W_out[:, f * 1024 + half * 512:f * 1024 + (half + 1) * 512],
                                    start=(f == 0), stop=(f == 7)), r=[f"W_out{f}", f"act{f}t{t}"], w=[f"P{k}"])
                            xs = xc[:, t * 1024 + half * 512:t * 1024 + (half + 1) * 512]
                            op("dve", lambda e: e.tensor_tensor(out=xs, in0=PS[k][:], in1=xs, op=ALU.add),
                               r=[f"P{k}", f"xc{t}h{half}"], w=[f"xc{t}h{half}"])
                        norm_sq(t, 0)
                        norm_rstd(t, 1, 0)
                        norm_scale(t, 0)
                    for t in range(4):
                        norm_tr(b, 1, t)
                    if b == 0 and c == 0:
                        dump("x1", xc[:], XC_ALL)
                    ck("s5")
                    if b == 0 and c == 0:
                        dump("h2T", actT[:], ACT_ALL)
                    ck("s6")
                    for j in range(NJ):
                        rs_ = j % 4
                        rres = [f"R{rs_}a", f"R{rs_}b"]
                        ru = ring[:, rs_ * 2048:(rs_ + 1) * 2048]
                        pg, pu = 2 + 2 * (j % 3), 3 + 2 * (j % 3)
                        for kc in range(8):
                            op("pe", lambda e, pg=pg, ru=ru, kc=kc: e.matmul(
                                PS[pg][:], ru[:, kc * 128:(kc + 1) * 128], actT[:, kc * 512:(kc + 1) * 512],
                                start=(kc == 0), stop=(kc == 7)), r=rres + ACT(kc), w=[f"P{pg}"])
                        for kc in range(8):
                            op("pe", lambda e, pu=pu, ru=ru, kc=kc: e.matmul(
                                PS[pu][:], ru[:, 1024 + kc * 128:1024 + (kc + 1) * 128], actT[:, kc * 512:(kc + 1) * 512],
                                start=(kc == 0), stop=(kc == 7)), r=rres + ACT(kc), w=[f"P{pu}"])
                        sg = sgt[:, (j % 2) * 512:(j % 2 + 1) * 512]
                        op("act", lambda e, pg=pg, sg=sg: e.activation(out=sg, in_=PS[pg][:], func=AF.Silu),
                           r=[f"P{pg}"], w=[f"sgt{j % 2}"])
                        op("dve", lambda e, pu=pu, sg=sg, j=j: e.tensor_tensor(out=slot(j), in0=PS[pu][:], in1=sg, op=ALU.mult),
                           r=[f"P{pu}", f"sgt{j % 2}"], w=SR(j))
                        if j + RING_PF < NJ:
                            ring_issue_gu(j + RING_PF)
                        else:
                            ring_issue_dn(2 * rs_)
                            ring_issue_dn(2 * rs_ + 1)
                    if b == 0 and c == 0:
                        dump("aT", S[:], S_ALL)
                    ck("s7a")
                    for j in range(NJ):
                        hs = j % 8
                        rres = [f"R{hs // 2}{'ab'[hs % 2]}"]
                        ru = ring[:, hs * 1024:(hs + 1) * 1024]
                        op("dve", lambda e, ru=ru: e.tensor_tensor(out=ru, in0=ru, in1=Gf[:], op=ALU.mult),
                           r=rres + ["Gf"], w=rres)
                        for t in range(4):
                            for half in range(2):
                                k = t * 2 + half
                                op("pe", lambda e, k=k, t=t, half=half, j=j, ru=ru: e.matmul(
                                    PS[k][:], S[:, j * 512 + t * 128:j * 512 + (t + 1) * 128],
                                    ru[:, half * 512:(half + 1) * 512], start=(j == 0), stop=(j == NJ - 1)),
                                   r=rres + SR(j), w=[f"P{k}"])
                        ring_issue_dn(j + 8)
                    ck("s7b")
                    junk = S[:, 20 * 512:22 * 512]
                    stags = []
                    for t in range(4):
                        if t < 3:
                            stags.append((Sf[:, (8 + 4 * t) * 256:(8 + 4 * t) * 256 + 1024], SR(8 + 4 * t, 4)))
                        else:
                            stags.append((sgt[:, 0:1024], ["sgt0", "sgt1"]))
                    for t in range(4):
                        xr = [f"xc{t}h0", f"xc{t}h1"]
                        xt = xc[:, t * 1024:(t + 1) * 1024]
                        stag, stres = stags[t]
                        for half in range(2):
                            k = t * 2 + half
                            xs = xc[:, t * 1024 + half * 512:t * 1024 + (half + 1) * 512]
                            op("dve", lambda e: e.tensor_tensor(out=xs, in0=PS[k][:], in1=xs, op=ALU.add),
                               r=[f"P{k}", f"xc{t}h{half}"], w=[f"xc{t}h{half}"])
                        sres = [f"st{8 + t}"]
                        op("dve", lambda e: e.memset(st[:, 8 + t:9 + t], 0.0), w=sres)
                        op("act", lambda e: e.activation(out=junk, in_=xt, func=AF.Square, accum_out=st[:, 8 + t:9 + t]),
                           r=xr + sres, w=SR(20, 2) + sres)
                        op("act", lambda e: e.activation(out=st[:, 12 + t:13 + t], in_=st[:, 8 + t:9 + t], func=AF.Ln,
                                                         bias=cols[:, 15:16], scale=1.0 / D), r=sres + ["cols"], w=sres)
                        op("act", lambda e: e.activation(out=st[:, 12 + t:13 + t], in_=st[:, 12 + t:13 + t], func=AF.Exp,
                                                         scale=-0.5), r=sres, w=sres)
                    for t in range(4):
                        xr = [f"xc{t}h0", f"xc{t}h1"]
                        xt = xc[:, t * 1024:(t + 1) * 1024]
                        stag, stres = stags[t]
                        sres = [f"st{8 + t}"]
                        op("dve", lambda e: e.scalar_tensor_tensor(out=stag, in0=xt, scalar=st[:, 12 + t:13 + t], in1=gfin[:],
                                                                   op0=ALU.mult, op1=ALU.mult),
                           r=xr + sres + ["gfin"], w=stres)
                        pending_stores.append((stag, stres, t, tok0))

            flush_stores()

        nops, nwait = P.run(body)
        print(f"[kernel] ops={nops} waits={nwait}", flush=True)
    return nc


_NC_CACHE = {}


def _prep_shared(inp):
    f = np.float32
    w_qup = inp["w_q_up"][0]
    qcols = [h * 192 + i for h in range(4) for i in range(128)] + [h * 192 + 128 + i for h in range(4) for i in range(64)]
    w_kvup = inp["w_kv_up"][0]
    kvcols = [h * 256 + i for h in range(4) for i in range(128)] + [h * 256 + 128 + i for h in range(4) for i in range(128)]
    wg = inp["w_ffn_gate"][0].reshape(8, 128, NJ, 128)
    wu = inp["w_ffn_up"][0].reshape(8, 128, NJ, 128)
    wgu = np.stack([wg.transpose(2, 1, 0, 3), wu.transpose(2, 1, 0, 3)], axis=2)
    b_ada = inp["b_ada"][0]
    shared = {
        "w_ada": np.ascontiguousarray(inp["w_ada"][0], dtype=f),
        "b_adaT": np.ascontiguousarray(b_ada.reshape(48, 128).T, dtype=f),
        "b_gate": np.ascontiguousarray(np.concatenate([b_ada[2048:3072], b_ada[5120:6144]])[None, :], dtype=f),
        "g_mixT": np.ascontiguousarray(inp["g_mix"][0].reshape(8, 128).T, dtype=f),
        "g_ffnT": np.ascontiguousarray(inp["g_ffn"][0].reshape(8, 128).T, dtype=f),
        "w_in": np.ascontiguousarray(inp["w_in"][0], dtype=f),
        "lam": np.ascontiguousarray(np.concatenate([inp["lambda_q1"][0], inp["lambda_k1"][0],
                                                    inp["lambda_q2"][0], inp["lambda_k2"][0]])[None, :], dtype=f),
        "g_diff": np.ascontiguousarray(inp["g_diff_out"][0].reshape(128, 1), dtype=f),
        "g_qlT": np.ascontiguousarray(inp["g_q_lat"][0].reshape(3, 128).T, dtype=f),
        "w_qup": np.ascontiguousarray(w_qup[:, qcols], dtype=f),
        "g_kvlT": np.ascontiguousarray(inp["g_kv_lat"][0].reshape(2, 128).T, dtype=f),
        "w_kvup": np.ascontiguousarray(w_kvup[:, kvcols], dtype=f),
        "w_out": np.ascontiguousarray(inp["w_out"][0], dtype=f),
        "w_gu": np.ascontiguousarray(wgu.reshape(NJ, 128, 2048), dtype=f),
        "w_dn": np.ascontiguousarray(inp["w_ffn_down"][0], dtype=f),
        "g_final": np.ascontiguousarray(inp["g_final"].reshape(1, D), dtype=f),
    }
    return shared


def kernel(**inputs):
    inp = {k: np.asarray(v) for k, v in inputs.items()}
    if "nc" not in _NC_CACHE:
        _NC_CACHE["nc"] = build()
    nc = _NC_CACHE["nc"]
    shared = _prep_shared(inp)
    x = inp["x"].astype(np.float32, copy=False)
    c = inp["c"].astype(np.float32, copy=False)
    in_maps = []
    for core in range(NCORES):
        m = dict(shared)
        m["x"] = np.ascontiguousarray(x[core * BPC:(core + 1) * BPC].reshape(BPC * SEQ, D))
        cc = c[core * BPC:(core + 1) * BPC]
        m["cT"] = np.ascontiguousarray(cc.reshape(BPC, 8, 128).transpose(2, 1, 0).reshape(128, 8 * BPC))
        in_maps.append(m)
    res = run_bass_kernel_spmd(nc, in_maps, core_ids=list(range(NCORES)))
    if DBG:
        LAST_DBG["dbg"] = np.asarray(res.results[0]["dbg"])
        LAST_DBG["map"] = dict(DBG_MAP)
    outs = [np.asarray(r["out"]).reshape(BPC, SEQ, D) for r in res.results]
    return np.concatenate(outs, axis=0).astype(np.float32, copy=False)
```

```python
import contextlib
import math
import numpy as np
import concourse.bass as bass
import concourse.mybir as mybir
from concourse.bass_utils import run_bass_kernel_spmd

F32 = mybir.dt.float32
BF16 = mybir.dt.bfloat16
AF = mybir.ActivationFunctionType
ALU = mybir.AluOpType

NCORES = 8
SEQ = 2048
D = 1024
BPC = 4
CH = 512
NCH = SEQ // CH
DFF = 2816
NJ = DFF // 128
INW = 2240
EPS = 1e-6
LAMBDA_INIT = 0.8 - 0.6 * math.exp(-0.3 * 0)
SLOPES = [2.0 ** (-8.0 * (i + 1) / 4) for i in range(4)]
SAME_ENGINE_SYNC = True
import os
STOP = os.environ.get("KSTOP", "")
DBG = bool(os.environ.get("KDBG", ""))
DBGW = 48000
DBG_MAP = {}
LAST_DBG = {}


class _Stop(Exception):
    pass


def ck(label):
    if STOP == label:
        raise _Stop()


class Prog:
    def __init__(self, nc, es):
        self.nc = nc
        self.es = es
        self.eng = {"pe": nc.tensor, "act": nc.scalar, "dve": nc.vector, "pool": nc.gpsimd, "sp": nc.sync}
        self.esem = {k: es.enter_context(nc.semaphore("sem_" + k)) for k in self.eng}
        self.dsem = {}
        self.meta = []
        self.mode = "analyze"
        self.idx = 0
        self.nwait = 0

    def op(self, eng, fn, r=(), w=(), dma=None):
        if self.mode == "analyze":
            self.meta.append((eng, tuple(r), tuple(w), dma))
            return
        i = self.idx
        self.idx += 1
        e = self.eng[eng]
        need = {}
        for j in self.deps[i]:
            key, val = self.tick[j]
            if need.get(key, 0) < val:
                need[key] = val
        wd = self.waited[eng]
        for key, val in need.items():
            if wd.get(key, 0) >= val:
                continue
            wd[key] = val
            sem = self.dsem[key[1]] if key[0] == "d" else self.esem[key[1]]
            e.wait_ge(sem, val)
            self.nwait += 1
        inst = fn(e)
        if dma is not None:
            inst.then_inc(self.dsem[dma], 16)
        elif i in self.signaled:
            inst.then_inc(self.esem[eng], 1)

    def analyze(self):
        ops = self.meta
        last_w = {}
        readers = {}
        last_dma = {}
        deps = []
        for i, (eng, r, w, dma) in enumerate(ops):
            d = set()
            for x in r:
                if x in last_w:
                    d.add(last_w[x])
                if x[0] == "P" and x[1:].isdigit():
                    rd = readers.get(x)
                    if rd:
                        d.update(j for k_, j in rd.items() if k_ != eng)
            for x in w:
                if x in last_w:
                    d.add(last_w[x])
                rd = readers.get(x)
                if rd:
                    d.update(rd.values())
            if dma is not None and dma in last_dma:
                d.add(last_dma[dma])
            d.discard(i)
            for x in w:
                last_w[x] = i
                readers[x] = {}
            for x in r:
                if x in w:
                    continue
                key = eng if dma is None else ("dma", i)
                readers.setdefault(x, {})[key] = i
            if dma is not None:
                last_dma[dma] = i
            kept = []
            for j in d:
                ej, _, _, dj = ops[j]
                if dj is None and dma is None and ej == eng and (eng == "pe" or not SAME_ENGINE_SYNC):
                    continue
                kept.append(j)
            deps.append(kept)
        signaled = set()
        for kept in deps:
            for j in kept:
                if ops[j][3] is None:
                    signaled.add(j)
        tick = {}
        cnt = {k: 0 for k in self.eng}
        dcnt = {}
        for i, (eng, r, w, dma) in enumerate(ops):
            if dma is not None:
                if dma not in self.dsem:
                    self.dsem[dma] = self.es.enter_context(self.nc.semaphore("d_" + dma))
                dcnt[dma] = dcnt.get(dma, 0) + 16
                tick[i] = (("d", dma), dcnt[dma])
            elif i in signaled:
                cnt[eng] += 1
                tick[i] = (("e", eng), cnt[eng])
        self.deps, self.signaled, self.tick, self.dcnt = deps, signaled, tick, dcnt
        self.waited = {k: {} for k in self.eng}

    def run(self, body):
        self.mode = "analyze"
        try:
            body()
        except _Stop:
            pass
        self.analyze()
        self.mode = "emit"
        self.idx = 0
        try:
            body()
        except _Stop:
            pass
        assert self.idx == len(self.meta), (self.idx, len(self.meta))
        sp = self.eng["sp"]
        for name, sem in self.dsem.items():
            if self.waited["sp"].get(("d", name), 0) < self.dcnt[name]:
                sp.wait_ge(sem, self.dcnt[name])
        return len(self.meta), self.nwait


def build():
    nc = bass.Bass("TRN2", target_bir_lowering=False)

    def dram(name, shape, dt=F32, kind="ExternalInput"):
        return nc.dram_tensor(name, list(shape), dt, kind=kind).ap()

    x_d = dram("x", [BPC * SEQ, D])
    out_d = dram("out", [BPC * SEQ, D], kind="ExternalOutput")
    cT_d = dram("cT", [128, 8 * BPC])
    wada_d = dram("w_ada", [D, 6 * D])
    badaT_d = dram("b_adaT", [128, 48])
    bgate_d = dram("b_gate", [1, 2 * D])
    gmixT_d = dram("g_mixT", [128, 8])
    gffnT_d = dram("g_ffnT", [128, 8])
    win_d = dram("w_in", [D, INW])
    lam_d = dram("lam", [1, 256])
    gdiff_d = dram("g_diff", [128, 1])
    gqlT_d = dram("g_qlT", [128, 3])
    wqup_d = dram("w_qup", [384, 768])
    gkvlT_d = dram("g_kvlT", [128, 2])
    wkvup_d = dram("w_kvup", [256, 1024])
    wout_d = dram("w_out", [D, D])
    wgu_d = dram("w_gu", [NJ, 128, 2048])
    wdn_d = dram("w_dn", [DFF, D])
    gfin_d = dram("g_final", [1, D])
    gates_d = dram("gates_scr", [BPC, 2 * D], kind="Internal")
    rope_d = dram("rope_scr", [2, 64, SEQ], kind="Internal")
    wgu_s = dram("wgu_bf16_scr", [NJ, 128, 2048], BF16, kind="Internal")
    wdn_s = dram("wdn_bf16_scr", [DFF, D], BF16, kind="Internal")
    dbg_d = dram("dbg", [128, DBGW], kind="ExternalOutput") if DBG else None

    es = contextlib.ExitStack()
    with es:
        def sb(name, shape, dt):
            return es.enter_context(nc.sbuf_tensor(name, list(shape), dt))

        W_in = sb("W_in", [128, 8 * INW], BF16)
        W_qup = sb("W_qup", [128, 3 * 768], BF16)
        W_kvup = sb("W_kvup", [128, 2 * 1024], BF16)
        W_out = sb("W_out", [128, 8 * 1024], BF16)
        dkT = sb("dkT", [128, 4 * SEQ], BF16)
        dv = sb("dv", [128, 16 * 512], BF16)
        knT = sb("knT", [128, 4 * SEQ], BF16)
        kpeT = sb("kpeT", [128, SEQ], BF16)
        vv = sb("vv", [128, 16 * 512], BF16)
        xc = sb("xc", [128, 4 * 1024], F32)
        actT = sb("actT", [128, 8 * 512], BF16)
        S = sb("S", [128, NJ * 512], BF16)
        ring = sb("ring", [128, 4 * 2048], BF16)
        Ga = sb("Ga", [128, 1024], F32)
        Gf = sb("Gf", [128, 1024], F32)
        gfin = sb("gfin", [128, 1024], F32)
        sgt = sb("sgt", [128, 2 * 512], F32)
        ident = sb("ident", [128, 128], BF16)
        ones = sb("ones", [128, 128], BF16)
        tri = sb("tri", [128, 128], BF16)
        alibi = sb("alibi", [128, 64], F32)
        AB = sb("AB", [128, BPC * 2 * 2 * 8], F32)
        st = sb("st", [128, 16], F32)
        cols = sb("cols", [128, 32], F32)
        PS = [es.enter_context(nc.psum_tensor(f"ps{k}", [128, 512], F32)) for k in range(8)]

        P = Prog(nc, es)
        op = P.op

        def body():
            Sf = S[:].bitcast(F32)
            modT = sgt[:, 256:256 + 48 * BPC]
            xcb = xc[:].bitcast(BF16)

            def slot(j, n=1):
                return S[:, j * 512:(j + n) * 512]

            def slotf(j):
                return Sf[:, j * 256:j * 256 + 512]

            def SR(j, n=1):
                return [f"S{k}" for k in range(j, j + n)]

            dbg_off = [0]

            def dump(label, ap, res, nparts=128):
                if not DBG or label in DBG_MAP and P.mode == "analyze":
                    return
                wdt = ap.shape[-1]
                if P.mode == "analyze":
                    DBG_MAP[label] = (dbg_off[0], wdt, nparts)
                o = DBG_MAP[label][0]
                dbg_off[0] = o + wdt
                op("pool", lambda e: e.dma_start(out=dbg_d[0:nparts, o:o + wdt], in_=ap), r=res, w=["dbg_" + label],
                   dma="dbg")

            XC_ALL = [f"xc{t}h{h}" for t in range(4) for h in range(2)]
            S_ALL = SR(0, NJ)
            ACT_ALL = [f"act{k}t{t}" for k in range(8) for t in range(4)]

            def ACT(k):
                return [f"act{k}t{t}" for t in range(4)]

            def AB_col(b, af, ab, kc):
                o = ((b * 2 + af) * 2 + ab) * 8 + kc
                return AB[:, o:o + 1]

            C_NLAM, C_GDP, C_GD, C_GQ, C_GKV = 0, 1, 2, 3, 6

            tmpf = Sf[:, 0:128]
            op("pool", lambda e: e.memset(tmpf, 1.0), w=SR(0))
            op("pool", lambda e: e.affine_select(out=tmpf, in_=tmpf, pattern=[[-1, 128]], compare_op=ALU.is_equal,
                                                 fill=0.0, base=0, channel_multiplier=1), r=SR(0), w=SR(0))
            op("dve", lambda e: e.tensor_copy(out=ident[:], in_=tmpf), r=SR(0), w=["ident"])
            tmpf2 = Sf[:, 256:384]
            op("pool", lambda e: e.memset(tmpf2, 1.0), w=SR(1))
            op("pool", lambda e: e.affine_select(out=tmpf2, in_=tmpf2, pattern=[[1, 128]], compare_op=ALU.is_ge,
                                                 fill=0.0, base=0, channel_multiplier=-1), r=SR(1), w=SR(1))
            op("dve", lambda e: e.tensor_copy(out=tri[:], in_=tmpf2), r=SR(1), w=["tri"])
            op("dve", lambda e: e.memset(ones[:], 1.0), w=["ones"])
            op("dve", lambda e: e.memset(cols[:, 15:16], EPS), w=["cols"])
            tmpa = Sf[:, 512:528]
            op("pool", lambda e: e.iota(tmpa, pattern=[[-128, 16]], base=-127, channel_multiplier=1,
                                        allow_small_or_imprecise_dtypes=True), w=SR(2))
            tmpa0 = Sf[:, 768:784]
            op("pool", lambda e: e.iota(tmpa0, pattern=[[-128, 16]], base=0, channel_multiplier=1,
                                        allow_small_or_imprecise_dtypes=True), w=SR(3))
            op("dve", lambda e: e.tensor_scalar(out=alibi[:, 0:16], in0=tmpa0, scalar1=SLOPES[0], scalar2=None,
                                                op0=ALU.mult), r=SR(3), w=["alibi"])
            for h in range(1, 4):
                op("dve", lambda e, h=h: e.tensor_scalar(out=alibi[:, h * 16:(h + 1) * 16], in0=tmpa, scalar1=SLOPES[h],
                                                         scalar2=None, op0=ALU.mult), r=SR(2), w=["alibi"])

            ck("consts")
            win_v = win_d.rearrange("(kc p) n -> p kc n", p=128)
            for kc in range(8):
                op("pool", lambda e, kc=kc: e.dma_start(out=W_in[:, kc * INW:(kc + 1) * INW], in_=win_v[:, kc, :]),
                   w=["W_in"], dma=f"w{kc % 2}")
            wq_v = wqup_d.rearrange("(kc p) n -> p kc n", p=128)
            for kc in range(3):
                op("pool", lambda e, kc=kc: e.dma_start(out=W_qup[:, kc * 768:(kc + 1) * 768], in_=wq_v[:, kc, :]),
                   w=["W_qup"], dma="w2")
            wkv_v = wkvup_d.rearrange("(kc p) n -> p kc n", p=128)
            for kc in range(2):
                op("pool", lambda e, kc=kc: e.dma_start(out=W_kvup[:, kc * 1024:(kc + 1) * 1024], in_=wkv_v[:, kc, :]),
                   w=["W_kvup"], dma="w3")
            wo_v = wout_d.rearrange("(kc p) n -> p kc n", p=128)

            ck("weights")
            cTs = Sf[:, 1024:1024 + 8 * BPC]
            op("sp", lambda e: e.dma_start(out=cTs, in_=cT_d[:, :]), w=SR(4), dma="p0")
            op("sp", lambda e: e.dma_start(out=cols[:, 16:24], in_=gmixT_d[:, :]), w=["cols"], dma="p1")
            op("sp", lambda e: e.dma_start(out=cols[:, 24:32], in_=gffnT_d[:, :]), w=["cols"], dma="p1")
            op("sp", lambda e: e.dma_start(out=cols[:, C_GD:C_GD + 1], in_=gdiff_d[:, :]), w=["cols"], dma="p1")
            op("sp", lambda e: e.dma_start(out=cols[:, C_GQ:C_GQ + 3], in_=gqlT_d[:, :]), w=["cols"], dma="p1")
            op("sp", lambda e: e.dma_start(out=cols[:, C_GKV:C_GKV + 2], in_=gkvlT_d[:, :]), w=["cols"], dma="p1")
            badaT = Sf[:, 1280:1280 + 48]
            op("sp", lambda e: e.dma_start(out=badaT, in_=badaT_d[:, :]), w=SR(5), dma="p2")
            op("sp", lambda e: e.dma_start(out=gfin[:], in_=gfin_d[:, :].partition_broadcast(128)), w=["gfin"], dma="p3")
            lamb = Sf[:, 1536:1536 + 256]
            op("sp", lambda e: e.dma_start(out=lamb, in_=lam_d[:, :].partition_broadcast(128)), w=SR(6), dma="p4")
            bg = Sf[0:BPC, 2048:2048 + 2048]
            op("sp", lambda e: e.dma_start(out=bg, in_=bgate_d[:, :].partition_broadcast(BPC)), w=SR(8, 8), dma="p5")

            ck("params")
            lprod = Sf[:, 1792:1792 + 128]
            op("dve", lambda e: e.tensor_tensor(out=lprod[:, 0:64], in0=lamb[:, 0:64], in1=lamb[:, 64:128], op=ALU.mult),
               r=SR(6), w=SR(7))
            op("dve", lambda e: e.tensor_tensor(out=lprod[:, 64:128], in0=lamb[:, 128:192], in1=lamb[:, 192:256], op=ALU.mult),
               r=SR(6), w=SR(7))
            op("dve", lambda e: e.reduce_sum(out=cols[:, 8:9], in_=lprod[:, 0:64], axis=mybir.AxisListType.X), r=SR(7), w=["cols"])
            op("dve", lambda e: e.reduce_sum(out=cols[:, 9:10], in_=lprod[:, 64:128], axis=mybir.AxisListType.X), r=SR(7), w=["cols"])
            op("act", lambda e: e.activation(out=cols[:, 10:12], in_=cols[:, 8:10], func=AF.Exp), r=["cols"], w=["cols"])
            op("dve", lambda e: e.tensor_tensor(out=cols[:, C_NLAM:C_NLAM + 1], in0=cols[:, 11:12], in1=cols[:, 10:11],
                                                op=ALU.subtract), r=["cols"], w=["cols"])
            op("dve", lambda e: e.tensor_scalar(out=cols[:, C_NLAM:C_NLAM + 1], in0=cols[:, C_NLAM:C_NLAM + 1],
                                                scalar1=-LAMBDA_INIT, scalar2=None, op0=ALU.add), r=["cols"], w=["cols"])
            op("dve", lambda e: e.tensor_scalar(out=cols[:, C_GDP:C_GDP + 1], in0=cols[:, C_GD:C_GD + 1],
                                                scalar1=1.0 - LAMBDA_INIT, scalar2=None, op0=ALU.mult), r=["cols"], w=["cols"])

            ck("lam")
            scb = sgt[:].bitcast(BF16)[:, 0:8 * BPC]
            SCB = ["sgt0", "sgt1"]
            op("act", lambda e: e.activation(out=scb, in_=cTs, func=AF.Silu), r=SR(4), w=SCB)
            wada_v = wada_d.rearrange("(kc p) n -> p kc n", p=128)
            gsb = actT[:].bitcast(F32)[0:BPC, 0:2048]
            bank = 0
            for u in range(12):
                sl = u % 4
                if sl < 2:
                    unit = xcb[:, sl * 4096:(sl + 1) * 4096]
                    ures = XC_ALL[sl * 4:(sl + 1) * 4]
                else:
                    unit = ring[:, (sl - 2) * 4096:(sl - 1) * 4096]
                    ures = [f"R{2 * (sl - 2) + i}{ab}" for i in range(2) for ab in "ab"]
                op("pool", lambda e, unit=unit, u=u: e.dma_start(
                    out=unit.rearrange("p (kc n) -> p kc n", kc=8), in_=wada_v[:, :, u * 512:(u + 1) * 512]),
                   w=ures, dma=f"ada{sl}")
                if u in (4, 5, 10, 11):
                    g = 0 if u < 6 else 1
                    half = u % 2 if u < 6 else (u - 10)
                    pb = PS[bank % 8]
                    pres = [f"P{bank % 8}"]
                    bank += 1
                    for kc in range(8):
                        op("pe", lambda e, pb=pb, unit=unit, kc=kc: e.matmul(
                            pb[0:BPC, :], scb[:, kc * BPC:(kc + 1) * BPC], unit[:, kc * 512:(kc + 1) * 512],
                            start=(kc == 0), stop=(kc == 7)), r=ures + SCB, w=pres)
                    o = g * 1024 + half * 512
                    op("dve", lambda e, pb=pb, o=o: e.tensor_tensor(out=gsb[:, o:o + 512], in0=pb[0:BPC, :],
                                                                    in1=bg[:, o:o + 512], op=ALU.add),
                       r=pres + SR(8, 8), w=ACT_ALL)
                else:
                    for mm in range(4):
                        m = 4 * u + mm
                        pb = PS[bank % 8]
                        pres = [f"P{bank % 8}"]
                        bank += 1
                        for kc in range(8):
                            op("pe", lambda e, pb=pb, unit=unit, kc=kc, mm=mm: e.matmul(
                                pb[:, 0:BPC], unit[:, kc * 512 + mm * 128:kc * 512 + (mm + 1) * 128],
                                scb[:, kc * BPC:(kc + 1) * BPC], start=(kc == 0), stop=(kc == 7)),
                               r=ures + SCB, w=pres)
                        op("dve", lambda e, pb=pb, m=m: e.tensor_scalar(
                            out=modT[:, m * BPC:(m + 1) * BPC], in0=pb[:, 0:BPC], scalar1=badaT[:, m:m + 1], scalar2=None,
                            op0=ALU.add), r=pres + SR(5), w=["sgt0"])
            op("sp", lambda e: e.dma_start(out=gates_d[:, :], in_=gsb), r=ACT_ALL, w=["gates_d"], dma="p6")
            modT3 = modT.rearrange("p (m b) -> p m b", b=BPC)
            for b in range(BPC):
                for af, (m_sh, m_sc, gcol) in enumerate([(0, 8, 16), (24, 32, 24)]):
                    oA = ((b * 2 + af) * 2 + 0) * 8
                    oB = ((b * 2 + af) * 2 + 1) * 8
                    op("dve", lambda e, b=b, m_sc=m_sc, gcol=gcol, oA=oA: e.scalar_tensor_tensor(
                        out=AB[:, oA:oA + 8], in0=modT3[:, m_sc:m_sc + 8, b], scalar=1.0, in1=cols[:, gcol:gcol + 8],
                        op0=ALU.add, op1=ALU.mult), r=["sgt0", "cols"], w=["AB"])
                    op("dve", lambda e, b=b, m_sh=m_sh, oB=oB: e.tensor_copy(out=AB[:, oB:oB + 8],
                                                                             in_=modT3[:, m_sh:m_sh + 8, b]),
                       r=["sgt0"], w=["AB"])

            ck("ada")
            xcf = xc[:]
            pos = xcf[0:64, 0:2048]
            ang = xcf[0:64, 2048:4096]
            op("pool", lambda e: e.iota(pos, pattern=[[1, 2048]], base=0, channel_multiplier=0,
                                        allow_small_or_imprecise_dtypes=True), w=XC_ALL)
            op("pool", lambda e: e.iota(cols[0:32, 12:13], pattern=[[0, 1]], base=0, channel_multiplier=1,
                                        allow_small_or_imprecise_dtypes=True), w=["cols"])
            op("pool", lambda e: e.iota(cols[32:64, 12:13], pattern=[[0, 1]], base=0, channel_multiplier=1,
                                        allow_small_or_imprecise_dtypes=True), w=["cols"])
            op("act", lambda e: e.activation(out=cols[0:64, 13:14], in_=cols[0:64, 12:13], func=AF.Exp,
                                             scale=-math.log(10000.0) / 32.0), r=["cols"], w=["cols"])
            op("dve", lambda e: e.tensor_scalar(out=ang, in0=pos, scalar1=cols[0:64, 13:14], scalar2=None, op0=ALU.mult),
               r=XC_ALL + ["cols"], w=XC_ALL)
            TWO_PI = 2.0 * math.pi
            MAGIC = 12582912.0
            PI_LO = 3.141592
            for which, off in ((0, 0.5 * math.pi), (1, 0.0)):
                dst = Sf[0:64, which * 2048:(which + 1) * 2048]
                op("dve", lambda e, off=off: e.tensor_scalar(out=pos, in0=ang, scalar1=off, scalar2=1.0 / TWO_PI, op0=ALU.add,
                                                             op1=ALU.mult), r=XC_ALL, w=XC_ALL)
                op("dve", lambda e: e.tensor_scalar(out=pos, in0=pos, scalar1=MAGIC, scalar2=None, op0=ALU.add),
                   r=XC_ALL, w=XC_ALL)
                op("dve", lambda e: e.tensor_scalar(out=pos, in0=pos, scalar1=-MAGIC, scalar2=None, op0=ALU.add),
                   r=XC_ALL, w=XC_ALL)
                op("dve", lambda e: e.scalar_tensor_tensor(out=pos, in0=pos, scalar=-TWO_PI, in1=ang, op0=ALU.mult,
                                                           op1=ALU.add), r=XC_ALL, w=XC_ALL)
                op("dve", lambda e, off=off: e.tensor_scalar(out=pos, in0=pos, scalar1=off, scalar2=None, op0=ALU.add),
                   r=XC_ALL, w=XC_ALL)
                op("dve", lambda e: e.tensor_scalar(out=pos, in0=pos, scalar1=-PI_LO, scalar2=PI_LO,
                                                    op0=ALU.max, op1=ALU.min), r=XC_ALL, w=XC_ALL)
                op("act", lambda e, dst=dst: e.activation(out=dst, in_=pos, func=AF.Sin), r=XC_ALL, w=S_ALL)
            sgn_hi = Sf[32:64, 2048:4096]
            op("dve", lambda e: e.tensor_scalar(out=sgn_hi, in0=sgn_hi, scalar1=-1.0, scalar2=None, op0=ALU.mult),
               r=S_ALL, w=S_ALL)
            op("sp", lambda e: e.dma_start(out=rope_d[0], in_=Sf[0:64, 0:2048]), r=S_ALL, w=["rope_d"], dma="p7")
            op("sp", lambda e: e.dma_start(out=rope_d[1], in_=Sf[0:64, 2048:4096]), r=S_ALL, w=["rope_d"], dma="p7")

            ck("rope")
            dump("AB", AB[:], ["AB"])
            dump("modT", modT, ["sgt0"])
            dump("cols", cols[:], ["cols"])
            dump("alibi", alibi[:], ["alibi"])
            dump("tri", tri[:], ["tri"])
            dump("ident", ident[:], ["ident"])
            evac_rr = [0]

            def evac_copy(dst, src, r, w, force_act=False):
                evac_rr[0] += 1
                if force_act or evac_rr[0] % 2 == 0:
                    op("act", lambda e: e.activation(out=dst, in_=src, func=AF.Copy), r=r, w=w)
                else:
                    op("dve", lambda e: e.tensor_copy(out=dst, in_=src), r=r, w=w)

            def hn_buf(t):
                return S[:, t * 1024:(t + 1) * 1024], SR(2 * t, 2)

            def norm_sq(t, col0):
                xr = [f"xc{t}h0", f"xc{t}h1"]
                xt = xc[:, t * 1024:(t + 1) * 1024]
                hn, hres = hn_buf(t)
                sres = [f"st{t}"]
                op("dve", lambda e: e.memset(st[:, col0 + t:col0 + t + 1], 0.0), w=sres)
                op("act", lambda e: e.activation(out=hn, in_=xt, func=AF.Square, accum_out=st[:, col0 + t:col0 + t + 1]),
                   r=xr + sres, w=hres + sres)

            def norm_rstd(t0, n, col0):
                sres = [f"st{t}" for t in range(t0, t0 + n)]
                src_ = st[:, col0 + t0:col0 + t0 + n]
                dst_ = st[:, col0 + 4 + t0:col0 + 4 + t0 + n]
                op("act", lambda e: e.activation(out=dst_, in_=src_, func=AF.Ln, bias=cols[:, 15:16], scale=1.0 / D),
                   r=sres + ["cols"], w=sres)
                op("act", lambda e: e.activation(out=dst_, in_=dst_, func=AF.Exp, scale=-0.5), r=sres, w=sres)

            def norm_scale(t, col0):
                xr = [f"xc{t}h0", f"xc{t}h1"]
                xt = xc[:, t * 1024:(t + 1) * 1024]
                hn, hres = hn_buf(t)
                op("dve", lambda e: e.tensor_scalar(out=hn, in0=xt, scalar1=st[:, col0 + 4 + t:col0 + 5 + t], scalar2=None,
                                                    op0=ALU.mult), r=xr + [f"st{t}"], w=hres)

            def norm_tr(b, af, t):
                hn, hres = hn_buf(t)
                pb = PS[t % 2]
                pbb = pb[:].bitcast(BF16)
                pres = [f"P{t % 2}"]
                for kc in range(8):
                    op("pe", lambda e, kc=kc: e.transpose(pbb[:, kc * 128:(kc + 1) * 128],
                                                          hn[:, kc * 128:(kc + 1) * 128], ident[:]),
                       r=hres + ["ident"], w=pres)
                for kc in range(8):
                    dst = actT[:, kc * 512 + t * 128:kc * 512 + (t + 1) * 128]
                    src_ = pbb[:, kc * 128:(kc + 1) * 128]
                    a_ = AB_col(b, af, 0, kc)
                    b_ = AB_col(b, af, 1, kc)
                    if t % 2 == 0:
                        op("dve", lambda e, dst=dst, src_=src_, a_=a_, b_=b_: e.tensor_scalar(
                            out=dst, in0=src_, scalar1=a_, scalar2=b_, op0=ALU.mult, op1=ALU.add),
                           r=pres + ["AB"], w=[f"act{kc}t{t}"])
                    else:
                        op("act", lambda e, dst=dst, src_=src_, a_=a_, b_=b_: e.activation(
                            out=dst, in_=src_, func=AF.Identity, bias=b_, scale=a_),
                           r=pres + ["AB"], w=[f"act{kc}t{t}"])

            def lat_norm(banks, n, gc0, rstd_slot, dst_act0, inv_n, ssbank):
                for i in range(n):
                    sq = actT[:, (5 + i % 2) * 512:(6 + i % 2) * 512]
                    op("act", lambda e, sq=sq, i=i: e.activation(out=sq, in_=PS[banks[i]][:], func=AF.Square),
                       r=[f"P{banks[i]}"], w=ACT(5 + i % 2))
                    op("pe", lambda e, sq=sq, i=i: e.matmul(PS[ssbank][:], ones[:], sq, start=(i == 0), stop=(i == n - 1)),
                       r=ACT(5 + i % 2) + ["ones"], w=[f"P{ssbank}"])
                rs = slotf(rstd_slot)
                op("act", lambda e: e.activation(out=rs, in_=PS[ssbank][:], func=AF.Ln, bias=cols[:, 15:16], scale=inv_n),
                   r=[f"P{ssbank}", "cols"], w=SR(rstd_slot, 2))
                op("act", lambda e: e.activation(out=rs, in_=rs, func=AF.Exp, scale=-0.5), r=SR(rstd_slot, 2), w=SR(rstd_slot, 2))
                for i in range(n):
                    dst = actT[:, (dst_act0 + i) * 512:(dst_act0 + i + 1) * 512]
                    op("dve", lambda e, dst=dst, i=i: e.scalar_tensor_tensor(
                        out=dst, in0=PS[banks[i]][:], scalar=cols[:, gc0 + i:gc0 + i + 1], in1=rs, op0=ALU.mult,
                        op1=ALU.mult), r=[f"P{banks[i]}", "cols"] + SR(rstd_slot, 2), w=ACT(dst_act0 + i))

            cosT = Sf[0:64, 12 * 256:12 * 256 + 512]
            sgnT = Sf[0:64, 14 * 256:14 * 256 + 512]
            t2 = Sf[0:64, 20 * 256:20 * 256 + 512]

            def rope(pbank, dst, dres):
                pb = PS[pbank]
                pres = [f"P{pbank}"]
                op("dve", lambda e: e.tensor_tensor(out=t2[0:32, :], in0=pb[32:64, :], in1=sgnT[32:64, :], op=ALU.mult),
                   r=pres + SR(14, 2), w=SR(20, 2))
                op("dve", lambda e: e.tensor_tensor(out=t2[32:64, :], in0=pb[0:32, :], in1=sgnT[0:32, :], op=ALU.mult),
                   r=pres + SR(14, 2), w=SR(20, 2))
                op("dve", lambda e: e.tensor_tensor(out=pb[0:64, :], in0=pb[0:64, :], in1=cosT, op=ALU.mult),
                   r=pres + SR(12, 2), w=pres)
                op("dve", lambda e: e.tensor_tensor(out=dst, in0=pb[0:64, :], in1=t2, op=ALU.add),
                   r=pres + SR(20, 2), w=dres)

            ring_ctr = [0]

            def ring_next():
                s = ring_ctr[0] % 4
                ring_ctr[0] += 1
                return s

            RING_PF = 4
            QK_PF = 3
            ring_issued = [0]

            def ring_issue_gu(u):
                if u >= NJ:
                    return
                rs_ = u % 4
                ru = ring[:, rs_ * 2048:(rs_ + 1) * 2048]
                op("pool", lambda e: e.dma_start(out=ru, in_=wgu_s[u]), r=[f"wgu_s{u}"],
                   w=[f"R{rs_}a", f"R{rs_}b"], dma=f"rg{rs_}a")

            def ring_issue_dn(j):
                if j >= NJ:
                    return
                hs = j % 8
                ru = ring[:, hs * 1024:(hs + 1) * 1024]
                nm = f"R{hs // 2}{'ab'[hs % 2]}"
                op("pool", lambda e: e.dma_start(out=ru, in_=wdn_s[j * 128:(j + 1) * 128, :]),
                   r=[f"wdn_s{j}"], w=[nm], dma="rg" + nm[1:])

            pending_stores = []

            def flush_stores():
                for stag_, stres_, t_, tok0_ in pending_stores:
                    op("sp", lambda e, stag_=stag_, t_=t_, tok0_=tok0_: e.dma_start(
                        out=out_d[tok0_ + t_ * 128:tok0_ + (t_ + 1) * 128, :], in_=stag_),
                       r=stres_, w=[f"out{t_}"], dma=f"so{t_}")
                pending_stores.clear()

            QSCALE_D = 64 ** -0.5
            QSCALE_M = 192 ** -0.5

            for b in range(BPC):
                op("sp", lambda e, b=b: e.dma_start(out=Ga[:], in_=gates_d[b:b + 1, 0:1024].partition_broadcast(128)),
                   r=["gates_d"], w=["Ga"], dma="ga")
                op("sp", lambda e, b=b: e.dma_start(out=Gf[:], in_=gates_d[b:b + 1, 1024:2048].partition_broadcast(128)),
                   r=["gates_d"], w=["Gf"], dma="gf")
                for kc in range(8):
                    op("pool", lambda e, kc=kc: e.dma_start(out=W_out[:, kc * 1024:(kc + 1) * 1024], in_=wo_v[:, kc, :]),
                       w=[f"W_out{kc}"], dma=f"w{4 + kc % 2}")
                    op("pool", lambda e, kc=kc: e.tensor_tensor(out=W_out[:, kc * 1024:(kc + 1) * 1024],
                                                                in0=W_out[:, kc * 1024:(kc + 1) * 1024], in1=Ga[:], op=ALU.mult),
                       r=[f"W_out{kc}", "Ga"], w=[f"W_out{kc}"])
                if b == 0:
                    for j in range(NJ):
                        op("pool", lambda e, j=j: e.dma_start(out=wgu_s[j], in_=wgu_d[j]), w=[f"wgu_s{j}"], dma=f"pc{j % 4}")
                    for j in range(NJ):
                        op("pool", lambda e, j=j: e.dma_start(out=wdn_s[j * 128:(j + 1) * 128, :],
                                                              in_=wdn_d[j * 128:(j + 1) * 128, :]),
                           w=[f"wdn_s{j}"], dma=f"pc{j % 4}")
                ck("g")
                for c in range(NCH):
                    tok0 = b * SEQ + c * CH
                    for t in range(4):
                        op("sp", lambda e, t=t, tok0=tok0: e.dma_start(out=xc[:, t * 1024:(t + 1) * 1024],
                                                                      in_=x_d[tok0 + t * 128:tok0 + (t + 1) * 128, :]),
                           w=[f"xc{t}h0", f"xc{t}h1"], dma=f"ld{t}")
                    ck("xl")
                    for t in range(4):
                        norm_sq(t, 0)
                    flush_stores()
                    norm_rstd(0, 4, 0)
                    for t in range(4):
                        norm_scale(t, 0)
                    for t in range(4):
                        norm_tr(b, 0, t)
                    if b == 0 and c == 0:
                        dump("Ga", Ga[:], ["Ga"])
                        dump("hT", actT[:], ACT_ALL)
                        dump("st", st[:], [f"st{t}" for t in range(4)])
                    ck("s1")
                    rr = 0
                    for m in range(8):
                        k = 2 + rr % 6
                        rr += 1
                        for kc in range(8):
                            op("pe", lambda e, k=k, kc=kc, m=m: e.matmul(
                                PS[k][:], W_in[:, kc * INW + m * 128:kc * INW + (m + 1) * 128],
                                actT[:, kc * 512:(kc + 1) * 512], start=(kc == 0), stop=(kc == 7)),
                               r=["W_in"] + ACT(kc), w=[f"P{k}"])
                        if m < 4:
                            evac_copy(slot(m), PS[k][:], [f"P{k}"], SR(m))
                        else:
                            h = m - 4
                            evac_copy(dkT[:, h * SEQ + c * CH:h * SEQ + (c + 1) * CH], PS[k][:], [f"P{k}"], [f"dk{h}c{c}"])
                    for t in range(4):
                        k = 2 + rr % 6
                        rr += 1
                        for kc in range(8):
                            op("pe", lambda e, k=k, kc=kc, t=t: e.matmul(
                                PS[k][:], actT[:, kc * 512 + t * 128:kc * 512 + (t + 1) * 128],
                                W_in[:, kc * INW + 1024:kc * INW + 1536], start=(kc == 0), stop=(kc == 7)),
                               r=["W_in", f"act{kc}t{t}"], w=[f"P{k}"])
                        tile = c * 4 + t
                        evac_copy(dv[:, tile * 512:(tile + 1) * 512], PS[k][:], [f"P{k}"], [f"dv{tile}"])
                    if b == 0 and c == 0:
                        dump("dqT", slot(0, 4), SR(0, 4))
                        dump("dkT", dkT[:, 0:512], ["dk0c0"])
                        dump("dv0", dv[:, 0:512], ["dv0"])
                    ck("s2a")
                    lat_banks = [(0, 1536), (1, 1664), (2, 1792), (4, 1920), (5, 2048)]
                    for k, col0 in lat_banks:
                        for kc in range(8):
                            op("pe", lambda e, k=k, kc=kc, col0=col0: e.matmul(
                                PS[k][:], W_in[:, kc * INW + col0:kc * INW + col0 + 128], actT[:, kc * 512:(kc + 1) * 512],
                                start=(kc == 0), stop=(kc == 7)), r=["W_in"] + ACT(kc), w=[f"P{k}"])
                    for kc in range(8):
                        op("pe", lambda e, kc=kc: e.matmul(
                            PS[6][0:64, :], W_in[:, kc * INW + 2176:kc * INW + 2240], actT[:, kc * 512:(kc + 1) * 512],
                            start=(kc == 0), stop=(kc == 7)), r=["W_in"] + ACT(kc), w=["P6"])
                    lat_norm([0, 1, 2], 3, C_GQ, 16, 0, 1.0 / 384, 3)
                    lat_norm([4, 5], 2, C_GKV, 18, 3, 1.0 / 256, 7)
                    op("sp", lambda e, c=c: e.dma_start(out=cosT, in_=rope_d[0][:, c * CH:(c + 1) * CH]),
                       r=["rope_d"], w=SR(12, 2), dma="rp0")
                    op("sp", lambda e, c=c: e.dma_start(out=sgnT, in_=rope_d[1][:, c * CH:(c + 1) * CH]),
                       r=["rope_d"], w=SR(14, 2), dma="rp1")
                    ck("s2b")
                    for h in range(4):
                        for kc in range(3):
                            op("pe", lambda e, h=h, kc=kc: e.matmul(
                                PS[h][:], W_qup[:, kc * 768 + h * 128:kc * 768 + (h + 1) * 128],
                                actT[:, kc * 512:(kc + 1) * 512], start=(kc == 0), stop=(kc == 2)),
                               r=["W_qup"] + ACT(kc), w=[f"P{h}"])
                        evac_copy(slot(4 + h), PS[h][:], [f"P{h}"], SR(4 + h), force_act=True)
                    for h in range(4):
                        for kc in range(3):
                            op("pe", lambda e, h=h, kc=kc: e.matmul(
                                PS[h][0:64, :], W_qup[:, kc * 768 + 512 + h * 64:kc * 768 + 512 + (h + 1) * 64],
                                actT[:, kc * 512:(kc + 1) * 512], start=(kc == 0), stop=(kc == 2)),
                               r=["W_qup"] + ACT(kc), w=[f"P{h}"])
                    for h in range(4):
                        rope(h, S[0:64, (8 + h) * 512:(9 + h) * 512], SR(8 + h))
                    if b == 0 and c == 0:
                        dump("qnT", slot(4, 4), SR(4, 4))
                        dump("qpeT", S[0:64, 8 * 512:12 * 512], SR(8, 4), nparts=64)
                        dump("cosT", cosT, SR(12, 2), nparts=64)
                        dump("sgnT", sgnT, SR(14, 2), nparts=64)
                    ck("s3a")
                    kvb = [4, 5, 7, 4, 5, 7, 4, 5]
                    for h in range(4):
                        kb_ = kvb[h]
                        for kc in range(2):
                            op("pe", lambda e, h=h, kc=kc, kb_=kb_: e.matmul(
                                PS[kb_][:], W_kvup[:, kc * 1024 + h * 128:kc * 1024 + (h + 1) * 128],
                                actT[:, (3 + kc) * 512:(4 + kc) * 512], start=(kc == 0), stop=(kc == 1)),
                               r=["W_kvup"] + ACT(3 + kc), w=[f"P{kb_}"])
                        evac_copy(knT[:, h * SEQ + c * CH:h * SEQ + (c + 1) * CH], PS[kb_][:], [f"P{kb_}"], [f"kn{h}c{c}"], force_act=True)
                    for t in range(4):
                        kb_ = kvb[4 + t]
                        for kc in range(2):
                            op("pe", lambda e, t=t, kc=kc, kb_=kb_: e.matmul(
                                PS[kb_][:], actT[:, (3 + kc) * 512 + t * 128:(3 + kc) * 512 + (t + 1) * 128],
                                W_kvup[:, kc * 1024 + 512:kc * 1024 + 1024], start=(kc == 0), stop=(kc == 1)),
                               r=["W_kvup", f"act{3 + kc}t{t}"], w=[f"P{kb_}"])
                        tile = c * 4 + t
                        evac_copy(vv[:, tile * 512:(tile + 1) * 512], PS[kb_][:], [f"P{kb_}"], [f"v{tile}"], force_act=True)
                    rope(6, kpeT[0:64, c * CH:(c + 1) * CH], [f"kpe{c}"])
                    if b == 0 and c == 0:
                        dump("knT", knT[:, 0:512], ["kn0c0"])
                        dump("kpeT", kpeT[0:64, 0:512], ["kpe0"], nparts=64)
                        dump("vv0", vv[:, 0:512], ["v0"])
                    ck("s3b")
                    for u in range(RING_PF):
                        ring_issue_gu(u)
                    free_banks = [0, 1, 2, 3]

                    def ring_alloc():
                        return free_banks.pop(0)

                    def ring_free(k_):
                        free_banks.append(k_)

                    nkb = 4 * c + 4
                    pendA, pendS, pendB = [None], [None], [None]

                    def make_epilogue(mi, h, mm, is_diff, ob, rb):
                        obr, rbr = [f"P{ob}"], [f"P{rb}"]
                        rinv = slotf(16)
                        E1 = slotf(18)
                        E2 = slotf(20)
                        sq = actT[:, 7 * 512:8 * 512]

                        def partA():
                            op("dve", lambda e: e.reciprocal(out=rinv, in_=PS[rb][:]), r=rbr, w=SR(16, 2))
                            if not is_diff:
                                f = 4 + h
                                op("dve", lambda e: e.tensor_tensor(out=actT[:, f * 512:(f + 1) * 512], in0=PS[ob][:],
                                                                    in1=rinv, op=ALU.mult), r=obr + SR(16, 2), w=ACT(f))
                            elif mm == 0:
                                op("dve", lambda e: e.tensor_tensor(out=E1, in0=PS[ob][:], in1=rinv, op=ALU.mult),
                                   r=obr + SR(16, 2), w=SR(18, 2))
                            else:
                                op("dve", lambda e: e.tensor_tensor(out=E2, in0=PS[ob][:], in1=rinv, op=ALU.mult),
                                   r=obr + SR(16, 2), w=SR(20, 2))
                                op("dve", lambda e: e.scalar_tensor_tensor(out=E1, in0=E2, scalar=cols[:, C_NLAM:C_NLAM + 1],
                                                                           in1=E1, op0=ALU.mult, op1=ALU.add),
                                   r=SR(18, 4) + ["cols"], w=SR(18, 2))

                        def partS():
                            if is_diff and mm == 1:
                                op("act", lambda e: e.activation(out=sq, in_=E1, func=AF.Square), r=SR(18, 2), w=ACT(7))

                        def partB():
                            if not (is_diff and mm == 1):
                                return
                            ssb = ring_alloc()
                            op("pe", lambda e: e.matmul(PS[ssb][:], ones[:], sq, start=True, stop=True),
                               r=ACT(7) + ["ones"], w=[f"P{ssb}"])
                            op("act", lambda e: e.activation(out=E2, in_=PS[ssb][:], func=AF.Ln, bias=cols[:, 15:16],
                                                             scale=1.0 / 128), r=[f"P{ssb}", "cols"], w=SR(20, 2))
                            ring_free(ssb)
                            op("act", lambda e: e.activation(out=E2, in_=E2, func=AF.Exp, scale=-0.5), r=SR(20, 2), w=SR(20, 2))
                            op("dve", lambda e: e.scalar_tensor_tensor(
                                out=actT[:, h * 512:(h + 1) * 512], in0=E1, scalar=cols[:, C_GDP:C_GDP + 1], in1=E2,
                                op0=ALU.mult, op1=ALU.mult), r=SR(18, 4) + ["cols"], w=ACT(h))

                        return partA, partS, partB

                    for mi in range(12):
                        is_diff = mi < 8
                        if is_diff:
                            h, mm = mi // 2, mi % 2
                        else:
                            h, mm = mi - 8, 0
                        ob, rb = 4 + 2 * (mi % 2), 5 + 2 * (mi % 2)
                        obr, rbr = [f"P{ob}"], [f"P{rb}"]

                        def qk(kb, sbk):
                            qlo = max(0, kb - 4 * c) * 128
                            kc_ = kb // 4
                            if is_diff:
                                pr = slice(mm * 64, mm * 64 + 64)
                                op("pe", lambda e: e.matmul(
                                    PS[sbk][:, qlo:512], dkT[pr, h * SEQ + kb * 128:h * SEQ + kb * 128 + 128],
                                    S[pr, h * 512 + qlo:h * 512 + 512], start=True, stop=True),
                                   r=[f"dk{h}c{kc_}"] + SR(h), w=[f"P{sbk}"])
                            else:
                                op("pe", lambda e: e.matmul(
                                    PS[sbk][:, qlo:512], knT[:, h * SEQ + kb * 128:h * SEQ + kb * 128 + 128],
                                    S[:, (4 + h) * 512 + qlo:(4 + h) * 512 + 512], start=True, stop=False),
                                   r=[f"kn{h}c{kc_}"] + SR(4 + h), w=[f"P{sbk}"])
                                op("pe", lambda e: e.matmul(
                                    PS[sbk][:, qlo:512], kpeT[0:64, kb * 128:kb * 128 + 128],
                                    S[0:64, (8 + h) * 512 + qlo:(8 + h) * 512 + 512], start=False, stop=True),
                                   r=[f"kpe{kc_}"] + SR(8 + h), w=[f"P{sbk}"])

                        bank_of = {}
                        for kb0 in range(min(QK_PF, nkb)):
                            bank_of[kb0] = ring_alloc()
                            qk(kb0, bank_of[kb0])
                        for kb in range(nkb):
                            sbk = bank_of[kb]
                            if kb + QK_PF < nkb:
                                bank_of[kb + QK_PF] = ring_alloc()
                                qk(kb + QK_PF, bank_of[kb + QK_PF])
                            qb0 = max(0, kb - 4 * c)
                            qlo = qb0 * 128
                            pt = S[:, (12 + sbk) * 512:(13 + sbk) * 512]
                            ptr = SR(12 + sbk)
                            if is_diff:
                                groups = [(0, 1), (2, 3)] if h == 0 else [(0, 1, 2, 3)]
                                for g in groups:
                                    if g[-1] < qb0:
                                        continue
                                    lo, hi = max(g[0], qb0) * 128, (g[-1] + 1) * 128
                                    delta = (4 * c + g[0] + 1 - kb) if h == 0 else (4 * c + 3 - kb)
                                    op("act", lambda e, lo=lo, hi=hi, delta=delta: e.activation(
                                        out=pt[:, lo:hi], in_=PS[sbk][:, lo:hi], func=AF.Exp,
                                        bias=alibi[:, h * 16 + delta:h * 16 + delta + 1], scale=QSCALE_D),
                                       r=[f"P{sbk}", "alibi"], w=ptr)
                            else:
                                op("act", lambda e: e.activation(out=pt[:, qlo:512], in_=PS[sbk][:, qlo:512], func=AF.Exp,
                                                                 scale=QSCALE_M), r=[f"P{sbk}"], w=ptr)
                            ring_free(sbk)
                            if kb >= 4 * c:
                                op("pool", lambda e: e.tensor_tensor(out=pt[:, qlo:qlo + 128], in0=pt[:, qlo:qlo + 128],
                                                                     in1=tri[:], op=ALU.mult), r=ptr + ["tri"], w=ptr)
                            vsrc = dv if is_diff else vv
                            vres = f"dv{kb}" if is_diff else f"v{kb}"
                            op("pe", lambda e: e.matmul(
                                PS[ob][:, qlo:512], vsrc[:, kb * 512 + h * 128:kb * 512 + (h + 1) * 128], pt[:, qlo:512],
                                start=(kb == 0), stop=(kb == nkb - 1)), r=ptr + [vres], w=obr)
                            op("pe", lambda e: e.matmul(
                                PS[rb][:, qlo:512], ones[:], pt[:, qlo:512], start=(kb == 0), stop=(kb == nkb - 1)),
                               r=ptr + ["ones"], w=rbr)
                            if kb == 0 and pendA[0] is not None:
                                pendA[0]()
                                pendA[0] = None
                            if kb == min(5, nkb - 1) and pendS[0] is not None:
                                pendS[0]()
                                pendS[0] = None
                            if kb == min(8, nkb - 1) and pendB[0] is not None:
                                pendB[0]()
                                pendB[0] = None
                        pendA[0], pendS[0], pendB[0] = make_epilogue(mi, h, mm, is_diff, ob, rb)
                    pendA[0]()
                    pendS[0]()
                    pendB[0]()

                    if b == 0 and c == 0:
                        dump("mergedT", actT[:], ACT_ALL)
                    ck("s4")
                    for t in range(4):
                        for half in range(2):
                            k = t * 2 + half
                            for f in range(8):
                                op("pe", lambda e, f=f: e.matmul(
                                    PS[k][:], actT[:, f * 512 + t * 128:f * 512 + (t + 1) * 128],
                                    W_out[:, f * 1024 + half * 512:f * 1024 + (half + 1) * 512],
                                    start=(f == 0), stop=(f == 7)), r=[f"W_out{f}", f"act{f}t{t}"], w=[f"P{k}"])
                            xs = xc[:, t * 1024 + half * 512:t * 1024 + (half + 1) * 512]
                            op("dve", lambda e: e.tensor_tensor(out=xs, in0=PS[k][:], in1=xs, op=ALU.add),
                               r=[f"P{k}", f"xc{t}h{half}"], w=[f"xc{t}h{half}"])
                        norm_sq(t, 0)
                        norm_rstd(t, 1, 0)
                        norm_scale(t, 0)
                    for t in range(4):
                        norm_tr(b, 1, t)
                    if b == 0 and c == 0:
                        dump("x1", xc[:], XC_ALL)
                    ck("s5")
                    if b == 0 and c == 0:
                        dump("h2T", actT[:], ACT_ALL)
                    ck("s6")
                    for j in range(NJ):
                        rs_ = j % 4
                        rres = [f"R{rs_}a", f"R{rs_}b"]
                        ru = ring[:, rs_ * 2048:(rs_ + 1) * 2048]
                        pg, pu = 2 + 2 * (j % 3), 3 + 2 * (j % 3)
                        for kc in range(8):
                            op("pe", lambda e, pg=pg, ru=ru, kc=kc: e.matmul(
                                PS[pg][:], ru[:, kc * 128:(kc + 1) * 128], actT[:, kc * 512:(kc + 1) * 512],
                                start=(kc == 0), stop=(kc == 7)), r=rres + ACT(kc), w=[f"P{pg}"])
                        for kc in range(8):
                            op("pe", lambda e, pu=pu, ru=ru, kc=kc: e.matmul(
                                PS[pu][:], ru[:, 1024 + kc * 128:1024 + (kc + 1) * 128], actT[:, kc * 512:(kc + 1) * 512],
                                start=(kc == 0), stop=(kc == 7)), r=rres + ACT(kc), w=[f"P{pu}"])
                        sg = sgt[:, (j % 2) * 512:(j % 2 + 1) * 512]
                        op("act", lambda e, pg=pg, sg=sg: e.activation(out=sg, in_=PS[pg][:], func=AF.Silu),
                           r=[f"P{pg}"], w=[f"sgt{j % 2}"])
                        op("dve", lambda e, pu=pu, sg=sg, j=j: e.tensor_tensor(out=slot(j), in0=PS[pu][:], in1=sg, op=ALU.mult),
                           r=[f"P{pu}", f"sgt{j % 2}"], w=SR(j))
                        if j + RING_PF < NJ:
                            ring_issue_gu(j + RING_PF)
                        else:
                            ring_issue_dn(2 * rs_)
                            ring_issue_dn(2 * rs_ + 1)
                    if b == 0 and c == 0:
                        dump("aT", S[:], S_ALL)
                    ck("s7a")
                    for j in range(NJ):
                        hs = j % 8
                        rres = [f"R{hs // 2}{'ab'[hs % 2]}"]
                        ru = ring[:, hs * 1024:(hs + 1) * 1024]
                        op("dve", lambda e, ru=ru: e.tensor_tensor(out=ru, in0=ru, in1=Gf[:], op=ALU.mult),
                           r=rres + ["Gf"], w=rres)
                        for t in range(4):
                            for half in range(2):
                                k = t * 2 + half
                                op("pe", lambda e, k=k, t=t, half=half, j=j, ru=ru: e.matmul(
                                    PS[k][:], S[:, j * 512 + t * 128:j * 512 + (t + 1) * 128],
                                    ru[:, half * 512:(half + 1) * 512], start=(j == 0), stop=(j == NJ - 1)),
                                   r=rres + SR(j), w=[f"P{k}"])
                        ring_issue_dn(j + 8)
                    ck("s7b")
                    junk = S[:, 20 * 512:22 * 512]
                    stags = []
                    for t in range(4):
                        if t < 3:
                            stags.append((Sf[:, (8 + 4 * t) * 256:(8 + 4 * t) * 256 + 1024], SR(8 + 4 * t, 4)))
                        else:
                            stags.append((sgt[:, 0:1024], ["sgt0", "sgt1"]))
                    for t in range(4):
                        xr = [f"xc{t}h0", f"xc{t}h1"]
                        xt = xc[:, t * 1024:(t + 1) * 1024]
                        stag, stres = stags[t]
                        for half in range(2):
                            k = t * 2 + half
                            xs = xc[:, t * 1024 + half * 512:t * 1024 + (half + 1) * 512]
                            op("dve", lambda e: e.tensor_tensor(out=xs, in0=PS[k][:], in1=xs, op=ALU.add),
                               r=[f"P{k}", f"xc{t}h{half}"], w=[f"xc{t}h{half}"])
                        sres = [f"st{8 + t}"]
                        op("dve", lambda e: e.memset(st[:, 8 + t:9 + t], 0.0), w=sres)
                        op("act", lambda e: e.activation(out=junk, in_=xt, func=AF.Square, accum_out=st[:, 8 + t:9 + t]),
                           r=xr + sres, w=SR(20, 2) + sres)
                        op("act", lambda e: e.activation(out=st[:, 12 + t:13 + t], in_=st[:, 8 + t:9 + t], func=AF.Ln,
                                                         bias=cols[:, 15:16], scale=1.0 / D), r=sres + ["cols"], w=sres)
                        op("act", lambda e: e.activation(out=st[:, 12 + t:13 + t], in_=st[:, 12 + t:13 + t], func=AF.Exp,
                                                         scale=-0.5), r=sres, w=sres)
                    for t in range(4):
                        xr = [f"xc{t}h0", f"xc{t}h1"]
                        xt = xc[:, t * 1024:(t + 1) * 1024]
                        stag, stres = stags[t]
                        sres = [f"st{8 + t}"]
                        op("dve", lambda e: e.scalar_tensor_tensor(out=stag, in0=xt, scalar=st[:, 12 + t:13 + t], in1=gfin[:],
                                                                   op0=ALU.mult, op1=ALU.mult),
                           r=xr + sres + ["gfin"], w=stres)
                        pending_stores.append((stag, stres, t, tok0))

            flush_stores()

        nops, nwait = P.run(body)
        print(f"[kernel] ops={nops} waits={nwait}", flush=True)
    return nc


_NC_CACHE = {}


def _prep_shared(inp):
    f = np.float32
    w_qup = inp["w_q_up"][0]
    qcols = [h * 192 + i for h in range(4) for i in range(128)] + [h * 192 + 128 + i for h in range(4) for i in range(64)]
    w_kvup = inp["w_kv_up"][0]
    kvcols = [h * 256 + i for h in range(4) for i in range(128)] + [h * 256 + 128 + i for h in range(4) for i in range(128)]
    wg = inp["w_ffn_gate"][0].reshape(8, 128, NJ, 128)
    wu = inp["w_ffn_up"][0].reshape(8, 128, NJ, 128)
    wgu = np.stack([wg.transpose(2, 1, 0, 3), wu.transpose(2, 1, 0, 3)], axis=2)
    b_ada = inp["b_ada"][0]
    shared = {
        "w_ada": np.ascontiguousarray(inp["w_ada"][0], dtype=f),
        "b_adaT": np.ascontiguousarray(b_ada.reshape(48, 128).T, dtype=f),
        "b_gate": np.ascontiguousarray(np.concatenate([b_ada[2048:3072], b_ada[5120:6144]])[None, :], dtype=f),
        "g_mixT": np.ascontiguousarray(inp["g_mix"][0].reshape(8, 128).T, dtype=f),
        "g_ffnT": np.ascontiguousarray(inp["g_ffn"][0].reshape(8, 128).T, dtype=f),
        "w_in": np.ascontiguousarray(inp["w_in"][0], dtype=f),
        "lam": np.ascontiguousarray(np.concatenate([inp["lambda_q1"][0], inp["lambda_k1"][0],
                                                    inp["lambda_q2"][0], inp["lambda_k2"][0]])[None, :], dtype=f),
        "g_diff": np.ascontiguousarray(inp["g_diff_out"][0].reshape(128, 1), dtype=f),
        "g_qlT": np.ascontiguousarray(inp["g_q_lat"][0].reshape(3, 128).T, dtype=f),
        "w_qup": np.ascontiguousarray(w_qup[:, qcols], dtype=f),
        "g_kvlT": np.ascontiguousarray(inp["g_kv_lat"][0].reshape(2, 128).T, dtype=f),
        "w_kvup": np.ascontiguousarray(w_kvup[:, kvcols], dtype=f),
        "w_out": np.ascontiguousarray(inp["w_out"][0], dtype=f),
        "w_gu": np.ascontiguousarray(wgu.reshape(NJ, 128, 2048), dtype=f),
        "w_dn": np.ascontiguousarray(inp["w_ffn_down"][0], dtype=f),
        "g_final": np.ascontiguousarray(inp["g_final"].reshape(1, D), dtype=f),
    }
    return shared


def kernel(**inputs):
    inp = {k: np.asarray(v) for k, v in inputs.items()}
    if "nc" not in _NC_CACHE:
        _NC_CACHE["nc"] = build()
    nc = _NC_CACHE["nc"]
    shared = _prep_shared(inp)
    x = inp["x"].astype(np.float32, copy=False)
    c = inp["c"].astype(np.float32, copy=False)
    in_maps = []
    for core in range(NCORES):
        m = dict(shared)
        m["x"] = np.ascontiguousarray(x[core * BPC:(core + 1) * BPC].reshape(BPC * SEQ, D))
        cc = c[core * BPC:(core + 1) * BPC]
        m["cT"] = np.ascontiguousarray(cc.reshape(BPC, 8, 128).transpose(2, 1, 0).reshape(128, 8 * BPC))
        in_maps.append(m)
    res = run_bass_kernel_spmd(nc, in_maps, core_ids=list(range(NCORES)))
    if DBG:
        LAST_DBG["dbg"] = np.asarray(res.results[0]["dbg"])
        LAST_DBG["map"] = dict(DBG_MAP)
    outs = [np.asarray(r["out"]).reshape(BPC, SEQ, D) for r in res.results]
    return np.concatenate(outs, axis=0).astype(np.float32, copy=False)
```
